# Optimizing a Trainium2 kernel written in Bass

```python
import jax
import jax.numpy as jnp
from jax import lax
import numpy as np

D_MODEL = 1024
BATCH = 8
SEQ = 4096
DEPTH = 4

GRID_W = 64
CTX_LEN = 256
N_MOD = 9
D_FF = 2816
DEEPNORM_ALPHA = (2.0 * DEPTH) ** 0.25
DEEPNORM_BETA = (8.0 * DEPTH) ** -0.25
LN_EPS = 1e-6

GLA_HEADS = 4
GLA_DK = 32
GLA_DV = 64
GLA_GATE_RANK = 16
GLA_GATE_TAU = 16.0
GLA_CHUNK = 64
GLA_W = GLA_HEADS * GLA_DV

SWA_Q_HEADS = 8
SWA_KV_HEADS = 2
SWA_HEAD_DIM = 64
SWA_WINDOW = 128
SWA_BLOCK = 128
SWA_W = SWA_Q_HEADS * SWA_HEAD_DIM
ROPE_BASE = 10000.0

RWKV_HEADS = 4
RWKV_N = 64
RWKV_DECAY_RANK = 64
RWKV_A_RANK = 64
RWKV_GATE_RANK = 128
RWKV_GN_EPS = 64e-5
RWKV_W = RWKV_HEADS * RWKV_N
RWKV_SIZES = (RWKV_W, RWKV_W, RWKV_W, RWKV_DECAY_RANK, RWKV_DECAY_RANK,
              RWKV_A_RANK, RWKV_A_RANK, RWKV_GATE_RANK)
RWKV_IN = 3 * RWKV_W + 2 * RWKV_DECAY_RANK + 2 * RWKV_A_RANK + RWKV_GATE_RANK

IN_SIZES = (GLA_HEADS * GLA_DK, GLA_HEADS * GLA_DK, GLA_W, GLA_W, GLA_GATE_RANK, GLA_GATE_RANK,
            SWA_W, SWA_KV_HEADS * SWA_HEAD_DIM, SWA_KV_HEADS * SWA_HEAD_DIM, RWKV_IN)
D_IN = (2 * GLA_HEADS * GLA_DK + 2 * GLA_W + 2 * GLA_GATE_RANK
        + SWA_W + 2 * SWA_KV_HEADS * SWA_HEAD_DIM + RWKV_IN)
D_MIX = GLA_W + SWA_W + RWKV_W

kernel_name = 'hybrid_gla_swa_rwkv7_macaron_dit'


def split_cols(t, sizes):
    parts, start = [], 0
    for s in sizes:
        parts.append(t[..., start:start + s])
        start += s
    return parts


def flip(t):
    return t[:, ::-1]


def layer_norm(x, g, b):
    xf = x.astype(jnp.float32)
    mu = jnp.mean(xf, axis=-1, keepdims=True)
    var = jnp.mean(jnp.square(xf - mu), axis=-1, keepdims=True)
    y = (xf - mu) * lax.rsqrt(var + LN_EPS) * g.astype(jnp.float32) + b.astype(jnp.float32)
    return y.astype(x.dtype)


def post_norm(x, h, g, b):
    return layer_norm(DEEPNORM_ALPHA * x + h, g, b)


def adaln_params(cvec, w, b):
    m = jax.nn.silu(cvec) @ w + b
    return m.reshape(cvec.shape[0], 1, N_MOD, D_MODEL)


def modulate(h, m, i):
    return h * (1.0 + m[:, :, 3 * i + 1]) + m[:, :, 3 * i]


def swiglu(u, wg, wu, wd):
    return (jax.nn.silu(u @ wg) * (u @ wu)) @ wd


def half_ffn(h, m, i, wg, wu, wd, g, b):
    u = modulate(h, m, i)
    return post_norm(h, 0.5 * m[:, :, 3 * i + 2] * swiglu(u, wg, wu, wd), g, b)


def centred_shift(f):
    zero = jnp.zeros_like(f[:, :1])
    prev = jnp.concatenate([zero, f[:, :-1]], axis=1)
    nxt = jnp.concatenate([f[:, 1:], zero], axis=1)
    return 0.5 * (prev + nxt)


def axial_rope(t, row, col):
    half = t.shape[-1] // 2
    quarter = half // 2
    inv_freq = ROPE_BASE ** (-jnp.arange(quarter, dtype=jnp.float32) / quarter)

    def rotate(seg, pos):
        ang = pos.astype(jnp.float32)[:, None] * inv_freq
        cos = jnp.cos(ang)[None, :, None, :]
        sin = jnp.sin(ang)[None, :, None, :]
        s1 = seg[..., :quarter].astype(jnp.float32)
        s2 = seg[..., quarter:].astype(jnp.float32)
        return jnp.concatenate([s1 * cos - s2 * sin, s1 * sin + s2 * cos], axis=-1)

    out = jnp.concatenate([rotate(t[..., :half], row), rotate(t[..., half:], col)], axis=-1)
    return out.astype(t.dtype)


def sink_softmax(s, sink):
    sink = sink.astype(jnp.float32)
    m = jnp.maximum(jnp.max(s, axis=-1, keepdims=True), sink)
    p = jnp.exp(s - m)
    return p / (jnp.sum(p, axis=-1, keepdims=True) + jnp.exp(sink - m))


def bidirectional(scan_fn, fwd, bwd, s0_f, s0_b):
    y_f, s_f = scan_fn(*fwd, s0_f)
    y_b, s_b = scan_fn(*[flip(t) for t in bwd], s0_b)
    return y_f + flip(y_b), s_f, s_b


def gla_chunk_scan(q, k, v, log_g, s0):
    bsz, n, h, _ = q.shape
    dv = v.shape[-1]
    nc = n // GLA_CHUNK

    def chunks(t):
        t = t.astype(jnp.float32).reshape(bsz, nc, GLA_CHUNK, h, t.shape[-1])
        return jnp.moveaxis(t, 1, 0)

    pos = jnp.arange(GLA_CHUNK)
    lower = (pos[:, None] >= pos[None, :])[None, :, :, None, None]

    def step(S, inp):
        qc, kc, vc, gc = inp
        b = jnp.cumsum(gc, axis=1)
        rel = jnp.exp(jnp.where(lower, b[:, :, None] - b[:, None, :], -jnp.inf))
        att = jnp.einsum('bihd,bjhd,bijhd->bhij', qc, kc, rel)
        o = (jnp.einsum('bhij,bjhe->bihe', att, vc)
             + jnp.einsum('bihd,bhde->bihe', qc * jnp.exp(b), S))
        b_last = b[:, -1]
        S = (jnp.exp(b_last)[..., None] * S
             + jnp.einsum('bjhd,bjhe->bhde', kc * jnp.exp(b_last[:, None] - b), vc))
        return S, o

    S, o = lax.scan(step, s0, (chunks(q), chunks(k), chunks(v), chunks(log_g)))
    return jnp.moveaxis(o, 0, 1).reshape(bsz, n, h, dv), S


def gla_features(parts, gate_up, gate_bias):
    qa, ka, va, ga, zf, zb = parts
    bsz, n, _ = qa.shape
    q = qa.reshape(bsz, n, GLA_HEADS, GLA_DK) * (GLA_DK ** -0.5)
    k = ka.reshape(bsz, n, GLA_HEADS, GLA_DK)
    v = va.reshape(bsz, n, GLA_HEADS, GLA_DV)
    log_g = []
    for d, z in enumerate((zf, zb)):
        lg = jax.nn.log_sigmoid((z @ gate_up[d] + gate_bias[d]).astype(jnp.float32)) / GLA_GATE_TAU
        log_g.append(lg.reshape(bsz, n, GLA_HEADS, GLA_DK))
    return q, k, v, ga, log_g[0], log_g[1]


def gla_output(o, ga, norm_g):
    bsz, n = o.shape[:2]
    o = o * lax.rsqrt(jnp.mean(jnp.square(o), axis=-1, keepdims=True) + LN_EPS)
    o = o.reshape(bsz, n, GLA_W) * norm_g
    return (o * jax.nn.silu(ga)).astype(ga.dtype)


def swa_latent(q, k, v, k_ctx, v_ctx, sink):
    bsz, n, hq, hd = q.shape
    grp = hq // SWA_KV_HEADS
    nb = n // SWA_BLOCK
    scale = hd ** -0.5
    pad = ((0, 0), (SWA_BLOCK, SWA_BLOCK), (0, 0), (0, 0))
    kp, vp = jnp.pad(k, pad), jnp.pad(v, pad)
    qb = jnp.moveaxis(q.reshape(bsz, nb, SWA_BLOCK, SWA_KV_HEADS, grp, hd), 1, 0)
    sink_b = sink.reshape(SWA_KV_HEADS, grp)[None, :, :, None, None]

    def block(args):
        i, qi = args
        kb = lax.dynamic_slice_in_dim(kp, i * SWA_BLOCK, 3 * SWA_BLOCK, axis=1)
        vb = lax.dynamic_slice_in_dim(vp, i * SWA_BLOCK, 3 * SWA_BLOCK, axis=1)
        qpos = i * SWA_BLOCK + jnp.arange(SWA_BLOCK)
        kpos = (i - 1) * SWA_BLOCK + jnp.arange(3 * SWA_BLOCK)
        valid = ((jnp.abs(kpos[None, :] - qpos[:, None]) <= SWA_WINDOW)
                 & (kpos >= 0)[None, :] & (kpos < n)[None, :])
        s_loc = jnp.einsum('bqhgd,bkhd->bhgqk', qi, kb, preferred_element_type=jnp.float32) * scale
        s_loc = jnp.where(valid, s_loc, -jnp.inf)
        s_ctx = jnp.einsum('bqhgd,bkhd->bhgqk', qi, k_ctx, preferred_element_type=jnp.float32) * scale
        p = sink_softmax(jnp.concatenate([s_loc, s_ctx], axis=-1), sink_b).astype(v.dtype)
        return (jnp.einsum('bhgqk,bkhd->bqhgd', p[..., :3 * SWA_BLOCK], vb)
                + jnp.einsum('bhgqk,bkhd->bqhgd', p[..., 3 * SWA_BLOCK:], v_ctx))

    o = lax.map(block, (jnp.arange(nb), qb))
    return jnp.moveaxis(o, 0, 1).reshape(bsz, n, hq * hd)


def swa_context(q, k, v, sink):
    bsz, lc, hq, hd = q.shape
    grp = hq // SWA_KV_HEADS
    qg = q.reshape(bsz, lc, SWA_KV_HEADS, grp, hd)
    s = jnp.einsum('bqhgd,bkhd->bhgqk', qg, k, preferred_element_type=jnp.float32) * (hd ** -0.5)
    p = sink_softmax(s, sink.reshape(SWA_KV_HEADS, grp)[None, :, :, None, None]).astype(v.dtype)
    return jnp.einsum('bhgqk,bkhd->bqhgd', p, v).reshape(bsz, lc, hq * hd)


def rwkv7_scan(r, decay, k, v, kk, a, s0):
    def step(S, inp):
        r_t, w_t, k_t, v_t, kk_t, a_t = inp
        S = (S * w_t[:, :, None, :]
             - jnp.einsum('bhvk,bhk->bhv', S, kk_t)[..., None] * (kk_t * a_t)[:, :, None, :]
             + v_t[..., None] * k_t[:, :, None, :])
        return S, jnp.einsum('bhvk,bhk->bhv', S, r_t)

    xs = tuple(jnp.moveaxis(t.astype(jnp.float32), 1, 0) for t in (r, decay, k, v, kk, a))
    S, y = lax.scan(step, s0, xs)
    return jnp.moveaxis(y, 0, 1), S


def rwkv_features(f, mu, w0, w_up, a0, a_up, g_up, k_k, k_a):
    f = f + mu * (centred_shift(f) - f)
    r, k, v, zwf, zwb, zaf, zab, zg = split_cols(f, RWKV_SIZES)
    bsz, n, _ = f.shape

    def heads(t):
        return t.reshape(bsz, n, RWKV_HEADS, RWKV_N)

    kk = heads((k * k_k).astype(jnp.float32))
    kk = kk / jnp.maximum(jnp.sqrt(jnp.sum(jnp.square(kk), axis=-1, keepdims=True)), 1e-12)
    per_dir = []
    for d, (zw, za) in enumerate(((zwf, zaf), (zwb, zab))):
        w = -jax.nn.softplus(-(w0[d] + jnp.tanh(zw) @ w_up[d])) - 0.5
        decay = jnp.exp(-jnp.exp(w.astype(jnp.float32)))
        a = jax.nn.sigmoid(a0[d] + za @ a_up[d])
        k_d = k * (1.0 + (a - 1.0) * k_a)
        per_dir.append((heads(decay), heads(k_d), heads(a)))
    g = jax.nn.sigmoid(zg) @ g_up
    return heads(r), heads(v), kk, per_dir, g


def rwkv_mix(feats, s0_f, s0_b):
    r, v, kk, ((w_f, k_f, a_f), (w_b, k_b, a_b)), _ = feats
    return bidirectional(rwkv7_scan, (r, w_f, k_f, v, kk, a_f), (r, w_b, k_b, v, kk, a_b), s0_f, s0_b)


def rwkv_output(y, feats, r_k, gn_g, gn_b):
    r, v, _, ((_, k_f, _), (_, k_b, _)), g = feats
    bsz, n = y.shape[:2]
    mu = jnp.mean(y, axis=-1, keepdims=True)
    var = jnp.mean(jnp.square(y - mu), axis=-1, keepdims=True)
    yn = ((y - mu) * lax.rsqrt(var + RWKV_GN_EPS)).reshape(bsz, n, RWKV_W) * gn_g + gn_b
    bonus = jnp.sum(r * (0.5 * (k_f + k_b)) * r_k, axis=-1, keepdims=True) * v
    return ((yn + bonus.reshape(bsz, n, RWKV_W)) * g).astype(g.dtype)


def token_mixing(u_x, u_c, row, col, w_in, gla_up, gla_bias, gla_g, sink,
                 mu, w0, w_up, a0, a_up, g_up, k_k, k_a, r_k, gn_g, gn_b, need_ctx_out):
    bsz, n, _ = u_x.shape
    lc = u_c.shape[1]
    px = split_cols(u_x @ w_in, IN_SIZES)
    pc = split_cols(u_c @ w_in, IN_SIZES)

    qx, kx, vx, gx, lfx, lbx = gla_features(px[:6], gla_up, gla_bias)
    qc, kc, vc, gc, lfc, lbc = gla_features(pc[:6], gla_up, gla_bias)
    zero_a = jnp.zeros((bsz, GLA_HEADS, GLA_DK, GLA_DV), jnp.float32)
    oc, sa_f, sa_b = bidirectional(gla_chunk_scan, (qc, kc, vc, lfc), (qc, kc, vc, lbc), zero_a, zero_a)
    ox, _, _ = bidirectional(gla_chunk_scan, (qx, kx, vx, lfx), (qx, kx, vx, lbx), sa_f, sa_b)
    out_a = gla_output(ox, gx, gla_g)

    sq_x = axial_rope(px[6].reshape(bsz, n, SWA_Q_HEADS, SWA_HEAD_DIM), row, col)
    sk_x = axial_rope(px[7].reshape(bsz, n, SWA_KV_HEADS, SWA_HEAD_DIM), row, col)
    sv_x = px[8].reshape(bsz, n, SWA_KV_HEADS, SWA_HEAD_DIM)
    sk_c = pc[7].reshape(bsz, lc, SWA_KV_HEADS, SWA_HEAD_DIM)
    sv_c = pc[8].reshape(bsz, lc, SWA_KV_HEADS, SWA_HEAD_DIM)
    out_b = swa_latent(sq_x, sk_x, sv_x, sk_c, sv_c, sink)

    fx = rwkv_features(px[9], mu, w0, w_up, a0, a_up, g_up, k_k, k_a)
    fc = rwkv_features(pc[9], mu, w0, w_up, a0, a_up, g_up, k_k, k_a)
    zero_c = jnp.zeros((bsz, RWKV_HEADS, RWKV_N, RWKV_N), jnp.float32)
    yc, sc_f, sc_b = rwkv_mix(fc, zero_c, zero_c)
    yx, _, _ = rwkv_mix(fx, sc_f, sc_b)
    out_c = rwkv_output(yx, fx, r_k, gn_g, gn_b)

    mix_x = jnp.concatenate([out_a, out_b, out_c], axis=-1)
    if not need_ctx_out:
        return mix_x, None
    sq_c = pc[6].reshape(bsz, lc, SWA_Q_HEADS, SWA_HEAD_DIM)
    mix_c = jnp.concatenate([gla_output(oc, gc, gla_g),
                             swa_context(sq_c, sk_c, sv_c, sink),
                             rwkv_output(yc, fc, r_k, gn_g, gn_b)], axis=-1)
    return mix_x, mix_c


def setup_inputs(seed: int = 0) -> dict:
    key = jax.random.key(seed)
    ks = iter(jax.random.split(key, 32))
    L, D = DEPTH, D_MODEL

    def nrm(shape, scale):
        return scale * jax.random.normal(next(ks), shape, jnp.float32)

    return {
        'x': nrm((BATCH, SEQ, D), 1.0),
        'c': nrm((BATCH, D), 1.0),
        'ctx': nrm((BATCH, CTX_LEN, D), 1.0),
        'c_ctx': nrm((D,), 1.0),
        'w_ada': nrm((L, D, N_MOD * D), 0.5 * D ** -0.5),
        'b_ada': nrm((L, N_MOD * D), 0.02),
        'ffn1_wg': nrm((L, D, D_FF), D ** -0.5),
        'ffn1_wu': nrm((L, D, D_FF), D ** -0.5),
        'ffn1_wd': nrm((L, D_FF, D), DEEPNORM_BETA * D_FF ** -0.5),
        'ffn2_wg': nrm((L, D, D_FF), D ** -0.5),
        'ffn2_wu': nrm((L, D, D_FF), D ** -0.5),
        'ffn2_wd': nrm((L, D_FF, D), DEEPNORM_BETA * D_FF ** -0.5),
        'ln_g': 1.0 + nrm((L, 3, D), 0.02),
        'ln_b': nrm((L, 3, D), 0.02),
        'w_in': nrm((L, D, D_IN), D ** -0.5),
        'w_out': nrm((L, D_MIX, D), DEEPNORM_BETA * D_MIX ** -0.5),
        'gla_gate_up': nrm((L, 2, GLA_GATE_RANK, GLA_HEADS * GLA_DK), GLA_GATE_RANK ** -0.5),
        'gla_gate_bias': nrm((L, 2, GLA_HEADS * GLA_DK), 0.1),
        'gla_norm_g': 1.0 + nrm((L, GLA_W), 0.02),
        'swa_sink': nrm((L, SWA_Q_HEADS), 0.5),
        'rwkv_mu': jax.random.uniform(next(ks), (L, RWKV_IN), jnp.float32),
        'rwkv_w0': -2.0 + nrm((L, 2, RWKV_W), 0.5),
        'rwkv_w_up': nrm((L, 2, RWKV_DECAY_RANK, RWKV_W), 0.5 * RWKV_DECAY_RANK ** -0.5),
        'rwkv_a0': nrm((L, 2, RWKV_W), 0.1),
        'rwkv_a_up': nrm((L, 2, RWKV_A_RANK, RWKV_W), RWKV_A_RANK ** -0.5),
        'rwkv_g_up': nrm((L, RWKV_GATE_RANK, RWKV_W), RWKV_GATE_RANK ** -0.5),
        'rwkv_k_k': 0.85 + nrm((L, RWKV_W), 0.05),
        'rwkv_k_a': 1.0 + nrm((L, RWKV_W), 0.05),
        'rwkv_r_k': nrm((L, RWKV_HEADS, RWKV_N), 0.1),
        'rwkv_gn_g': 1.0 + nrm((L, RWKV_W), 0.02),
        'rwkv_gn_b': nrm((L, RWKV_W), 0.02),
    }


def reference(x, c, ctx, c_ctx, w_ada, b_ada, ffn1_wg, ffn1_wu, ffn1_wd, ffn2_wg, ffn2_wu, ffn2_wd,
              ln_g, ln_b, w_in, w_out, gla_gate_up, gla_gate_bias, gla_norm_g, swa_sink,
              rwkv_mu, rwkv_w0, rwkv_w_up, rwkv_a0, rwkv_a_up, rwkv_g_up, rwkv_k_k, rwkv_k_a,
              rwkv_r_k, rwkv_gn_g, rwkv_gn_b):
    n = x.shape[1]
    n_rows = n // GRID_W
    row = jnp.repeat(jnp.arange(n_rows, dtype=jnp.int32), GRID_W)
    col = jnp.tile(jnp.arange(GRID_W, dtype=jnp.int32), n_rows)
    h_x, h_c = x, ctx
    for l in range(DEPTH):
        last = l == DEPTH - 1
        m_x = adaln_params(c, w_ada[l], b_ada[l])
        m_c = adaln_params(c_ctx[None], w_ada[l], b_ada[l])

        h_x = half_ffn(h_x, m_x, 0, ffn1_wg[l], ffn1_wu[l], ffn1_wd[l], ln_g[l, 0], ln_b[l, 0])
        h_c = half_ffn(h_c, m_c, 0, ffn1_wg[l], ffn1_wu[l], ffn1_wd[l], ln_g[l, 0], ln_b[l, 0])

        mix_x, mix_c = token_mixing(
            modulate(h_x, m_x, 1), modulate(h_c, m_c, 1), row, col, w_in[l],
            gla_gate_up[l], gla_gate_bias[l], gla_norm_g[l], swa_sink[l],
            rwkv_mu[l], rwkv_w0[l], rwkv_w_up[l], rwkv_a0[l], rwkv_a_up[l], rwkv_g_up[l],
            rwkv_k_k[l], rwkv_k_a[l], rwkv_r_k[l], rwkv_gn_g[l], rwkv_gn_b[l], not last)
        h_x = post_norm(h_x, m_x[:, :, 5] * (mix_x @ w_out[l]), ln_g[l, 1], ln_b[l, 1])

        h_x = half_ffn(h_x, m_x, 2, ffn2_wg[l], ffn2_wu[l], ffn2_wd[l], ln_g[l, 2], ln_b[l, 2])
        if not last:
            h_c = post_norm(h_c, m_c[:, :, 5] * (mix_c @ w_out[l]), ln_g[l, 1], ln_b[l, 1])
            h_c = half_ffn(h_c, m_c, 2, ffn2_wg[l], ffn2_wu[l], ffn2_wd[l], ln_g[l, 2], ln_b[l, 2])
    return h_x
```

```python
import numpy as np
from contextlib import ExitStack
import concourse.bass as bass
import concourse.mybir as mybir
from concourse.bass_utils import run_bass_kernel_spmd

F32 = mybir.dt.float32
BF16 = mybir.dt.bfloat16
AF = mybir.ActivationFunctionType
ALU = mybir.AluOpType
AX = mybir.AxisListType

D = 1024
DFF = 2816
NFF = DFF // 128
DIN = 2720
ALPHA = 8.0 ** 0.25
LN_EPS = 1e-6
GN_EPS = 64e-5
C_ID = 0
C_TRI = 128
C_M01 = 384
C_GBD = 640
C_GHM = 896
C_RBD = 900
C_RHM = 1412
C_BO = 1414
C_SWM = 1542
C_NI = 1798
C_END = 1862


class Prog:
    def __init__(self, nc, es):
        self.nc = nc
        self.es = es
        self.eng = {"pe": nc.tensor, "act": nc.scalar, "dve": nc.vector, "pool": nc.gpsimd, "sp": nc.sync}
        self.sem = {e: es.enter_context(nc.semaphore("s_" + e)) for e in ("pe", "act", "dve", "pool")}
        self.cnt = {e: 0 for e in self.sem}
        self.known = {e: {} for e in self.eng}
        self.lastw = {}
        self.rd = {}
        self.NDS = 12
        self.dsem = {q: [es.enter_context(nc.semaphore("d_%s%d" % (q, i))) for i in range(self.NDS)]
                     for q in ("sp", "pool", "act")}
        self.dcnt = {q: 0 for q in self.dsem}
        self.semobj = {}
        for e in self.sem:
            self.semobj[e] = self.sem[e]
        for q in self.dsem:
            for i, s in enumerate(self.dsem[q]):
                self.semobj[(q, i)] = s
        self.ntiles = 0
        self.ninst = 0

    def sb(self, shape, dt=F32, name=None):
        self.ntiles += 1
        return self.es.enter_context(self.nc.sbuf_tensor("%s_%d" % (name or "t", self.ntiles), list(shape), dt))

    def ps(self, shape, dt=F32, name=None):
        self.ntiles += 1
        return self.es.enter_context(self.nc.psum_tensor("%s_%d" % (name or "p", self.ntiles), list(shape), dt))

    def _wait(self, e, ev):
        s, v = ev
        if self.known[e].get(s, 0) >= v:
            return
        self.eng[e].wait_ge(self.semobj[s], v)
        self.known[e][s] = v
        self.ninst += 1

    def _deps(self, e, reads, writes):
        for k in reads:
            ev = self.lastw.get(k)
            if ev is not None and not (e == "pe" and ev[0] == "pe"):
                self._wait(e, ev)
        for k in writes:
            ev = self.lastw.get(k)
            if ev is not None and not (e == "pe" and ev[0] == "pe"):
                self._wait(e, ev)
            for ev in self.rd.get(k, {}).values():
                if ev[0] == e:
                    continue
                self._wait(e, ev)

    def _record(self, ev, reads, writes):
        for k in reads:
            self.rd.setdefault(k, {})[ev[0]] = ev
        for k in writes:
            self.lastw[k] = ev
            self.rd[k] = {}

    def op(self, e, fn, *args, reads=(), writes=(), inc=True, **kw):
        self._deps(e, reads, writes)
        inst = getattr(self.eng[e], fn)(*args, **kw)
        self.ninst += 1
        ev = (e, self.cnt[e] + 1)
        if inc:
            inst.then_inc(self.sem[e], 1)
            self.cnt[e] += 1
        self._record(ev, reads, writes)
        return inst

    def dma(self, q, out, in_, reads=(), writes=(), **kw):
        self._deps(q, reads, writes)
        j = self.dcnt[q]
        self.dcnt[q] += 1
        slot = j % self.NDS
        s = (q, slot)
        need = 16 * (j // self.NDS)
        if need > 0:
            self._wait(q, (s, need))
        inst = self.eng[q].dma_start(out=out, in_=in_, **kw)
        inst.then_inc(self.dsem[q][slot], 16)
        self.ninst += 1
        ev = (s, need + 16)
        self._record(ev, reads, writes)
        return ev

    def barrier(self):
        evs = [(e, self.cnt[e]) for e in self.sem if self.cnt[e] > 0]
        for q in self.dsem:
            j = self.dcnt[q]
            for slot in range(self.NDS):
                n = (j - slot + self.NDS - 1) // self.NDS if j > slot else 0
                if n > 0:
                    evs.append(((q, slot), 16 * n))
        for e in self.eng:
            for ev in evs:
                self._wait(e, ev)
        self.lastw = {}
        self.rd = {}

    def finish(self, e, evs):
        for ev in evs:
            self._wait(e, ev)


def _ffn_pieces():
    out = []
    f = 0
    while f < NFF:
        w = min(4, NFF - f)
        out.append((f, w))
        f += w
    return out


def build(SEQ, LC, DEPTH, dbg=None):
    nc = bass.Bass("TRN2", target_bir_lowering=False)
    es0 = ExitStack()
    P = Prog(nc, es0)
    dram = lambda name, shape, dt=F32, kind="ExternalInput": nc.dram_tensor(name, list(shape), dt, kind=kind).ap()
    L = DEPTH
    T = LC + SEQ
    NTL = T // 128
    NCH = T // 64
    xT = dram("xT", [8, 128, SEQ])
    cxT = dram("cxT", [8, 128, LC])
    ccT = dram("ccT", [128, 8, 2])
    w_ada = dram("w_ada", [L, D, 9 * D])
    b_adaT = dram("b_adaT", [128, L, 72])
    lnT = dram("lnT", [128, L, 3, 2, 8])
    ffn_w = {}
    for nm in ("ffn1_wg", "ffn1_wu", "ffn2_wg", "ffn2_wu"):
        ffn_w[nm] = dram(nm, [L, D, DFF])
    for nm in ("ffn1_wd", "ffn2_wd"):
        ffn_w[nm] = dram(nm, [L, DFF, D])
    w_in = dram("w_in", [L, D, DIN])
    w_out = dram("w_out", [L, D, D])
    consts = dram("consts", [128, C_END])
    ropeM = dram("ropeM", [SEQ, 2, 32])
    gla_up = dram("gla_up", [L, 2, 17, 128])
    gla_g = dram("gla_g", [L, 256])
    sinkB = dram("sinkB", [128, L, 8])
    rw_vec = dram("rw_vec", [128, L, 2, 7])
    rw_mu = dram("rw_mu", [128, L, 11])
    rw_wup = dram("rw_wup", [L, 2, 64, 256])
    rw_aup = dram("rw_aup", [L, 2, 64, 256])
    rw_gup = dram("rw_gup", [L, 128, 256])
    rw_gn = dram("rw_gn", [L, 2, 256])
    outT = dram("outT", [8, 128, SEQ], kind="ExternalOutput")
    HX = dram("HX", [8, 128, SEQ], kind="Internal")
    HC = dram("HC", [8, 128, LC], kind="Internal")
    PF = dram("PF", [DIN, T], kind="Internal")
    PM = dram("PM", [T, DIN], kind="Internal")
    MIX = dram("MIX", [T, D], kind="Internal")
    OG = dram("OG", [T, 256], kind="Internal")
    RF = dram("RF", [21, 128, T], kind="Internal")
    YD = dram("YD", [2, T, 256], kind="Internal")

    TG = 512
    groups = [("c", 0, LC)] + [("x", t0, min(TG, SEQ - t0)) for t0 in range(0, SEQ, TG)]

    def toff(st, t0):
        return t0 if st == "c" else LC + t0

    cst = P.sb([128, C_END], F32, "cst")
    P.dma("sp", cst[:], consts, writes=["cst"])
    ident = cst[:, C_ID:C_ID + 128]
    onesm = P.sb([128, 128], F32, "onesm")
    P.op("dve", "memset", onesm[:], 1.0 / D, writes=["onesm"])
    cc = P.sb([128, 8, 2], F32, "cc")
    P.dma("sp", cc[:], ccT, writes=["cc"])
    sil = P.sb([128, 8, 2], F32, "sil")
    P.op("act", "activation", sil[:], cc[:], AF.Silu, reads=["cc"], writes=["sil"])
    bada = P.sb([128, L, 72], F32, "bada")
    P.dma("sp", bada[:], b_adaT, writes=["bada"])
    lnp = P.sb([128, L, 3, 2, 8], F32, "lnp")
    P.dma("sp", lnp[:], lnT, writes=["lnp"])
    mT = P.sb([128, L, 72, 2], F32, "mT")
    kc = P.sb([128, 8], F32, "kc")
    for i, val in enumerate([LN_EPS / (ALPHA * ALPHA), 1.0, -0.5, LN_EPS, GN_EPS, 0.0, 1e-24]):
        P.op("dve", "memset", kc[:, i:i + 1], val, writes=["kc"])
    epsb = kc[:, 0:1]
    scl = P.sb([128, 4, 8], F32, "scl")

    pG = [P.ps([128, 512], F32, "pG%d" % i) for i in range(2)]
    pU = [P.ps([128, 512], F32, "pU%d" % i) for i in range(2)]
    pD = [P.ps([128, 512], F32, "pD%d" % i) for i in range(2)]
    pS = [P.ps([128, 512], F32, "pS%d" % i) for i in range(2)]

    es = ExitStack()
    P.es = es
    wa = [P.sb([128, 8, 512], F32, "wa%d" % i) for i in range(2)]
    nblk = 0
    for l in range(L):
        pm = pS[l % 2]
        for cb in range(18):
            wt = wa[nblk % 2]
            wk = "wa%d" % (nblk % 2)
            nblk += 1
            src = w_ada[l, :, cb * 512:(cb + 1) * 512].rearrange("(k p) j -> p k j", p=128)
            P.dma("sp", wt[:], src, writes=[wk])
            for jj in range(4):
                j = cb * 4 + jj
                for k in range(8):
                    P.op("pe", "matmul", pm[:, 2 * j:2 * j + 2], wt[:, k, jj * 128:(jj + 1) * 128], sil[:, k, :],
                         start=(k == 0), stop=(k == 7), reads=[wk, "sil"], writes=["pS%d" % (l % 2)],
                         inc=(k == 7 and jj == 3))
        P.op("dve", "tensor_tensor", mT[:, l, :, :], pm[:, 0:144].rearrange("p (j s) -> p j s", s=2),
             bada[:, l, :].unsqueeze(2).to_broadcast([128, 72, 2]), ALU.add,
             reads=["pS%d" % (l % 2), "bada"], writes=["mT"])
    for gi, (st, t0, n) in enumerate(groups):
        b = wa[gi % 2]
        bk = "wa%d" % (gi % 2)
        src = (cxT if st == "c" else xT)[:, :, t0:t0 + n].rearrange("c p t -> p c t")
        P.dma("sp", b[:, :, :n], src, writes=[bk])
        P.dma("sp", (HC if st == "c" else HX)[:, :, t0:t0 + n].rearrange("c p t -> p c t"), b[:, :, :n],
              reads=[bk], writes=[("H", st, t0)])
    P.barrier()
    es.close()

    def hsrc(st, t0, n):
        return (HC if st == "c" else HX)[:, :, t0:t0 + n].rearrange("c p t -> p c t")

    cnts = {"g": 0, "wgu": 0, "wd": 0, "ps": 0}

    def mod_scalars(l, sub, s):
        j0 = 3 * sub * 8
        coef = (0.5 if sub != 1 else 1.0) / ALPHA
        P.op("dve", "tensor_scalar", scl[:, 0, :], mT[:, l, j0 + 8:j0 + 16, s], 1.0, None, ALU.add,
             reads=["mT"], writes=["scl0"])
        P.op("dve", "tensor_copy", scl[:, 1, :], mT[:, l, j0:j0 + 8, s], reads=["mT"], writes=["scl1"])
        P.op("dve", "tensor_scalar", scl[:, 2, :], mT[:, l, j0 + 16:j0 + 24, s], coef, None, ALU.mult,
             reads=["mT"], writes=["scl2"])

    def alloc_ln_tiles():
        t = {}
        t["vv"] = P.sb([128, 8, TG], F32, "vv")
        t["yo"] = P.sb([128, 8, TG], F32, "yo")
        t["vsq"] = [P.sb([128, TG], F32, "vsq%d" % i) for i in range(2)]
        t["mean_sb"] = P.sb([128, TG], F32, "mean_sb")
        t["msq"] = P.sb([128, TG], F32, "msq")
        t["var"] = P.sb([128, TG], F32, "var")
        t["rstd"] = P.sb([128, TG], F32, "rstd")
        t["tt"] = [P.sb([128, TG], F32, "tt%d" % i) for i in range(2)]
        return t

    def layer_norm_store(t, st, t0, n, l, sub):
        vv, yo, vsq, mean_sb, msq, var, rstd, tt = (t[k] for k in ("vv", "yo", "vsq", "mean_sb", "msq", "var", "rstd", "tt"))
        pmean, pev2 = pS[0], pS[1]
        for c in range(8):
            q = vsq[c % 2]
            qk = "vsq%d" % (c % 2)
            P.op("act", "activation", q[:, :n], vv[:, c, :n], AF.Square, reads=[("vv", c)], writes=[qk])
            P.op("pe", "matmul", pmean[:, :n], onesm[:], vv[:, c, :n], start=(c == 0), stop=(c == 7),
                 reads=["onesm", ("vv", c)], writes=["pS0"], inc=(c == 7))
            P.op("pe", "matmul", pev2[:, :n], onesm[:], q[:, :n], start=(c == 0), stop=(c == 7),
                 reads=["onesm", qk], writes=["pS1"], inc=True)
        P.op("act", "activation", mean_sb[:, :n], pmean[:, :n], AF.Copy, reads=["pS0"], writes=["mean_sb"])
        P.op("dve", "tensor_tensor", msq[:, :n], mean_sb[:, :n], mean_sb[:, :n], ALU.mult,
             reads=["mean_sb"], writes=["msq"])
        P.op("dve", "tensor_tensor", var[:, :n], pev2[:, :n], msq[:, :n], ALU.subtract,
             reads=["pS1", "msq"], writes=["var"])
        P.op("act", "activation", var[:, :n], var[:, :n], AF.Sqrt, bias=epsb, reads=["var", "kc"], writes=["var"])
        P.op("dve", "reciprocal", rstd[:, :n], var[:, :n], reads=["var"], writes=["rstd"])
        for c in range(8):
            tq = tt[c % 2]
            tk = "tt%d" % (c % 2)
            P.op("dve", "tensor_tensor", tq[:, :n], vv[:, c, :n], mean_sb[:, :n], ALU.subtract,
                 reads=[("vv", c), "mean_sb"], writes=[tk])
            P.op("dve", "tensor_tensor", tq[:, :n], tq[:, :n], rstd[:, :n], ALU.mult,
                 reads=[tk, "rstd"], writes=[tk])
            P.op("act", "activation", yo[:, c, :n], tq[:, :n], AF.Identity,
                 scale=lnp[:, l, sub, 0, c:c + 1], bias=lnp[:, l, sub, 1, c:c + 1],
                 reads=[tk, "lnp"], writes=[("yo", c)])
        return P.dma("sp", hsrc(st, t0, n), yo[:, :, :n], reads=[("yo", c) for c in range(8)],
                     writes=[("H", st, t0)])

    def ffn_sublayer(l, sub, wg_ap, wu_ap, wd_ap, do_ctx=True):
        es = ExitStack()
        P.es = es
        hT = [P.sb([128, 8, TG], F32, "hT%d" % i) for i in range(2)]
        uT = P.sb([128, 8, TG], BF16, "uT")
        hff = P.sb([128, NFF, TG], BF16, "hff")
        lt = alloc_ln_tiles()
        vv = lt["vv"]
        sg = [P.sb([128, TG], F32, "sg%d" % i) for i in range(2)]
        wgu = [P.sb([128, 2, 8, 512], BF16, "wgu%d" % i) for i in range(2)]
        wd = [P.sb([128, NFF, 128], BF16, "wd%d" % i) for i in range(3)]
        cur_s = None
        for (st, t0, n) in groups:
            if st == "c" and not do_ctx:
                continue
            s = 1 if st == "c" else 0
            if s != cur_s:
                mod_scalars(l, sub, s)
                cur_s = s
            gi = cnts["g"]
            cnts["g"] += 1
            h = hT[gi % 2]
            hk = "hT%d" % (gi % 2)
            P.dma("sp", h[:, :, :n], hsrc(st, t0, n), reads=[("H", st, t0)], writes=[hk])
            for c in range(8):
                P.op("act", "activation", uT[:, c, :n], h[:, c, :n], AF.Identity,
                     scale=scl[:, 0, c:c + 1], bias=scl[:, 1, c:c + 1],
                     reads=[hk, "scl0", "scl1"], writes=[("uT", c)])
            for (f0, fw) in _ffn_pieces():
                wi = cnts["wgu"] % 2
                cnts["wgu"] += 1
                wt = wgu[wi]
                P.dma("pool", wt[:, 0, :, :fw * 128],
                      wg_ap[:, f0 * 128:(f0 + fw) * 128].rearrange("(k p) j -> p k j", p=128), writes=[("wgu", wi, 0)])
                P.dma("pool", wt[:, 1, :, :fw * 128],
                      wu_ap[:, f0 * 128:(f0 + fw) * 128].rearrange("(k p) j -> p k j", p=128), writes=[("wgu", wi, 1)])
                for ff in range(fw):
                    f = f0 + ff
                    pi = cnts["ps"] % 2
                    cnts["ps"] += 1
                    for k in range(8):
                        P.op("pe", "matmul", pG[pi][:, :n], wt[:, 0, k, ff * 128:(ff + 1) * 128], uT[:, k, :n],
                             start=(k == 0), stop=(k == 7), reads=[("wgu", wi, 0), ("uT", k)], writes=["pG%d" % pi],
                             inc=(k == 7))
                    for k in range(8):
                        P.op("pe", "matmul", pU[pi][:, :n], wt[:, 1, k, ff * 128:(ff + 1) * 128], uT[:, k, :n],
                             start=(k == 0), stop=(k == 7), reads=[("wgu", wi, 1), ("uT", k)], writes=["pU%d" % pi],
                             inc=(k == 7))
                    P.op("act", "activation", sg[pi][:, :n], pG[pi][:, :n], AF.Silu,
                         reads=["pG%d" % pi], writes=["sg%d" % pi])
                    P.op("dve", "tensor_tensor", hff[:, f, :n], sg[pi][:, :n], pU[pi][:, :n], ALU.mult,
                         reads=["sg%d" % pi, "pU%d" % pi], writes=[("hff", f)])
            for c in range(8):
                wi = cnts["wd"] % 3
                cnts["wd"] += 1
                P.dma("pool", wd[wi][:], wd_ap[:, c * 128:(c + 1) * 128].rearrange("(f p) j -> p f j", p=128),
                      writes=[("wd", wi)])
                pi = c % 2
                for f in range(NFF):
                    P.op("pe", "matmul", pD[pi][:, :n], wd[wi][:, f, :], hff[:, f, :n],
                         start=(f == 0), stop=(f == NFF - 1), reads=[("wd", wi), ("hff", f)], writes=["pD%d" % pi],
                         inc=(f == NFF - 1))
                P.op("dve", "scalar_tensor_tensor", vv[:, c, :n], pD[pi][:, :n], scl[:, 2, c:c + 1], h[:, c, :n],
                     ALU.mult, ALU.add, reads=["pD%d" % pi, "scl2", hk], writes=[("vv", c)])
            layer_norm_store(lt, st, t0, n, l, sub)
        P.barrier()
        es.close()

    def mm(out, lhsT, rhs, reads, writes, start=True, stop=True, inc=True):
        P.op("pe", "matmul", out, lhsT, rhs, start=start, stop=stop, reads=reads, writes=writes, inc=inc)

    def act(out, in_, func, reads, writes, **kw):
        P.op("act", "activation", out, in_, func, reads=reads, writes=writes, **kw)

    def tt_(out, a, b, op, reads, writes, e="dve"):
        P.op(e, "tensor_tensor", out, a, b, op, reads=reads, writes=writes)

    def ts_(out, a, s1, s2, op0, op1, reads, writes, e="dve"):
        if op1 is None:
            P.op(e, "tensor_scalar", out, a, s1, None, op0, reads=reads, writes=writes)
        else:
            P.op(e, "tensor_scalar", out, a, s1, s2, op0, op1, reads=reads, writes=writes)

    def stt(out, a, s, b, op0, op1, reads, writes):
        P.op("dve", "scalar_tensor_tensor", out, a, s, b, op0, op1, reads=reads, writes=writes)

    def tr(out, in_, reads, writes, np_=128, inc=True):
        P.op("pe", "transpose", out, in_, ident[:np_, :np_], reads=list(reads) + ["cst"], writes=writes, inc=inc)

    def in_pieces():
        out = []
        c = 0
        while c < DIN:
            w = min(512, DIN - c)
            out.append((c, w))
            c += w
        return out

    def inproj(l, do_ctx=True):
        es = ExitStack()
        P.es = es
        hT = [P.sb([128, 8, TG], F32, "hT%d" % i) for i in range(2)]
        uT = P.sb([128, 8, TG], BF16, "uT")
        wt_ = [P.sb([128, 8, 512], BF16, "wi%d" % i) for i in range(2)]
        stg = [P.sb([128, 512], F32, "stg%d" % i) for i in range(4)]
        nst = [0]
        cur_s = None
        for (st, t0, n) in groups:
            s = 1 if st == "c" else 0
            off = toff(st, t0)
            if s != cur_s:
                mod_scalars(l, 1, s)
                cur_s = s
            gi = cnts["g"]
            cnts["g"] += 1
            h = hT[gi % 2]
            hk = "hT%d" % (gi % 2)
            P.dma("sp", h[:, :, :n], hsrc(st, t0, n), reads=[("H", st, t0)], writes=[hk])
            for c in range(8):
                act(uT[:, c, :n], h[:, c, :n], AF.Identity, [hk, "scl0", "scl1"], [("uT", c)],
                    scale=scl[:, 0, c:c + 1], bias=scl[:, 1, c:c + 1])
            for (c0, w) in in_pieces():
                wi = cnts["wgu"] % 2
                cnts["wgu"] += 1
                wt = wt_[wi]
                wk = ("wi", wi)
                P.dma("pool", wt[:, :, :w], w_in[l][:, c0:c0 + w].rearrange("(k p) j -> p k j", p=128), writes=[wk])
                j = 0
                while j < w:
                    cw = min(128, w - j)
                    pi = cnts["ps"] % 2
                    cnts["ps"] += 1
                    for k in range(8):
                        mm(pG[pi][:cw, :n], wt[:, k, j:j + cw], uT[:, k, :n], [wk, ("uT", k)], ["pG%d" % pi],
                           start=(k == 0), stop=(k == 7), inc=(k == 7))
                    si = nst[0] % 4
                    nst[0] += 1
                    act(stg[si][:cw, :n], pG[pi][:cw, :n], AF.Copy, ["pG%d" % pi], [("stg", si)])
                    P.dma("sp", PF[c0 + j:c0 + j + cw, off:off + n], stg[si][:cw, :n], reads=[("stg", si)],
                          writes=[("PF", off)])
                    j += cw
                for ts in range(n // 128):
                    pi = cnts["ps"] % 2
                    cnts["ps"] += 1
                    for k in range(8):
                        mm(pU[pi][:, :w], uT[:, k, ts * 128:(ts + 1) * 128], wt[:, k, :w], [wk, ("uT", k)],
                           ["pU%d" % pi], start=(k == 0), stop=(k == 7), inc=(k == 7))
                    si = nst[0] % 4
                    nst[0] += 1
                    P.op("dve", "tensor_copy", stg[si][:, :w], pU[pi][:, :w], reads=["pU%d" % pi], writes=[("stg", si)])
                    P.dma("sp", PM[off + ts * 128:off + (ts + 1) * 128, c0:c0 + w], stg[si][:, :w],
                          reads=[("stg", si)], writes=[("PM", off)])
        P.barrier()
        es.close()

    def swa(l, do_ctx):
        es = ExitStack()
        P.es = es
        kT_all = P.sb([64, 2, T], BF16, "kT_all")
        V_all = P.sb([128, NTL, 2, 65], BF16, "V_all")
        esink = P.sb([128, 8], F32, "esink")
        sk = P.sb([128, L, 8], F32, "sk")
        P.dma("sp", sk[:], sinkB, writes=["sk"])
        act(esink[:], sk[:, l, :], AF.Exp, ["sk"], ["esink"])
        P.op("dve", "memset", V_all[:], 1.0, writes=["V_all"])
        km = [P.sb([128, 2, 64], F32, "km%d" % i) for i in range(2)]
        vm = [P.sb([128, 2, 64], F32, "vm%d" % i) for i in range(2)]
        rp = [P.sb([128, 2, 32], F32, "rp%d" % i) for i in range(2)]
        kr = P.sb([128, 2, 64], F32, "kr")
        qm = [P.sb([128, 8, 64], F32, "qm%d" % i) for i in range(2)]
        qr = P.sb([128, 8, 64], F32, "qr")
        ta = P.sb([128, 8, 32], F32, "ta")
        tb = P.sb([128, 8, 32], F32, "tb")
        qT = P.sb([64, 8, 128], BF16, "qT")
        pT = [P.sb([128, 512], BF16, "pT%d" % i) for i in range(10)]
        den = P.sb([128, 4], F32, "den")
        mixb = [P.sb([128, 512], F32, "mixb%d" % i) for i in range(2)]

        def rope(dst, src, nh, tab, rk_src, rk_dst, rk_tab):
            sv = src.rearrange("p h (a s i) -> p h a s i", a=2, s=2)
            dv = dst.rearrange("p h (a s i) -> p h a s i", a=2, s=2)
            cos = tab[:, 0, :].rearrange("p (a i) -> p a i", a=2).unsqueeze(1).to_broadcast([128, nh, 2, 16])
            sin = tab[:, 1, :].rearrange("p (a i) -> p a i", a=2).unsqueeze(1).to_broadcast([128, nh, 2, 16])
            tav = ta[:, :nh, :].rearrange("p h (a i) -> p h a i", a=2)
            tbv = tb[:, :nh, :].rearrange("p h (a i) -> p h a i", a=2)
            tt_(tav, sv[:, :, :, 0, :], cos, ALU.mult, [rk_src, rk_tab], ["ta"])
            tt_(tbv, sv[:, :, :, 1, :], sin, ALU.mult, [rk_src, rk_tab], ["tb"])
            tt_(dv[:, :, :, 0, :], tav, tbv, ALU.subtract, ["ta", "tb"], [rk_dst])
            tt_(tav, sv[:, :, :, 0, :], sin, ALU.mult, [rk_src, rk_tab], ["ta"])
            tt_(tbv, sv[:, :, :, 1, :], cos, ALU.mult, [rk_src, rk_tab], ["tb"])
            tt_(dv[:, :, :, 1, :], tav, tbv, ALU.add, ["ta", "tb"], [rk_dst])

        for tt in range(NTL):
            i2 = tt % 2
            r0 = tt * 128
            P.dma("sp", km[i2][:], PM[r0:r0 + 128, 1312:1440].rearrange("t (h d) -> t h d", h=2),
                  reads=[("PM", 0)], writes=[("km", i2)])
            P.dma("sp", vm[i2][:], PM[r0:r0 + 128, 1440:1568].rearrange("t (h d) -> t h d", h=2),
                  reads=[("PM", 0)], writes=[("vm", i2)])
            P.op("dve", "tensor_copy", V_all[:, tt, :, 0:64], vm[i2][:], reads=[("vm", i2)], writes=["V_all"])
            if r0 >= LC:
                P.dma("sp", rp[i2][:], ropeM[r0 - LC:r0 - LC + 128], writes=[("rp", i2)])
                rope(kr[:], km[i2][:], 2, rp[i2], ("km", i2), "kr", ("rp", i2))
                ksrc, kk_ = kr, "kr"
            else:
                ksrc, kk_ = km[i2], ("km", i2)
            for hk in range(2):
                tr(pS[0][:64, hk * 128:(hk + 1) * 128], ksrc[:, hk, :], [kk_], ["pS0"], inc=(hk == 1))
            act(kT_all[:, :, r0:r0 + 128], pS[0][:64, 0:256].rearrange("p (h t) -> p h t", h=2), AF.Copy,
                ["pS0"], ["kT_all"])
        nmix = 0
        for tt in range(NTL):
            r0 = tt * 128
            isx = r0 >= LC
            if not isx and not do_ctx:
                continue
            i2 = tt % 2
            P.dma("sp", qm[i2][:], PM[r0:r0 + 128, 800:1312].rearrange("t (h d) -> t h d", h=8),
                  reads=[("PM", 0)], writes=[("qm", i2)])
            if isx:
                P.dma("sp", rp[i2][:], ropeM[r0 - LC:r0 - LC + 128], writes=[("rp", i2)])
                rope(qr[:], qm[i2][:], 8, rp[i2], ("qm", i2), "qr", ("rp", i2))
                qsrc, qk_ = qr, "qr"
            else:
                qsrc, qk_ = qm[i2], ("qm", i2)
            for half in range(2):
                for hh in range(4):
                    tr(pS[half][:64, hh * 128:(hh + 1) * 128], qsrc[:, half * 4 + hh, :], [qk_], ["pS%d" % half],
                       inc=(hh == 3))
                act(qT[:, half * 4:half * 4 + 4, :], pS[half][:64, :].rearrange("p (h t) -> p h t", h=4), AF.Copy,
                    ["pS%d" % half], ["qT"])
            keys = []
            if isx:
                nct = LC // 128
                if tt - 1 >= nct:
                    keys.append((tt - 1, 0))
                keys.append((tt, None))
                if tt + 1 < NTL:
                    keys.append((tt + 1, 1))
            keys += [(c, None) for c in range(LC // 128)]
            mb = mixb[nmix % 2]
            mbk = ("mixb", nmix % 2)
            nmix += 1
            for hk in range(2):
                for ki, (kt, mid) in enumerate(keys):
                    pi = cnts["ps"] % 2
                    cnts["ps"] += 1
                    for g in range(4):
                        mm(pG[pi][:, g * 128:(g + 1) * 128], kT_all[:, hk, kt * 128:(kt + 1) * 128], qT[:, hk * 4 + g, :],
                           ["kT_all", "qT"], ["pG%d" % pi], inc=(g == 3))
                    pt = pT[hk * 5 + ki]
                    ptk = ("pT", hk * 5 + ki)
                    act(pt[:], pG[pi][:], AF.Exp, ["pG%d" % pi], [ptk], scale=0.125)
                    if mid is not None:
                        mk = cst[:, C_SWM + mid * 128:C_SWM + (mid + 1) * 128].unsqueeze(1).to_broadcast([128, 4, 128])
                        tt_(pt[:].rearrange("p (g q) -> p g q", g=4), pt[:].rearrange("p (g q) -> p g q", g=4), mk,
                            ALU.mult, [ptk, "cst"], [ptk])
                po = pD[hk]
                for g in range(4):
                    for ki, (kt, mid) in enumerate(keys):
                        mm(po[:, g * 65:(g + 1) * 65], pT[hk * 5 + ki][:, g * 128:(g + 1) * 128], V_all[:, kt, hk, :],
                           [("pT", hk * 5 + ki), "V_all"], ["pD%d" % hk], start=(ki == 0), stop=(ki == len(keys) - 1),
                           inc=(ki == len(keys) - 1 and g == 3))
                pov = po[:, 0:260].rearrange("p (g e) -> p g e", g=4)
                tt_(den[:], pov[:, :, 64], esink[:, hk * 4:hk * 4 + 4], ALU.add, ["pD%d" % hk, "esink"], ["den"])
                P.op("dve", "reciprocal", den[:], den[:], reads=["den"], writes=["den"])
                tt_(mb[:, hk * 256:(hk + 1) * 256].rearrange("p (g e) -> p g e", g=4), pov[:, :, 0:64],
                    den[:].unsqueeze(2).to_broadcast([128, 4, 64]), ALU.mult, ["pD%d" % hk, "den"], [mbk])
            P.dma("sp", MIX[r0:r0 + 128, 256:768], mb[:], reads=[mbk], writes=[("MIX", "b", tt)])
        P.barrier()
        es.close()

    def run_interleaved(gens):
        gens = list(gens)
        while gens:
            for g in list(gens):
                try:
                    next(g)
                except StopIteration:
                    gens.remove(g)

    def chunk_order(d):
        nc_c = LC // 64
        if d == 0:
            return list(range(NCH))
        return list(range(nc_c - 1, -1, -1)) + list(range(NCH - 1, nc_c - 1, -1))

    def gla(l, do_ctx):
        es = ExitStack()
        P.es = es
        upa = P.sb([17, 2, 128], F32, "upa")
        P.dma("sp", upa[:], gla_up[l].rearrange("d r c -> r d c"), writes=["upa"])
        gng = P.sb([64, 256], F32, "gng")
        P.dma("sp", gng[:], gla_g[l].partition_broadcast(64), writes=["gng"])
        Sx = [P.sb([128, 256], F32, "Sx%d" % d) for d in range(2)]
        for d in range(2):
            P.op("dve", "memset", Sx[d][:], 0.0, writes=[("Sx", d)])
        triI = [cst[0:64, C_TRI + d * 64:C_TRI + (d + 1) * 64] for d in range(2)]
        incl = [cst[0:64, C_M01 + d * 64:C_M01 + (d + 1) * 64] for d in range(2)]
        gbd = cst[:, C_GBD:C_GBD + 256]
        pb = {0: (pG[0], "pG0", pU[0], "pU0", pD[0], "pD0"), 1: (pG[1], "pG1", pU[1], "pU1", pD[1], "pD1")}

        def unit(d):
            pa, pak, pbb, pbk, pc_, pck = pb[d]
            tl = {}
            for nm, shp in (("qT", [128, 64]), ("kT", [128, 64]), ("zT", [17, 64]), ("kM", [64, 128]), ("vM", [64, 256]),
                            ("gaM", [64, 256]), ("ofM", [64, 256])):
                tl[nm] = [P.sb(shp, F32, "g%s%d_%d" % (nm, d, i)) for i in range(2)]
            for i in range(2):
                P.op("dve", "memset", tl["zT"][i][:], 1.0, writes=[("zT", d, i)])
            sp_ = P.sb([64, 128], F32, "gsp%d" % d)
            ebT = P.sb([128, 64], F32, "gebT%d" % d)
            enbT = P.sb([128, 64], F32, "genbT%d" % d)
            enbM = P.sb([64, 128], F32, "genbM%d" % d)
            qs = P.sb([128, 64], F32, "gqs%d" % d)
            kmk = P.sb([128, 4, 64], F32, "gkmk%d" % d)
            KtM = P.sb([64, 128], F32, "gKtM%d" % d)
            att = P.sb([64, 4, 64], F32, "gatt%d" % d)
            tmpS = P.sb([128, 256], F32, "gtmpS%d" % d)
            osb = P.sb([64, 4, 64], F32, "gosb%d" % d)
            osq = P.sb([64, 4, 64], F32, "gosq%d" % d)
            ssq = P.sb([64, 4], F32, "gssq%d" % d)
            sga = P.sb([64, 256], F32, "gsga%d" % d)
            K = lambda s: (s, d)
            for ui, c in enumerate(chunk_order(d)):
                i2 = ui % 2
                r0 = c * 64
                isx = r0 >= LC
                KB = lambda s: (s, d, i2)
                P.dma("sp", tl["qT"][i2][:], PF[0:128, r0:r0 + 64], reads=[("PF", 0)], writes=[KB("qT")])
                P.dma("sp", tl["kT"][i2][:], PF[128:256, r0:r0 + 64], reads=[("PF", 0)], writes=[KB("kT")])
                P.dma("sp", tl["zT"][i2][0:16, :], PF[768 + 16 * d:784 + 16 * d, r0:r0 + 64], reads=[("PF", 0)],
                      writes=[KB("zT")])
                P.dma("sp", tl["kM"][i2][:], PM[r0:r0 + 64, 128:256], reads=[("PM", 0)], writes=[KB("kM")])
                P.dma("sp", tl["vM"][i2][:], PM[r0:r0 + 64, 256:512], reads=[("PM", 0)], writes=[KB("vM")])
                fin = (d == 1) and (isx or do_ctx)
                if fin:
                    P.dma("sp", tl["gaM"][i2][:], PM[r0:r0 + 64, 512:768], reads=[("PM", 0)], writes=[KB("gaM")])
                    P.dma("sp", tl["ofM"][i2][:], OG[r0:r0 + 64, :], reads=[("OG", c)], writes=[KB("ofM")])
                qT, kT, zT, kM, vM = (tl[n_][i2] for n_ in ("qT", "kT", "zT", "kM", "vM"))
                yield
                mm(pa[:64, 0:128], zT[:], upa[:, d, :], [KB("zT"), "upa"], [pak])
                act(sp_[:], pa[:64, 0:128], AF.Exp, [pak], [K("sp")], scale=-1.0)
                act(sp_[:], sp_[:], AF.Ln, [K("sp"), "kc"], [K("sp")], bias=kc[:64, 1:2])
                yield
                mm(pa[:, 128:192], sp_[:], triI[d], [K("sp"), "cst"], [pak], inc=False)
                mm(pa[:64, 256:384], triI[d], sp_[:], [K("sp"), "cst"], [pak])
                act(ebT[:], pa[:, 128:192], AF.Exp, [pak], [K("ebT")], scale=1.0 / 16)
                act(enbT[:], pa[:, 128:192], AF.Exp, [pak], [K("enbT")], scale=-1.0 / 16)
                act(enbM[:], pa[:64, 256:384], AF.Exp, [pak], [K("enbM")], scale=-1.0 / 16)
                yield
                stt(qs[:], qT[:], 32.0 ** -0.5, ebT[:], ALU.mult, ALU.mult, [KB("qT"), K("ebT")], [K("qs")])
                for h in range(4):
                    stt(kmk[:, h, :], kT[:], cst[:, C_GHM + h:C_GHM + h + 1], enbT[:], ALU.mult, ALU.mult,
                        [KB("kT"), K("enbT"), "cst"], [K("kmk")])
                tt_(KtM[:], kM[:], enbM[:], ALU.mult, [KB("kM"), K("enbM")], [K("KtM")])
                yield
                for h in range(4):
                    mm(pbb[:64, h * 64:(h + 1) * 64], kmk[:, h, :], qs[:], [K("kmk"), K("qs")], [pbk], inc=(h == 3))
                tt_(att[:], pbb[:64, 0:256].rearrange("p (h t) -> p h t", h=4),
                    incl[d].unsqueeze(1).to_broadcast([64, 4, 64]), ALU.mult, [pbk, "cst"], [K("att")])
                yield
                for h in range(4):
                    mm(pc_[:64, h * 64:(h + 1) * 64], att[:, h, :], vM[:, h * 64:(h + 1) * 64], [K("att"), KB("vM")], [pck],
                       start=True, stop=False, inc=False)
                    mm(pc_[:64, h * 64:(h + 1) * 64], qs[:], Sx[d][:, h * 64:(h + 1) * 64], [K("qs"), ("Sx", d)], [pck],
                       start=False, stop=True, inc=(h == 3))
                mm(pbb[:, 256:512], KtM[:], vM[:], [K("KtM"), KB("vM")], [pbk])
                yield
                tt_(tmpS[:], pbb[:, 256:512], gbd, ALU.mult, [pbk, "cst"], [K("tmpS")])
                tt_(tmpS[:], tmpS[:], Sx[d][:], ALU.add, [K("tmpS"), ("Sx", d)], [K("tmpS")])
                last = 63 if d == 0 else 0
                ts_(Sx[d][:], tmpS[:], ebT[:, last:last + 1], None, ALU.mult, None, [K("tmpS"), K("ebT")], [("Sx", d)])
                if d == 0:
                    P.op("dve", "tensor_copy", osb[:].rearrange("p h e -> p (h e)"), pc_[:64, 0:256], reads=[pck],
                         writes=[K("osb")])
                    P.dma("sp", OG[r0:r0 + 64, :], osb[:].rearrange("p h e -> p (h e)"), reads=[K("osb")],
                          writes=[("OG", c)])
                elif fin:
                    gaM, ofM = tl["gaM"][i2], tl["ofM"][i2]
                    tt_(osb[:].rearrange("p h e -> p (h e)"), pc_[:64, 0:256], ofM[:], ALU.add, [pck, KB("ofM")],
                        [K("osb")])
                    tt_(osq[:], osb[:], osb[:], ALU.mult, [K("osb")], [K("osq")])
                    P.op("dve", "tensor_reduce", ssq[:], osq[:], AX.X, ALU.add, reads=[K("osq")], writes=[K("ssq")])
                    act(ssq[:], ssq[:], AF.Sqrt, [K("ssq"), "kc"], [K("ssq")], scale=1.0 / 64, bias=kc[:64, 3:4])
                    P.op("dve", "reciprocal", ssq[:], ssq[:], reads=[K("ssq")], writes=[K("ssq")])
                    tt_(osb[:], osb[:], ssq[:].unsqueeze(2).to_broadcast([64, 4, 64]), ALU.mult, [K("osb"), K("ssq")],
                        [K("osb")])
                    tt_(osb[:].rearrange("p h e -> p (h e)"), osb[:].rearrange("p h e -> p (h e)"), gng[:], ALU.mult,
                        [K("osb"), "gng"], [K("osb")])
                    act(sga[:], gaM[:], AF.Silu, [KB("gaM")], [K("sga")])
                    tt_(osb[:].rearrange("p h e -> p (h e)"), osb[:].rearrange("p h e -> p (h e)"), sga[:], ALU.mult,
                        [K("osb"), K("sga")], [K("osb")])
                    P.dma("sp", MIX[r0:r0 + 64, 0:256], osb[:].rearrange("p h e -> p (h e)"), reads=[K("osb")],
                          writes=[("MIX", "a", c)])
                yield

        run_interleaved([unit(0)])
        run_interleaved([unit(1)])
        P.barrier()
        es.close()

    def rwkv_prep(l):
        es = ExitStack()
        P.es = es
        rv = P.sb([128, 2, 7], F32, "rv")
        P.dma("sp", rv[:], rw_vec[:, l], writes=["rv"])
        nw0 = P.sb([128, 2, 2], F32, "nw0")
        ts_(nw0[:], rv[:, :, 3:5], -1.0, None, ALU.mult, None, ["rv"], ["nw0"])
        mu = P.sb([128, 11], F32, "mu")
        P.dma("sp", mu[:], rw_mu[:, l], writes=["mu"])
        wup = P.sb([64, 2, 256], F32, "wup")
        P.dma("sp", wup[:], rw_wup[l].rearrange("d r c -> r d c"), writes=["wup"])
        aup = P.sb([64, 2, 256], F32, "aup")
        P.dma("sp", aup[:], rw_aup[l].rearrange("d r c -> r d c"), writes=["aup"])
        bo = cst[:, C_BO:C_BO + 128]
        NB = 16
        bufs = [P.sb([128, TG + 2], F32, "rb%d" % i) for i in range(NB)]
        fbufs = [P.sb([128, TG + 2], F32, "rf%d" % i) for i in range(13)]
        nb = [0]

        def newb():
            i = nb[0] % NB
            nb[0] += 1
            return bufs[i], ("rb", i)

        R0 = 1568
        srcs = [(R0 + 128 * j, 128) for j in range(6)] + [(R0 + 768, 64), (R0 + 832, 64), (R0 + 896, 64), (R0 + 960, 64),
                                                          (R0 + 1024, 128)]
        for (st, t0, n) in groups:
            off = toff(st, t0)
            seq_lo = 0 if st == "c" else LC
            seq_hi = LC if st == "c" else T
            lo = max(off - 1, seq_lo)
            hi = min(off + n + 1, seq_hi)
            f = []
            for j, (r0, nr) in enumerate(srcs):
                ld, ldk = newb()
                P.op("dve", "memset", ld[:, 0:n + 2], 0.0, writes=[ldk])
                P.dma("sp", ld[:nr, lo - off + 1:hi - off + 1], PF[r0:r0 + nr, lo:hi], reads=[("PF", 0)], writes=[ldk])
                sh, shk = newb()
                tt_(sh[:nr, :n], ld[:nr, 0:n], ld[:nr, 2:n + 2], ALU.add, [ldk], [shk])
                stt(sh[:nr, :n], sh[:nr, :n], 0.5, ld[:nr, 1:n + 1], ALU.mult, ALU.subtract, [shk, ldk], [shk])
                fo, fok = fbufs[j], ("rf", j)
                stt(fo[:nr, :n], sh[:nr, :n], mu[:nr, j:j + 1], ld[:nr, 1:n + 1], ALU.mult, ALU.add, [shk, ldk, "mu"], [fok])
                f.append((fo, fok))
            rr, kk_, vv_ = f[0:2], f[2:4], f[4:6]
            zw, za, zg = f[6:8], f[8:10], f[10]
            for pc in range(2):
                P.dma("sp", RF[0 + pc, :, off:off + n], rr[pc][0][:, :n], reads=[rr[pc][1]], writes=[("RF", off)])
                P.dma("sp", RF[2 + pc, :, off:off + n], vv_[pc][0][:, :n], reads=[vv_[pc][1]], writes=[("RF", off)])
            kkn = []
            for pc in range(2):
                k_, kk2 = kk_[pc]
                sq, sqk = newb()
                act(sq[:, :n], k_[:, :n], AF.Square, [kk2, "rv"], [sqk], scale=rv[:, pc, 0:1])
                mm(pS[0][:, :n], bo, sq[:, :n], ["cst", sqk], ["pS0"])
                act(sq[:, :n], pS[0][:, :n], AF.Sqrt, ["pS0"], [sqk])
                ts_(sq[:, :n], sq[:, :n], 1e-12, None, ALU.max, None, [sqk], [sqk])
                P.op("dve", "reciprocal", sq[:, :n], sq[:, :n], reads=[sqk], writes=[sqk])
                kn, knk = fbufs[11 + pc], ("rf", 11 + pc)
                stt(kn[:, :n], k_[:, :n], rv[:, pc, 0:1], sq[:, :n], ALU.mult, ALU.mult, [kk2, "rv", sqk], [knk])
                P.dma("sp", RF[4 + pc, :, off:off + n], kn[:, :n], reads=[knk], writes=[("RF", off)])
                kkn.append((kn, knk))
            ksum = [None, None]
            for d in range(2):
                th, thk = newb()
                act(th[:64, :n], zw[d][0][:64, :n], AF.Tanh, [zw[d][1]], [thk])
                for pc in range(2):
                    mm(pS[1][:, :n], wup[:, d, pc * 128:(pc + 1) * 128], th[:64, :n], ["wup", thk], ["pS1"])
                    ew, ewk = newb()
                    act(ew[:, :n], pS[1][:, :n], AF.Exp, ["pS1", "nw0"], [ewk], scale=-1.0, bias=nw0[:, pc, d:d + 1])
                    act(ew[:, :n], ew[:, :n], AF.Ln, [ewk, "kc"], [ewk], bias=kc[:, 1:2])
                    act(ew[:, :n], ew[:, :n], AF.Exp, [ewk, "kc"], [ewk], scale=-1.0, bias=kc[:, 2:3])
                    P.dma("sp", RF[10 + 6 * d + pc, :, off:off + n], ew[:, :n], reads=[ewk], writes=[("RF", off)])
                    mm(pS[0][:, :n], aup[:, d, pc * 128:(pc + 1) * 128], za[d][0][:64, :n], ["aup", za[d][1]], ["pS0"])
                    a_, ak = newb()
                    act(a_[:, :n], pS[0][:, :n], AF.Sigmoid, ["pS0", "rv"], [ak], bias=rv[:, pc, 5 + d:6 + d])
                    bd, bdk = newb()
                    tt_(bd[:, :n], a_[:, :n], kkn[pc][0][:, :n], ALU.mult, [ak, kkn[pc][1]], [bdk])
                    P.dma("sp", RF[8 + 6 * d + pc, :, off:off + n], bd[:, :n], reads=[bdk], writes=[("RF", off)])
                    ts_(a_[:, :n], a_[:, :n], 1.0, rv[:, pc, 1:2], ALU.subtract, ALU.mult, [ak, "rv"], [ak])
                    kd, kdk = newb()
                    stt(kd[:, :n], a_[:, :n], 1.0, kk_[pc][0][:, :n], ALU.add, ALU.mult, [ak, kk_[pc][1]], [kdk])
                    P.dma("sp", RF[6 + 6 * d + pc, :, off:off + n], kd[:, :n], reads=[kdk], writes=[("RF", off)])
                    if d == 0:
                        ksum[pc] = (kd, kdk)
                    else:
                        kf, kfk = ksum[pc]
                        tt_(kd[:, :n], kd[:, :n], kf[:, :n], ALU.add, [kdk, kfk], [kdk])
                        ts_(kd[:, :n], kd[:, :n], 0.5, rv[:, pc, 2:3], ALU.mult, ALU.mult, [kdk, "rv"], [kdk])
                        tt_(kd[:, :n], kd[:, :n], rr[pc][0][:, :n], ALU.mult, [kdk, rr[pc][1]], [kdk])
                        P.dma("sp", RF[18 + pc, :, off:off + n], kd[:, :n], reads=[kdk], writes=[("RF", off)])
            sz, szk = newb()
            act(sz[:, :n], zg[0][:, :n], AF.Sigmoid, [zg[1]], [szk])
            P.dma("sp", RF[20, :, off:off + n], sz[:, :n], reads=[szk], writes=[("RF", off)])
        P.barrier()
        es.close()

    def rwkv_scan(l):
        import os as _os
        _rwcut = int(_os.environ.get('RW_CUT', '99'))
        es = ExitStack()
        P.es = es
        tri = [cst[0:64, C_TRI + i * 64:C_TRI + (i + 1) * 64] for i in range(4)]
        m01 = [cst[0:64, C_M01 + i * 64:C_M01 + (i + 1) * 64] for i in range(4)]
        id64 = cst[0:64, C_ID:C_ID + 64]
        Hx = [[P.sb([128, 256], F32, "Hx%d%d" % (d, pc)) for pc in range(2)] for d in range(2)]
        for d in range(2):
            for pc in range(2):
                P.op("dve", "memset", Hx[d][pc][:], 0.0, writes=[("Hx", d, pc)])
        bc4 = lambda m: m.unsqueeze(1).to_broadcast([64, 4, 64])
        banks = {"A": (pG[0], "pG0"), "B": (pG[1], "pG1"), "C": (pU[0], "pU0"), "D": (pU[1], "pU1"),
                 "E": (pD[0], "pD0"), "F": (pD[1], "pD1"), "G": (pS[0], "pS0"), "H": (pS[1], "pS1")}

        def unit(d):
            K = lambda s: (s, d)
            cm = [P.sb([128, 6, 64], F32, "cm%d_%d" % (d, i)) for i in range(2)]
            dm = [P.sb([128, 6, 64], F32, "dm%d_%d" % (d, i)) for i in range(2)]
            ewM = P.sb([64, 256], F32, "ewM%d" % d)
            vM = P.sb([64, 256], F32, "rvM%d" % d)
            Pd = P.sb([128, 2, 3, 64], F32, "Pd%d" % d)
            fT = P.sb([128, 2, 4, 64], F32, "fT%d" % d)
            mT_ = P.sb([128, 2, 2, 2, 64], F32, "mK%d" % d)
            KBM = P.sb([64, 2, 256], F32, "KBM%d" % d)
            sc = P.sb([64, 5, 4, 64], F32, "sc%d" % d)
            Tt = P.sb([64, 4, 64], F32, "Tt%d" % d)
            TtT = P.sb([64, 4, 64], F32, "TtT%d" % d)
            Nn = [P.sb([64, 2, 4, 64], F32, "Nn%d_%d" % (d, i)) for i in range(2)]
            Xs = P.sb([64, 4, 64], F32, "Xs%d" % d)
            Us = P.sb([64, 4, 64], F32, "Us%d" % d)
            Ys = P.sb([64, 256], F32, "Ys%d" % d)
            tmpH = P.sb([128, 256], F32, "tmpH%d" % d)
            bA, kA = banks["A"]; bB, kB = banks["B"]; bC, kC = banks["C"]; bD, kD = banks["D"]
            bE, kE = banks["E"]; bF, kF = banks["F"]; bG, kG = banks["G"]; bH, kH = banks["H"]
            v4 = lambda ap: ap.rearrange("p (h t) -> p h t", h=4)
            for ui, c in enumerate(chunk_order(d)):
                i2 = ui % 2
                r0 = c * 64
                KB = lambda s: (s, d, i2)
                P.dma("sp", cm[i2][:], RF[0:6, :, r0:r0 + 64].rearrange("j p t -> p j t"), reads=[("RF", 0)],
                      writes=[KB("cm")])
                P.dma("sp", dm[i2][:], RF[6 + 6 * d:12 + 6 * d, :, r0:r0 + 64].rearrange("j p t -> p j t"),
                      reads=[("RF", 0)], writes=[KB("dm")])
                cmt, dmt = cm[i2], dm[i2]
                yield
                if _rwcut <= 1:
                    continue
                for pc in range(2):
                    tr(bG[:64, pc * 128:(pc + 1) * 128], dmt[:, 4 + pc, :], [KB("dm")], [kG], inc=False)
                    tr(bG[:64, 256 + pc * 128:256 + (pc + 1) * 128], cmt[:, 2 + pc, :], [KB("cm")], [kG], inc=(pc == 1))
                P.op("dve", "tensor_copy", ewM[:], bG[:64, 0:256], reads=[kG], writes=[K("ewM")])
                P.op("dve", "tensor_copy", vM[:], bG[:64, 256:512], reads=[kG], writes=[K("vM")])
                yield
                if _rwcut <= 2:
                    continue
                for pc in range(2):
                    for ie in range(2):
                        mm(bH[:, (pc * 2 + ie) * 64:(pc * 2 + ie + 1) * 64], ewM[:, pc * 128:(pc + 1) * 128], tri[2 * ie + d],
                           [K("ewM"), "cst"], [kH], inc=(pc == 1 and ie == 1))
                lv = bH[:, 0:256].rearrange("p (c i t) -> p c i t", c=2, i=2)
                act(Pd[:, :, 0:2, :], lv, AF.Exp, [kH], [K("Pd")])
                act(Pd[:, :, 2, :], lv[:, :, 0, :], AF.Exp, [kH], [K("Pd")], scale=-1.0)
                yield
                if _rwcut <= 3:
                    continue
                tt_(fT[:, :, 0, :], cmt[:, 0:2, :], Pd[:, :, 0, :], ALU.mult, [KB("cm"), K("Pd")], [K("fT")])
                tt_(fT[:, :, 1, :], cmt[:, 4:6, :], Pd[:, :, 1, :], ALU.mult, [KB("cm"), K("Pd")], [K("fT")])
                tt_(fT[:, :, 2, :], dmt[:, 0:2, :], Pd[:, :, 2, :], ALU.mult, [KB("dm"), K("Pd")], [K("fT")])
                tt_(fT[:, :, 3, :], dmt[:, 2:4, :], Pd[:, :, 2, :], ALU.mult, [KB("dm"), K("Pd")], [K("fT")])
                for hh in range(2):
                    ts_(mT_[:, :, :, hh, :], fT[:, :, 2:4, :], cst[:, C_RHM + hh:C_RHM + hh + 1], None, ALU.mult, None,
                        [K("fT"), "cst"], [K("mK")])
                yield
                if _rwcut <= 4:
                    continue
                for pc in range(2):
                    tr(bG[:64, pc * 128:(pc + 1) * 128], fT[:, pc, 2, :], [K("fT")], [kG], inc=False)
                    tr(bG[:64, 256 + pc * 128:256 + (pc + 1) * 128], fT[:, pc, 3, :], [K("fT")], [kG], inc=(pc == 1))
                act(KBM[:, 0, :], bG[:64, 0:256], AF.Copy, [kG], [K("KBM")])
                act(KBM[:, 1, :], bG[:64, 256:512], AF.Copy, [kG], [K("KBM")], scale=-1.0)
                for h in range(4):
                    pc, hh = h // 2, h % 2
                    KdTm, BdTm = mT_[:, pc, 0, hh, :], mT_[:, pc, 1, hh, :]
                    KKeT, RpT = fT[:, pc, 1, :], fT[:, pc, 0, :]
                    rd_ = [K("mK"), K("fT")]
                    mm(bA[:64, h * 64:(h + 1) * 64], KdTm, KKeT, rd_, [kA], inc=False)
                    mm(bA[:64, 256 + h * 64:256 + (h + 1) * 64], BdTm, KKeT, rd_, [kA], inc=(h == 3))
                    mm(bB[:64, h * 64:(h + 1) * 64], KKeT, BdTm, rd_, [kB], inc=False)
                    mm(bB[:64, 256 + h * 64:256 + (h + 1) * 64], KdTm, RpT, rd_, [kB], inc=(h == 3))
                    mm(bC[:64, h * 64:(h + 1) * 64], BdTm, RpT, rd_, [kC], inc=(h == 3))
                yield
                if _rwcut <= 5:
                    continue
                tt_(sc[:, 0], v4(bA[:64, 0:256]), bc4(m01[2 + d]), ALU.mult, [kA, "cst"], [K("sc0")])
                tt_(sc[:, 1], v4(bA[:64, 256:512]), bc4(m01[2 + d]), ALU.mult, [kA, "cst"], [K("sc1")])
                tt_(sc[:, 2], v4(bB[:64, 0:256]), bc4(m01[3 - d]), ALU.mult, [kB, "cst"], [K("sc2")])
                tt_(sc[:, 3], v4(bB[:64, 256:512]), bc4(m01[d]), ALU.mult, [kB, "cst"], [K("sc3")])
                stt(sc[:, 4], v4(bC[:64, 0:256]), -1.0, bc4(m01[d]), ALU.mult, ALU.mult, [kC, "cst"], [K("sc4")])
                stt(Tt[:], sc[:, 1], -1.0, bc4(id64), ALU.mult, ALU.add, [K("sc1"), "cst"], [K("Tt")])
                stt(TtT[:], sc[:, 2], -1.0, bc4(id64), ALU.mult, ALU.add, [K("sc2"), "cst"], [K("TtT")])
                yield
                if _rwcut <= 6:
                    continue
                Ncur, NTcur, nk = sc[:, 1], sc[:, 2], [K("sc1"), K("sc2")]
                for lev in range(5):
                    lastl = lev == 4
                    for h in range(4):
                        mm(bD[:64, h * 64:(h + 1) * 64], NTcur[:, h, :], Ncur[:, h, :], nk, [kD], inc=(lastl and h == 3))
                        if not lastl:
                            mm(bD[:64, 256 + h * 64:256 + (h + 1) * 64], Ncur[:, h, :], NTcur[:, h, :], nk, [kD],
                               inc=(h == 3))
                    nn = Nn[lev % 2]
                    nnk = (K("Nn"), lev % 2)
                    if lastl:
                        act(nn[:, 0], v4(bD[:64, 0:256]), AF.Copy, [kD], [nnk])
                    else:
                        act(nn[:].rearrange("p a h t -> p (a h) t"), bD[:64, :].rearrange("p (a t) -> p a t", a=8),
                            AF.Copy, [kD], [nnk])
                    yield
                    for h in range(4):
                        mm(bE[:64, h * 64:(h + 1) * 64], TtT[:, h, :], nn[:, 0, h, :], [K("TtT"), nnk], [kE],
                           inc=(lastl and h == 3))
                        if not lastl:
                            mm(bE[:64, 256 + h * 64:256 + (h + 1) * 64], nn[:, 0, h, :], TtT[:, h, :], [K("TtT"), nnk], [kE],
                               inc=(h == 3))
                    tt_(Tt[:], Tt[:], v4(bE[:64, 0:256]), ALU.add, [K("Tt"), kE], [K("Tt")])
                    if not lastl:
                        tt_(TtT[:], TtT[:], v4(bE[:64, 256:512]), ALU.add, [K("TtT"), kE], [K("TtT")])
                    Ncur, NTcur, nk = nn[:, 0], nn[:, 1], [nnk]
                    yield
                for h in range(4):
                    pc = h // 2
                    mm(bC[:64, 256 + h * 64:256 + (h + 1) * 64], fT[:, pc, 1, :], Hx[d][pc][:, h * 64:(h + 1) * 64],
                       [K("fT"), ("Hx", d, pc)], [kC], start=True, stop=False, inc=False)
                    mm(bC[:64, 256 + h * 64:256 + (h + 1) * 64], sc[:, 0, h, :], vM[:, h * 64:(h + 1) * 64],
                       [K("sc0"), K("vM")], [kC], start=False, stop=True, inc=(h == 3))
                P.op("dve", "tensor_copy", Xs[:], v4(bC[:64, 256:512]), reads=[kC], writes=[K("Xs")])
                yield
                if _rwcut <= 7:
                    continue
                for h in range(4):
                    mm(bF[:64, h * 64:(h + 1) * 64], Tt[:, h, :], Xs[:, h, :], [K("Tt"), K("Xs")], [kF], inc=(h == 3))
                act(Us[:], v4(bF[:64, 0:256]), AF.Copy, [kF], [K("Us")])
                yield
                if _rwcut <= 8:
                    continue
                for h in range(4):
                    pc = h // 2
                    o_ = bF[:64, 256 + h * 64:256 + (h + 1) * 64]
                    mm(o_, fT[:, pc, 0, :], Hx[d][pc][:, h * 64:(h + 1) * 64], [K("fT"), ("Hx", d, pc)], [kF],
                       start=True, stop=False, inc=False)
                    mm(o_, sc[:, 3, h, :], vM[:, h * 64:(h + 1) * 64], [K("sc3"), K("vM")], [kF], start=False, stop=False,
                       inc=False)
                    mm(o_, sc[:, 4, h, :], Us[:, h, :], [K("sc4"), K("Us")], [kF], start=False, stop=True, inc=(h == 3))
                P.op("dve", "tensor_copy", Ys[:], bF[:64, 256:512], reads=[kF], writes=[K("Ys")])
                P.dma("sp", YD[d, r0:r0 + 64, :], Ys[:], reads=[K("Ys")], writes=[("YD", d, c)])
                lastc = 63 if d == 0 else 0
                for pc in range(2):
                    o_ = bH[:, 256:512] if pc == 0 else bG[:, 0:256]
                    ok_ = kH if pc == 0 else kG
                    mm(o_, KBM[:, 0, pc * 128:(pc + 1) * 128], vM[:], [K("KBM"), K("vM")], [ok_], start=True, stop=False,
                       inc=False)
                    mm(o_, KBM[:, 1, pc * 128:(pc + 1) * 128], Us[:].rearrange("p h e -> p (h e)"), [K("KBM"), K("Us")],
                       [ok_], start=False, stop=True, inc=True)
                    tt_(tmpH[:], o_, cst[:, C_RBD + pc * 256:C_RBD + (pc + 1) * 256], ALU.mult, [ok_, "cst"], [K("tmpH")])
                    tt_(tmpH[:], tmpH[:], Hx[d][pc][:], ALU.add, [K("tmpH"), ("Hx", d, pc)], [K("tmpH")])
                    ts_(Hx[d][pc][:], tmpH[:], Pd[:, pc, 0, lastc:lastc + 1], None, ALU.mult, None, [K("tmpH"), K("Pd")],
                        [("Hx", d, pc)])
                yield
                if _rwcut <= 9:
                    continue

        run_interleaved([unit(0)])
        run_interleaved([unit(1)])
        P.barrier()
        es.close()

    def rwkv_final(l, do_ctx):
        es = ExitStack()
        P.es = es
        gup = P.sb([128, 256], F32, "gup")
        P.dma("sp", gup[:], rw_gup[l], writes=["gup"])
        gnb = P.sb([128, 2, 256], F32, "gnb")
        for i in range(2):
            P.dma("sp", gnb[:, i, :], rw_gn[l, i].partition_broadcast(128), writes=["gnb"])
        yf = [P.sb([128, 4, 64], F32, "yf%d" % i) for i in range(2)]
        yb = [P.sb([128, 4, 64], F32, "yb%d" % i) for i in range(2)]
        fp = [P.sb([128, 5, 128], F32, "fp%d" % i) for i in range(2)]
        tm = P.sb([128, 2, 256], F32, "ftm")
        ysq = P.sb([128, 4, 64], F32, "ysq")
        st4 = P.sb([128, 4, 4], F32, "st4")
        gsb = P.sb([128, 256], F32, "gsb")
        ni = 0
        for tt in range(NTL):
            r0 = tt * 128
            if r0 < LC and not do_ctx:
                continue
            i2 = ni % 2
            ni += 1
            P.dma("sp", yf[i2][:].rearrange("p h e -> p (h e)"), YD[0, r0:r0 + 128, :],
                  reads=[("YD", 0, 2 * tt), ("YD", 0, 2 * tt + 1)], writes=[("yf", i2)])
            P.dma("sp", yb[i2][:].rearrange("p h e -> p (h e)"), YD[1, r0:r0 + 128, :],
                  reads=[("YD", 1, 2 * tt), ("YD", 1, 2 * tt + 1)], writes=[("yb", i2)])
            for j, pn in enumerate([18, 19, 2, 3, 20]):
                P.dma("sp", fp[i2][:, j, :], RF[pn, :, r0:r0 + 128], reads=[("RF", 0)], writes=[("fp", i2)])
            y = yf[i2]
            tt_(y[:], y[:], yb[i2][:], ALU.add, [("yf", i2), ("yb", i2)], [("yf", i2)])
            for j in range(4):
                tr(pG[0][:, j * 128:(j + 1) * 128], fp[i2][:, j, :], [("fp", i2)], ["pG0"], inc=(j == 3))
            act(tm[:].rearrange("p a c -> p (a c)"), pG[0][:, :], AF.Copy, ["pG0"], ["ftm"])
            mm(pG[1][:, 0:256], fp[i2][:, 4, :], gup[:], [("fp", i2), "gup"], ["pG1"])
            act(gsb[:], pG[1][:, 0:256], AF.Copy, ["pG1"], ["gsb"])
            P.op("dve", "tensor_reduce", st4[:, 0, :], y[:], AX.X, ALU.add, reads=[("yf", i2)], writes=["st4"])
            tt_(ysq[:], y[:], y[:], ALU.mult, [("yf", i2)], ["ysq"])
            P.op("dve", "tensor_reduce", st4[:, 1, :], ysq[:], AX.X, ALU.add, reads=["ysq"], writes=["st4"])
            P.op("dve", "tensor_reduce", st4[:, 2, :], tm[:, 0, :].rearrange("p (h e) -> p h e", h=4), AX.X, ALU.add,
                 reads=["ftm"], writes=["st4"])
            ts_(st4[:, 0, :], st4[:, 0, :], 1.0 / 64, None, ALU.mult, None, ["st4"], ["st4"])
            tt_(st4[:, 3, :], st4[:, 0, :], st4[:, 0, :], ALU.mult, ["st4"], ["st4"])
            stt(st4[:, 1, :], st4[:, 1, :], 1.0 / 64, st4[:, 3, :], ALU.mult, ALU.subtract, ["st4"], ["st4"])
            act(st4[:, 1, :], st4[:, 1, :], AF.Sqrt, ["st4", "kc"], ["st4"], bias=kc[:, 4:5])
            P.op("dve", "reciprocal", st4[:, 1, :], st4[:, 1, :], reads=["st4"], writes=["st4"])
            tt_(y[:], y[:], st4[:, 0, :].unsqueeze(2).to_broadcast([128, 4, 64]), ALU.subtract, [("yf", i2), "st4"],
                [("yf", i2)])
            tt_(y[:], y[:], st4[:, 1, :].unsqueeze(2).to_broadcast([128, 4, 64]), ALU.mult, [("yf", i2), "st4"],
                [("yf", i2)])
            yv = y[:].rearrange("p h e -> p (h e)")
            tt_(yv, yv, gnb[:, 0, :], ALU.mult, [("yf", i2), "gnb"], [("yf", i2)])
            tt_(yv, yv, gnb[:, 1, :], ALU.add, [("yf", i2), "gnb"], [("yf", i2)])
            tt_(ysq[:], tm[:, 1, :].rearrange("p (h e) -> p h e", h=4),
                st4[:, 2, :].unsqueeze(2).to_broadcast([128, 4, 64]), ALU.mult, ["ftm", "st4"], ["ysq"])
            tt_(y[:], y[:], ysq[:], ALU.add, [("yf", i2), "ysq"], [("yf", i2)])
            tt_(yv, yv, gsb[:], ALU.mult, [("yf", i2), "gsb"], [("yf", i2)])
            P.dma("sp", MIX[r0:r0 + 128, 768:1024], yv, reads=[("yf", i2)], writes=[("MIX", "c", tt)])
        P.barrier()
        es.close()

    def outproj(l, do_ctx):
        es = ExitStack()
        P.es = es
        hT = [P.sb([128, 8, TG], F32, "hT%d" % i) for i in range(2)]
        lt = alloc_ln_tiles()
        vv = lt["vv"]
        mixT = P.sb([128, 8, TG], BF16, "mixT")
        mtile = [P.sb([128, 1024], F32, "mtile%d" % i) for i in range(2)]
        wo = P.sb([128, 8, 1024], BF16, "wo")
        for hf in range(2):
            P.dma("pool", wo[:, :, hf * 512:(hf + 1) * 512],
                  w_out[l][:, hf * 512:(hf + 1) * 512].rearrange("(k p) j -> p k j", p=128), writes=["wo"])
        cur_s = None
        nm = 0
        for (st, t0, n) in groups:
            if st == "c" and not do_ctx:
                continue
            s = 1 if st == "c" else 0
            off = toff(st, t0)
            if s != cur_s:
                mod_scalars(l, 1, s)
                cur_s = s
            gi = cnts["g"]
            cnts["g"] += 1
            h = hT[gi % 2]
            hk = "hT%d" % (gi % 2)
            P.dma("sp", h[:, :, :n], hsrc(st, t0, n), reads=[("H", st, t0)], writes=[hk])
            for ts in range(n // 128):
                mt = mtile[nm % 2]
                mk = ("mtile", nm % 2)
                nm += 1
                P.dma("sp", mt[:], MIX[off + ts * 128:off + (ts + 1) * 128, :],
                      reads=[kk for kk in list(P.lastw.keys()) if isinstance(kk, tuple) and kk[0] == "MIX"], writes=[mk])
                for half in range(2):
                    pi = half
                    for j in range(4):
                        tr(pG[pi][:, j * 128:(j + 1) * 128], mt[:, (half * 4 + j) * 128:(half * 4 + j + 1) * 128], [mk],
                           ["pG%d" % pi], inc=(j == 3))
                    act(mixT[:, half * 4:half * 4 + 4, ts * 128:(ts + 1) * 128],
                        pG[pi][:, :].rearrange("p (j t) -> p j t", j=4), AF.Copy, ["pG%d" % pi], ["mixT"])
            for c in range(8):
                pi = c % 2
                for k in range(8):
                    mm(pD[pi][:, :n], wo[:, k, c * 128:(c + 1) * 128], mixT[:, k, :n], ["wo", "mixT"], ["pD%d" % pi],
                       start=(k == 0), stop=(k == 7), inc=(k == 7))
                stt(vv[:, c, :n], pD[pi][:, :n], scl[:, 2, c:c + 1], h[:, c, :n], ALU.mult, ALU.add,
                    ["pD%d" % pi, "scl2", hk], [("vv", c)])
            layer_norm_store(lt, st, t0, n, l, 1)
        P.barrier()
        es.close()

    for l in range(L):
        last = l == L - 1
        ffn_sublayer(l, 0, ffn_w["ffn1_wg"][l], ffn_w["ffn1_wu"][l], ffn_w["ffn1_wd"][l])
        if dbg == "ffn1":
            break
        inproj(l)
        if dbg == "inproj":
            break
        if dbg in (None, "swa", "mix"):
            swa(l, not last)
        if dbg in (None, "gla", "mix"):
            gla(l, not last)
        if dbg in (None, "rwkv", "mix"):
            import os as _os
            _rs = int(_os.environ.get("RW_STOP", "9"))
            rwkv_prep(l)
            if _rs >= 2:
                rwkv_scan(l)
            if _rs >= 3:
                rwkv_final(l, not last)
        if dbg in ("swa", "gla", "rwkv", "mix"):
            break
        outproj(l, not last)
        if dbg == "outproj":
            break
        ffn_sublayer(l, 2, ffn_w["ffn2_wg"][l], ffn_w["ffn2_wu"][l], ffn_w["ffn2_wd"][l], do_ctx=not last)

    es = ExitStack()
    P.es = es
    evs = []
    if dbg in ("swa", "gla", "rwkv", "mix"):
        mixo = dram("mixo", [SEQ, D], kind="ExternalOutput")
        ob = [P.sb([128, 1024], F32, "ob%d" % i) for i in range(2)]
        for tt in range(SEQ // 128):
            r0 = LC + tt * 128
            P.dma("sp", ob[tt % 2][:], MIX[r0:r0 + 128, :], writes=[("ob", tt % 2)])
            evs.append(P.dma("sp", mixo[tt * 128:(tt + 1) * 128, :], ob[tt % 2][:], reads=[("ob", tt % 2)],
                             writes=[("mixo", tt)]))
    ob2 = [P.sb([128, 8, TG], F32, "ob2%d" % i) for i in range(2)]
    for gi, (st, t0, n) in enumerate(groups):
        if st == "c":
            continue
        b = ob2[gi % 2]
        bk = "ob2%d" % (gi % 2)
        P.dma("sp", b[:, :, :n], hsrc(st, t0, n), reads=[("H", st, t0)], writes=[bk])
        evs.append(P.dma("sp", outT[:, :, t0:t0 + n].rearrange("c p t -> p c t"), b[:, :, :n], reads=[bk],
                         writes=[("out", t0)]))
    P.finish("sp", evs)
    es.close()
    es0.close()
    print("program instructions:", P.ninst)
    return nc


def _consts():
    c = np.zeros((128, C_END), np.float32)
    c[:, C_ID:C_ID + 128] = np.eye(128)
    s = np.arange(64)[:, None]
    t = np.arange(64)[None, :]
    c[0:64, C_TRI + 0:C_TRI + 64] = -1.0 * (s <= t)
    c[0:64, C_TRI + 64:C_TRI + 128] = -1.0 * (s >= t)
    c[0:64, C_TRI + 128:C_TRI + 192] = -1.0 * (s < t)
    c[0:64, C_TRI + 192:C_TRI + 256] = -1.0 * (s > t)
    c[0:64, C_M01 + 0:C_M01 + 64] = (s <= t)
    c[0:64, C_M01 + 64:C_M01 + 128] = (s >= t)
    c[0:64, C_M01 + 128:C_M01 + 192] = (s < t)
    c[0:64, C_M01 + 192:C_M01 + 256] = (s > t)
    p = np.arange(128)[:, None]
    col = np.arange(256)[None, :]
    c[:, C_GBD:C_GBD + 256] = (p // 32 == col // 64)
    for h in range(4):
        c[:, C_GHM + h] = (np.arange(128) // 32 == h)
    for pc in range(2):
        c[:, C_RBD + pc * 256:C_RBD + (pc + 1) * 256] = ((2 * pc + p // 64) == col // 64)
    for hh in range(2):
        c[:, C_RHM + hh] = (np.arange(128) // 64 == hh)
    q = np.arange(128)[None, :]
    c[:, C_BO:C_BO + 128] = (p // 64 == q // 64)
    c[:, C_SWM:C_SWM + 128] = (p >= q)
    c[:, C_SWM + 128:C_SWM + 256] = (p <= q)
    c[0:64, C_NI:C_NI + 64] = -np.eye(64)
    return c


def _rope_table(SEQ):
    pos = np.arange(SEQ)
    row = (pos // 64).astype(np.float32)
    col = (pos % 64).astype(np.float32)
    inv = (np.float32(10000.0) ** (-np.arange(16, dtype=np.float32) / np.float32(16))).astype(np.float32)
    tab = np.zeros((SEQ, 2, 32), np.float32)
    for a, pp in enumerate((row, col)):
        ang = (pp[:, None] * inv[None, :]).astype(np.float32)
        tab[:, 0, a * 16:(a + 1) * 16] = np.cos(ang)
        tab[:, 1, a * 16:(a + 1) * 16] = np.sin(ang)
    return tab


def _prep_shared(inp, L, SEQ):
    f = lambda a: np.ascontiguousarray(a, dtype=np.float32)
    m = {}
    m["w_ada"] = f(inp["w_ada"][:L])
    m["b_adaT"] = f(inp["b_ada"][:L].reshape(L, 72, 128).transpose(2, 0, 1))
    ln = np.stack([inp["ln_g"][:L], inp["ln_b"][:L]], axis=2)
    m["lnT"] = f(ln.reshape(L, 3, 2, 8, 128).transpose(4, 0, 1, 2, 3))
    for nm in ("ffn1_wg", "ffn1_wu", "ffn1_wd", "ffn2_wg", "ffn2_wu", "ffn2_wd", "w_in", "w_out"):
        m[nm] = f(inp[nm][:L])
    m["consts"] = _consts()
    m["ropeM"] = _rope_table(SEQ)
    m["gla_up"] = f(np.concatenate([inp["gla_gate_up"][:L], inp["gla_gate_bias"][:L][:, :, None, :]], axis=2))
    m["gla_g"] = f(inp["gla_norm_g"][:L])
    m["sinkB"] = f(np.broadcast_to(inp["swa_sink"][:L][None], (128, L, 8)))
    vecs = np.stack([inp["rwkv_k_k"][:L], inp["rwkv_k_a"][:L], inp["rwkv_r_k"][:L].reshape(L, 256),
                     inp["rwkv_w0"][:L, 0], inp["rwkv_w0"][:L, 1], inp["rwkv_a0"][:L, 0], inp["rwkv_a0"][:L, 1]],
                    axis=-1)
    m["rw_vec"] = f(vecs.reshape(L, 2, 128, 7).transpose(2, 0, 1, 3))
    mu = inp["rwkv_mu"][:L]
    mut = np.zeros((128, L, 11), np.float32)
    for j in range(6):
        mut[:, :, j] = mu[:, 128 * j:128 * (j + 1)].T
    for j, r0 in enumerate((768, 832, 896, 960)):
        mut[:64, :, 6 + j] = mu[:, r0:r0 + 64].T
    mut[:, :, 10] = mu[:, 1024:1152].T
    m["rw_mu"] = mut
    m["rw_wup"] = f(inp["rwkv_w_up"][:L])
    m["rw_aup"] = f(inp["rwkv_a_up"][:L])
    m["rw_gup"] = f(inp["rwkv_g_up"][:L])
    m["rw_gn"] = f(np.stack([inp["rwkv_gn_g"][:L], inp["rwkv_gn_b"][:L]], axis=1))
    return m


def _prep_core(inp, b):
    f = lambda a: np.ascontiguousarray(a, dtype=np.float32)
    m = {}
    m["xT"] = f(inp["x"][b].T.reshape(8, 128, -1))
    m["cxT"] = f(inp["ctx"][b].T.reshape(8, 128, -1))
    cc = np.stack([inp["c"][b], inp["c_ctx"]], axis=-1)
    m["ccT"] = f(cc.reshape(8, 128, 2).transpose(1, 0, 2))
    return m


def run(inp, SEQ, LC, DEPTH, ncores, dbg=None, trace=False):
    nc = build(SEQ, LC, DEPTH, dbg=dbg)
    shared = _prep_shared(inp, DEPTH, SEQ)
    in_maps = []
    for b in range(ncores):
        m = dict(shared)
        m.update(_prep_core(inp, b))
        in_maps.append(m)
    res = run_bass_kernel_spmd(nc, in_maps, core_ids=list(range(ncores)), trace=trace)
    outs = [r["outT"].reshape(1024, SEQ).T for r in res.results]
    return np.stack(outs, axis=0), res


def kernel(**inputs):
    inp = {k: np.asarray(v) for k, v in inputs.items()}
    out, _ = run(inp, 4096, 256, 4, 8)
    return np.ascontiguousarray(out.astype(np.float32))
```

```python
import numpy as np
from contextlib import ExitStack
import concourse.bass as bass
import concourse.mybir as mybir
from concourse.bass_utils import run_bass_kernel_spmd

F32 = mybir.dt.float32
BF16 = mybir.dt.bfloat16
AF = mybir.ActivationFunctionType
ALU = mybir.AluOpType
AX = mybir.AxisListType

D = 1024
DFF = 2816
NFF = DFF // 128
DIN = 2720
ALPHA = 8.0 ** 0.25
LN_EPS = 1e-6
GN_EPS = 64e-5
C_ID = 0
C_TRI = 128
C_M01 = 384
C_GBD = 640
C_GHM = 896
C_RBD = 900
C_RHM = 1412
C_BO = 1414
C_SWM = 1542
C_NI = 1798
C_END = 1862


class Prog:
    def __init__(self, nc, es):
        self.nc = nc
        self.es = es
        self.eng = {"pe": nc.tensor, "act": nc.scalar, "dve": nc.vector, "pool": nc.gpsimd, "sp": nc.sync}
        self.sem = {e: es.enter_context(nc.semaphore("s_" + e)) for e in ("pe", "act", "dve", "pool")}
        self.cnt = {e: 0 for e in self.sem}
        self.known = {e: {} for e in self.eng}
        self.lastw = {}
        self.rd = {}
        self.NDS = 12
        self.dsem = {q: [es.enter_context(nc.semaphore("d_%s%d" % (q, i))) for i in range(self.NDS)]
                     for q in ("sp", "pool", "act")}
        self.dcnt = {q: 0 for q in self.dsem}
        self.semobj = {}
        for e in self.sem:
            self.semobj[e] = self.sem[e]
        for q in self.dsem:
            for i, s in enumerate(self.dsem[q]):
                self.semobj[(q, i)] = s
        self.ntiles = 0
        self.ninst = 0

    def sb(self, shape, dt=F32, name=None):
        self.ntiles += 1
        return self.es.enter_context(self.nc.sbuf_tensor("%s_%d" % (name or "t", self.ntiles), list(shape), dt))

    def ps(self, shape, dt=F32, name=None):
        self.ntiles += 1
        return self.es.enter_context(self.nc.psum_tensor("%s_%d" % (name or "p", self.ntiles), list(shape), dt))

    def _wait(self, e, ev):
        s, v = ev
        if self.known[e].get(s, 0) >= v:
            return
        self.eng[e].wait_ge(self.semobj[s], v)
        self.known[e][s] = v
        self.ninst += 1

    def _deps(self, e, reads, writes):
        for k in reads:
            ev = self.lastw.get(k)
            if ev is not None and not (e == "pe" and ev[0] == "pe"):
                self._wait(e, ev)
        for k in writes:
            ev = self.lastw.get(k)
            if ev is not None and not (e == "pe" and ev[0] == "pe"):
                self._wait(e, ev)
            for ev in self.rd.get(k, {}).values():
                if ev[0] == e:
                    continue
                self._wait(e, ev)

    def _record(self, ev, reads, writes):
        for k in reads:
            self.rd.setdefault(k, {})[ev[0]] = ev
        for k in writes:
            self.lastw[k] = ev
            self.rd[k] = {}

    def op(self, e, fn, *args, reads=(), writes=(), inc=True, **kw):
        self._deps(e, reads, writes)
        inst = getattr(self.eng[e], fn)(*args, **kw)
        self.ninst += 1
        ev = (e, self.cnt[e] + 1)
        if inc:
            inst.then_inc(self.sem[e], 1)
            self.cnt[e] += 1
        self._record(ev, reads, writes)
        return inst

    def dma(self, q, out, in_, reads=(), writes=(), **kw):
        self._deps(q, reads, writes)
        j = self.dcnt[q]
        self.dcnt[q] += 1
        slot = j % self.NDS
        s = (q, slot)
        need = 16 * (j // self.NDS)
        if need > 0:
            self._wait(q, (s, need))
        inst = self.eng[q].dma_start(out=out, in_=in_, **kw)
        inst.then_inc(self.dsem[q][slot], 16)
        self.ninst += 1
        ev = (s, need + 16)
        self._record(ev, reads, writes)
        return ev

    def barrier(self):
        evs = [(e, self.cnt[e]) for e in self.sem if self.cnt[e] > 0]
        for q in self.dsem:
            j = self.dcnt[q]
            for slot in range(self.NDS):
                n = (j - slot + self.NDS - 1) // self.NDS if j > slot else 0
                if n > 0:
                    evs.append(((q, slot), 16 * n))
        for e in self.eng:
            for ev in evs:
                self._wait(e, ev)
        self.lastw = {}
        self.rd = {}

    def finish(self, e, evs):
        for ev in evs:
            self._wait(e, ev)


def _ffn_pieces():
    out = []
    f = 0
    while f < NFF:
        w = min(4, NFF - f)
        out.append((f, w))
        f += w
    return out


def build(SEQ, LC, DEPTH, dbg=None):
    nc = bass.Bass("TRN2", target_bir_lowering=False)
    es0 = ExitStack()
    P = Prog(nc, es0)
    dram = lambda name, shape, dt=F32, kind="ExternalInput": nc.dram_tensor(name, list(shape), dt, kind=kind).ap()
    L = DEPTH
    T = LC + SEQ
    NTL = T // 128
    NCH = T // 64
    xT = dram("xT", [8, 128, SEQ])
    cxT = dram("cxT", [8, 128, LC])
    ccT = dram("ccT", [128, 8, 2])
    w_ada = dram("w_ada", [L, D, 9 * D])
    b_adaT = dram("b_adaT", [128, L, 72])
    lnT = dram("lnT", [128, L, 3, 2, 8])
    ffn_w = {}
    for nm in ("ffn1_wg", "ffn1_wu", "ffn2_wg", "ffn2_wu"):
        ffn_w[nm] = dram(nm, [L, D, DFF])
    for nm in ("ffn1_wd", "ffn2_wd"):
        ffn_w[nm] = dram(nm, [L, DFF, D])
    w_in = dram("w_in", [L, D, DIN])
    w_out = dram("w_out", [L, D, D])
    consts = dram("consts", [128, C_END])
    ropeM = dram("ropeM", [SEQ, 2, 32])
    gla_up = dram("gla_up", [L, 2, 17, 128])
    gla_g = dram("gla_g", [L, 256])
    sinkB = dram("sinkB", [128, L, 8])
    rw_vec = dram("rw_vec", [128, L, 2, 7])
    rw_mu = dram("rw_mu", [128, L, 11])
    rw_wup = dram("rw_wup", [L, 2, 64, 256])
    rw_aup = dram("rw_aup", [L, 2, 64, 256])
    rw_gup = dram("rw_gup", [L, 128, 256])
    rw_gn = dram("rw_gn", [L, 2, 256])
    outT = dram("outT", [8, 128, SEQ], kind="ExternalOutput")
    HX = dram("HX", [8, 128, SEQ], kind="Internal")
    HC = dram("HC", [8, 128, LC], kind="Internal")
    PF = dram("PF", [DIN, T], kind="Internal")
    PM = dram("PM", [T, DIN], kind="Internal")
    MIX = dram("MIX", [T, D], kind="Internal")
    OG = dram("OG", [T, 256], kind="Internal")
    RF = dram("RF", [21, 128, T], kind="Internal")
    YD = dram("YD", [2, T, 256], kind="Internal")
    NPC = len(_ffn_pieces())
    WGU = dram("WGU", [1, 2, 2, NPC, 128, 8, 512], BF16, kind="Internal")
    WDS = dram("WDS", [1, 2, 8, 128, NFF, 128], BF16, kind="Internal")

    TG = 512
    groups = [("c", 0, LC)] + [("x", t0, min(TG, SEQ - t0)) for t0 in range(0, SEQ, TG)]

    def toff(st, t0):
        return t0 if st == "c" else LC + t0

    cst = P.sb([128, C_END], F32, "cst")
    P.dma("sp", cst[:], consts, writes=["cst"])
    ident = cst[:, C_ID:C_ID + 128]
    onesm = P.sb([128, 128], F32, "onesm")
    P.op("dve", "memset", onesm[:], 1.0 / D, writes=["onesm"])
    cc = P.sb([128, 8, 2], F32, "cc")
    P.dma("sp", cc[:], ccT, writes=["cc"])
    sil = P.sb([128, 8, 2], F32, "sil")
    P.op("act", "activation", sil[:], cc[:], AF.Silu, reads=["cc"], writes=["sil"])
    bada = P.sb([128, L, 72], F32, "bada")
    P.dma("sp", bada[:], b_adaT, writes=["bada"])
    lnp = P.sb([128, L, 3, 2, 8], F32, "lnp")
    P.dma("sp", lnp[:], lnT, writes=["lnp"])
    mT = P.sb([128, L, 72, 2], F32, "mT")
    kc = P.sb([128, 8], F32, "kc")
    for i, val in enumerate([LN_EPS / (ALPHA * ALPHA), 1.0, -0.5, LN_EPS, GN_EPS, 0.0, 1e-24]):
        P.op("dve", "memset", kc[:, i:i + 1], val, writes=["kc"])
    epsb = kc[:, 0:1]
    scl = P.sb([128, 4, 8], F32, "scl")

    pG = [P.ps([128, 512], F32, "pG%d" % i) for i in range(2)]
    pU = [P.ps([128, 512], F32, "pU%d" % i) for i in range(2)]
    pD = [P.ps([128, 512], F32, "pD%d" % i) for i in range(2)]
    pS = [P.ps([128, 512], F32, "pS%d" % i) for i in range(2)]

    es = ExitStack()
    P.es = es
    wa = [P.sb([128, 8, 512], F32, "wa%d" % i) for i in range(2)]
    nblk = 0
    for l in range(L):
        pm = pS[l % 2]
        for cb in range(18):
            wt = wa[nblk % 2]
            wk = "wa%d" % (nblk % 2)
            nblk += 1
            src = w_ada[l, :, cb * 512:(cb + 1) * 512].rearrange("(k p) j -> p k j", p=128)
            P.dma("sp", wt[:], src, writes=[wk])
            for jj in range(4):
                j = cb * 4 + jj
                for k in range(8):
                    P.op("pe", "matmul", pm[:, 2 * j:2 * j + 2], wt[:, k, jj * 128:(jj + 1) * 128], sil[:, k, :],
                         start=(k == 0), stop=(k == 7), reads=[wk, "sil"], writes=["pS%d" % (l % 2)],
                         inc=(k == 7 and jj == 3))
        P.op("dve", "tensor_tensor", mT[:, l, :, :], pm[:, 0:144].rearrange("p (j s) -> p j s", s=2),
             bada[:, l, :].unsqueeze(2).to_broadcast([128, 72, 2]), ALU.add,
             reads=["pS%d" % (l % 2), "bada"], writes=["mT"])
    for gi, (st, t0, n) in enumerate(groups):
        b = wa[gi % 2]
        bk = "wa%d" % (gi % 2)
        src = (cxT if st == "c" else xT)[:, :, t0:t0 + n].rearrange("c p t -> p c t")
        P.dma("sp", b[:, :, :n], src, writes=[bk])
        P.dma("sp", (HC if st == "c" else HX)[:, :, t0:t0 + n].rearrange("c p t -> p c t"), b[:, :, :n],
              reads=[bk], writes=[("H", st, t0)])
    P.barrier()
    es.close()

    def convert_weights(l):
        es = ExitStack()
        P.es = es
        cvt = [P.sb([128, 8, 512], BF16, "cvt%d" % i) for i in range(4)]
        cvd = [P.sb([128, NFF, 128], BF16, "cvd%d" % i) for i in range(4)]
        ncv = [0, 0]
        for fi, pre in enumerate(("ffn1", "ffn2")):
            for gu, nm in enumerate(("_wg", "_wu")):
                wap = ffn_w[pre + nm][l]
                for pi_, (f0, fw) in enumerate(_ffn_pieces()):
                    bi = ncv[0] % 4
                    ncv[0] += 1
                    P.dma("pool", cvt[bi][:, :, :fw * 128],
                          wap[:, f0 * 128:(f0 + fw) * 128].rearrange("(k p) j -> p k j", p=128), writes=[("cvt", bi)])
                    P.dma("sp", WGU[0, fi, gu, pi_][:, :, :fw * 128], cvt[bi][:, :, :fw * 128], reads=[("cvt", bi)],
                          writes=[("WGU", fi)])
            wap = ffn_w[pre + "_wd"][l]
            for c in range(8):
                bi = ncv[1] % 4
                ncv[1] += 1
                P.dma("pool", cvd[bi][:], wap[:, c * 128:(c + 1) * 128].rearrange("(f p) j -> p f j", p=128),
                      writes=[("cvd", bi)])
                P.dma("sp", WDS[0, fi, c], cvd[bi][:], reads=[("cvd", bi)], writes=[("WDS", fi)])
        P.barrier()
        es.close()

    def hsrc(st, t0, n):
        return (HC if st == "c" else HX)[:, :, t0:t0 + n].rearrange("c p t -> p c t")

    cnts = {"g": 0, "wgu": 0, "wd": 0, "ps": 0}

    def mod_scalars(l, sub, s):
        j0 = 3 * sub * 8
        coef = (0.5 if sub != 1 else 1.0) / ALPHA
        P.op("dve", "tensor_scalar", scl[:, 0, :], mT[:, l, j0 + 8:j0 + 16, s], 1.0, None, ALU.add,
             reads=["mT"], writes=["scl0"])
        P.op("dve", "tensor_copy", scl[:, 1, :], mT[:, l, j0:j0 + 8, s], reads=["mT"], writes=["scl1"])
        P.op("dve", "tensor_scalar", scl[:, 2, :], mT[:, l, j0 + 16:j0 + 24, s], coef, None, ALU.mult,
             reads=["mT"], writes=["scl2"])

    def alloc_ln_tiles():
        t = {}
        t["vv"] = P.sb([128, 8, TG], F32, "vv")
        t["yo"] = P.sb([128, 8, TG], F32, "yo")
        t["vsq"] = [P.sb([128, TG], F32, "vsq%d" % i) for i in range(2)]
        t["mean_sb"] = P.sb([128, TG], F32, "mean_sb")
        t["msq"] = P.sb([128, TG], F32, "msq")
        t["var"] = P.sb([128, TG], F32, "var")
        t["rstd"] = P.sb([128, TG], F32, "rstd")
        t["tt"] = [P.sb([128, TG], F32, "tt%d" % i) for i in range(2)]
        return t

    def layer_norm_store(t, st, t0, n, l, sub):
        vv, yo, vsq, mean_sb, msq, var, rstd, tt = (t[k] for k in ("vv", "yo", "vsq", "mean_sb", "msq", "var", "rstd", "tt"))
        pmean, pev2 = pS[0], pS[1]
        for c in range(8):
            q = vsq[c % 2]
            qk = "vsq%d" % (c % 2)
            P.op("act", "activation", q[:, :n], vv[:, c, :n], AF.Square, reads=[("vv", c)], writes=[qk])
            P.op("pe", "matmul", pmean[:, :n], onesm[:], vv[:, c, :n], start=(c == 0), stop=(c == 7),
                 reads=["onesm", ("vv", c)], writes=["pS0"], inc=(c == 7))
            P.op("pe", "matmul", pev2[:, :n], onesm[:], q[:, :n], start=(c == 0), stop=(c == 7),
                 reads=["onesm", qk], writes=["pS1"], inc=True)
        P.op("act", "activation", mean_sb[:, :n], pmean[:, :n], AF.Copy, reads=["pS0"], writes=["mean_sb"])
        P.op("dve", "tensor_tensor", msq[:, :n], mean_sb[:, :n], mean_sb[:, :n], ALU.mult,
             reads=["mean_sb"], writes=["msq"])
        P.op("dve", "tensor_tensor", var[:, :n], pev2[:, :n], msq[:, :n], ALU.subtract,
             reads=["pS1", "msq"], writes=["var"])
        P.op("act", "activation", var[:, :n], var[:, :n], AF.Sqrt, bias=epsb, reads=["var", "kc"], writes=["var"])
        P.op("dve", "reciprocal", rstd[:, :n], var[:, :n], reads=["var"], writes=["rstd"])
        for c in range(8):
            tq = tt[c % 2]
            tk = "tt%d" % (c % 2)
            P.op("dve", "tensor_tensor", tq[:, :n], vv[:, c, :n], mean_sb[:, :n], ALU.subtract,
                 reads=[("vv", c), "mean_sb"], writes=[tk])
            P.op("dve", "tensor_tensor", tq[:, :n], tq[:, :n], rstd[:, :n], ALU.mult,
                 reads=[tk, "rstd"], writes=[tk])
            P.op("act", "activation", yo[:, c, :n], tq[:, :n], AF.Identity,
                 scale=lnp[:, l, sub, 0, c:c + 1], bias=lnp[:, l, sub, 1, c:c + 1],
                 reads=[tk, "lnp"], writes=[("yo", c)])
        return P.dma("pool", hsrc(st, t0, n), yo[:, :, :n], reads=[("yo", c) for c in range(8)],
                     writes=[("H", st, t0)])

    def ffn_sublayer(l, sub, wg_ap, wu_ap, wd_ap, do_ctx=True):
        fi = 0 if sub == 0 else 1
        es = ExitStack()
        P.es = es
        hT = [P.sb([128, 8, TG], F32, "hT%d" % i) for i in range(2)]
        uT = P.sb([128, 8, TG], BF16, "uT")
        hff = P.sb([128, NFF, TG], BF16, "hff")
        lt = alloc_ln_tiles()
        vv = lt["vv"]
        sg = [P.sb([128, TG], F32, "sg%d" % i) for i in range(2)]
        wgu = [P.sb([128, 2, 8, 512], BF16, "wgu%d" % i) for i in range(2)]
        wd = [P.sb([128, NFF, 128], BF16, "wd%d" % i) for i in range(3)]
        cur_s = None
        for (st, t0, n) in groups:
            if st == "c" and not do_ctx:
                continue
            s = 1 if st == "c" else 0
            if s != cur_s:
                mod_scalars(l, sub, s)
                cur_s = s
            gi = cnts["g"]
            cnts["g"] += 1
            h = hT[gi % 2]
            hk = "hT%d" % (gi % 2)
            P.dma("pool", h[:, :, :n], hsrc(st, t0, n), reads=[("H", st, t0)], writes=[hk])
            for c in range(8):
                P.op("act", "activation", uT[:, c, :n], h[:, c, :n], AF.Identity,
                     scale=scl[:, 0, c:c + 1], bias=scl[:, 1, c:c + 1],
                     reads=[hk, "scl0", "scl1"], writes=[("uT", c)])
            for pi_, (f0, fw) in enumerate(_ffn_pieces()):
                wi = cnts["wgu"] % 2
                cnts["wgu"] += 1
                wt = wgu[wi]
                P.dma("sp", wt[:, 0, :, :fw * 128], WGU[0, fi, 0, pi_][:, :, :fw * 128], writes=[("wgu", wi, 0)])
                P.dma("sp", wt[:, 1, :, :fw * 128], WGU[0, fi, 1, pi_][:, :, :fw * 128], writes=[("wgu", wi, 1)])
                for ff in range(fw):
                    f = f0 + ff
                    pi = cnts["ps"] % 2
                    cnts["ps"] += 1
                    for k in range(8):
                        P.op("pe", "matmul", pG[pi][:, :n], wt[:, 0, k, ff * 128:(ff + 1) * 128], uT[:, k, :n],
                             start=(k == 0), stop=(k == 7), reads=[("wgu", wi, 0), ("uT", k)], writes=["pG%d" % pi],
                             inc=(k == 7))
                    for k in range(8):
                        P.op("pe", "matmul", pU[pi][:, :n], wt[:, 1, k, ff * 128:(ff + 1) * 128], uT[:, k, :n],
                             start=(k == 0), stop=(k == 7), reads=[("wgu", wi, 1), ("uT", k)], writes=["pU%d" % pi],
                             inc=(k == 7))
                    P.op("act", "activation", sg[pi][:, :n], pG[pi][:, :n], AF.Silu,
                         reads=["pG%d" % pi], writes=["sg%d" % pi])
                    P.op("dve", "tensor_tensor", hff[:, f, :n], sg[pi][:, :n], pU[pi][:, :n], ALU.mult,
                         reads=["sg%d" % pi, "pU%d" % pi], writes=[("hff", f)])
            for c in range(8):
                wi = cnts["wd"] % 3
                cnts["wd"] += 1
                P.dma("sp", wd[wi][:], WDS[0, fi, c], writes=[("wd", wi)])
                pi = c % 2
                for f in range(NFF):
                    P.op("pe", "matmul", pD[pi][:, :n], wd[wi][:, f, :], hff[:, f, :n],
                         start=(f == 0), stop=(f == NFF - 1), reads=[("wd", wi), ("hff", f)], writes=["pD%d" % pi],
                         inc=(f == NFF - 1))
                P.op("dve", "scalar_tensor_tensor", vv[:, c, :n], pD[pi][:, :n], scl[:, 2, c:c + 1], h[:, c, :n],
                     ALU.mult, ALU.add, reads=["pD%d" % pi, "scl2", hk], writes=[("vv", c)])
            layer_norm_store(lt, st, t0, n, l, sub)
        P.barrier()
        es.close()

    def mm(out, lhsT, rhs, reads, writes, start=True, stop=True, inc=True):
        P.op("pe", "matmul", out, lhsT, rhs, start=start, stop=stop, reads=reads, writes=writes, inc=inc)

    def act(out, in_, func, reads, writes, **kw):
        P.op("act", "activation", out, in_, func, reads=reads, writes=writes, **kw)

    def tt_(out, a, b, op, reads, writes, e="dve"):
        P.op(e, "tensor_tensor", out, a, b, op, reads=reads, writes=writes)

    def ts_(out, a, s1, s2, op0, op1, reads, writes, e="dve"):
        if op1 is None:
            P.op(e, "tensor_scalar", out, a, s1, None, op0, reads=reads, writes=writes)
        else:
            P.op(e, "tensor_scalar", out, a, s1, s2, op0, op1, reads=reads, writes=writes)

    def stt(out, a, s, b, op0, op1, reads, writes):
        P.op("dve", "scalar_tensor_tensor", out, a, s, b, op0, op1, reads=reads, writes=writes)

    def tr(out, in_, reads, writes, np_=128, inc=True):
        P.op("pe", "transpose", out, in_, ident[:np_, :np_], reads=list(reads) + ["cst"], writes=writes, inc=inc)

    def in_pieces():
        out = []
        c = 0
        while c < DIN:
            w = min(512, DIN - c)
            out.append((c, w))
            c += w
        return out

    def inproj(l, do_ctx=True):
        es = ExitStack()
        P.es = es
        hT = [P.sb([128, 8, TG], F32, "hT%d" % i) for i in range(2)]
        uT = P.sb([128, 8, TG], BF16, "uT")
        wt_ = [P.sb([128, 8, 512], BF16, "wi%d" % i) for i in range(2)]
        stg = [P.sb([128, 512], F32, "stg%d" % i) for i in range(4)]
        nst = [0]
        cur_s = None
        for (st, t0, n) in groups:
            s = 1 if st == "c" else 0
            off = toff(st, t0)
            if s != cur_s:
                mod_scalars(l, 1, s)
                cur_s = s
            gi = cnts["g"]
            cnts["g"] += 1
            h = hT[gi % 2]
            hk = "hT%d" % (gi % 2)
            P.dma("sp", h[:, :, :n], hsrc(st, t0, n), reads=[("H", st, t0)], writes=[hk])
            for c in range(8):
                act(uT[:, c, :n], h[:, c, :n], AF.Identity, [hk, "scl0", "scl1"], [("uT", c)],
                    scale=scl[:, 0, c:c + 1], bias=scl[:, 1, c:c + 1])
            for (c0, w) in in_pieces():
                wi = cnts["wgu"] % 2
                cnts["wgu"] += 1
                wt = wt_[wi]
                wk = ("wi", wi)
                P.dma("pool", wt[:, :, :w], w_in[l][:, c0:c0 + w].rearrange("(k p) j -> p k j", p=128), writes=[wk])
                j = 0
                while j < w:
                    cw = min(128, w - j)
                    pi = cnts["ps"] % 2
                    cnts["ps"] += 1
                    for k in range(8):
                        mm(pG[pi][:cw, :n], wt[:, k, j:j + cw], uT[:, k, :n], [wk, ("uT", k)], ["pG%d" % pi],
                           start=(k == 0), stop=(k == 7), inc=(k == 7))
                    si = nst[0] % 4
                    nst[0] += 1
                    act(stg[si][:cw, :n], pG[pi][:cw, :n], AF.Copy, ["pG%d" % pi], [("stg", si)])
                    P.dma("sp", PF[c0 + j:c0 + j + cw, off:off + n], stg[si][:cw, :n], reads=[("stg", si)],
                          writes=[("PF", off)])
                    j += cw
                for ts in range(n // 128):
                    pi = cnts["ps"] % 2
                    cnts["ps"] += 1
                    for k in range(8):
                        mm(pU[pi][:, :w], uT[:, k, ts * 128:(ts + 1) * 128], wt[:, k, :w], [wk, ("uT", k)],
                           ["pU%d" % pi], start=(k == 0), stop=(k == 7), inc=(k == 7))
                    si = nst[0] % 4
                    nst[0] += 1
                    P.op("dve", "tensor_copy", stg[si][:, :w], pU[pi][:, :w], reads=["pU%d" % pi], writes=[("stg", si)])
                    P.dma("sp", PM[off + ts * 128:off + (ts + 1) * 128, c0:c0 + w], stg[si][:, :w],
                          reads=[("stg", si)], writes=[("PM", off)])
        P.barrier()
        es.close()

    def swa(l, do_ctx):
        es = ExitStack()
        P.es = es
        kT_all = P.sb([64, 2, T], BF16, "kT_all")
        V_all = P.sb([128, NTL, 2, 65], BF16, "V_all")
        esink = P.sb([128, 8], F32, "esink")
        sk = P.sb([128, L, 8], F32, "sk")
        P.dma("sp", sk[:], sinkB, writes=["sk"])
        act(esink[:], sk[:, l, :], AF.Exp, ["sk"], ["esink"])
        P.op("dve", "memset", V_all[:], 1.0, writes=["V_all"])
        km = [P.sb([128, 2, 64], F32, "km%d" % i) for i in range(2)]
        vm = [P.sb([128, 2, 64], F32, "vm%d" % i) for i in range(2)]
        rp = [P.sb([128, 2, 32], F32, "rp%d" % i) for i in range(2)]
        kr = P.sb([128, 2, 64], F32, "kr")
        qm = [P.sb([128, 8, 64], F32, "qm%d" % i) for i in range(2)]
        qr = P.sb([128, 8, 64], F32, "qr")
        ta = P.sb([128, 8, 32], F32, "ta")
        tb = P.sb([128, 8, 32], F32, "tb")
        qT = P.sb([64, 8, 128], BF16, "qT")
        pT = [P.sb([128, 512], BF16, "pT%d" % i) for i in range(10)]
        den = P.sb([128, 4], F32, "den")
        mixb = [P.sb([128, 512], F32, "mixb%d" % i) for i in range(2)]

        def rope(dst, src, nh, tab, rk_src, rk_dst, rk_tab):
            sv = src.rearrange("p h (a s i) -> p h a s i", a=2, s=2)
            dv = dst.rearrange("p h (a s i) -> p h a s i", a=2, s=2)
            cos = tab[:, 0, :].rearrange("p (a i) -> p a i", a=2).unsqueeze(1).to_broadcast([128, nh, 2, 16])
            sin = tab[:, 1, :].rearrange("p (a i) -> p a i", a=2).unsqueeze(1).to_broadcast([128, nh, 2, 16])
            tav = ta[:, :nh, :].rearrange("p h (a i) -> p h a i", a=2)
            tbv = tb[:, :nh, :].rearrange("p h (a i) -> p h a i", a=2)
            tt_(tav, sv[:, :, :, 0, :], cos, ALU.mult, [rk_src, rk_tab], ["ta"])
            tt_(tbv, sv[:, :, :, 1, :], sin, ALU.mult, [rk_src, rk_tab], ["tb"])
            tt_(dv[:, :, :, 0, :], tav, tbv, ALU.subtract, ["ta", "tb"], [rk_dst])
            tt_(tav, sv[:, :, :, 0, :], sin, ALU.mult, [rk_src, rk_tab], ["ta"])
            tt_(tbv, sv[:, :, :, 1, :], cos, ALU.mult, [rk_src, rk_tab], ["tb"])
            tt_(dv[:, :, :, 1, :], tav, tbv, ALU.add, ["ta", "tb"], [rk_dst])

        for tt in range(NTL):
            i2 = tt % 2
            r0 = tt * 128
            P.dma("sp", km[i2][:], PM[r0:r0 + 128, 1312:1440].rearrange("t (h d) -> t h d", h=2),
                  reads=[("PM", 0)], writes=[("km", i2)])
            P.dma("sp", vm[i2][:], PM[r0:r0 + 128, 1440:1568].rearrange("t (h d) -> t h d", h=2),
                  reads=[("PM", 0)], writes=[("vm", i2)])
            P.op("dve", "tensor_copy", V_all[:, tt, :, 0:64], vm[i2][:], reads=[("vm", i2)], writes=["V_all"])
            if r0 >= LC:
                P.dma("sp", rp[i2][:], ropeM[r0 - LC:r0 - LC + 128], writes=[("rp", i2)])
                rope(kr[:], km[i2][:], 2, rp[i2], ("km", i2), "kr", ("rp", i2))
                ksrc, kk_ = kr, "kr"
            else:
                ksrc, kk_ = km[i2], ("km", i2)
            for hk in range(2):
                tr(pS[0][:64, hk * 128:(hk + 1) * 128], ksrc[:, hk, :], [kk_], ["pS0"], inc=(hk == 1))
            act(kT_all[:, :, r0:r0 + 128], pS[0][:64, 0:256].rearrange("p (h t) -> p h t", h=2), AF.Copy,
                ["pS0"], ["kT_all"])
        nmix = 0
        for tt in range(NTL):
            r0 = tt * 128
            isx = r0 >= LC
            if not isx and not do_ctx:
                continue
            i2 = tt % 2
            P.dma("sp", qm[i2][:], PM[r0:r0 + 128, 800:1312].rearrange("t (h d) -> t h d", h=8),
                  reads=[("PM", 0)], writes=[("qm", i2)])
            if isx:
                P.dma("sp", rp[i2][:], ropeM[r0 - LC:r0 - LC + 128], writes=[("rp", i2)])
                rope(qr[:], qm[i2][:], 8, rp[i2], ("qm", i2), "qr", ("rp", i2))
                qsrc, qk_ = qr, "qr"
            else:
                qsrc, qk_ = qm[i2], ("qm", i2)
            for half in range(2):
                for hh in range(4):
                    tr(pS[half][:64, hh * 128:(hh + 1) * 128], qsrc[:, half * 4 + hh, :], [qk_], ["pS%d" % half],
                       inc=(hh == 3))
                act(qT[:, half * 4:half * 4 + 4, :], pS[half][:64, :].rearrange("p (h t) -> p h t", h=4), AF.Copy,
                    ["pS%d" % half], ["qT"])
            keys = []
            if isx:
                nct = LC // 128
                if tt - 1 >= nct:
                    keys.append((tt - 1, 0))
                keys.append((tt, None))
                if tt + 1 < NTL:
                    keys.append((tt + 1, 1))
            keys += [(c, None) for c in range(LC // 128)]
            mb = mixb[nmix % 2]
            mbk = ("mixb", nmix % 2)
            nmix += 1
            for hk in range(2):
                for ki, (kt, mid) in enumerate(keys):
                    pi = cnts["ps"] % 2
                    cnts["ps"] += 1
                    for g in range(4):
                        mm(pG[pi][:, g * 128:(g + 1) * 128], kT_all[:, hk, kt * 128:(kt + 1) * 128], qT[:, hk * 4 + g, :],
                           ["kT_all", "qT"], ["pG%d" % pi], inc=(g == 3))
                    pt = pT[hk * 5 + ki]
                    ptk = ("pT", hk * 5 + ki)
                    act(pt[:], pG[pi][:], AF.Exp, ["pG%d" % pi], [ptk], scale=0.125)
                    if mid is not None:
                        mk = cst[:, C_SWM + mid * 128:C_SWM + (mid + 1) * 128].unsqueeze(1).to_broadcast([128, 4, 128])
                        tt_(pt[:].rearrange("p (g q) -> p g q", g=4), pt[:].rearrange("p (g q) -> p g q", g=4), mk,
                            ALU.mult, [ptk, "cst"], [ptk])
                po = pD[hk]
                for g in range(4):
                    for ki, (kt, mid) in enumerate(keys):
                        mm(po[:, g * 65:(g + 1) * 65], pT[hk * 5 + ki][:, g * 128:(g + 1) * 128], V_all[:, kt, hk, :],
                           [("pT", hk * 5 + ki), "V_all"], ["pD%d" % hk], start=(ki == 0), stop=(ki == len(keys) - 1),
                           inc=(ki == len(keys) - 1 and g == 3))
                pov = po[:, 0:260].rearrange("p (g e) -> p g e", g=4)
                tt_(den[:], pov[:, :, 64], esink[:, hk * 4:hk * 4 + 4], ALU.add, ["pD%d" % hk, "esink"], ["den"])
                P.op("dve", "reciprocal", den[:], den[:], reads=["den"], writes=["den"])
                tt_(mb[:, hk * 256:(hk + 1) * 256].rearrange("p (g e) -> p g e", g=4), pov[:, :, 0:64],
                    den[:].unsqueeze(2).to_broadcast([128, 4, 64]), ALU.mult, ["pD%d" % hk, "den"], [mbk])
            P.dma("sp", MIX[r0:r0 + 128, 256:768], mb[:], reads=[mbk], writes=[("MIX", "b", tt)])
        P.barrier()
        es.close()

    def run_interleaved(gens):
        gens = list(gens)
        while gens:
            for g in list(gens):
                try:
                    next(g)
                except StopIteration:
                    gens.remove(g)

    def chunk_order(d):
        nc_c = LC // 64
        if d == 0:
            return list(range(NCH))
        return list(range(nc_c - 1, -1, -1)) + list(range(NCH - 1, nc_c - 1, -1))

    def gla(l, do_ctx):
        es = ExitStack()
        P.es = es
        upa = P.sb([17, 2, 128], F32, "upa")
        P.dma("sp", upa[:], gla_up[l].rearrange("d r c -> r d c"), writes=["upa"])
        gng = P.sb([64, 256], F32, "gng")
        P.dma("sp", gng[:], gla_g[l].partition_broadcast(64), writes=["gng"])
        Sx = [P.sb([128, 256], F32, "Sx%d" % d) for d in range(2)]
        for d in range(2):
            P.op("dve", "memset", Sx[d][:], 0.0, writes=[("Sx", d)])
        triI = [cst[0:64, C_TRI + d * 64:C_TRI + (d + 1) * 64] for d in range(2)]
        incl = [cst[0:64, C_M01 + d * 64:C_M01 + (d + 1) * 64] for d in range(2)]
        gbd = cst[:, C_GBD:C_GBD + 256]
        pb = {0: (pG[0], "pG0", pU[0], "pU0", pD[0], "pD0"), 1: (pG[1], "pG1", pU[1], "pU1", pD[1], "pD1")}

        def unit(d):
            pa, pak, pbb, pbk, pc_, pck = pb[d]
            tl = {}
            for nm, shp in (("qT", [128, 64]), ("kT", [128, 64]), ("zT", [17, 64]), ("kM", [64, 128]), ("vM", [64, 256]),
                            ("gaM", [64, 256]), ("ofM", [64, 256])):
                tl[nm] = [P.sb(shp, F32, "g%s%d_%d" % (nm, d, i)) for i in range(2)]
            for i in range(2):
                P.op("dve", "memset", tl["zT"][i][:], 1.0, writes=[("zT", d, i)])
            sp_ = P.sb([64, 128], F32, "gsp%d" % d)
            ebT = P.sb([128, 64], F32, "gebT%d" % d)
            enbT = P.sb([128, 64], F32, "genbT%d" % d)
            enbM = P.sb([64, 128], F32, "genbM%d" % d)
            qs = P.sb([128, 64], F32, "gqs%d" % d)
            kmk = P.sb([128, 4, 64], F32, "gkmk%d" % d)
            KtM = P.sb([64, 128], F32, "gKtM%d" % d)
            att = P.sb([64, 4, 64], F32, "gatt%d" % d)
            tmpS = P.sb([128, 256], F32, "gtmpS%d" % d)
            osb = P.sb([64, 4, 64], F32, "gosb%d" % d)
            osq = P.sb([64, 4, 64], F32, "gosq%d" % d)
            ssq = P.sb([64, 4], F32, "gssq%d" % d)
            sga = P.sb([64, 256], F32, "gsga%d" % d)
            K = lambda s: (s, d)
            for ui, c in enumerate(chunk_order(d)):
                i2 = ui % 2
                r0 = c * 64
                isx = r0 >= LC
                KB = lambda s: (s, d, i2)
                P.dma("sp", tl["qT"][i2][:], PF[0:128, r0:r0 + 64], reads=[("PF", 0)], writes=[KB("qT")])
                P.dma("sp", tl["kT"][i2][:], PF[128:256, r0:r0 + 64], reads=[("PF", 0)], writes=[KB("kT")])
                P.dma("sp", tl["zT"][i2][0:16, :], PF[768 + 16 * d:784 + 16 * d, r0:r0 + 64], reads=[("PF", 0)],
                      writes=[KB("zT")])
                P.dma("sp", tl["kM"][i2][:], PM[r0:r0 + 64, 128:256], reads=[("PM", 0)], writes=[KB("kM")])
                P.dma("sp", tl["vM"][i2][:], PM[r0:r0 + 64, 256:512], reads=[("PM", 0)], writes=[KB("vM")])
                fin = (d == 1) and (isx or do_ctx)
                if fin:
                    P.dma("sp", tl["gaM"][i2][:], PM[r0:r0 + 64, 512:768], reads=[("PM", 0)], writes=[KB("gaM")])
                    P.dma("sp", tl["ofM"][i2][:], OG[r0:r0 + 64, :], reads=[("OG", c)], writes=[KB("ofM")])
                qT, kT, zT, kM, vM = (tl[n_][i2] for n_ in ("qT", "kT", "zT", "kM", "vM"))
                yield
                mm(pa[:64, 0:128], zT[:], upa[:, d, :], [KB("zT"), "upa"], [pak])
                act(sp_[:], pa[:64, 0:128], AF.Exp, [pak], [K("sp")], scale=-1.0)
                act(sp_[:], sp_[:], AF.Ln, [K("sp"), "kc"], [K("sp")], bias=kc[:64, 1:2])
                yield
                mm(pa[:, 128:192], sp_[:], triI[d], [K("sp"), "cst"], [pak], inc=False)
                mm(pa[:64, 256:384], triI[d], sp_[:], [K("sp"), "cst"], [pak])
                act(ebT[:], pa[:, 128:192], AF.Exp, [pak], [K("ebT")], scale=1.0 / 16)
                act(enbT[:], pa[:, 128:192], AF.Exp, [pak], [K("enbT")], scale=-1.0 / 16)
                act(enbM[:], pa[:64, 256:384], AF.Exp, [pak], [K("enbM")], scale=-1.0 / 16)
                yield
                stt(qs[:], qT[:], 32.0 ** -0.5, ebT[:], ALU.mult, ALU.mult, [KB("qT"), K("ebT")], [K("qs")])
                for h in range(4):
                    stt(kmk[:, h, :], kT[:], cst[:, C_GHM + h:C_GHM + h + 1], enbT[:], ALU.mult, ALU.mult,
                        [KB("kT"), K("enbT"), "cst"], [K("kmk")])
                tt_(KtM[:], kM[:], enbM[:], ALU.mult, [KB("kM"), K("enbM")], [K("KtM")])
                yield
                for h in range(4):
                    mm(pbb[:64, h * 64:(h + 1) * 64], kmk[:, h, :], qs[:], [K("kmk"), K("qs")], [pbk], inc=(h == 3))
                tt_(att[:], pbb[:64, 0:256].rearrange("p (h t) -> p h t", h=4),
                    incl[d].unsqueeze(1).to_broadcast([64, 4, 64]), ALU.mult, [pbk, "cst"], [K("att")])
                yield
                for h in range(4):
                    mm(pc_[:64, h * 64:(h + 1) * 64], att[:, h, :], vM[:, h * 64:(h + 1) * 64], [K("att"), KB("vM")], [pck],
                       start=True, stop=False, inc=False)
                    mm(pc_[:64, h * 64:(h + 1) * 64], qs[:], Sx[d][:, h * 64:(h + 1) * 64], [K("qs"), ("Sx", d)], [pck],
                       start=False, stop=True, inc=(h == 3))
                mm(pbb[:, 256:512], KtM[:], vM[:], [K("KtM"), KB("vM")], [pbk])
                yield
                tt_(tmpS[:], pbb[:, 256:512], gbd, ALU.mult, [pbk, "cst"], [K("tmpS")])
                tt_(tmpS[:], tmpS[:], Sx[d][:], ALU.add, [K("tmpS"), ("Sx", d)], [K("tmpS")])
                last = 63 if d == 0 else 0
                ts_(Sx[d][:], tmpS[:], ebT[:, last:last + 1], None, ALU.mult, None, [K("tmpS"), K("ebT")], [("Sx", d)])
                if d == 0:
                    P.op("dve", "tensor_copy", osb[:].rearrange("p h e -> p (h e)"), pc_[:64, 0:256], reads=[pck],
                         writes=[K("osb")])
                    P.dma("sp", OG[r0:r0 + 64, :], osb[:].rearrange("p h e -> p (h e)"), reads=[K("osb")],
                          writes=[("OG", c)])
                elif fin:
                    gaM, ofM = tl["gaM"][i2], tl["ofM"][i2]
                    tt_(osb[:].rearrange("p h e -> p (h e)"), pc_[:64, 0:256], ofM[:], ALU.add, [pck, KB("ofM")],
                        [K("osb")])
                    tt_(osq[:], osb[:], osb[:], ALU.mult, [K("osb")], [K("osq")])
                    P.op("dve", "tensor_reduce", ssq[:], osq[:], AX.X, ALU.add, reads=[K("osq")], writes=[K("ssq")])
                    act(ssq[:], ssq[:], AF.Sqrt, [K("ssq"), "kc"], [K("ssq")], scale=1.0 / 64, bias=kc[:64, 3:4])
                    P.op("dve", "reciprocal", ssq[:], ssq[:], reads=[K("ssq")], writes=[K("ssq")])
                    tt_(osb[:], osb[:], ssq[:].unsqueeze(2).to_broadcast([64, 4, 64]), ALU.mult, [K("osb"), K("ssq")],
                        [K("osb")])
                    tt_(osb[:].rearrange("p h e -> p (h e)"), osb[:].rearrange("p h e -> p (h e)"), gng[:], ALU.mult,
                        [K("osb"), "gng"], [K("osb")])
                    act(sga[:], gaM[:], AF.Silu, [KB("gaM")], [K("sga")])
                    tt_(osb[:].rearrange("p h e -> p (h e)"), osb[:].rearrange("p h e -> p (h e)"), sga[:], ALU.mult,
                        [K("osb"), K("sga")], [K("osb")])
                    P.dma("sp", MIX[r0:r0 + 64, 0:256], osb[:].rearrange("p h e -> p (h e)"), reads=[K("osb")],
                          writes=[("MIX", "a", c)])
                yield

        run_interleaved([unit(0)])
        run_interleaved([unit(1)])
        P.barrier()
        es.close()

    def rwkv_prep(l):
        es = ExitStack()
        P.es = es
        rv = P.sb([128, 2, 7], F32, "rv")
        P.dma("sp", rv[:], rw_vec[:, l], writes=["rv"])
        nw0 = P.sb([128, 2, 2], F32, "nw0")
        ts_(nw0[:], rv[:, :, 3:5], -1.0, None, ALU.mult, None, ["rv"], ["nw0"])
        mu = P.sb([128, 11], F32, "mu")
        P.dma("sp", mu[:], rw_mu[:, l], writes=["mu"])
        wup = P.sb([64, 2, 256], F32, "wup")
        P.dma("sp", wup[:], rw_wup[l].rearrange("d r c -> r d c"), writes=["wup"])
        aup = P.sb([64, 2, 256], F32, "aup")
        P.dma("sp", aup[:], rw_aup[l].rearrange("d r c -> r d c"), writes=["aup"])
        bo = cst[:, C_BO:C_BO + 128]
        NB = 16
        bufs = [P.sb([128, TG + 2], F32, "rb%d" % i) for i in range(NB)]
        fbufs = [P.sb([128, TG + 2], F32, "rf%d" % i) for i in range(13)]
        nb = [0]

        def newb():
            i = nb[0] % NB
            nb[0] += 1
            return bufs[i], ("rb", i)

        R0 = 1568
        srcs = [(R0 + 128 * j, 128) for j in range(6)] + [(R0 + 768, 64), (R0 + 832, 64), (R0 + 896, 64), (R0 + 960, 64),
                                                          (R0 + 1024, 128)]
        for (st, t0, n) in groups:
            off = toff(st, t0)
            seq_lo = 0 if st == "c" else LC
            seq_hi = LC if st == "c" else T
            lo = max(off - 1, seq_lo)
            hi = min(off + n + 1, seq_hi)
            f = []
            for j, (r0, nr) in enumerate(srcs):
                ld, ldk = newb()
                P.op("dve", "memset", ld[:, 0:n + 2], 0.0, writes=[ldk])
                P.dma("sp", ld[:nr, lo - off + 1:hi - off + 1], PF[r0:r0 + nr, lo:hi], reads=[("PF", 0)], writes=[ldk])
                sh, shk = newb()
                tt_(sh[:nr, :n], ld[:nr, 0:n], ld[:nr, 2:n + 2], ALU.add, [ldk], [shk])
                stt(sh[:nr, :n], sh[:nr, :n], 0.5, ld[:nr, 1:n + 1], ALU.mult, ALU.subtract, [shk, ldk], [shk])
                fo, fok = fbufs[j], ("rf", j)
                stt(fo[:nr, :n], sh[:nr, :n], mu[:nr, j:j + 1], ld[:nr, 1:n + 1], ALU.mult, ALU.add, [shk, ldk, "mu"], [fok])
                f.append((fo, fok))
            rr, kk_, vv_ = f[0:2], f[2:4], f[4:6]
            zw, za, zg = f[6:8], f[8:10], f[10]
            for pc in range(2):
                P.dma("sp", RF[0 + pc, :, off:off + n], rr[pc][0][:, :n], reads=[rr[pc][1]], writes=[("RF", off)])
                P.dma("sp", RF[2 + pc, :, off:off + n], vv_[pc][0][:, :n], reads=[vv_[pc][1]], writes=[("RF", off)])
            kkn = []
            for pc in range(2):
                k_, kk2 = kk_[pc]
                sq, sqk = newb()
                act(sq[:, :n], k_[:, :n], AF.Square, [kk2, "rv"], [sqk], scale=rv[:, pc, 0:1])
                mm(pS[0][:, :n], bo, sq[:, :n], ["cst", sqk], ["pS0"])
                act(sq[:, :n], pS[0][:, :n], AF.Sqrt, ["pS0"], [sqk])
                ts_(sq[:, :n], sq[:, :n], 1e-12, None, ALU.max, None, [sqk], [sqk])
                P.op("dve", "reciprocal", sq[:, :n], sq[:, :n], reads=[sqk], writes=[sqk])
                kn, knk = fbufs[11 + pc], ("rf", 11 + pc)
                stt(kn[:, :n], k_[:, :n], rv[:, pc, 0:1], sq[:, :n], ALU.mult, ALU.mult, [kk2, "rv", sqk], [knk])
                P.dma("sp", RF[4 + pc, :, off:off + n], kn[:, :n], reads=[knk], writes=[("RF", off)])
                kkn.append((kn, knk))
            ksum = [None, None]
            for d in range(2):
                th, thk = newb()
                act(th[:64, :n], zw[d][0][:64, :n], AF.Tanh, [zw[d][1]], [thk])
                for pc in range(2):
                    mm(pS[1][:, :n], wup[:, d, pc * 128:(pc + 1) * 128], th[:64, :n], ["wup", thk], ["pS1"])
                    ew, ewk = newb()
                    act(ew[:, :n], pS[1][:, :n], AF.Exp, ["pS1", "nw0"], [ewk], scale=-1.0, bias=nw0[:, pc, d:d + 1])
                    act(ew[:, :n], ew[:, :n], AF.Ln, [ewk, "kc"], [ewk], bias=kc[:, 1:2])
                    act(ew[:, :n], ew[:, :n], AF.Exp, [ewk, "kc"], [ewk], scale=-1.0, bias=kc[:, 2:3])
                    P.dma("sp", RF[10 + 6 * d + pc, :, off:off + n], ew[:, :n], reads=[ewk], writes=[("RF", off)])
                    mm(pS[0][:, :n], aup[:, d, pc * 128:(pc + 1) * 128], za[d][0][:64, :n], ["aup", za[d][1]], ["pS0"])
                    a_, ak = newb()
                    act(a_[:, :n], pS[0][:, :n], AF.Sigmoid, ["pS0", "rv"], [ak], bias=rv[:, pc, 5 + d:6 + d])
                    bd, bdk = newb()
                    tt_(bd[:, :n], a_[:, :n], kkn[pc][0][:, :n], ALU.mult, [ak, kkn[pc][1]], [bdk])
                    P.dma("sp", RF[8 + 6 * d + pc, :, off:off + n], bd[:, :n], reads=[bdk], writes=[("RF", off)])
                    ts_(a_[:, :n], a_[:, :n], 1.0, rv[:, pc, 1:2], ALU.subtract, ALU.mult, [ak, "rv"], [ak])
                    kd, kdk = newb()
                    stt(kd[:, :n], a_[:, :n], 1.0, kk_[pc][0][:, :n], ALU.add, ALU.mult, [ak, kk_[pc][1]], [kdk])
                    P.dma("sp", RF[6 + 6 * d + pc, :, off:off + n], kd[:, :n], reads=[kdk], writes=[("RF", off)])
                    if d == 0:
                        ksum[pc] = (kd, kdk)
                    else:
                        kf, kfk = ksum[pc]
                        tt_(kd[:, :n], kd[:, :n], kf[:, :n], ALU.add, [kdk, kfk], [kdk])
                        ts_(kd[:, :n], kd[:, :n], 0.5, rv[:, pc, 2:3], ALU.mult, ALU.mult, [kdk, "rv"], [kdk])
                        tt_(kd[:, :n], kd[:, :n], rr[pc][0][:, :n], ALU.mult, [kdk, rr[pc][1]], [kdk])
                        P.dma("sp", RF[18 + pc, :, off:off + n], kd[:, :n], reads=[kdk], writes=[("RF", off)])
            sz, szk = newb()
            act(sz[:, :n], zg[0][:, :n], AF.Sigmoid, [zg[1]], [szk])
            P.dma("sp", RF[20, :, off:off + n], sz[:, :n], reads=[szk], writes=[("RF", off)])
        P.barrier()
        es.close()

    def rwkv_scan(l):
        import os as _os
        _rwcut = int(_os.environ.get('RW_CUT', '99'))
        es = ExitStack()
        P.es = es
        tri = [cst[0:64, C_TRI + i * 64:C_TRI + (i + 1) * 64] for i in range(4)]
        m01 = [cst[0:64, C_M01 + i * 64:C_M01 + (i + 1) * 64] for i in range(4)]
        id64 = cst[0:64, C_ID:C_ID + 64]
        Hx = [[P.sb([128, 256], F32, "Hx%d%d" % (d, pc)) for pc in range(2)] for d in range(2)]
        for d in range(2):
            for pc in range(2):
                P.op("dve", "memset", Hx[d][pc][:], 0.0, writes=[("Hx", d, pc)])
        bc4 = lambda m: m.unsqueeze(1).to_broadcast([64, 4, 64])
        banks = {"A": (pG[0], "pG0"), "B": (pG[1], "pG1"), "C": (pU[0], "pU0"), "D": (pU[1], "pU1"),
                 "E": (pD[0], "pD0"), "F": (pD[1], "pD1"), "G": (pS[0], "pS0"), "H": (pS[1], "pS1")}

        def unit(d):
            K = lambda s: (s, d)
            cm = [P.sb([128, 6, 64], F32, "cm%d_%d" % (d, i)) for i in range(2)]
            dm = [P.sb([128, 6, 64], F32, "dm%d_%d" % (d, i)) for i in range(2)]
            ewM = P.sb([64, 256], F32, "ewM%d" % d)
            vM = P.sb([64, 256], F32, "rvM%d" % d)
            Pd = P.sb([128, 2, 3, 64], F32, "Pd%d" % d)
            fT = P.sb([128, 2, 4, 64], F32, "fT%d" % d)
            mT_ = P.sb([128, 2, 2, 2, 64], F32, "mK%d" % d)
            KBM = P.sb([64, 2, 256], F32, "KBM%d" % d)
            sc = P.sb([64, 5, 4, 64], F32, "sc%d" % d)
            Tt = P.sb([64, 4, 64], F32, "Tt%d" % d)
            TtT = P.sb([64, 4, 64], F32, "TtT%d" % d)
            Nn = [P.sb([64, 2, 4, 64], F32, "Nn%d_%d" % (d, i)) for i in range(2)]
            Xs = P.sb([64, 4, 64], F32, "Xs%d" % d)
            Us = P.sb([64, 4, 64], F32, "Us%d" % d)
            Ys = P.sb([64, 256], F32, "Ys%d" % d)
            tmpH = P.sb([128, 256], F32, "tmpH%d" % d)
            bA, kA = banks["A"]; bB, kB = banks["B"]; bC, kC = banks["C"]; bD, kD = banks["D"]
            bE, kE = banks["E"]; bF, kF = banks["F"]; bG, kG = banks["G"]; bH, kH = banks["H"]
            v4 = lambda ap: ap.rearrange("p (h t) -> p h t", h=4)
            for ui, c in enumerate(chunk_order(d)):
                i2 = ui % 2
                r0 = c * 64
                KB = lambda s: (s, d, i2)
                P.dma("sp", cm[i2][:], RF[0:6, :, r0:r0 + 64].rearrange("j p t -> p j t"), reads=[("RF", 0)],
                      writes=[KB("cm")])
                P.dma("sp", dm[i2][:], RF[6 + 6 * d:12 + 6 * d, :, r0:r0 + 64].rearrange("j p t -> p j t"),
                      reads=[("RF", 0)], writes=[KB("dm")])
                cmt, dmt = cm[i2], dm[i2]
                yield
                if _rwcut <= 1:
                    continue
                for pc in range(2):
                    tr(bG[:64, pc * 128:(pc + 1) * 128], dmt[:, 4 + pc, :], [KB("dm")], [kG], inc=False)
                    tr(bG[:64, 256 + pc * 128:256 + (pc + 1) * 128], cmt[:, 2 + pc, :], [KB("cm")], [kG], inc=(pc == 1))
                P.op("dve", "tensor_copy", ewM[:], bG[:64, 0:256], reads=[kG], writes=[K("ewM")])
                P.op("dve", "tensor_copy", vM[:], bG[:64, 256:512], reads=[kG], writes=[K("vM")])
                yield
                if _rwcut <= 2:
                    continue
                for pc in range(2):
                    for ie in range(2):
                        mm(bH[:, (pc * 2 + ie) * 64:(pc * 2 + ie + 1) * 64], ewM[:, pc * 128:(pc + 1) * 128], tri[2 * ie + d],
                           [K("ewM"), "cst"], [kH], inc=(pc == 1 and ie == 1))
                lv = bH[:, 0:256].rearrange("p (c i t) -> p c i t", c=2, i=2)
                act(Pd[:, :, 0:2, :], lv, AF.Exp, [kH], [K("Pd")])
                act(Pd[:, :, 2, :], lv[:, :, 0, :], AF.Exp, [kH], [K("Pd")], scale=-1.0)
                yield
                if _rwcut <= 3:
                    continue
                tt_(fT[:, :, 0, :], cmt[:, 0:2, :], Pd[:, :, 0, :], ALU.mult, [KB("cm"), K("Pd")], [K("fT")])
                tt_(fT[:, :, 1, :], cmt[:, 4:6, :], Pd[:, :, 1, :], ALU.mult, [KB("cm"), K("Pd")], [K("fT")])
                tt_(fT[:, :, 2, :], dmt[:, 0:2, :], Pd[:, :, 2, :], ALU.mult, [KB("dm"), K("Pd")], [K("fT")])
                tt_(fT[:, :, 3, :], dmt[:, 2:4, :], Pd[:, :, 2, :], ALU.mult, [KB("dm"), K("Pd")], [K("fT")])
                for hh in range(2):
                    ts_(mT_[:, :, :, hh, :], fT[:, :, 2:4, :], cst[:, C_RHM + hh:C_RHM + hh + 1], None, ALU.mult, None,
                        [K("fT"), "cst"], [K("mK")])
                yield
                if _rwcut <= 4:
                    continue
                for pc in range(2):
                    tr(bG[:64, pc * 128:(pc + 1) * 128], fT[:, pc, 2, :], [K("fT")], [kG], inc=False)
                    tr(bG[:64, 256 + pc * 128:256 + (pc + 1) * 128], fT[:, pc, 3, :], [K("fT")], [kG], inc=(pc == 1))
                act(KBM[:, 0, :], bG[:64, 0:256], AF.Copy, [kG], [K("KBM")])
                act(KBM[:, 1, :], bG[:64, 256:512], AF.Copy, [kG], [K("KBM")], scale=-1.0)
                for h in range(4):
                    pc, hh = h // 2, h % 2
                    KdTm, BdTm = mT_[:, pc, 0, hh, :], mT_[:, pc, 1, hh, :]
                    KKeT, RpT = fT[:, pc, 1, :], fT[:, pc, 0, :]
                    rd_ = [K("mK"), K("fT")]
                    mm(bA[:64, h * 64:(h + 1) * 64], KdTm, KKeT, rd_, [kA], inc=False)
                    mm(bA[:64, 256 + h * 64:256 + (h + 1) * 64], BdTm, KKeT, rd_, [kA], inc=(h == 3))
                    mm(bB[:64, h * 64:(h + 1) * 64], KKeT, BdTm, rd_, [kB], inc=False)
                    mm(bB[:64, 256 + h * 64:256 + (h + 1) * 64], KdTm, RpT, rd_, [kB], inc=(h == 3))
                    mm(bC[:64, h * 64:(h + 1) * 64], BdTm, RpT, rd_, [kC], inc=(h == 3))
                yield
                if _rwcut <= 5:
                    continue
                tt_(sc[:, 0], v4(bA[:64, 0:256]), bc4(m01[2 + d]), ALU.mult, [kA, "cst"], [K("sc0")])
                tt_(sc[:, 1], v4(bA[:64, 256:512]), bc4(m01[2 + d]), ALU.mult, [kA, "cst"], [K("sc1")])
                tt_(sc[:, 2], v4(bB[:64, 0:256]), bc4(m01[3 - d]), ALU.mult, [kB, "cst"], [K("sc2")])
                tt_(sc[:, 3], v4(bB[:64, 256:512]), bc4(m01[d]), ALU.mult, [kB, "cst"], [K("sc3")])
                stt(sc[:, 4], v4(bC[:64, 0:256]), -1.0, bc4(m01[d]), ALU.mult, ALU.mult, [kC, "cst"], [K("sc4")])
                stt(Tt[:], sc[:, 1], -1.0, bc4(id64), ALU.mult, ALU.add, [K("sc1"), "cst"], [K("Tt")])
                stt(TtT[:], sc[:, 2], -1.0, bc4(id64), ALU.mult, ALU.add, [K("sc2"), "cst"], [K("TtT")])
                yield
                if _rwcut <= 6:
                    continue
                Ncur, NTcur, nk = sc[:, 1], sc[:, 2], [K("sc1"), K("sc2")]
                for lev in range(5):
                    lastl = lev == 4
                    for h in range(4):
                        mm(bD[:64, h * 64:(h + 1) * 64], NTcur[:, h, :], Ncur[:, h, :], nk, [kD], inc=(lastl and h == 3))
                        if not lastl:
                            mm(bD[:64, 256 + h * 64:256 + (h + 1) * 64], Ncur[:, h, :], NTcur[:, h, :], nk, [kD],
                               inc=(h == 3))
                    nn = Nn[lev % 2]
                    nnk = (K("Nn"), lev % 2)
                    if lastl:
                        act(nn[:, 0], v4(bD[:64, 0:256]), AF.Copy, [kD], [nnk])
                    else:
                        act(nn[:].rearrange("p a h t -> p (a h) t"), bD[:64, :].rearrange("p (a t) -> p a t", a=8),
                            AF.Copy, [kD], [nnk])
                    yield
                    for h in range(4):
                        mm(bE[:64, h * 64:(h + 1) * 64], TtT[:, h, :], nn[:, 0, h, :], [K("TtT"), nnk], [kE],
                           inc=(lastl and h == 3))
                        if not lastl:
                            mm(bE[:64, 256 + h * 64:256 + (h + 1) * 64], nn[:, 0, h, :], TtT[:, h, :], [K("TtT"), nnk], [kE],
                               inc=(h == 3))
                    tt_(Tt[:], Tt[:], v4(bE[:64, 0:256]), ALU.add, [K("Tt"), kE], [K("Tt")])
                    if not lastl:
                        tt_(TtT[:], TtT[:], v4(bE[:64, 256:512]), ALU.add, [K("TtT"), kE], [K("TtT")])
                    Ncur, NTcur, nk = nn[:, 0], nn[:, 1], [nnk]
                    yield
                for h in range(4):
                    pc = h // 2
                    mm(bC[:64, 256 + h * 64:256 + (h + 1) * 64], fT[:, pc, 1, :], Hx[d][pc][:, h * 64:(h + 1) * 64],
                       [K("fT"), ("Hx", d, pc)], [kC], start=True, stop=False, inc=False)
                    mm(bC[:64, 256 + h * 64:256 + (h + 1) * 64], sc[:, 0, h, :], vM[:, h * 64:(h + 1) * 64],
                       [K("sc0"), K("vM")], [kC], start=False, stop=True, inc=(h == 3))
                P.op("dve", "tensor_copy", Xs[:], v4(bC[:64, 256:512]), reads=[kC], writes=[K("Xs")])
                yield
                if _rwcut <= 7:
                    continue
                for h in range(4):
                    mm(bF[:64, h * 64:(h + 1) * 64], Tt[:, h, :], Xs[:, h, :], [K("Tt"), K("Xs")], [kF], inc=(h == 3))
                act(Us[:], v4(bF[:64, 0:256]), AF.Copy, [kF], [K("Us")])
                yield
                if _rwcut <= 8:
                    continue
                for h in range(4):
                    pc = h // 2
                    o_ = bF[:64, 256 + h * 64:256 + (h + 1) * 64]
                    mm(o_, fT[:, pc, 0, :], Hx[d][pc][:, h * 64:(h + 1) * 64], [K("fT"), ("Hx", d, pc)], [kF],
                       start=True, stop=False, inc=False)
                    mm(o_, sc[:, 3, h, :], vM[:, h * 64:(h + 1) * 64], [K("sc3"), K("vM")], [kF], start=False, stop=False,
                       inc=False)
                    mm(o_, sc[:, 4, h, :], Us[:, h, :], [K("sc4"), K("Us")], [kF], start=False, stop=True, inc=(h == 3))
                P.op("dve", "tensor_copy", Ys[:], bF[:64, 256:512], reads=[kF], writes=[K("Ys")])
                P.dma("sp", YD[d, r0:r0 + 64, :], Ys[:], reads=[K("Ys")], writes=[("YD", d, c)])
                lastc = 63 if d == 0 else 0
                for pc in range(2):
                    o_ = bH[:, 256:512] if pc == 0 else bG[:, 0:256]
                    ok_ = kH if pc == 0 else kG
                    mm(o_, KBM[:, 0, pc * 128:(pc + 1) * 128], vM[:], [K("KBM"), K("vM")], [ok_], start=True, stop=False,
                       inc=False)
                    mm(o_, KBM[:, 1, pc * 128:(pc + 1) * 128], Us[:].rearrange("p h e -> p (h e)"), [K("KBM"), K("Us")],
                       [ok_], start=False, stop=True, inc=True)
                    tt_(tmpH[:], o_, cst[:, C_RBD + pc * 256:C_RBD + (pc + 1) * 256], ALU.mult, [ok_, "cst"], [K("tmpH")])
                    tt_(tmpH[:], tmpH[:], Hx[d][pc][:], ALU.add, [K("tmpH"), ("Hx", d, pc)], [K("tmpH")])
                    ts_(Hx[d][pc][:], tmpH[:], Pd[:, pc, 0, lastc:lastc + 1], None, ALU.mult, None, [K("tmpH"), K("Pd")],
                        [("Hx", d, pc)])
                yield
                if _rwcut <= 9:
                    continue

        run_interleaved([unit(0)])
        run_interleaved([unit(1)])
        P.barrier()
        es.close()

    def rwkv_final(l, do_ctx):
        es = ExitStack()
        P.es = es
        gup = P.sb([128, 256], F32, "gup")
        P.dma("sp", gup[:], rw_gup[l], writes=["gup"])
        gnb = P.sb([128, 2, 256], F32, "gnb")
        for i in range(2):
            P.dma("sp", gnb[:, i, :], rw_gn[l, i].partition_broadcast(128), writes=["gnb"])
        yf = [P.sb([128, 4, 64], F32, "yf%d" % i) for i in range(2)]
        yb = [P.sb([128, 4, 64], F32, "yb%d" % i) for i in range(2)]
        fp = [P.sb([128, 5, 128], F32, "fp%d" % i) for i in range(2)]
        tm = P.sb([128, 2, 256], F32, "ftm")
        ysq = P.sb([128, 4, 64], F32, "ysq")
        st4 = P.sb([128, 4, 4], F32, "st4")
        gsb = P.sb([128, 256], F32, "gsb")
        ni = 0
        for tt in range(NTL):
            r0 = tt * 128
            if r0 < LC and not do_ctx:
                continue
            i2 = ni % 2
            ni += 1
            P.dma("sp", yf[i2][:].rearrange("p h e -> p (h e)"), YD[0, r0:r0 + 128, :],
                  reads=[("YD", 0, 2 * tt), ("YD", 0, 2 * tt + 1)], writes=[("yf", i2)])
            P.dma("sp", yb[i2][:].rearrange("p h e -> p (h e)"), YD[1, r0:r0 + 128, :],
                  reads=[("YD", 1, 2 * tt), ("YD", 1, 2 * tt + 1)], writes=[("yb", i2)])
            for j, pn in enumerate([18, 19, 2, 3, 20]):
                P.dma("sp", fp[i2][:, j, :], RF[pn, :, r0:r0 + 128], reads=[("RF", 0)], writes=[("fp", i2)])
            y = yf[i2]
            tt_(y[:], y[:], yb[i2][:], ALU.add, [("yf", i2), ("yb", i2)], [("yf", i2)])
            for j in range(4):
                tr(pG[0][:, j * 128:(j + 1) * 128], fp[i2][:, j, :], [("fp", i2)], ["pG0"], inc=(j == 3))
            act(tm[:].rearrange("p a c -> p (a c)"), pG[0][:, :], AF.Copy, ["pG0"], ["ftm"])
            mm(pG[1][:, 0:256], fp[i2][:, 4, :], gup[:], [("fp", i2), "gup"], ["pG1"])
            act(gsb[:], pG[1][:, 0:256], AF.Copy, ["pG1"], ["gsb"])
            P.op("dve", "tensor_reduce", st4[:, 0, :], y[:], AX.X, ALU.add, reads=[("yf", i2)], writes=["st4"])
            tt_(ysq[:], y[:], y[:], ALU.mult, [("yf", i2)], ["ysq"])
            P.op("dve", "tensor_reduce", st4[:, 1, :], ysq[:], AX.X, ALU.add, reads=["ysq"], writes=["st4"])
            P.op("dve", "tensor_reduce", st4[:, 2, :], tm[:, 0, :].rearrange("p (h e) -> p h e", h=4), AX.X, ALU.add,
                 reads=["ftm"], writes=["st4"])
            ts_(st4[:, 0, :], st4[:, 0, :], 1.0 / 64, None, ALU.mult, None, ["st4"], ["st4"])
            tt_(st4[:, 3, :], st4[:, 0, :], st4[:, 0, :], ALU.mult, ["st4"], ["st4"])
            stt(st4[:, 1, :], st4[:, 1, :], 1.0 / 64, st4[:, 3, :], ALU.mult, ALU.subtract, ["st4"], ["st4"])
            act(st4[:, 1, :], st4[:, 1, :], AF.Sqrt, ["st4", "kc"], ["st4"], bias=kc[:, 4:5])
            P.op("dve", "reciprocal", st4[:, 1, :], st4[:, 1, :], reads=["st4"], writes=["st4"])
            tt_(y[:], y[:], st4[:, 0, :].unsqueeze(2).to_broadcast([128, 4, 64]), ALU.subtract, [("yf", i2), "st4"],
                [("yf", i2)])
            tt_(y[:], y[:], st4[:, 1, :].unsqueeze(2).to_broadcast([128, 4, 64]), ALU.mult, [("yf", i2), "st4"],
                [("yf", i2)])
            yv = y[:].rearrange("p h e -> p (h e)")
            tt_(yv, yv, gnb[:, 0, :], ALU.mult, [("yf", i2), "gnb"], [("yf", i2)])
            tt_(yv, yv, gnb[:, 1, :], ALU.add, [("yf", i2), "gnb"], [("yf", i2)])
            tt_(ysq[:], tm[:, 1, :].rearrange("p (h e) -> p h e", h=4),
                st4[:, 2, :].unsqueeze(2).to_broadcast([128, 4, 64]), ALU.mult, ["ftm", "st4"], ["ysq"])
            tt_(y[:], y[:], ysq[:], ALU.add, [("yf", i2), "ysq"], [("yf", i2)])
            tt_(yv, yv, gsb[:], ALU.mult, [("yf", i2), "gsb"], [("yf", i2)])
            P.dma("sp", MIX[r0:r0 + 128, 768:1024], yv, reads=[("yf", i2)], writes=[("MIX", "c", tt)])
        P.barrier()
        es.close()

    def outproj(l, do_ctx):
        es = ExitStack()
        P.es = es
        hT = [P.sb([128, 8, TG], F32, "hT%d" % i) for i in range(2)]
        lt = alloc_ln_tiles()
        vv = lt["vv"]
        mixT = P.sb([128, 8, TG], BF16, "mixT")
        mtile = [P.sb([128, 1024], F32, "mtile%d" % i) for i in range(2)]
        wo = P.sb([128, 8, 1024], BF16, "wo")
        for hf in range(2):
            P.dma("pool", wo[:, :, hf * 512:(hf + 1) * 512],
                  w_out[l][:, hf * 512:(hf + 1) * 512].rearrange("(k p) j -> p k j", p=128), writes=["wo"])
        cur_s = None
        nm = 0
        for (st, t0, n) in groups:
            if st == "c" and not do_ctx:
                continue
            s = 1 if st == "c" else 0
            off = toff(st, t0)
            if s != cur_s:
                mod_scalars(l, 1, s)
                cur_s = s
            gi = cnts["g"]
            cnts["g"] += 1
            h = hT[gi % 2]
            hk = "hT%d" % (gi % 2)
            P.dma("sp", h[:, :, :n], hsrc(st, t0, n), reads=[("H", st, t0)], writes=[hk])
            for ts in range(n // 128):
                mt = mtile[nm % 2]
                mk = ("mtile", nm % 2)
                nm += 1
                P.dma("sp", mt[:], MIX[off + ts * 128:off + (ts + 1) * 128, :],
                      reads=[kk for kk in list(P.lastw.keys()) if isinstance(kk, tuple) and kk[0] == "MIX"], writes=[mk])
                for half in range(2):
                    pi = half
                    for j in range(4):
                        tr(pG[pi][:, j * 128:(j + 1) * 128], mt[:, (half * 4 + j) * 128:(half * 4 + j + 1) * 128], [mk],
                           ["pG%d" % pi], inc=(j == 3))
                    act(mixT[:, half * 4:half * 4 + 4, ts * 128:(ts + 1) * 128],
                        pG[pi][:, :].rearrange("p (j t) -> p j t", j=4), AF.Copy, ["pG%d" % pi], ["mixT"])
            for c in range(8):
                pi = c % 2
                for k in range(8):
                    mm(pD[pi][:, :n], wo[:, k, c * 128:(c + 1) * 128], mixT[:, k, :n], ["wo", "mixT"], ["pD%d" % pi],
                       start=(k == 0), stop=(k == 7), inc=(k == 7))
                stt(vv[:, c, :n], pD[pi][:, :n], scl[:, 2, c:c + 1], h[:, c, :n], ALU.mult, ALU.add,
                    ["pD%d" % pi, "scl2", hk], [("vv", c)])
            layer_norm_store(lt, st, t0, n, l, 1)
        P.barrier()
        es.close()

    for l in range(L):
        last = l == L - 1
        convert_weights(l)
        ffn_sublayer(l, 0, ffn_w["ffn1_wg"][l], ffn_w["ffn1_wu"][l], ffn_w["ffn1_wd"][l])
        if dbg == "ffn1":
            break
        inproj(l)
        if dbg == "inproj":
            break
        if dbg in (None, "swa", "mix"):
            swa(l, not last)
        if dbg in (None, "gla", "mix"):
            gla(l, not last)
        if dbg in (None, "rwkv", "mix"):
            import os as _os
            _rs = int(_os.environ.get("RW_STOP", "9"))
            rwkv_prep(l)
            if _rs >= 2:
                rwkv_scan(l)
            if _rs >= 3:
                rwkv_final(l, not last)
        if dbg in ("swa", "gla", "rwkv", "mix"):
            break
        outproj(l, not last)
        if dbg == "outproj":
            break
        ffn_sublayer(l, 2, ffn_w["ffn2_wg"][l], ffn_w["ffn2_wu"][l], ffn_w["ffn2_wd"][l], do_ctx=not last)

    es = ExitStack()
    P.es = es
    evs = []
    if dbg in ("swa", "gla", "rwkv", "mix"):
        mixo = dram("mixo", [SEQ, D], kind="ExternalOutput")
        ob = [P.sb([128, 1024], F32, "ob%d" % i) for i in range(2)]
        for tt in range(SEQ // 128):
            r0 = LC + tt * 128
            P.dma("sp", ob[tt % 2][:], MIX[r0:r0 + 128, :], writes=[("ob", tt % 2)])
            evs.append(P.dma("sp", mixo[tt * 128:(tt + 1) * 128, :], ob[tt % 2][:], reads=[("ob", tt % 2)],
                             writes=[("mixo", tt)]))
    ob2 = [P.sb([128, 8, TG], F32, "ob2%d" % i) for i in range(2)]
    for gi, (st, t0, n) in enumerate(groups):
        if st == "c":
            continue
        b = ob2[gi % 2]
        bk = "ob2%d" % (gi % 2)
        P.dma("sp", b[:, :, :n], hsrc(st, t0, n), reads=[("H", st, t0)], writes=[bk])
        evs.append(P.dma("sp", outT[:, :, t0:t0 + n].rearrange("c p t -> p c t"), b[:, :, :n], reads=[bk],
                         writes=[("out", t0)]))
    P.finish("sp", evs)
    es.close()
    es0.close()
    print("program instructions:", P.ninst)
    return nc


def _consts():
    c = np.zeros((128, C_END), np.float32)
    c[:, C_ID:C_ID + 128] = np.eye(128)
    s = np.arange(64)[:, None]
    t = np.arange(64)[None, :]
    c[0:64, C_TRI + 0:C_TRI + 64] = -1.0 * (s <= t)
    c[0:64, C_TRI + 64:C_TRI + 128] = -1.0 * (s >= t)
    c[0:64, C_TRI + 128:C_TRI + 192] = -1.0 * (s < t)
    c[0:64, C_TRI + 192:C_TRI + 256] = -1.0 * (s > t)
    c[0:64, C_M01 + 0:C_M01 + 64] = (s <= t)
    c[0:64, C_M01 + 64:C_M01 + 128] = (s >= t)
    c[0:64, C_M01 + 128:C_M01 + 192] = (s < t)
    c[0:64, C_M01 + 192:C_M01 + 256] = (s > t)
    p = np.arange(128)[:, None]
    col = np.arange(256)[None, :]
    c[:, C_GBD:C_GBD + 256] = (p // 32 == col // 64)
    for h in range(4):
        c[:, C_GHM + h] = (np.arange(128) // 32 == h)
    for pc in range(2):
        c[:, C_RBD + pc * 256:C_RBD + (pc + 1) * 256] = ((2 * pc + p // 64) == col // 64)
    for hh in range(2):
        c[:, C_RHM + hh] = (np.arange(128) // 64 == hh)
    q = np.arange(128)[None, :]
    c[:, C_BO:C_BO + 128] = (p // 64 == q // 64)
    c[:, C_SWM:C_SWM + 128] = (p >= q)
    c[:, C_SWM + 128:C_SWM + 256] = (p <= q)
    c[0:64, C_NI:C_NI + 64] = -np.eye(64)
    return c


def _rope_table(SEQ):
    pos = np.arange(SEQ)
    row = (pos // 64).astype(np.float32)
    col = (pos % 64).astype(np.float32)
    inv = (np.float32(10000.0) ** (-np.arange(16, dtype=np.float32) / np.float32(16))).astype(np.float32)
    tab = np.zeros((SEQ, 2, 32), np.float32)
    for a, pp in enumerate((row, col)):
        ang = (pp[:, None] * inv[None, :]).astype(np.float32)
        tab[:, 0, a * 16:(a + 1) * 16] = np.cos(ang)
        tab[:, 1, a * 16:(a + 1) * 16] = np.sin(ang)
    return tab


def _prep_shared(inp, L, SEQ):
    f = lambda a: np.ascontiguousarray(a, dtype=np.float32)
    m = {}
    m["w_ada"] = f(inp["w_ada"][:L])
    m["b_adaT"] = f(inp["b_ada"][:L].reshape(L, 72, 128).transpose(2, 0, 1))
    ln = np.stack([inp["ln_g"][:L], inp["ln_b"][:L]], axis=2)
    m["lnT"] = f(ln.reshape(L, 3, 2, 8, 128).transpose(4, 0, 1, 2, 3))
    for nm in ("ffn1_wg", "ffn1_wu", "ffn1_wd", "ffn2_wg", "ffn2_wu", "ffn2_wd", "w_in", "w_out"):
        m[nm] = f(inp[nm][:L])
    m["consts"] = _consts()
    m["ropeM"] = _rope_table(SEQ)
    m["gla_up"] = f(np.concatenate([inp["gla_gate_up"][:L], inp["gla_gate_bias"][:L][:, :, None, :]], axis=2))
    m["gla_g"] = f(inp["gla_norm_g"][:L])
    m["sinkB"] = f(np.broadcast_to(inp["swa_sink"][:L][None], (128, L, 8)))
    vecs = np.stack([inp["rwkv_k_k"][:L], inp["rwkv_k_a"][:L], inp["rwkv_r_k"][:L].reshape(L, 256),
                     inp["rwkv_w0"][:L, 0], inp["rwkv_w0"][:L, 1], inp["rwkv_a0"][:L, 0], inp["rwkv_a0"][:L, 1]],
                    axis=-1)
    m["rw_vec"] = f(vecs.reshape(L, 2, 128, 7).transpose(2, 0, 1, 3))
    mu = inp["rwkv_mu"][:L]
    mut = np.zeros((128, L, 11), np.float32)
    for j in range(6):
        mut[:, :, j] = mu[:, 128 * j:128 * (j + 1)].T
    for j, r0 in enumerate((768, 832, 896, 960)):
        mut[:64, :, 6 + j] = mu[:, r0:r0 + 64].T
    mut[:, :, 10] = mu[:, 1024:1152].T
    m["rw_mu"] = mut
    m["rw_wup"] = f(inp["rwkv_w_up"][:L])
    m["rw_aup"] = f(inp["rwkv_a_up"][:L])
    m["rw_gup"] = f(inp["rwkv_g_up"][:L])
    m["rw_gn"] = f(np.stack([inp["rwkv_gn_g"][:L], inp["rwkv_gn_b"][:L]], axis=1))
    return m


def _prep_core(inp, b):
    f = lambda a: np.ascontiguousarray(a, dtype=np.float32)
    m = {}
    m["xT"] = f(inp["x"][b].T.reshape(8, 128, -1))
    m["cxT"] = f(inp["ctx"][b].T.reshape(8, 128, -1))
    cc = np.stack([inp["c"][b], inp["c_ctx"]], axis=-1)
    m["ccT"] = f(cc.reshape(8, 128, 2).transpose(1, 0, 2))
    return m


def run(inp, SEQ, LC, DEPTH, ncores, dbg=None, trace=False):
    nc = build(SEQ, LC, DEPTH, dbg=dbg)
    shared = _prep_shared(inp, DEPTH, SEQ)
    in_maps = []
    for b in range(ncores):
        m = dict(shared)
        m.update(_prep_core(inp, b))
        in_maps.append(m)
    res = run_bass_kernel_spmd(nc, in_maps, core_ids=list(range(ncores)), trace=trace)
    outs = [r["outT"].reshape(1024, SEQ).T for r in res.results]
    return np.stack(outs, axis=0), res


def kernel(**inputs):
    inp = {k: np.asarray(v) for k, v in inputs.items()}
    out, _ = run(inp, 4096, 256, 4, 8)
    return np.ascontiguousarray(out.astype(np.float32))
```

```python
import numpy as np
from contextlib import ExitStack
import concourse.bass as bass
import concourse.mybir as mybir
from concourse.bass_utils import run_bass_kernel_spmd

F32 = mybir.dt.float32
BF16 = mybir.dt.bfloat16
AF = mybir.ActivationFunctionType
ALU = mybir.AluOpType
AX = mybir.AxisListType

D = 1024
DFF = 2816
NFF = DFF // 128
DIN = 2720
ALPHA = 8.0 ** 0.25
LN_EPS = 1e-6
USE_WSCRATCH = False
GN_EPS = 64e-5
C_ID = 0
C_TRI = 128
C_M01 = 384
C_GBD = 640
C_GHM = 896
C_RBD = 900
C_RHM = 1412
C_BO = 1414
C_SWM = 1542
C_NI = 1798
C_END = 1862


class Prog:
    def __init__(self, nc, es):
        self.nc = nc
        self.es = es
        self.eng = {"pe": nc.tensor, "act": nc.scalar, "dve": nc.vector, "pool": nc.gpsimd, "sp": nc.sync}
        self.sem = {e: es.enter_context(nc.semaphore("s_" + e)) for e in ("pe", "act", "dve", "pool")}
        self.cnt = {e: 0 for e in self.sem}
        self.known = {e: {} for e in self.eng}
        self.lastw = {}
        self.rd = {}
        self.NDS = 12
        self.dsem = {q: [es.enter_context(nc.semaphore("d_%s%d" % (q, i))) for i in range(self.NDS)]
                     for q in ("sp", "pool", "act")}
        self.dcnt = {q: 0 for q in self.dsem}
        self.semobj = {}
        for e in self.sem:
            self.semobj[e] = self.sem[e]
        for q in self.dsem:
            for i, s in enumerate(self.dsem[q]):
                self.semobj[(q, i)] = s
        self.ntiles = 0
        self.ninst = 0

    def sb(self, shape, dt=F32, name=None):
        self.ntiles += 1
        return self.es.enter_context(self.nc.sbuf_tensor("%s_%d" % (name or "t", self.ntiles), list(shape), dt))

    def ps(self, shape, dt=F32, name=None):
        self.ntiles += 1
        return self.es.enter_context(self.nc.psum_tensor("%s_%d" % (name or "p", self.ntiles), list(shape), dt))

    def _wait(self, e, ev):
        s, v = ev
        if self.known[e].get(s, 0) >= v:
            return
        self.eng[e].wait_ge(self.semobj[s], v)
        self.known[e][s] = v
        self.ninst += 1

    def _deps(self, e, reads, writes):
        for k in reads:
            ev = self.lastw.get(k)
            if ev is not None and not (e == "pe" and ev[0] == "pe"):
                self._wait(e, ev)
        for k in writes:
            ev = self.lastw.get(k)
            if ev is not None and not (e == "pe" and ev[0] == "pe"):
                self._wait(e, ev)
            for ev in self.rd.get(k, {}).values():
                if ev[0] == e:
                    continue
                self._wait(e, ev)

    def _record(self, ev, reads, writes):
        for k in reads:
            self.rd.setdefault(k, {})[ev[0]] = ev
        for k in writes:
            self.lastw[k] = ev
            self.rd[k] = {}

    def op(self, e, fn, *args, reads=(), writes=(), inc=True, **kw):
        self._deps(e, reads, writes)
        inst = getattr(self.eng[e], fn)(*args, **kw)
        self.ninst += 1
        ev = (e, self.cnt[e] + 1)
        if inc:
            inst.then_inc(self.sem[e], 1)
            self.cnt[e] += 1
        self._record(ev, reads, writes)
        return inst

    def dma(self, q, out, in_, reads=(), writes=(), **kw):
        self._deps(q, reads, writes)
        j = self.dcnt[q]
        self.dcnt[q] += 1
        slot = j % self.NDS
        s = (q, slot)
        need = 16 * (j // self.NDS)
        if need > 0:
            self._wait(q, (s, need))
        inst = self.eng[q].dma_start(out=out, in_=in_, **kw)
        inst.then_inc(self.dsem[q][slot], 16)
        self.ninst += 1
        ev = (s, need + 16)
        self._record(ev, reads, writes)
        return ev

    def barrier(self):
        evs = [(e, self.cnt[e]) for e in self.sem if self.cnt[e] > 0]
        for q in self.dsem:
            j = self.dcnt[q]
            for slot in range(self.NDS):
                n = (j - slot + self.NDS - 1) // self.NDS if j > slot else 0
                if n > 0:
                    evs.append(((q, slot), 16 * n))
        for e in self.eng:
            for ev in evs:
                self._wait(e, ev)
        self.lastw = {}
        self.rd = {}

    def finish(self, e, evs):
        for ev in evs:
            self._wait(e, ev)


def _ffn_pieces():
    out = []
    f = 0
    while f < NFF:
        w = min(4, NFF - f)
        out.append((f, w))
        f += w
    return out


def build(SEQ, LC, DEPTH, dbg=None):
    nc = bass.Bass("TRN2", target_bir_lowering=False)
    es0 = ExitStack()
    P = Prog(nc, es0)
    dram = lambda name, shape, dt=F32, kind="ExternalInput": nc.dram_tensor(name, list(shape), dt, kind=kind).ap()
    L = DEPTH
    T = LC + SEQ
    NTL = T // 128
    NCH = T // 64
    xT = dram("xT", [8, 128, SEQ])
    cxT = dram("cxT", [8, 128, LC])
    ccT = dram("ccT", [128, 8, 2])
    w_ada = dram("w_ada", [L, D, 9 * D])
    b_adaT = dram("b_adaT", [128, L, 72])
    lnT = dram("lnT", [128, L, 3, 2, 8])
    ffn_w = {}
    for nm in ("ffn1_wg", "ffn1_wu", "ffn2_wg", "ffn2_wu"):
        ffn_w[nm] = dram(nm, [L, D, DFF])
    for nm in ("ffn1_wd", "ffn2_wd"):
        ffn_w[nm] = dram(nm, [L, DFF, D])
    w_in = dram("w_in", [L, D, DIN])
    w_out = dram("w_out", [L, D, D])
    consts = dram("consts", [128, C_END])
    ropeM = dram("ropeM", [SEQ, 2, 32])
    gla_up = dram("gla_up", [L, 2, 17, 128])
    gla_g = dram("gla_g", [L, 256])
    sinkB = dram("sinkB", [128, L, 8])
    rw_vec = dram("rw_vec", [128, L, 2, 7])
    rw_mu = dram("rw_mu", [128, L, 11])
    rw_wup = dram("rw_wup", [L, 2, 64, 256])
    rw_aup = dram("rw_aup", [L, 2, 64, 256])
    rw_gup = dram("rw_gup", [L, 128, 256])
    rw_gn = dram("rw_gn", [L, 2, 256])
    outT = dram("outT", [8, 128, SEQ], kind="ExternalOutput")
    HX = dram("HX", [8, 128, SEQ], kind="Internal")
    HC = dram("HC", [8, 128, LC], kind="Internal")
    PF = dram("PF", [DIN, T], kind="Internal")
    PM = dram("PM", [T, DIN], kind="Internal")
    MIX = dram("MIX", [T, D], kind="Internal")
    OGD = dram("OGD", [2, T, 256], kind="Internal")
    RF = dram("RF", [21, 128, T], kind="Internal")
    YD = dram("YD", [2, T, 256], kind="Internal")
    NPC = len(_ffn_pieces())
    WGU = dram("WGU", [1, 2, 2, NPC, 128, 8, 512], BF16, kind="Internal") if USE_WSCRATCH else None
    WDS = dram("WDS", [1, 2, 8, 128, NFF, 128], BF16, kind="Internal") if USE_WSCRATCH else None

    TG = 512
    groups = [("c", 0, LC)] + [("x", t0, min(TG, SEQ - t0)) for t0 in range(0, SEQ, TG)]

    def toff(st, t0):
        return t0 if st == "c" else LC + t0

    cst = P.sb([128, C_END], F32, "cst")
    P.dma("sp", cst[:], consts, writes=["cst"])
    ident = cst[:, C_ID:C_ID + 128]
    onesm = P.sb([128, 128], F32, "onesm")
    P.op("dve", "memset", onesm[:], 1.0 / D, writes=["onesm"])
    cc = P.sb([128, 8, 2], F32, "cc")
    P.dma("sp", cc[:], ccT, writes=["cc"])
    sil = P.sb([128, 8, 2], F32, "sil")
    P.op("act", "activation", sil[:], cc[:], AF.Silu, reads=["cc"], writes=["sil"])
    bada = P.sb([128, L, 72], F32, "bada")
    P.dma("sp", bada[:], b_adaT, writes=["bada"])
    lnp = P.sb([128, L, 3, 2, 8], F32, "lnp")
    P.dma("sp", lnp[:], lnT, writes=["lnp"])
    mT = P.sb([128, L, 72, 2], F32, "mT")
    kc = P.sb([128, 8], F32, "kc")
    for i, val in enumerate([LN_EPS / (ALPHA * ALPHA), 1.0, -0.5, LN_EPS, GN_EPS, 0.0, 1e-24]):
        P.op("dve", "memset", kc[:, i:i + 1], val, writes=["kc"])
    epsb = kc[:, 0:1]
    scl = P.sb([128, 4, 8], F32, "scl")

    pG = [P.ps([128, 512], F32, "pG%d" % i) for i in range(2)]
    pU = [P.ps([128, 512], F32, "pU%d" % i) for i in range(2)]
    pD = [P.ps([128, 512], F32, "pD%d" % i) for i in range(2)]
    pS = [P.ps([128, 512], F32, "pS%d" % i) for i in range(2)]

    es = ExitStack()
    P.es = es
    wa = [P.sb([128, 8, 512], F32, "wa%d" % i) for i in range(2)]
    nblk = 0
    for l in range(L):
        pm = pS[l % 2]
        for cb in range(18):
            wt = wa[nblk % 2]
            wk = "wa%d" % (nblk % 2)
            nblk += 1
            src = w_ada[l, :, cb * 512:(cb + 1) * 512].rearrange("(k p) j -> p k j", p=128)
            P.dma("sp", wt[:], src, writes=[wk])
            for jj in range(4):
                j = cb * 4 + jj
                for k in range(8):
                    P.op("pe", "matmul", pm[:, 2 * j:2 * j + 2], wt[:, k, jj * 128:(jj + 1) * 128], sil[:, k, :],
                         start=(k == 0), stop=(k == 7), reads=[wk, "sil"], writes=["pS%d" % (l % 2)],
                         inc=(k == 7 and jj == 3))
        P.op("dve", "tensor_tensor", mT[:, l, :, :], pm[:, 0:144].rearrange("p (j s) -> p j s", s=2),
             bada[:, l, :].unsqueeze(2).to_broadcast([128, 72, 2]), ALU.add,
             reads=["pS%d" % (l % 2), "bada"], writes=["mT"])
    for gi, (st, t0, n) in enumerate(groups):
        b = wa[gi % 2]
        bk = "wa%d" % (gi % 2)
        src = (cxT if st == "c" else xT)[:, :, t0:t0 + n].rearrange("c p t -> p c t")
        P.dma("sp", b[:, :, :n], src, writes=[bk])
        P.dma("sp", (HC if st == "c" else HX)[:, :, t0:t0 + n].rearrange("c p t -> p c t"), b[:, :, :n],
              reads=[bk], writes=[("H", st, t0)])
    P.barrier()
    es.close()

    def convert_weights(l):
        es = ExitStack()
        P.es = es
        cvt = [P.sb([128, 8, 512], BF16, "cvt%d" % i) for i in range(4)]
        cvd = [P.sb([128, NFF, 128], BF16, "cvd%d" % i) for i in range(4)]
        ncv = [0, 0]
        for fi, pre in enumerate(("ffn1", "ffn2")):
            for gu, nm in enumerate(("_wg", "_wu")):
                wap = ffn_w[pre + nm][l]
                for pi_, (f0, fw) in enumerate(_ffn_pieces()):
                    bi = ncv[0] % 4
                    ncv[0] += 1
                    P.dma("pool", cvt[bi][:, :, :fw * 128],
                          wap[:, f0 * 128:(f0 + fw) * 128].rearrange("(k p) j -> p k j", p=128), writes=[("cvt", bi)])
                    P.dma("sp", WGU[0, fi, gu, pi_][:, :, :fw * 128], cvt[bi][:, :, :fw * 128], reads=[("cvt", bi)],
                          writes=[("WGU", fi)])
            wap = ffn_w[pre + "_wd"][l]
            for c in range(8):
                bi = ncv[1] % 4
                ncv[1] += 1
                P.dma("pool", cvd[bi][:], wap[:, c * 128:(c + 1) * 128].rearrange("(f p) j -> p f j", p=128),
                      writes=[("cvd", bi)])
                P.dma("sp", WDS[0, fi, c], cvd[bi][:], reads=[("cvd", bi)], writes=[("WDS", fi)])
        P.barrier()
        es.close()

    def hsrc(st, t0, n):
        return (HC if st == "c" else HX)[:, :, t0:t0 + n].rearrange("c p t -> p c t")

    cnts = {"g": 0, "wgu": 0, "wd": 0, "ps": 0}

    def mod_scalars(l, sub, s):
        j0 = 3 * sub * 8
        coef = (0.5 if sub != 1 else 1.0) / ALPHA
        P.op("dve", "tensor_scalar", scl[:, 0, :], mT[:, l, j0 + 8:j0 + 16, s], 1.0, None, ALU.add,
             reads=["mT"], writes=["scl0"])
        P.op("dve", "tensor_copy", scl[:, 1, :], mT[:, l, j0:j0 + 8, s], reads=["mT"], writes=["scl1"])
        P.op("dve", "tensor_scalar", scl[:, 2, :], mT[:, l, j0 + 16:j0 + 24, s], coef, None, ALU.mult,
             reads=["mT"], writes=["scl2"])

    def alloc_ln_tiles():
        t = {}
        t["vv"] = P.sb([128, 8, TG], F32, "vv")
        t["yo"] = P.sb([128, 8, TG], F32, "yo")
        t["vsq"] = [P.sb([128, TG], F32, "vsq%d" % i) for i in range(2)]
        t["mean_sb"] = P.sb([128, TG], F32, "mean_sb")
        t["msq"] = P.sb([128, TG], F32, "msq")
        t["var"] = P.sb([128, TG], F32, "var")
        t["rstd"] = P.sb([128, TG], F32, "rstd")
        t["tt"] = [P.sb([128, TG], F32, "tt%d" % i) for i in range(2)]
        return t

    def layer_norm_store(t, st, t0, n, l, sub):
        vv, yo, vsq, mean_sb, msq, var, rstd, tt = (t[k] for k in ("vv", "yo", "vsq", "mean_sb", "msq", "var", "rstd", "tt"))
        pmean, pev2 = pS[0], pS[1]
        for c in range(8):
            q = vsq[c % 2]
            qk = "vsq%d" % (c % 2)
            P.op("act", "activation", q[:, :n], vv[:, c, :n], AF.Square, reads=[("vv", c)], writes=[qk])
            P.op("pe", "matmul", pmean[:, :n], onesm[:], vv[:, c, :n], start=(c == 0), stop=(c == 7),
                 reads=["onesm", ("vv", c)], writes=["pS0"], inc=(c == 7))
            P.op("pe", "matmul", pev2[:, :n], onesm[:], q[:, :n], start=(c == 0), stop=(c == 7),
                 reads=["onesm", qk], writes=["pS1"], inc=True)
        P.op("act", "activation", mean_sb[:, :n], pmean[:, :n], AF.Copy, reads=["pS0"], writes=["mean_sb"])
        P.op("dve", "tensor_tensor", msq[:, :n], mean_sb[:, :n], mean_sb[:, :n], ALU.mult,
             reads=["mean_sb"], writes=["msq"])
        P.op("dve", "tensor_tensor", var[:, :n], pev2[:, :n], msq[:, :n], ALU.subtract,
             reads=["pS1", "msq"], writes=["var"])
        P.op("act", "activation", var[:, :n], var[:, :n], AF.Sqrt, bias=epsb, reads=["var", "kc"], writes=["var"])
        P.op("dve", "reciprocal", rstd[:, :n], var[:, :n], reads=["var"], writes=["rstd"])
        for c in range(8):
            tq = tt[c % 2]
            tk = "tt%d" % (c % 2)
            P.op("dve", "tensor_tensor", tq[:, :n], vv[:, c, :n], mean_sb[:, :n], ALU.subtract,
                 reads=[("vv", c), "mean_sb"], writes=[tk])
            P.op("dve", "tensor_tensor", tq[:, :n], tq[:, :n], rstd[:, :n], ALU.mult,
                 reads=[tk, "rstd"], writes=[tk])
            P.op("act", "activation", yo[:, c, :n], tq[:, :n], AF.Identity,
                 scale=lnp[:, l, sub, 0, c:c + 1], bias=lnp[:, l, sub, 1, c:c + 1],
                 reads=[tk, "lnp"], writes=[("yo", c)])
        return P.dma("pool" if USE_WSCRATCH else "sp", hsrc(st, t0, n), yo[:, :, :n], reads=[("yo", c) for c in range(8)],
                     writes=[("H", st, t0)])

    def ffn_sublayer(l, sub, wg_ap, wu_ap, wd_ap, do_ctx=True):
        fi = 0 if sub == 0 else 1
        es = ExitStack()
        P.es = es
        hT = [P.sb([128, 8, TG], F32, "hT%d" % i) for i in range(2)]
        uT = P.sb([128, 8, TG], BF16, "uT")
        hff = P.sb([128, NFF, TG], BF16, "hff")
        lt = alloc_ln_tiles()
        vv = lt["vv"]
        sg = [P.sb([128, TG], F32, "sg%d" % i) for i in range(2)]
        wgu = [P.sb([128, 2, 8, 512], BF16, "wgu%d" % i) for i in range(2)]
        wd = [P.sb([128, NFF, 128], BF16, "wd%d" % i) for i in range(3)]
        cur_s = None
        for (st, t0, n) in groups:
            if st == "c" and not do_ctx:
                continue
            s = 1 if st == "c" else 0
            if s != cur_s:
                mod_scalars(l, sub, s)
                cur_s = s
            gi = cnts["g"]
            cnts["g"] += 1
            h = hT[gi % 2]
            hk = "hT%d" % (gi % 2)
            P.dma("pool" if USE_WSCRATCH else "sp", h[:, :, :n], hsrc(st, t0, n), reads=[("H", st, t0)], writes=[hk])
            for c in range(8):
                P.op("act", "activation", uT[:, c, :n], h[:, c, :n], AF.Identity,
                     scale=scl[:, 0, c:c + 1], bias=scl[:, 1, c:c + 1],
                     reads=[hk, "scl0", "scl1"], writes=[("uT", c)])
            for pi_, (f0, fw) in enumerate(_ffn_pieces()):
                wi = cnts["wgu"] % 2
                cnts["wgu"] += 1
                wt = wgu[wi]
                if USE_WSCRATCH:
                    P.dma("sp", wt[:, 0, :, :fw * 128], WGU[0, fi, 0, pi_][:, :, :fw * 128], writes=[("wgu", wi, 0)])
                    P.dma("sp", wt[:, 1, :, :fw * 128], WGU[0, fi, 1, pi_][:, :, :fw * 128], writes=[("wgu", wi, 1)])
                else:
                    P.dma("pool", wt[:, 0, :, :fw * 128],
                          wg_ap[:, f0 * 128:(f0 + fw) * 128].rearrange("(k p) j -> p k j", p=128), writes=[("wgu", wi, 0)])
                    P.dma("pool", wt[:, 1, :, :fw * 128],
                          wu_ap[:, f0 * 128:(f0 + fw) * 128].rearrange("(k p) j -> p k j", p=128), writes=[("wgu", wi, 1)])
                for ff in range(fw):
                    f = f0 + ff
                    pi = cnts["ps"] % 2
                    cnts["ps"] += 1
                    for k in range(8):
                        P.op("pe", "matmul", pG[pi][:, :n], wt[:, 0, k, ff * 128:(ff + 1) * 128], uT[:, k, :n],
                             start=(k == 0), stop=(k == 7), reads=[("wgu", wi, 0), ("uT", k)], writes=["pG%d" % pi],
                             inc=(k == 7))
                    for k in range(8):
                        P.op("pe", "matmul", pU[pi][:, :n], wt[:, 1, k, ff * 128:(ff + 1) * 128], uT[:, k, :n],
                             start=(k == 0), stop=(k == 7), reads=[("wgu", wi, 1), ("uT", k)], writes=["pU%d" % pi],
                             inc=(k == 7))
                    P.op("act", "activation", sg[pi][:, :n], pG[pi][:, :n], AF.Silu,
                         reads=["pG%d" % pi], writes=["sg%d" % pi])
                    P.op("dve", "tensor_tensor", hff[:, f, :n], sg[pi][:, :n], pU[pi][:, :n], ALU.mult,
                         reads=["sg%d" % pi, "pU%d" % pi], writes=[("hff", f)])
            for c in range(8):
                wi = cnts["wd"] % 3
                cnts["wd"] += 1
                if USE_WSCRATCH:
                    P.dma("sp", wd[wi][:], WDS[0, fi, c], writes=[("wd", wi)])
                else:
                    P.dma("pool", wd[wi][:], wd_ap[:, c * 128:(c + 1) * 128].rearrange("(f p) j -> p f j", p=128),
                          writes=[("wd", wi)])
                pi = c % 2
                for f in range(NFF):
                    P.op("pe", "matmul", pD[pi][:, :n], wd[wi][:, f, :], hff[:, f, :n],
                         start=(f == 0), stop=(f == NFF - 1), reads=[("wd", wi), ("hff", f)], writes=["pD%d" % pi],
                         inc=(f == NFF - 1))
                P.op("dve", "scalar_tensor_tensor", vv[:, c, :n], pD[pi][:, :n], scl[:, 2, c:c + 1], h[:, c, :n],
                     ALU.mult, ALU.add, reads=["pD%d" % pi, "scl2", hk], writes=[("vv", c)])
            layer_norm_store(lt, st, t0, n, l, sub)
        P.barrier()
        es.close()

    def mm(out, lhsT, rhs, reads, writes, start=True, stop=True, inc=True):
        P.op("pe", "matmul", out, lhsT, rhs, start=start, stop=stop, reads=reads, writes=writes, inc=inc)

    def act(out, in_, func, reads, writes, **kw):
        P.op("act", "activation", out, in_, func, reads=reads, writes=writes, **kw)

    def tt_(out, a, b, op, reads, writes, e="dve"):
        P.op(e, "tensor_tensor", out, a, b, op, reads=reads, writes=writes)

    def ts_(out, a, s1, s2, op0, op1, reads, writes, e="dve"):
        if op1 is None:
            P.op(e, "tensor_scalar", out, a, s1, None, op0, reads=reads, writes=writes)
        else:
            P.op(e, "tensor_scalar", out, a, s1, s2, op0, op1, reads=reads, writes=writes)

    def stt(out, a, s, b, op0, op1, reads, writes):
        P.op("dve", "scalar_tensor_tensor", out, a, s, b, op0, op1, reads=reads, writes=writes)

    def tr(out, in_, reads, writes, np_=128, inc=True):
        P.op("pe", "transpose", out, in_, ident[:np_, :np_], reads=list(reads) + ["cst"], writes=writes, inc=inc)

    def in_pieces():
        out = []
        c = 0
        while c < DIN:
            w = min(512, DIN - c)
            out.append((c, w))
            c += w
        return out

    def inproj(l, do_ctx=True):
        es = ExitStack()
        P.es = es
        hT = [P.sb([128, 8, TG], F32, "hT%d" % i) for i in range(2)]
        uT = P.sb([128, 8, TG], BF16, "uT")
        wt_ = [P.sb([128, 8, 512], BF16, "wi%d" % i) for i in range(2)]
        stg = [P.sb([128, 512], F32, "stg%d" % i) for i in range(4)]
        nst = [0]
        cur_s = None
        for (st, t0, n) in groups:
            s = 1 if st == "c" else 0
            off = toff(st, t0)
            if s != cur_s:
                mod_scalars(l, 1, s)
                cur_s = s
            gi = cnts["g"]
            cnts["g"] += 1
            h = hT[gi % 2]
            hk = "hT%d" % (gi % 2)
            P.dma("sp", h[:, :, :n], hsrc(st, t0, n), reads=[("H", st, t0)], writes=[hk])
            for c in range(8):
                act(uT[:, c, :n], h[:, c, :n], AF.Identity, [hk, "scl0", "scl1"], [("uT", c)],
                    scale=scl[:, 0, c:c + 1], bias=scl[:, 1, c:c + 1])
            for (c0, w) in in_pieces():
                wi = cnts["wgu"] % 2
                cnts["wgu"] += 1
                wt = wt_[wi]
                wk = ("wi", wi)
                P.dma("pool", wt[:, :, :w], w_in[l][:, c0:c0 + w].rearrange("(k p) j -> p k j", p=128), writes=[wk])
                j = 0
                while j < w:
                    cw = min(128, w - j)
                    pi = cnts["ps"] % 2
                    cnts["ps"] += 1
                    for k in range(8):
                        mm(pG[pi][:cw, :n], wt[:, k, j:j + cw], uT[:, k, :n], [wk, ("uT", k)], ["pG%d" % pi],
                           start=(k == 0), stop=(k == 7), inc=(k == 7))
                    si = nst[0] % 4
                    nst[0] += 1
                    act(stg[si][:cw, :n], pG[pi][:cw, :n], AF.Copy, ["pG%d" % pi], [("stg", si)])
                    P.dma("sp", PF[c0 + j:c0 + j + cw, off:off + n], stg[si][:cw, :n], reads=[("stg", si)],
                          writes=[("PF", off)])
                    j += cw
                for ts in range(n // 128):
                    pi = cnts["ps"] % 2
                    cnts["ps"] += 1
                    for k in range(8):
                        mm(pU[pi][:, :w], uT[:, k, ts * 128:(ts + 1) * 128], wt[:, k, :w], [wk, ("uT", k)],
                           ["pU%d" % pi], start=(k == 0), stop=(k == 7), inc=(k == 7))
                    si = nst[0] % 4
                    nst[0] += 1
                    P.op("dve", "tensor_copy", stg[si][:, :w], pU[pi][:, :w], reads=["pU%d" % pi], writes=[("stg", si)])
                    P.dma("sp", PM[off + ts * 128:off + (ts + 1) * 128, c0:c0 + w], stg[si][:, :w],
                          reads=[("stg", si)], writes=[("PM", off)])
        P.barrier()
        es.close()

    def swa(l, do_ctx):
        es = ExitStack()
        P.es = es
        kT_all = P.sb([64, 2, T], BF16, "kT_all")
        V_all = P.sb([128, NTL, 2, 65], BF16, "V_all")
        esink = P.sb([128, 8], F32, "esink")
        sk = P.sb([128, L, 8], F32, "sk")
        P.dma("sp", sk[:], sinkB, writes=["sk"])
        act(esink[:], sk[:, l, :], AF.Exp, ["sk"], ["esink"])
        P.op("dve", "memset", V_all[:], 1.0, writes=["V_all"])
        km = [P.sb([128, 2, 64], F32, "km%d" % i) for i in range(2)]
        vm = [P.sb([128, 2, 64], F32, "vm%d" % i) for i in range(2)]
        rp = [P.sb([128, 2, 32], F32, "rp%d" % i) for i in range(2)]
        kr = P.sb([128, 2, 64], F32, "kr")
        qm = [P.sb([128, 8, 64], F32, "qm%d" % i) for i in range(2)]
        qr = P.sb([128, 8, 64], F32, "qr")
        ta = P.sb([128, 8, 32], F32, "ta")
        tb = P.sb([128, 8, 32], F32, "tb")
        qT = P.sb([64, 8, 128], BF16, "qT")
        pT = [P.sb([128, 512], BF16, "pT%d" % i) for i in range(10)]
        den = P.sb([128, 4], F32, "den")
        mixb = [P.sb([128, 512], F32, "mixb%d" % i) for i in range(2)]

        def rope(dst, src, nh, tab, rk_src, rk_dst, rk_tab):
            sv = src.rearrange("p h (a s i) -> p h a s i", a=2, s=2)
            dv = dst.rearrange("p h (a s i) -> p h a s i", a=2, s=2)
            cos = tab[:, 0, :].rearrange("p (a i) -> p a i", a=2).unsqueeze(1).to_broadcast([128, nh, 2, 16])
            sin = tab[:, 1, :].rearrange("p (a i) -> p a i", a=2).unsqueeze(1).to_broadcast([128, nh, 2, 16])
            tav = ta[:, :nh, :].rearrange("p h (a i) -> p h a i", a=2)
            tbv = tb[:, :nh, :].rearrange("p h (a i) -> p h a i", a=2)
            tt_(tav, sv[:, :, :, 0, :], cos, ALU.mult, [rk_src, rk_tab], ["ta"])
            tt_(tbv, sv[:, :, :, 1, :], sin, ALU.mult, [rk_src, rk_tab], ["tb"])
            tt_(dv[:, :, :, 0, :], tav, tbv, ALU.subtract, ["ta", "tb"], [rk_dst])
            tt_(tav, sv[:, :, :, 0, :], sin, ALU.mult, [rk_src, rk_tab], ["ta"])
            tt_(tbv, sv[:, :, :, 1, :], cos, ALU.mult, [rk_src, rk_tab], ["tb"])
            tt_(dv[:, :, :, 1, :], tav, tbv, ALU.add, ["ta", "tb"], [rk_dst])

        for tt in range(NTL):
            i2 = tt % 2
            r0 = tt * 128
            P.dma("sp", km[i2][:], PM[r0:r0 + 128, 1312:1440].rearrange("t (h d) -> t h d", h=2),
                  reads=[("PM", 0)], writes=[("km", i2)])
            P.dma("sp", vm[i2][:], PM[r0:r0 + 128, 1440:1568].rearrange("t (h d) -> t h d", h=2),
                  reads=[("PM", 0)], writes=[("vm", i2)])
            P.op("dve", "tensor_copy", V_all[:, tt, :, 0:64], vm[i2][:], reads=[("vm", i2)], writes=["V_all"])
            if r0 >= LC:
                P.dma("sp", rp[i2][:], ropeM[r0 - LC:r0 - LC + 128], writes=[("rp", i2)])
                rope(kr[:], km[i2][:], 2, rp[i2], ("km", i2), "kr", ("rp", i2))
                ksrc, kk_ = kr, "kr"
            else:
                ksrc, kk_ = km[i2], ("km", i2)
            for hk in range(2):
                tr(pS[0][:64, hk * 128:(hk + 1) * 128], ksrc[:, hk, :], [kk_], ["pS0"], inc=(hk == 1))
            act(kT_all[:, :, r0:r0 + 128], pS[0][:64, 0:256].rearrange("p (h t) -> p h t", h=2), AF.Copy,
                ["pS0"], ["kT_all"])
        nmix = 0
        for tt in range(NTL):
            r0 = tt * 128
            isx = r0 >= LC
            if not isx and not do_ctx:
                continue
            i2 = tt % 2
            P.dma("sp", qm[i2][:], PM[r0:r0 + 128, 800:1312].rearrange("t (h d) -> t h d", h=8),
                  reads=[("PM", 0)], writes=[("qm", i2)])
            if isx:
                P.dma("sp", rp[i2][:], ropeM[r0 - LC:r0 - LC + 128], writes=[("rp", i2)])
                rope(qr[:], qm[i2][:], 8, rp[i2], ("qm", i2), "qr", ("rp", i2))
                qsrc, qk_ = qr, "qr"
            else:
                qsrc, qk_ = qm[i2], ("qm", i2)
            for half in range(2):
                for hh in range(4):
                    tr(pS[half][:64, hh * 128:(hh + 1) * 128], qsrc[:, half * 4 + hh, :], [qk_], ["pS%d" % half],
                       inc=(hh == 3))
                act(qT[:, half * 4:half * 4 + 4, :], pS[half][:64, :].rearrange("p (h t) -> p h t", h=4), AF.Copy,
                    ["pS%d" % half], ["qT"])
            keys = []
            if isx:
                nct = LC // 128
                if tt - 1 >= nct:
                    keys.append((tt - 1, 0))
                keys.append((tt, None))
                if tt + 1 < NTL:
                    keys.append((tt + 1, 1))
            keys += [(c, None) for c in range(LC // 128)]
            mb = mixb[nmix % 2]
            mbk = ("mixb", nmix % 2)
            nmix += 1
            for hk in range(2):
                for ki, (kt, mid) in enumerate(keys):
                    pi = cnts["ps"] % 2
                    cnts["ps"] += 1
                    for g in range(4):
                        mm(pG[pi][:, g * 128:(g + 1) * 128], kT_all[:, hk, kt * 128:(kt + 1) * 128], qT[:, hk * 4 + g, :],
                           ["kT_all", "qT"], ["pG%d" % pi], inc=(g == 3))
                    pt = pT[hk * 5 + ki]
                    ptk = ("pT", hk * 5 + ki)
                    act(pt[:], pG[pi][:], AF.Exp, ["pG%d" % pi], [ptk], scale=0.125)
                    if mid is not None:
                        mk = cst[:, C_SWM + mid * 128:C_SWM + (mid + 1) * 128].unsqueeze(1).to_broadcast([128, 4, 128])
                        tt_(pt[:].rearrange("p (g q) -> p g q", g=4), pt[:].rearrange("p (g q) -> p g q", g=4), mk,
                            ALU.mult, [ptk, "cst"], [ptk])
                po = pD[hk]
                for g in range(4):
                    for ki, (kt, mid) in enumerate(keys):
                        mm(po[:, g * 65:(g + 1) * 65], pT[hk * 5 + ki][:, g * 128:(g + 1) * 128], V_all[:, kt, hk, :],
                           [("pT", hk * 5 + ki), "V_all"], ["pD%d" % hk], start=(ki == 0), stop=(ki == len(keys) - 1),
                           inc=(ki == len(keys) - 1 and g == 3))
                pov = po[:, 0:260].rearrange("p (g e) -> p g e", g=4)
                tt_(den[:], pov[:, :, 64], esink[:, hk * 4:hk * 4 + 4], ALU.add, ["pD%d" % hk, "esink"], ["den"])
                P.op("dve", "reciprocal", den[:], den[:], reads=["den"], writes=["den"])
                tt_(mb[:, hk * 256:(hk + 1) * 256].rearrange("p (g e) -> p g e", g=4), pov[:, :, 0:64],
                    den[:].unsqueeze(2).to_broadcast([128, 4, 64]), ALU.mult, ["pD%d" % hk, "den"], [mbk])
            P.dma("sp", MIX[r0:r0 + 128, 256:768], mb[:], reads=[mbk], writes=[("MIX", "b", tt)])
        P.barrier()
        es.close()

    def run_interleaved(gens):
        gens = list(gens)
        while gens:
            for g in list(gens):
                try:
                    next(g)
                except StopIteration:
                    gens.remove(g)

    def chunk_order(d):
        nc_c = LC // 64
        if d == 0:
            return list(range(NCH))
        return list(range(nc_c - 1, -1, -1)) + list(range(NCH - 1, nc_c - 1, -1))

    def gla(l, do_ctx):
        es = ExitStack()
        P.es = es
        upa = P.sb([17, 2, 128], F32, "upa")
        P.dma("sp", upa[:], gla_up[l].rearrange("d r c -> r d c"), writes=["upa"])
        gng = P.sb([128, 256], F32, "gng")
        P.dma("sp", gng[:], gla_g[l].partition_broadcast(128), writes=["gng"])
        Sx = [P.sb([128, 256], F32, "Sx%d" % d) for d in range(2)]
        for d in range(2):
            P.op("dve", "memset", Sx[d][:], 0.0, writes=[("Sx", d)])
        triI = [cst[0:64, C_TRI + d * 64:C_TRI + (d + 1) * 64] for d in range(2)]
        incl = [cst[0:64, C_M01 + d * 64:C_M01 + (d + 1) * 64] for d in range(2)]
        gbd = cst[:, C_GBD:C_GBD + 256]
        pb = {0: (pG[0], "pG0", pU[0], "pU0", pD[0], "pD0"), 1: (pG[1], "pG1", pU[1], "pU1", pD[1], "pD1")}

        def unit(d):
            pa, pak, pbb, pbk, pc_, pck = pb[d]
            tl = {}
            for nm, shp in (("qT", [128, 64]), ("kT", [128, 64]), ("zT", [17, 64]), ("kM", [64, 128]), ("vM", [64, 256])
                            ):
                tl[nm] = [P.sb(shp, F32, "g%s%d_%d" % (nm, d, i)) for i in range(2)]
            for i in range(2):
                P.op("dve", "memset", tl["zT"][i][:], 1.0, writes=[("zT", d, i)])
            sp_ = P.sb([64, 128], F32, "gsp%d" % d)
            ebT = P.sb([128, 64], F32, "gebT%d" % d)
            enbT = P.sb([128, 64], F32, "genbT%d" % d)
            enbM = P.sb([64, 128], F32, "genbM%d" % d)
            qs = P.sb([128, 64], F32, "gqs%d" % d)
            kmk = P.sb([128, 4, 64], F32, "gkmk%d" % d)
            KtM = P.sb([64, 128], F32, "gKtM%d" % d)
            att = P.sb([64, 4, 64], F32, "gatt%d" % d)
            tmpS = P.sb([128, 256], F32, "gtmpS%d" % d)
            osb = P.sb([64, 4, 64], F32, "gosb%d" % d)
            osq = P.sb([64, 4, 64], F32, "gosq%d" % d)
            ssq = P.sb([64, 4], F32, "gssq%d" % d)
            sga = P.sb([64, 256], F32, "gsga%d" % d)
            K = lambda s: (s, d)
            for ui, c in enumerate(chunk_order(d)):
                i2 = ui % 2
                r0 = c * 64
                isx = r0 >= LC
                KB = lambda s: (s, d, i2)
                P.dma("sp", tl["qT"][i2][:], PF[0:128, r0:r0 + 64], reads=[("PF", 0)], writes=[KB("qT")])
                P.dma("sp", tl["kT"][i2][:], PF[128:256, r0:r0 + 64], reads=[("PF", 0)], writes=[KB("kT")])
                P.dma("sp", tl["zT"][i2][0:16, :], PF[768 + 16 * d:784 + 16 * d, r0:r0 + 64], reads=[("PF", 0)],
                      writes=[KB("zT")])
                P.dma("sp", tl["kM"][i2][:], PM[r0:r0 + 64, 128:256], reads=[("PM", 0)], writes=[KB("kM")])
                P.dma("sp", tl["vM"][i2][:], PM[r0:r0 + 64, 256:512], reads=[("PM", 0)], writes=[KB("vM")])
                qT, kT, zT, kM, vM = (tl[n_][i2] for n_ in ("qT", "kT", "zT", "kM", "vM"))
                yield
                mm(pa[:64, 0:128], zT[:], upa[:, d, :], [KB("zT"), "upa"], [pak])
                act(sp_[:], pa[:64, 0:128], AF.Exp, [pak], [K("sp")], scale=-1.0)
                act(sp_[:], sp_[:], AF.Ln, [K("sp"), "kc"], [K("sp")], bias=kc[:64, 1:2])
                yield
                mm(pa[:, 128:192], sp_[:], triI[d], [K("sp"), "cst"], [pak], inc=False)
                mm(pa[:64, 256:384], triI[d], sp_[:], [K("sp"), "cst"], [pak])
                act(ebT[:], pa[:, 128:192], AF.Exp, [pak], [K("ebT")], scale=1.0 / 16)
                act(enbT[:], pa[:, 128:192], AF.Exp, [pak], [K("enbT")], scale=-1.0 / 16)
                act(enbM[:], pa[:64, 256:384], AF.Exp, [pak], [K("enbM")], scale=-1.0 / 16)
                yield
                stt(qs[:], qT[:], 32.0 ** -0.5, ebT[:], ALU.mult, ALU.mult, [KB("qT"), K("ebT")], [K("qs")])
                for h in range(4):
                    stt(kmk[:, h, :], kT[:], cst[:, C_GHM + h:C_GHM + h + 1], enbT[:], ALU.mult, ALU.mult,
                        [KB("kT"), K("enbT"), "cst"], [K("kmk")])
                tt_(KtM[:], kM[:], enbM[:], ALU.mult, [KB("kM"), K("enbM")], [K("KtM")])
                yield
                for h in range(4):
                    mm(pbb[:64, h * 64:(h + 1) * 64], kmk[:, h, :], qs[:], [K("kmk"), K("qs")], [pbk], inc=(h == 3))
                tt_(att[:], pbb[:64, 0:256].rearrange("p (h t) -> p h t", h=4),
                    incl[d].unsqueeze(1).to_broadcast([64, 4, 64]), ALU.mult, [pbk, "cst"], [K("att")])
                yield
                for h in range(4):
                    mm(pc_[:64, h * 64:(h + 1) * 64], att[:, h, :], vM[:, h * 64:(h + 1) * 64], [K("att"), KB("vM")], [pck],
                       start=True, stop=False, inc=False)
                    mm(pc_[:64, h * 64:(h + 1) * 64], qs[:], Sx[d][:, h * 64:(h + 1) * 64], [K("qs"), ("Sx", d)], [pck],
                       start=False, stop=True, inc=(h == 3))
                mm(pbb[:, 256:512], KtM[:], vM[:], [K("KtM"), KB("vM")], [pbk])
                yield
                tt_(tmpS[:], pbb[:, 256:512], gbd, ALU.mult, [pbk, "cst"], [K("tmpS")])
                tt_(tmpS[:], tmpS[:], Sx[d][:], ALU.add, [K("tmpS"), ("Sx", d)], [K("tmpS")])
                last = 63 if d == 0 else 0
                ts_(Sx[d][:], tmpS[:], ebT[:, last:last + 1], None, ALU.mult, None, [K("tmpS"), K("ebT")], [("Sx", d)])
                P.op("dve", "tensor_copy", osb[:].rearrange("p h e -> p (h e)"), pc_[:64, 0:256], reads=[pck],
                     writes=[K("osb")])
                P.dma("sp", OGD[d, r0:r0 + 64, :], osb[:].rearrange("p h e -> p (h e)"), reads=[K("osb")],
                      writes=[("OGD", d, c)])
                yield

        run_interleaved([unit(0), unit(1)])
        of_ = [P.sb([128, 4, 64], F32, "gof%d" % i) for i in range(2)]
        ob_ = [P.sb([128, 4, 64], F32, "gob%d" % i) for i in range(2)]
        ga_ = [P.sb([128, 256], F32, "gga%d" % i) for i in range(2)]
        fsq = P.sb([128, 4, 64], F32, "gfsq")
        fss = P.sb([128, 4], F32, "gfss")
        ni = 0
        for tt in range(NTL):
            r0 = tt * 128
            if r0 < LC and not do_ctx:
                continue
            i2 = ni % 2
            ni += 1
            P.dma("sp", of_[i2][:].rearrange("p h e -> p (h e)"), OGD[0, r0:r0 + 128, :],
                  reads=[("OGD", 0, 2 * tt), ("OGD", 0, 2 * tt + 1)], writes=[("gof", i2)])
            P.dma("sp", ob_[i2][:].rearrange("p h e -> p (h e)"), OGD[1, r0:r0 + 128, :],
                  reads=[("OGD", 1, 2 * tt), ("OGD", 1, 2 * tt + 1)], writes=[("gob", i2)])
            P.dma("sp", ga_[i2][:], PM[r0:r0 + 128, 512:768], reads=[("PM", 0)], writes=[("gga", i2)])
            o = of_[i2]
            ok = ("gof", i2)
            ov = o[:].rearrange("p h e -> p (h e)")
            tt_(o[:], o[:], ob_[i2][:], ALU.add, [ok, ("gob", i2)], [ok])
            tt_(fsq[:], o[:], o[:], ALU.mult, [ok], ["gfsq"])
            P.op("dve", "tensor_reduce", fss[:], fsq[:], AX.X, ALU.add, reads=["gfsq"], writes=["gfss"])
            act(fss[:], fss[:], AF.Sqrt, ["gfss", "kc"], ["gfss"], scale=1.0 / 64, bias=kc[:, 3:4])
            P.op("dve", "reciprocal", fss[:], fss[:], reads=["gfss"], writes=["gfss"])
            tt_(o[:], o[:], fss[:].unsqueeze(2).to_broadcast([128, 4, 64]), ALU.mult, [ok, "gfss"], [ok])
            tt_(ov, ov, gng[:], ALU.mult, [ok, "gng"], [ok])
            act(ga_[i2][:], ga_[i2][:], AF.Silu, [("gga", i2)], [("gga", i2)])
            tt_(ov, ov, ga_[i2][:], ALU.mult, [ok, ("gga", i2)], [ok])
            P.dma("sp", MIX[r0:r0 + 128, 0:256], ov, reads=[ok], writes=[("MIX", "a", tt)])
        P.barrier()
        es.close()

    def rwkv_prep(l):
        es = ExitStack()
        P.es = es
        rv = P.sb([128, 2, 7], F32, "rv")
        P.dma("sp", rv[:], rw_vec[:, l], writes=["rv"])
        nw0 = P.sb([128, 2, 2], F32, "nw0")
        ts_(nw0[:], rv[:, :, 3:5], -1.0, None, ALU.mult, None, ["rv"], ["nw0"])
        mu = P.sb([128, 11], F32, "mu")
        P.dma("sp", mu[:], rw_mu[:, l], writes=["mu"])
        wup = P.sb([64, 2, 256], F32, "wup")
        P.dma("sp", wup[:], rw_wup[l].rearrange("d r c -> r d c"), writes=["wup"])
        aup = P.sb([64, 2, 256], F32, "aup")
        P.dma("sp", aup[:], rw_aup[l].rearrange("d r c -> r d c"), writes=["aup"])
        bo = cst[:, C_BO:C_BO + 128]
        NB = 16
        bufs = [P.sb([128, TG + 2], F32, "rb%d" % i) for i in range(NB)]
        fbufs = [P.sb([128, TG + 2], F32, "rf%d" % i) for i in range(13)]
        nb = [0]

        def newb():
            i = nb[0] % NB
            nb[0] += 1
            return bufs[i], ("rb", i)

        R0 = 1568
        srcs = [(R0 + 128 * j, 128) for j in range(6)] + [(R0 + 768, 64), (R0 + 832, 64), (R0 + 896, 64), (R0 + 960, 64),
                                                          (R0 + 1024, 128)]
        for (st, t0, n) in groups:
            off = toff(st, t0)
            seq_lo = 0 if st == "c" else LC
            seq_hi = LC if st == "c" else T
            lo = max(off - 1, seq_lo)
            hi = min(off + n + 1, seq_hi)
            f = []
            for j, (r0, nr) in enumerate(srcs):
                ld, ldk = newb()
                P.op("dve", "memset", ld[:, 0:n + 2], 0.0, writes=[ldk])
                P.dma("sp", ld[:nr, lo - off + 1:hi - off + 1], PF[r0:r0 + nr, lo:hi], reads=[("PF", 0)], writes=[ldk])
                sh, shk = newb()
                tt_(sh[:nr, :n], ld[:nr, 0:n], ld[:nr, 2:n + 2], ALU.add, [ldk], [shk])
                stt(sh[:nr, :n], sh[:nr, :n], 0.5, ld[:nr, 1:n + 1], ALU.mult, ALU.subtract, [shk, ldk], [shk])
                fo, fok = fbufs[j], ("rf", j)
                stt(fo[:nr, :n], sh[:nr, :n], mu[:nr, j:j + 1], ld[:nr, 1:n + 1], ALU.mult, ALU.add, [shk, ldk, "mu"], [fok])
                f.append((fo, fok))
            rr, kk_, vv_ = f[0:2], f[2:4], f[4:6]
            zw, za, zg = f[6:8], f[8:10], f[10]
            for pc in range(2):
                P.dma("sp", RF[0 + pc, :, off:off + n], rr[pc][0][:, :n], reads=[rr[pc][1]], writes=[("RF", off)])
                P.dma("sp", RF[2 + pc, :, off:off + n], vv_[pc][0][:, :n], reads=[vv_[pc][1]], writes=[("RF", off)])
            kkn = []
            for pc in range(2):
                k_, kk2 = kk_[pc]
                sq, sqk = newb()
                act(sq[:, :n], k_[:, :n], AF.Square, [kk2, "rv"], [sqk], scale=rv[:, pc, 0:1])
                mm(pS[0][:, :n], bo, sq[:, :n], ["cst", sqk], ["pS0"])
                act(sq[:, :n], pS[0][:, :n], AF.Sqrt, ["pS0"], [sqk])
                ts_(sq[:, :n], sq[:, :n], 1e-12, None, ALU.max, None, [sqk], [sqk])
                P.op("dve", "reciprocal", sq[:, :n], sq[:, :n], reads=[sqk], writes=[sqk])
                kn, knk = fbufs[11 + pc], ("rf", 11 + pc)
                stt(kn[:, :n], k_[:, :n], rv[:, pc, 0:1], sq[:, :n], ALU.mult, ALU.mult, [kk2, "rv", sqk], [knk])
                P.dma("sp", RF[4 + pc, :, off:off + n], kn[:, :n], reads=[knk], writes=[("RF", off)])
                kkn.append((kn, knk))
            ksum = [None, None]
            for d in range(2):
                th, thk = newb()
                act(th[:64, :n], zw[d][0][:64, :n], AF.Tanh, [zw[d][1]], [thk])
                for pc in range(2):
                    mm(pS[1][:, :n], wup[:, d, pc * 128:(pc + 1) * 128], th[:64, :n], ["wup", thk], ["pS1"])
                    ew, ewk = newb()
                    act(ew[:, :n], pS[1][:, :n], AF.Exp, ["pS1", "nw0"], [ewk], scale=-1.0, bias=nw0[:, pc, d:d + 1])
                    act(ew[:, :n], ew[:, :n], AF.Ln, [ewk, "kc"], [ewk], bias=kc[:, 1:2])
                    act(ew[:, :n], ew[:, :n], AF.Exp, [ewk, "kc"], [ewk], scale=-1.0, bias=kc[:, 2:3])
                    P.dma("sp", RF[10 + 6 * d + pc, :, off:off + n], ew[:, :n], reads=[ewk], writes=[("RF", off)])
                    mm(pS[0][:, :n], aup[:, d, pc * 128:(pc + 1) * 128], za[d][0][:64, :n], ["aup", za[d][1]], ["pS0"])
                    a_, ak = newb()
                    act(a_[:, :n], pS[0][:, :n], AF.Sigmoid, ["pS0", "rv"], [ak], bias=rv[:, pc, 5 + d:6 + d])
                    bd, bdk = newb()
                    tt_(bd[:, :n], a_[:, :n], kkn[pc][0][:, :n], ALU.mult, [ak, kkn[pc][1]], [bdk])
                    P.dma("sp", RF[8 + 6 * d + pc, :, off:off + n], bd[:, :n], reads=[bdk], writes=[("RF", off)])
                    ts_(a_[:, :n], a_[:, :n], 1.0, rv[:, pc, 1:2], ALU.subtract, ALU.mult, [ak, "rv"], [ak])
                    kd, kdk = newb()
                    stt(kd[:, :n], a_[:, :n], 1.0, kk_[pc][0][:, :n], ALU.add, ALU.mult, [ak, kk_[pc][1]], [kdk])
                    P.dma("sp", RF[6 + 6 * d + pc, :, off:off + n], kd[:, :n], reads=[kdk], writes=[("RF", off)])
                    if d == 0:
                        ksum[pc] = (kd, kdk)
                    else:
                        kf, kfk = ksum[pc]
                        tt_(kd[:, :n], kd[:, :n], kf[:, :n], ALU.add, [kdk, kfk], [kdk])
                        ts_(kd[:, :n], kd[:, :n], 0.5, rv[:, pc, 2:3], ALU.mult, ALU.mult, [kdk, "rv"], [kdk])
                        tt_(kd[:, :n], kd[:, :n], rr[pc][0][:, :n], ALU.mult, [kdk, rr[pc][1]], [kdk])
                        P.dma("sp", RF[18 + pc, :, off:off + n], kd[:, :n], reads=[kdk], writes=[("RF", off)])
            sz, szk = newb()
            act(sz[:, :n], zg[0][:, :n], AF.Sigmoid, [zg[1]], [szk])
            P.dma("sp", RF[20, :, off:off + n], sz[:, :n], reads=[szk], writes=[("RF", off)])
        P.barrier()
        es.close()

    def rwkv_scan(l):
        import os as _os
        _rwcut = int(_os.environ.get('RW_CUT', '99'))
        es = ExitStack()
        P.es = es
        tri = [cst[0:64, C_TRI + i * 64:C_TRI + (i + 1) * 64] for i in range(4)]
        m01 = [cst[0:64, C_M01 + i * 64:C_M01 + (i + 1) * 64] for i in range(4)]
        id64 = cst[0:64, C_ID:C_ID + 64]
        Hx = [[P.sb([128, 256], F32, "Hx%d%d" % (d, pc)) for pc in range(2)] for d in range(2)]
        for d in range(2):
            for pc in range(2):
                P.op("dve", "memset", Hx[d][pc][:], 0.0, writes=[("Hx", d, pc)])
        bc4 = lambda m: m.unsqueeze(1).to_broadcast([64, 4, 64])
        banks = {"A": (pG[0], "pG0"), "B": (pG[1], "pG1"), "C": (pU[0], "pU0"), "D": (pU[1], "pU1"),
                 "E": (pD[0], "pD0"), "F": (pD[1], "pD1"), "G": (pS[0], "pS0"), "H": (pS[1], "pS1")}

        def unit(d):
            K = lambda s: (s, d)
            cm = [P.sb([128, 6, 64], F32, "cm%d_%d" % (d, i)) for i in range(2)]
            dm = [P.sb([128, 6, 64], F32, "dm%d_%d" % (d, i)) for i in range(2)]
            ewM = P.sb([64, 256], F32, "ewM%d" % d)
            vM = P.sb([64, 256], F32, "rvM%d" % d)
            Pd = P.sb([128, 2, 3, 64], F32, "Pd%d" % d)
            fT = P.sb([128, 2, 4, 64], F32, "fT%d" % d)
            mT_ = P.sb([128, 2, 2, 2, 64], F32, "mK%d" % d)
            KBM = P.sb([64, 2, 256], F32, "KBM%d" % d)
            sc = P.sb([64, 5, 4, 64], F32, "sc%d" % d)
            Tt = P.sb([64, 4, 64], F32, "Tt%d" % d)
            TtT = P.sb([64, 4, 64], F32, "TtT%d" % d)
            Nn = [P.sb([64, 2, 4, 64], F32, "Nn%d_%d" % (d, i)) for i in range(2)]
            Xs = P.sb([64, 4, 64], F32, "Xs%d" % d)
            Us = P.sb([64, 4, 64], F32, "Us%d" % d)
            Ys = P.sb([64, 256], F32, "Ys%d" % d)
            tmpH = P.sb([128, 256], F32, "tmpH%d" % d)
            phys = [(pG[d], "pG%d" % d), (pU[d], "pU%d" % d), (pD[d], "pD%d" % d), (pS[d], "pS%d" % d)]
            bA, kA = phys[0]; bB, kB = phys[1]; bC, kC = phys[2]; bD, kD = phys[3]
            bE, kE = phys[0]; bF, kF = phys[1]; bG, kG = phys[2]; bH, kH = phys[3]
            v4 = lambda ap: ap.rearrange("p (h t) -> p h t", h=4)
            for ui, c in enumerate(chunk_order(d)):
                i2 = ui % 2
                r0 = c * 64
                KB = lambda s: (s, d, i2)
                P.dma("sp", cm[i2][:], RF[0:6, :, r0:r0 + 64].rearrange("j p t -> p j t"), reads=[("RF", 0)],
                      writes=[KB("cm")])
                P.dma("sp", dm[i2][:], RF[6 + 6 * d:12 + 6 * d, :, r0:r0 + 64].rearrange("j p t -> p j t"),
                      reads=[("RF", 0)], writes=[KB("dm")])
                cmt, dmt = cm[i2], dm[i2]
                yield
                if _rwcut <= 1:
                    continue
                for pc in range(2):
                    tr(bG[:64, pc * 128:(pc + 1) * 128], dmt[:, 4 + pc, :], [KB("dm")], [kG], inc=False)
                    tr(bG[:64, 256 + pc * 128:256 + (pc + 1) * 128], cmt[:, 2 + pc, :], [KB("cm")], [kG], inc=(pc == 1))
                P.op("dve", "tensor_copy", ewM[:], bG[:64, 0:256], reads=[kG], writes=[K("ewM")])
                P.op("dve", "tensor_copy", vM[:], bG[:64, 256:512], reads=[kG], writes=[K("vM")])
                yield
                if _rwcut <= 2:
                    continue
                for pc in range(2):
                    for ie in range(2):
                        mm(bH[:, (pc * 2 + ie) * 64:(pc * 2 + ie + 1) * 64], ewM[:, pc * 128:(pc + 1) * 128], tri[2 * ie + d],
                           [K("ewM"), "cst"], [kH], inc=(pc == 1 and ie == 1))
                lv = bH[:, 0:256].rearrange("p (c i t) -> p c i t", c=2, i=2)
                act(Pd[:, :, 0:2, :], lv, AF.Exp, [kH], [K("Pd")])
                act(Pd[:, :, 2, :], lv[:, :, 0, :], AF.Exp, [kH], [K("Pd")], scale=-1.0)
                yield
                if _rwcut <= 3:
                    continue
                tt_(fT[:, :, 0, :], cmt[:, 0:2, :], Pd[:, :, 0, :], ALU.mult, [KB("cm"), K("Pd")], [K("fT")])
                tt_(fT[:, :, 1, :], cmt[:, 4:6, :], Pd[:, :, 1, :], ALU.mult, [KB("cm"), K("Pd")], [K("fT")])
                tt_(fT[:, :, 2, :], dmt[:, 0:2, :], Pd[:, :, 2, :], ALU.mult, [KB("dm"), K("Pd")], [K("fT")])
                tt_(fT[:, :, 3, :], dmt[:, 2:4, :], Pd[:, :, 2, :], ALU.mult, [KB("dm"), K("Pd")], [K("fT")])
                for hh in range(2):
                    ts_(mT_[:, :, :, hh, :], fT[:, :, 2:4, :], cst[:, C_RHM + hh:C_RHM + hh + 1], None, ALU.mult, None,
                        [K("fT"), "cst"], [K("mK")])
                yield
                if _rwcut <= 4:
                    continue
                for pc in range(2):
                    tr(bG[:64, pc * 128:(pc + 1) * 128], fT[:, pc, 2, :], [K("fT")], [kG], inc=False)
                    tr(bG[:64, 256 + pc * 128:256 + (pc + 1) * 128], fT[:, pc, 3, :], [K("fT")], [kG], inc=(pc == 1))
                act(KBM[:, 0, :], bG[:64, 0:256], AF.Copy, [kG], [K("KBM")])
                act(KBM[:, 1, :], bG[:64, 256:512], AF.Copy, [kG], [K("KBM")], scale=-1.0)
                for h in range(4):
                    pc, hh = h // 2, h % 2
                    KdTm, BdTm = mT_[:, pc, 0, hh, :], mT_[:, pc, 1, hh, :]
                    KKeT, RpT = fT[:, pc, 1, :], fT[:, pc, 0, :]
                    rd_ = [K("mK"), K("fT")]
                    mm(bA[:64, h * 64:(h + 1) * 64], KdTm, KKeT, rd_, [kA], inc=False)
                    mm(bA[:64, 256 + h * 64:256 + (h + 1) * 64], BdTm, KKeT, rd_, [kA], inc=(h == 3))
                    mm(bB[:64, h * 64:(h + 1) * 64], KKeT, BdTm, rd_, [kB], inc=False)
                    mm(bB[:64, 256 + h * 64:256 + (h + 1) * 64], KdTm, RpT, rd_, [kB], inc=(h == 3))
                    mm(bC[:64, h * 64:(h + 1) * 64], BdTm, RpT, rd_, [kC], inc=(h == 3))
                yield
                if _rwcut <= 5:
                    continue
                tt_(sc[:, 0], v4(bA[:64, 0:256]), bc4(m01[2 + d]), ALU.mult, [kA, "cst"], [K("sc0")])
                tt_(sc[:, 1], v4(bA[:64, 256:512]), bc4(m01[2 + d]), ALU.mult, [kA, "cst"], [K("sc1")])
                tt_(sc[:, 2], v4(bB[:64, 0:256]), bc4(m01[3 - d]), ALU.mult, [kB, "cst"], [K("sc2")])
                tt_(sc[:, 3], v4(bB[:64, 256:512]), bc4(m01[d]), ALU.mult, [kB, "cst"], [K("sc3")])
                stt(sc[:, 4], v4(bC[:64, 0:256]), -1.0, bc4(m01[d]), ALU.mult, ALU.mult, [kC, "cst"], [K("sc4")])
                stt(Tt[:], sc[:, 1], -1.0, bc4(id64), ALU.mult, ALU.add, [K("sc1"), "cst"], [K("Tt")])
                stt(TtT[:], sc[:, 2], -1.0, bc4(id64), ALU.mult, ALU.add, [K("sc2"), "cst"], [K("TtT")])
                yield
                if _rwcut <= 6:
                    continue
                Ncur, NTcur, nk = sc[:, 1], sc[:, 2], [K("sc1"), K("sc2")]
                for lev in range(5):
                    lastl = lev == 4
                    for h in range(4):
                        mm(bD[:64, h * 64:(h + 1) * 64], NTcur[:, h, :], Ncur[:, h, :], nk, [kD], inc=(lastl and h == 3))
                        if not lastl:
                            mm(bD[:64, 256 + h * 64:256 + (h + 1) * 64], Ncur[:, h, :], NTcur[:, h, :], nk, [kD],
                               inc=(h == 3))
                    nn = Nn[lev % 2]
                    nnk = (K("Nn"), lev % 2)
                    if lastl:
                        act(nn[:, 0], v4(bD[:64, 0:256]), AF.Copy, [kD], [nnk])
                    else:
                        act(nn[:].rearrange("p a h t -> p (a h) t"), bD[:64, :].rearrange("p (a t) -> p a t", a=8),
                            AF.Copy, [kD], [nnk])
                    yield
                    for h in range(4):
                        mm(bE[:64, h * 64:(h + 1) * 64], TtT[:, h, :], nn[:, 0, h, :], [K("TtT"), nnk], [kE],
                           inc=(lastl and h == 3))
                        if not lastl:
                            mm(bE[:64, 256 + h * 64:256 + (h + 1) * 64], nn[:, 0, h, :], TtT[:, h, :], [K("TtT"), nnk], [kE],
                               inc=(h == 3))
                    tt_(Tt[:], Tt[:], v4(bE[:64, 0:256]), ALU.add, [K("Tt"), kE], [K("Tt")])
                    if not lastl:
                        tt_(TtT[:], TtT[:], v4(bE[:64, 256:512]), ALU.add, [K("TtT"), kE], [K("TtT")])
                    Ncur, NTcur, nk = nn[:, 0], nn[:, 1], [nnk]
                    yield
                for h in range(4):
                    pc = h // 2
                    mm(bC[:64, 256 + h * 64:256 + (h + 1) * 64], fT[:, pc, 1, :], Hx[d][pc][:, h * 64:(h + 1) * 64],
                       [K("fT"), ("Hx", d, pc)], [kC], start=True, stop=False, inc=False)
                    mm(bC[:64, 256 + h * 64:256 + (h + 1) * 64], sc[:, 0, h, :], vM[:, h * 64:(h + 1) * 64],
                       [K("sc0"), K("vM")], [kC], start=False, stop=True, inc=(h == 3))
                P.op("dve", "tensor_copy", Xs[:], v4(bC[:64, 256:512]), reads=[kC], writes=[K("Xs")])
                yield
                if _rwcut <= 7:
                    continue
                for h in range(4):
                    mm(bF[:64, h * 64:(h + 1) * 64], Tt[:, h, :], Xs[:, h, :], [K("Tt"), K("Xs")], [kF], inc=(h == 3))
                act(Us[:], v4(bF[:64, 0:256]), AF.Copy, [kF], [K("Us")])
                yield
                if _rwcut <= 8:
                    continue
                for h in range(4):
                    pc = h // 2
                    o_ = bF[:64, 256 + h * 64:256 + (h + 1) * 64]
                    mm(o_, fT[:, pc, 0, :], Hx[d][pc][:, h * 64:(h + 1) * 64], [K("fT"), ("Hx", d, pc)], [kF],
                       start=True, stop=False, inc=False)
                    mm(o_, sc[:, 3, h, :], vM[:, h * 64:(h + 1) * 64], [K("sc3"), K("vM")], [kF], start=False, stop=False,
                       inc=False)
                    mm(o_, sc[:, 4, h, :], Us[:, h, :], [K("sc4"), K("Us")], [kF], start=False, stop=True, inc=(h == 3))
                P.op("dve", "tensor_copy", Ys[:], bF[:64, 256:512], reads=[kF], writes=[K("Ys")])
                P.dma("sp", YD[d, r0:r0 + 64, :], Ys[:], reads=[K("Ys")], writes=[("YD", d, c)])
                lastc = 63 if d == 0 else 0
                for pc in range(2):
                    o_ = bH[:, 256:512] if pc == 0 else bG[:, 0:256]
                    ok_ = kH if pc == 0 else kG
                    mm(o_, KBM[:, 0, pc * 128:(pc + 1) * 128], vM[:], [K("KBM"), K("vM")], [ok_], start=True, stop=False,
                       inc=False)
                    mm(o_, KBM[:, 1, pc * 128:(pc + 1) * 128], Us[:].rearrange("p h e -> p (h e)"), [K("KBM"), K("Us")],
                       [ok_], start=False, stop=True, inc=True)
                    tt_(tmpH[:], o_, cst[:, C_RBD + pc * 256:C_RBD + (pc + 1) * 256], ALU.mult, [ok_, "cst"], [K("tmpH")])
                    tt_(tmpH[:], tmpH[:], Hx[d][pc][:], ALU.add, [K("tmpH"), ("Hx", d, pc)], [K("tmpH")])
                    ts_(Hx[d][pc][:], tmpH[:], Pd[:, pc, 0, lastc:lastc + 1], None, ALU.mult, None, [K("tmpH"), K("Pd")],
                        [("Hx", d, pc)])
                yield
                if _rwcut <= 9:
                    continue

        run_interleaved([unit(0), unit(1)])
        P.barrier()
        es.close()

    def rwkv_final(l, do_ctx):
        es = ExitStack()
        P.es = es
        gup = P.sb([128, 256], F32, "gup")
        P.dma("sp", gup[:], rw_gup[l], writes=["gup"])
        gnb = P.sb([128, 2, 256], F32, "gnb")
        for i in range(2):
            P.dma("sp", gnb[:, i, :], rw_gn[l, i].partition_broadcast(128), writes=["gnb"])
        yf = [P.sb([128, 4, 64], F32, "yf%d" % i) for i in range(2)]
        yb = [P.sb([128, 4, 64], F32, "yb%d" % i) for i in range(2)]
        fp = [P.sb([128, 5, 128], F32, "fp%d" % i) for i in range(2)]
        tm = P.sb([128, 2, 256], F32, "ftm")
        ysq = P.sb([128, 4, 64], F32, "ysq")
        st4 = P.sb([128, 4, 4], F32, "st4")
        gsb = P.sb([128, 256], F32, "gsb")
        ni = 0
        for tt in range(NTL):
            r0 = tt * 128
            if r0 < LC and not do_ctx:
                continue
            i2 = ni % 2
            ni += 1
            P.dma("sp", yf[i2][:].rearrange("p h e -> p (h e)"), YD[0, r0:r0 + 128, :],
                  reads=[("YD", 0, 2 * tt), ("YD", 0, 2 * tt + 1)], writes=[("yf", i2)])
            P.dma("sp", yb[i2][:].rearrange("p h e -> p (h e)"), YD[1, r0:r0 + 128, :],
                  reads=[("YD", 1, 2 * tt), ("YD", 1, 2 * tt + 1)], writes=[("yb", i2)])
            for j, pn in enumerate([18, 19, 2, 3, 20]):
                P.dma("sp", fp[i2][:, j, :], RF[pn, :, r0:r0 + 128], reads=[("RF", 0)], writes=[("fp", i2)])
            y = yf[i2]
            tt_(y[:], y[:], yb[i2][:], ALU.add, [("yf", i2), ("yb", i2)], [("yf", i2)])
            for j in range(4):
                tr(pG[0][:, j * 128:(j + 1) * 128], fp[i2][:, j, :], [("fp", i2)], ["pG0"], inc=(j == 3))
            act(tm[:].rearrange("p a c -> p (a c)"), pG[0][:, :], AF.Copy, ["pG0"], ["ftm"])
            mm(pG[1][:, 0:256], fp[i2][:, 4, :], gup[:], [("fp", i2), "gup"], ["pG1"])
            act(gsb[:], pG[1][:, 0:256], AF.Copy, ["pG1"], ["gsb"])
            P.op("dve", "tensor_reduce", st4[:, 0, :], y[:], AX.X, ALU.add, reads=[("yf", i2)], writes=["st4"])
            tt_(ysq[:], y[:], y[:], ALU.mult, [("yf", i2)], ["ysq"])
            P.op("dve", "tensor_reduce", st4[:, 1, :], ysq[:], AX.X, ALU.add, reads=["ysq"], writes=["st4"])
            P.op("dve", "tensor_reduce", st4[:, 2, :], tm[:, 0, :].rearrange("p (h e) -> p h e", h=4), AX.X, ALU.add,
                 reads=["ftm"], writes=["st4"])
            ts_(st4[:, 0, :], st4[:, 0, :], 1.0 / 64, None, ALU.mult, None, ["st4"], ["st4"])
            tt_(st4[:, 3, :], st4[:, 0, :], st4[:, 0, :], ALU.mult, ["st4"], ["st4"])
            stt(st4[:, 1, :], st4[:, 1, :], 1.0 / 64, st4[:, 3, :], ALU.mult, ALU.subtract, ["st4"], ["st4"])
            act(st4[:, 1, :], st4[:, 1, :], AF.Sqrt, ["st4", "kc"], ["st4"], bias=kc[:, 4:5])
            P.op("dve", "reciprocal", st4[:, 1, :], st4[:, 1, :], reads=["st4"], writes=["st4"])
            tt_(y[:], y[:], st4[:, 0, :].unsqueeze(2).to_broadcast([128, 4, 64]), ALU.subtract, [("yf", i2), "st4"],
                [("yf", i2)])
            tt_(y[:], y[:], st4[:, 1, :].unsqueeze(2).to_broadcast([128, 4, 64]), ALU.mult, [("yf", i2), "st4"],
                [("yf", i2)])
            yv = y[:].rearrange("p h e -> p (h e)")
            tt_(yv, yv, gnb[:, 0, :], ALU.mult, [("yf", i2), "gnb"], [("yf", i2)])
            tt_(yv, yv, gnb[:, 1, :], ALU.add, [("yf", i2), "gnb"], [("yf", i2)])
            tt_(ysq[:], tm[:, 1, :].rearrange("p (h e) -> p h e", h=4),
                st4[:, 2, :].unsqueeze(2).to_broadcast([128, 4, 64]), ALU.mult, ["ftm", "st4"], ["ysq"])
            tt_(y[:], y[:], ysq[:], ALU.add, [("yf", i2), "ysq"], [("yf", i2)])
            tt_(yv, yv, gsb[:], ALU.mult, [("yf", i2), "gsb"], [("yf", i2)])
            P.dma("sp", MIX[r0:r0 + 128, 768:1024], yv, reads=[("yf", i2)], writes=[("MIX", "c", tt)])
        P.barrier()
        es.close()

    def outproj(l, do_ctx):
        es = ExitStack()
        P.es = es
        hT = [P.sb([128, 8, TG], F32, "hT%d" % i) for i in range(2)]
        lt = alloc_ln_tiles()
        vv = lt["vv"]
        mixT = P.sb([128, 8, TG], BF16, "mixT")
        mtile = [P.sb([128, 1024], F32, "mtile%d" % i) for i in range(2)]
        wo = P.sb([128, 8, 1024], BF16, "wo")
        for hf in range(2):
            P.dma("pool", wo[:, :, hf * 512:(hf + 1) * 512],
                  w_out[l][:, hf * 512:(hf + 1) * 512].rearrange("(k p) j -> p k j", p=128), writes=["wo"])
        cur_s = None
        nm = 0
        for (st, t0, n) in groups:
            if st == "c" and not do_ctx:
                continue
            s = 1 if st == "c" else 0
            off = toff(st, t0)
            if s != cur_s:
                mod_scalars(l, 1, s)
                cur_s = s
            gi = cnts["g"]
            cnts["g"] += 1
            h = hT[gi % 2]
            hk = "hT%d" % (gi % 2)
            P.dma("sp", h[:, :, :n], hsrc(st, t0, n), reads=[("H", st, t0)], writes=[hk])
            for ts in range(n // 128):
                mt = mtile[nm % 2]
                mk = ("mtile", nm % 2)
                nm += 1
                P.dma("sp", mt[:], MIX[off + ts * 128:off + (ts + 1) * 128, :],
                      reads=[kk for kk in list(P.lastw.keys()) if isinstance(kk, tuple) and kk[0] == "MIX"], writes=[mk])
                for half in range(2):
                    pi = half
                    for j in range(4):
                        tr(pG[pi][:, j * 128:(j + 1) * 128], mt[:, (half * 4 + j) * 128:(half * 4 + j + 1) * 128], [mk],
                           ["pG%d" % pi], inc=(j == 3))
                    act(mixT[:, half * 4:half * 4 + 4, ts * 128:(ts + 1) * 128],
                        pG[pi][:, :].rearrange("p (j t) -> p j t", j=4), AF.Copy, ["pG%d" % pi], ["mixT"])
            for c in range(8):
                pi = c % 2
                for k in range(8):
                    mm(pD[pi][:, :n], wo[:, k, c * 128:(c + 1) * 128], mixT[:, k, :n], ["wo", "mixT"], ["pD%d" % pi],
                       start=(k == 0), stop=(k == 7), inc=(k == 7))
                stt(vv[:, c, :n], pD[pi][:, :n], scl[:, 2, c:c + 1], h[:, c, :n], ALU.mult, ALU.add,
                    ["pD%d" % pi, "scl2", hk], [("vv", c)])
            layer_norm_store(lt, st, t0, n, l, 1)
        P.barrier()
        es.close()

    for l in range(L):
        last = l == L - 1
        if USE_WSCRATCH:
            convert_weights(l)
        ffn_sublayer(l, 0, ffn_w["ffn1_wg"][l], ffn_w["ffn1_wu"][l], ffn_w["ffn1_wd"][l])
        if dbg == "ffn1":
            break
        inproj(l)
        if dbg == "inproj":
            break
        if dbg in (None, "swa", "mix"):
            swa(l, not last)
        if dbg in (None, "gla", "mix"):
            gla(l, not last)
        if dbg in (None, "rwkv", "mix"):
            import os as _os
            _rs = int(_os.environ.get("RW_STOP", "9"))
            rwkv_prep(l)
            if _rs >= 2:
                rwkv_scan(l)
            if _rs >= 3:
                rwkv_final(l, not last)
        if dbg in ("swa", "gla", "rwkv", "mix"):
            break
        outproj(l, not last)
        if dbg == "outproj":
            break
        ffn_sublayer(l, 2, ffn_w["ffn2_wg"][l], ffn_w["ffn2_wu"][l], ffn_w["ffn2_wd"][l], do_ctx=not last)

    es = ExitStack()
    P.es = es
    evs = []
    if dbg in ("swa", "gla", "rwkv", "mix"):
        mixo = dram("mixo", [SEQ, D], kind="ExternalOutput")
        ob = [P.sb([128, 1024], F32, "ob%d" % i) for i in range(2)]
        for tt in range(SEQ // 128):
            r0 = LC + tt * 128
            P.dma("sp", ob[tt % 2][:], MIX[r0:r0 + 128, :], writes=[("ob", tt % 2)])
            evs.append(P.dma("sp", mixo[tt * 128:(tt + 1) * 128, :], ob[tt % 2][:], reads=[("ob", tt % 2)],
                             writes=[("mixo", tt)]))
    ob2 = [P.sb([128, 8, TG], F32, "ob2%d" % i) for i in range(2)]
    for gi, (st, t0, n) in enumerate(groups):
        if st == "c":
            continue
        b = ob2[gi % 2]
        bk = "ob2%d" % (gi % 2)
        P.dma("sp", b[:, :, :n], hsrc(st, t0, n), reads=[("H", st, t0)], writes=[bk])
        evs.append(P.dma("sp", outT[:, :, t0:t0 + n].rearrange("c p t -> p c t"), b[:, :, :n], reads=[bk],
                         writes=[("out", t0)]))
    P.finish("sp", evs)
    es.close()
    es0.close()
    print("program instructions:", P.ninst)
    return nc


def _consts():
    c = np.zeros((128, C_END), np.float32)
    c[:, C_ID:C_ID + 128] = np.eye(128)
    s = np.arange(64)[:, None]
    t = np.arange(64)[None, :]
    c[0:64, C_TRI + 0:C_TRI + 64] = -1.0 * (s <= t)
    c[0:64, C_TRI + 64:C_TRI + 128] = -1.0 * (s >= t)
    c[0:64, C_TRI + 128:C_TRI + 192] = -1.0 * (s < t)
    c[0:64, C_TRI + 192:C_TRI + 256] = -1.0 * (s > t)
    c[0:64, C_M01 + 0:C_M01 + 64] = (s <= t)
    c[0:64, C_M01 + 64:C_M01 + 128] = (s >= t)
    c[0:64, C_M01 + 128:C_M01 + 192] = (s < t)
    c[0:64, C_M01 + 192:C_M01 + 256] = (s > t)
    p = np.arange(128)[:, None]
    col = np.arange(256)[None, :]
    c[:, C_GBD:C_GBD + 256] = (p // 32 == col // 64)
    for h in range(4):
        c[:, C_GHM + h] = (np.arange(128) // 32 == h)
    for pc in range(2):
        c[:, C_RBD + pc * 256:C_RBD + (pc + 1) * 256] = ((2 * pc + p // 64) == col // 64)
    for hh in range(2):
        c[:, C_RHM + hh] = (np.arange(128) // 64 == hh)
    q = np.arange(128)[None, :]
    c[:, C_BO:C_BO + 128] = (p // 64 == q // 64)
    c[:, C_SWM:C_SWM + 128] = (p >= q)
    c[:, C_SWM + 128:C_SWM + 256] = (p <= q)
    c[0:64, C_NI:C_NI + 64] = -np.eye(64)
    return c


def _rope_table(SEQ):
    pos = np.arange(SEQ)
    row = (pos // 64).astype(np.float32)
    col = (pos % 64).astype(np.float32)
    inv = (np.float32(10000.0) ** (-np.arange(16, dtype=np.float32) / np.float32(16))).astype(np.float32)
    tab = np.zeros((SEQ, 2, 32), np.float32)
    for a, pp in enumerate((row, col)):
        ang = (pp[:, None] * inv[None, :]).astype(np.float32)
        tab[:, 0, a * 16:(a + 1) * 16] = np.cos(ang)
        tab[:, 1, a * 16:(a + 1) * 16] = np.sin(ang)
    return tab


def _prep_shared(inp, L, SEQ):
    f = lambda a: np.ascontiguousarray(a, dtype=np.float32)
    m = {}
    m["w_ada"] = f(inp["w_ada"][:L])
    m["b_adaT"] = f(inp["b_ada"][:L].reshape(L, 72, 128).transpose(2, 0, 1))
    ln = np.stack([inp["ln_g"][:L], inp["ln_b"][:L]], axis=2)
    m["lnT"] = f(ln.reshape(L, 3, 2, 8, 128).transpose(4, 0, 1, 2, 3))
    for nm in ("ffn1_wg", "ffn1_wu", "ffn1_wd", "ffn2_wg", "ffn2_wu", "ffn2_wd", "w_in", "w_out"):
        m[nm] = f(inp[nm][:L])
    m["consts"] = _consts()
    m["ropeM"] = _rope_table(SEQ)
    m["gla_up"] = f(np.concatenate([inp["gla_gate_up"][:L], inp["gla_gate_bias"][:L][:, :, None, :]], axis=2))
    m["gla_g"] = f(inp["gla_norm_g"][:L])
    m["sinkB"] = f(np.broadcast_to(inp["swa_sink"][:L][None], (128, L, 8)))
    vecs = np.stack([inp["rwkv_k_k"][:L], inp["rwkv_k_a"][:L], inp["rwkv_r_k"][:L].reshape(L, 256),
                     inp["rwkv_w0"][:L, 0], inp["rwkv_w0"][:L, 1], inp["rwkv_a0"][:L, 0], inp["rwkv_a0"][:L, 1]],
                    axis=-1)
    m["rw_vec"] = f(vecs.reshape(L, 2, 128, 7).transpose(2, 0, 1, 3))
    mu = inp["rwkv_mu"][:L]
    mut = np.zeros((128, L, 11), np.float32)
    for j in range(6):
        mut[:, :, j] = mu[:, 128 * j:128 * (j + 1)].T
    for j, r0 in enumerate((768, 832, 896, 960)):
        mut[:64, :, 6 + j] = mu[:, r0:r0 + 64].T
    mut[:, :, 10] = mu[:, 1024:1152].T
    m["rw_mu"] = mut
    m["rw_wup"] = f(inp["rwkv_w_up"][:L])
    m["rw_aup"] = f(inp["rwkv_a_up"][:L])
    m["rw_gup"] = f(inp["rwkv_g_up"][:L])
    m["rw_gn"] = f(np.stack([inp["rwkv_gn_g"][:L], inp["rwkv_gn_b"][:L]], axis=1))
    return m


def _prep_core(inp, b):
    f = lambda a: np.ascontiguousarray(a, dtype=np.float32)
    m = {}
    m["xT"] = f(inp["x"][b].T.reshape(8, 128, -1))
    m["cxT"] = f(inp["ctx"][b].T.reshape(8, 128, -1))
    cc = np.stack([inp["c"][b], inp["c_ctx"]], axis=-1)
    m["ccT"] = f(cc.reshape(8, 128, 2).transpose(1, 0, 2))
    return m


def run(inp, SEQ, LC, DEPTH, ncores, dbg=None, trace=False):
    nc = build(SEQ, LC, DEPTH, dbg=dbg)
    shared = _prep_shared(inp, DEPTH, SEQ)
    in_maps = []
    for b in range(ncores):
        m = dict(shared)
        m.update(_prep_core(inp, b))
        in_maps.append(m)
    res = run_bass_kernel_spmd(nc, in_maps, core_ids=list(range(ncores)), trace=trace)
    outs = [r["outT"].reshape(1024, SEQ).T for r in res.results]
    return np.stack(outs, axis=0), res


def kernel(**inputs):
    inp = {k: np.asarray(v) for k, v in inputs.items()}
    out, _ = run(inp, 4096, 256, 4, 8)
    return np.ascontiguousarray(out.astype(np.float32))
```

```python
import numpy as np
from contextlib import ExitStack
import concourse.bass as bass
import concourse.mybir as mybir
from concourse.bass_utils import run_bass_kernel_spmd

F32 = mybir.dt.float32
BF16 = mybir.dt.bfloat16
AF = mybir.ActivationFunctionType
ALU = mybir.AluOpType
AX = mybir.AxisListType

D = 1024
DFF = 2816
NFF = DFF // 128
DIN = 2720
ALPHA = 8.0 ** 0.25
LN_EPS = 1e-6
USE_WSCRATCH = False
GN_EPS = 64e-5
C_ID = 0
C_TRI = 128
C_M01 = 384
C_GBD = 640
C_GHM = 896
C_RBD = 900
C_RHM = 1412
C_BO = 1414
C_SWM = 1542
C_NI = 1798
C_END = 1862


class Prog:
    def __init__(self, nc, es):
        self.nc = nc
        self.es = es
        self.eng = {"pe": nc.tensor, "act": nc.scalar, "dve": nc.vector, "pool": nc.gpsimd, "sp": nc.sync}
        self.sem = {e: es.enter_context(nc.semaphore("s_" + e)) for e in ("pe", "act", "dve", "pool")}
        self.cnt = {e: 0 for e in self.sem}
        self.known = {e: {} for e in self.eng}
        self.lastw = {}
        self.rd = {}
        self.NDS = 12
        self.dsem = {q: [es.enter_context(nc.semaphore("d_%s%d" % (q, i))) for i in range(self.NDS)]
                     for q in ("sp", "pool", "act")}
        self.dcnt = {q: 0 for q in self.dsem}
        self.semobj = {}
        for e in self.sem:
            self.semobj[e] = self.sem[e]
        for q in self.dsem:
            for i, s in enumerate(self.dsem[q]):
                self.semobj[(q, i)] = s
        self.ntiles = 0
        self.ninst = 0

    def sb(self, shape, dt=F32, name=None):
        self.ntiles += 1
        return self.es.enter_context(self.nc.sbuf_tensor("%s_%d" % (name or "t", self.ntiles), list(shape), dt))

    def ps(self, shape, dt=F32, name=None):
        self.ntiles += 1
        return self.es.enter_context(self.nc.psum_tensor("%s_%d" % (name or "p", self.ntiles), list(shape), dt))

    def _wait(self, e, ev):
        s, v = ev
        if self.known[e].get(s, 0) >= v:
            return
        self.eng[e].wait_ge(self.semobj[s], v)
        self.known[e][s] = v
        self.ninst += 1

    def _deps(self, e, reads, writes):
        for k in reads:
            ev = self.lastw.get(k)
            if ev is not None and not (e == "pe" and ev[0] == "pe"):
                self._wait(e, ev)
        for k in writes:
            ev = self.lastw.get(k)
            if ev is not None and not (e == "pe" and ev[0] == "pe"):
                self._wait(e, ev)
            for ev in self.rd.get(k, {}).values():
                if ev[0] == e:
                    continue
                self._wait(e, ev)

    def _record(self, ev, reads, writes):
        for k in reads:
            self.rd.setdefault(k, {})[ev[0]] = ev
        for k in writes:
            self.lastw[k] = ev
            self.rd[k] = {}

    def op(self, e, fn, *args, reads=(), writes=(), inc=True, **kw):
        self._deps(e, reads, writes)
        inst = getattr(self.eng[e], fn)(*args, **kw)
        self.ninst += 1
        ev = (e, self.cnt[e] + 1)
        if inc:
            inst.then_inc(self.sem[e], 1)
            self.cnt[e] += 1
        self._record(ev, reads, writes)
        return inst

    def dma(self, q, out, in_, reads=(), writes=(), **kw):
        self._deps(q, reads, writes)
        j = self.dcnt[q]
        self.dcnt[q] += 1
        slot = j % self.NDS
        s = (q, slot)
        need = 16 * (j // self.NDS)
        if need > 0:
            self._wait(q, (s, need))
        inst = self.eng[q].dma_start(out=out, in_=in_, **kw)
        inst.then_inc(self.dsem[q][slot], 16)
        self.ninst += 1
        ev = (s, need + 16)
        self._record(ev, reads, writes)
        return ev

    def barrier(self):
        evs = [(e, self.cnt[e]) for e in self.sem if self.cnt[e] > 0]
        for q in self.dsem:
            j = self.dcnt[q]
            for slot in range(self.NDS):
                n = (j - slot + self.NDS - 1) // self.NDS if j > slot else 0
                if n > 0:
                    evs.append(((q, slot), 16 * n))
        for e in self.eng:
            for ev in evs:
                self._wait(e, ev)
        self.lastw = {}
        self.rd = {}

    def finish(self, e, evs):
        for ev in evs:
            self._wait(e, ev)


def _ffn_pieces():
    out = []
    f = 0
    while f < NFF:
        w = min(4, NFF - f)
        out.append((f, w))
        f += w
    return out


def build(SEQ, LC, DEPTH, dbg=None):
    nc = bass.Bass("TRN2", target_bir_lowering=False)
    es0 = ExitStack()
    P = Prog(nc, es0)
    dram = lambda name, shape, dt=F32, kind="ExternalInput": nc.dram_tensor(name, list(shape), dt, kind=kind).ap()
    L = DEPTH
    T = LC + SEQ
    NTL = T // 128
    NCH = T // 64
    xT = dram("xT", [8, 128, SEQ])
    cxT = dram("cxT", [8, 128, LC])
    ccT = dram("ccT", [128, 8, 2])
    w_ada = dram("w_ada", [L, D, 9 * D])
    b_adaT = dram("b_adaT", [128, L, 72])
    lnT = dram("lnT", [128, L, 3, 2, 8])
    ffn_w = {}
    for nm in ("ffn1_wg", "ffn1_wu", "ffn2_wg", "ffn2_wu"):
        ffn_w[nm] = dram(nm, [L, D, DFF])
    for nm in ("ffn1_wd", "ffn2_wd"):
        ffn_w[nm] = dram(nm, [L, DFF, D])
    w_in = dram("w_in", [L, D, DIN])
    w_out = dram("w_out", [L, D, D])
    consts = dram("consts", [128, C_END])
    ropeM = dram("ropeM", [SEQ, 2, 32])
    gla_up = dram("gla_up", [L, 2, 17, 128])
    gla_g = dram("gla_g", [L, 256])
    sinkB = dram("sinkB", [128, L, 8])
    rw_vec = dram("rw_vec", [128, L, 2, 7])
    rw_mu = dram("rw_mu", [128, L, 11])
    rw_wup = dram("rw_wup", [L, 2, 64, 256])
    rw_aup = dram("rw_aup", [L, 2, 64, 256])
    rw_gup = dram("rw_gup", [L, 128, 256])
    rw_gn = dram("rw_gn", [L, 2, 256])
    outT = dram("outT", [8, 128, SEQ], kind="ExternalOutput")
    HX = dram("HX", [8, 128, SEQ], kind="Internal")
    HC = dram("HC", [8, 128, LC], kind="Internal")
    PF = dram("PF", [DIN, T], kind="Internal")
    PM = dram("PM", [T, DIN], kind="Internal")
    MIX = dram("MIX", [T, D], kind="Internal")
    OGD = dram("OGD", [2, T, 256], kind="Internal")
    RF = dram("RF", [21, 128, T], kind="Internal")
    YD = dram("YD", [2, T, 256], kind="Internal")
    NPC = len(_ffn_pieces())
    WGU = dram("WGU", [1, 2, 2, NPC, 128, 8, 512], BF16, kind="Internal") if USE_WSCRATCH else None
    WDS = dram("WDS", [1, 2, 8, 128, NFF, 128], BF16, kind="Internal") if USE_WSCRATCH else None

    TG = 512
    groups = [("c", 0, LC)] + [("x", t0, min(TG, SEQ - t0)) for t0 in range(0, SEQ, TG)]

    def toff(st, t0):
        return t0 if st == "c" else LC + t0

    cst = P.sb([128, C_END], F32, "cst")
    P.dma("sp", cst[:], consts, writes=["cst"])
    ident = cst[:, C_ID:C_ID + 128]
    onesm = P.sb([128, 128], F32, "onesm")
    P.op("dve", "memset", onesm[:], 1.0 / D, writes=["onesm"])
    cc = P.sb([128, 8, 2], F32, "cc")
    P.dma("sp", cc[:], ccT, writes=["cc"])
    sil = P.sb([128, 8, 2], F32, "sil")
    P.op("act", "activation", sil[:], cc[:], AF.Silu, reads=["cc"], writes=["sil"])
    bada = P.sb([128, L, 72], F32, "bada")
    P.dma("sp", bada[:], b_adaT, writes=["bada"])
    lnp = P.sb([128, L, 3, 2, 8], F32, "lnp")
    P.dma("sp", lnp[:], lnT, writes=["lnp"])
    mT = P.sb([128, L, 72, 2], F32, "mT")
    kc = P.sb([128, 8], F32, "kc")
    for i, val in enumerate([LN_EPS / (ALPHA * ALPHA), 1.0, -0.5, LN_EPS, GN_EPS, 0.0, 1e-24]):
        P.op("dve", "memset", kc[:, i:i + 1], val, writes=["kc"])
    epsb = kc[:, 0:1]
    scl = P.sb([128, 4, 8], F32, "scl")

    pG = [P.ps([128, 512], F32, "pG%d" % i) for i in range(2)]
    pU = [P.ps([128, 512], F32, "pU%d" % i) for i in range(2)]
    pD = [P.ps([128, 512], F32, "pD%d" % i) for i in range(2)]
    pS = [P.ps([128, 512], F32, "pS%d" % i) for i in range(2)]

    es = ExitStack()
    P.es = es
    wa = [P.sb([128, 8, 512], F32, "wa%d" % i) for i in range(2)]
    nblk = 0
    for l in range(L):
        pm = pS[l % 2]
        for cb in range(18):
            wt = wa[nblk % 2]
            wk = "wa%d" % (nblk % 2)
            nblk += 1
            src = w_ada[l, :, cb * 512:(cb + 1) * 512].rearrange("(k p) j -> p k j", p=128)
            P.dma("sp", wt[:], src, writes=[wk])
            for jj in range(4):
                j = cb * 4 + jj
                for k in range(8):
                    P.op("pe", "matmul", pm[:, 2 * j:2 * j + 2], wt[:, k, jj * 128:(jj + 1) * 128], sil[:, k, :],
                         start=(k == 0), stop=(k == 7), reads=[wk, "sil"], writes=["pS%d" % (l % 2)],
                         inc=(k == 7 and jj == 3))
        P.op("dve", "tensor_tensor", mT[:, l, :, :], pm[:, 0:144].rearrange("p (j s) -> p j s", s=2),
             bada[:, l, :].unsqueeze(2).to_broadcast([128, 72, 2]), ALU.add,
             reads=["pS%d" % (l % 2), "bada"], writes=["mT"])
    for gi, (st, t0, n) in enumerate(groups):
        b = wa[gi % 2]
        bk = "wa%d" % (gi % 2)
        src = (cxT if st == "c" else xT)[:, :, t0:t0 + n].rearrange("c p t -> p c t")
        P.dma("sp", b[:, :, :n], src, writes=[bk])
        P.dma("sp", (HC if st == "c" else HX)[:, :, t0:t0 + n].rearrange("c p t -> p c t"), b[:, :, :n],
              reads=[bk], writes=[("H", st, t0)])
    P.barrier()
    es.close()

    def convert_weights(l):
        es = ExitStack()
        P.es = es
        cvt = [P.sb([128, 8, 512], BF16, "cvt%d" % i) for i in range(4)]
        cvd = [P.sb([128, NFF, 128], BF16, "cvd%d" % i) for i in range(4)]
        ncv = [0, 0]
        for fi, pre in enumerate(("ffn1", "ffn2")):
            for gu, nm in enumerate(("_wg", "_wu")):
                wap = ffn_w[pre + nm][l]
                for pi_, (f0, fw) in enumerate(_ffn_pieces()):
                    bi = ncv[0] % 4
                    ncv[0] += 1
                    P.dma("pool", cvt[bi][:, :, :fw * 128],
                          wap[:, f0 * 128:(f0 + fw) * 128].rearrange("(k p) j -> p k j", p=128), writes=[("cvt", bi)])
                    P.dma("sp", WGU[0, fi, gu, pi_][:, :, :fw * 128], cvt[bi][:, :, :fw * 128], reads=[("cvt", bi)],
                          writes=[("WGU", fi)])
            wap = ffn_w[pre + "_wd"][l]
            for c in range(8):
                bi = ncv[1] % 4
                ncv[1] += 1
                P.dma("pool", cvd[bi][:], wap[:, c * 128:(c + 1) * 128].rearrange("(f p) j -> p f j", p=128),
                      writes=[("cvd", bi)])
                P.dma("sp", WDS[0, fi, c], cvd[bi][:], reads=[("cvd", bi)], writes=[("WDS", fi)])
        P.barrier()
        es.close()

    def hsrc(st, t0, n):
        return (HC if st == "c" else HX)[:, :, t0:t0 + n].rearrange("c p t -> p c t")

    cnts = {"g": 0, "wgu": 0, "wd": 0, "ps": 0}

    def mod_scalars(l, sub, s):
        j0 = 3 * sub * 8
        coef = (0.5 if sub != 1 else 1.0) / ALPHA
        P.op("dve", "tensor_scalar", scl[:, 0, :], mT[:, l, j0 + 8:j0 + 16, s], 1.0, None, ALU.add,
             reads=["mT"], writes=["scl0"])
        P.op("dve", "tensor_copy", scl[:, 1, :], mT[:, l, j0:j0 + 8, s], reads=["mT"], writes=["scl1"])
        P.op("dve", "tensor_scalar", scl[:, 2, :], mT[:, l, j0 + 16:j0 + 24, s], coef, None, ALU.mult,
             reads=["mT"], writes=["scl2"])

    def alloc_ln_tiles():
        t = {}
        t["vv"] = P.sb([128, 8, TG], F32, "vv")
        t["yo"] = P.sb([128, 8, TG], F32, "yo")
        t["vsq"] = [P.sb([128, TG], F32, "vsq%d" % i) for i in range(2)]
        t["mean_sb"] = P.sb([128, TG], F32, "mean_sb")
        t["msq"] = P.sb([128, TG], F32, "msq")
        t["var"] = P.sb([128, TG], F32, "var")
        t["rstd"] = P.sb([128, TG], F32, "rstd")
        t["tt"] = [P.sb([128, TG], F32, "tt%d" % i) for i in range(2)]
        return t

    def layer_norm_store(t, st, t0, n, l, sub):
        vv, yo, vsq, mean_sb, msq, var, rstd, tt = (t[k] for k in ("vv", "yo", "vsq", "mean_sb", "msq", "var", "rstd", "tt"))
        pmean, pev2 = pS[0], pS[1]
        for c in range(8):
            q = vsq[c % 2]
            qk = "vsq%d" % (c % 2)
            P.op("act", "activation", q[:, :n], vv[:, c, :n], AF.Square, reads=[("vv", c)], writes=[qk])
            P.op("pe", "matmul", pmean[:, :n], onesm[:], vv[:, c, :n], start=(c == 0), stop=(c == 7),
                 reads=["onesm", ("vv", c)], writes=["pS0"], inc=(c == 7))
            P.op("pe", "matmul", pev2[:, :n], onesm[:], q[:, :n], start=(c == 0), stop=(c == 7),
                 reads=["onesm", qk], writes=["pS1"], inc=True)
        P.op("act", "activation", mean_sb[:, :n], pmean[:, :n], AF.Copy, reads=["pS0"], writes=["mean_sb"])
        P.op("dve", "tensor_tensor", msq[:, :n], mean_sb[:, :n], mean_sb[:, :n], ALU.mult,
             reads=["mean_sb"], writes=["msq"])
        P.op("dve", "tensor_tensor", var[:, :n], pev2[:, :n], msq[:, :n], ALU.subtract,
             reads=["pS1", "msq"], writes=["var"])
        P.op("act", "activation", var[:, :n], var[:, :n], AF.Sqrt, bias=epsb, reads=["var", "kc"], writes=["var"])
        P.op("dve", "reciprocal", rstd[:, :n], var[:, :n], reads=["var"], writes=["rstd"])
        for c in range(8):
            tq = tt[c % 2]
            tk = "tt%d" % (c % 2)
            P.op("dve", "tensor_tensor", tq[:, :n], vv[:, c, :n], mean_sb[:, :n], ALU.subtract,
                 reads=[("vv", c), "mean_sb"], writes=[tk])
            P.op("dve", "tensor_tensor", tq[:, :n], tq[:, :n], rstd[:, :n], ALU.mult,
                 reads=[tk, "rstd"], writes=[tk])
            P.op("act", "activation", yo[:, c, :n], tq[:, :n], AF.Identity,
                 scale=lnp[:, l, sub, 0, c:c + 1], bias=lnp[:, l, sub, 1, c:c + 1],
                 reads=[tk, "lnp"], writes=[("yo", c)])
        return P.dma("pool" if USE_WSCRATCH else "sp", hsrc(st, t0, n), yo[:, :, :n], reads=[("yo", c) for c in range(8)],
                     writes=[("H", st, t0)])

    def ffn_sublayer(l, sub, wg_ap, wu_ap, wd_ap, do_ctx=True):
        fi = 0 if sub == 0 else 1
        es = ExitStack()
        P.es = es
        hT = [P.sb([128, 8, TG], F32, "hT%d" % i) for i in range(2)]
        uT = P.sb([128, 8, TG], BF16, "uT")
        hff = P.sb([128, NFF, TG], BF16, "hff")
        lt = alloc_ln_tiles()
        vv = lt["vv"]
        sg = [P.sb([128, TG], F32, "sg%d" % i) for i in range(2)]
        wgu = [P.sb([128, 2, 8, 512], BF16, "wgu%d" % i) for i in range(2)]
        wd = [P.sb([128, NFF, 128], BF16, "wd%d" % i) for i in range(3)]
        cur_s = None
        for (st, t0, n) in groups:
            if st == "c" and not do_ctx:
                continue
            s = 1 if st == "c" else 0
            if s != cur_s:
                mod_scalars(l, sub, s)
                cur_s = s
            gi = cnts["g"]
            cnts["g"] += 1
            h = hT[gi % 2]
            hk = "hT%d" % (gi % 2)
            P.dma("pool" if USE_WSCRATCH else "sp", h[:, :, :n], hsrc(st, t0, n), reads=[("H", st, t0)], writes=[hk])
            for c in range(8):
                P.op("act", "activation", uT[:, c, :n], h[:, c, :n], AF.Identity,
                     scale=scl[:, 0, c:c + 1], bias=scl[:, 1, c:c + 1],
                     reads=[hk, "scl0", "scl1"], writes=[("uT", c)])
            for pi_, (f0, fw) in enumerate(_ffn_pieces()):
                wi = cnts["wgu"] % 2
                cnts["wgu"] += 1
                wt = wgu[wi]
                if USE_WSCRATCH:
                    P.dma("sp", wt[:, 0, :, :fw * 128], WGU[0, fi, 0, pi_][:, :, :fw * 128], writes=[("wgu", wi, 0)])
                    P.dma("sp", wt[:, 1, :, :fw * 128], WGU[0, fi, 1, pi_][:, :, :fw * 128], writes=[("wgu", wi, 1)])
                else:
                    P.dma("pool", wt[:, 0, :, :fw * 128],
                          wg_ap[:, f0 * 128:(f0 + fw) * 128].rearrange("(k p) j -> p k j", p=128), writes=[("wgu", wi, 0)])
                    P.dma("pool", wt[:, 1, :, :fw * 128],
                          wu_ap[:, f0 * 128:(f0 + fw) * 128].rearrange("(k p) j -> p k j", p=128), writes=[("wgu", wi, 1)])
                for ff in range(fw):
                    f = f0 + ff
                    pi = cnts["ps"] % 2
                    cnts["ps"] += 1
                    for k in range(8):
                        P.op("pe", "matmul", pG[pi][:, :n], wt[:, 0, k, ff * 128:(ff + 1) * 128], uT[:, k, :n],
                             start=(k == 0), stop=(k == 7), reads=[("wgu", wi, 0), ("uT", k)], writes=["pG%d" % pi],
                             inc=(k == 7))
                    for k in range(8):
                        P.op("pe", "matmul", pU[pi][:, :n], wt[:, 1, k, ff * 128:(ff + 1) * 128], uT[:, k, :n],
                             start=(k == 0), stop=(k == 7), reads=[("wgu", wi, 1), ("uT", k)], writes=["pU%d" % pi],
                             inc=(k == 7))
                    P.op("act", "activation", sg[pi][:, :n], pG[pi][:, :n], AF.Silu,
                         reads=["pG%d" % pi], writes=["sg%d" % pi])
                    P.op("dve", "tensor_tensor", hff[:, f, :n], sg[pi][:, :n], pU[pi][:, :n], ALU.mult,
                         reads=["sg%d" % pi, "pU%d" % pi], writes=[("hff", f)])
            for c in range(8):
                wi = cnts["wd"] % 3
                cnts["wd"] += 1
                if USE_WSCRATCH:
                    P.dma("sp", wd[wi][:], WDS[0, fi, c], writes=[("wd", wi)])
                else:
                    P.dma("pool", wd[wi][:], wd_ap[:, c * 128:(c + 1) * 128].rearrange("(f p) j -> p f j", p=128),
                          writes=[("wd", wi)])
                pi = c % 2
                for f in range(NFF):
                    P.op("pe", "matmul", pD[pi][:, :n], wd[wi][:, f, :], hff[:, f, :n],
                         start=(f == 0), stop=(f == NFF - 1), reads=[("wd", wi), ("hff", f)], writes=["pD%d" % pi],
                         inc=(f == NFF - 1))
                P.op("dve", "scalar_tensor_tensor", vv[:, c, :n], pD[pi][:, :n], scl[:, 2, c:c + 1], h[:, c, :n],
                     ALU.mult, ALU.add, reads=["pD%d" % pi, "scl2", hk], writes=[("vv", c)])
            layer_norm_store(lt, st, t0, n, l, sub)
        P.barrier()
        es.close()

    def mm(out, lhsT, rhs, reads, writes, start=True, stop=True, inc=True):
        P.op("pe", "matmul", out, lhsT, rhs, start=start, stop=stop, reads=reads, writes=writes, inc=inc)

    def act(out, in_, func, reads, writes, **kw):
        P.op("act", "activation", out, in_, func, reads=reads, writes=writes, **kw)

    def tt_(out, a, b, op, reads, writes, e="dve"):
        P.op(e, "tensor_tensor", out, a, b, op, reads=reads, writes=writes)

    def ts_(out, a, s1, s2, op0, op1, reads, writes, e="dve"):
        if op1 is None:
            P.op(e, "tensor_scalar", out, a, s1, None, op0, reads=reads, writes=writes)
        else:
            P.op(e, "tensor_scalar", out, a, s1, s2, op0, op1, reads=reads, writes=writes)

    def stt(out, a, s, b, op0, op1, reads, writes):
        P.op("dve", "scalar_tensor_tensor", out, a, s, b, op0, op1, reads=reads, writes=writes)

    def tr(out, in_, reads, writes, np_=128, inc=True):
        P.op("pe", "transpose", out, in_, ident[:np_, :np_], reads=list(reads) + ["cst"], writes=writes, inc=inc)

    def in_pieces():
        out = []
        c = 0
        while c < DIN:
            w = min(512, DIN - c)
            out.append((c, w))
            c += w
        return out

    def inproj(l, do_ctx=True):
        es = ExitStack()
        P.es = es
        hT = [P.sb([128, 8, TG], F32, "hT%d" % i) for i in range(2)]
        uT = P.sb([128, 8, TG], BF16, "uT")
        wt_ = [P.sb([128, 8, 512], BF16, "wi%d" % i) for i in range(2)]
        stg = [P.sb([128, 512], F32, "stg%d" % i) for i in range(4)]
        nst = [0]
        cur_s = None
        for (st, t0, n) in groups:
            s = 1 if st == "c" else 0
            off = toff(st, t0)
            if s != cur_s:
                mod_scalars(l, 1, s)
                cur_s = s
            gi = cnts["g"]
            cnts["g"] += 1
            h = hT[gi % 2]
            hk = "hT%d" % (gi % 2)
            P.dma("sp", h[:, :, :n], hsrc(st, t0, n), reads=[("H", st, t0)], writes=[hk])
            for c in range(8):
                act(uT[:, c, :n], h[:, c, :n], AF.Identity, [hk, "scl0", "scl1"], [("uT", c)],
                    scale=scl[:, 0, c:c + 1], bias=scl[:, 1, c:c + 1])
            for (c0, w) in in_pieces():
                wi = cnts["wgu"] % 2
                cnts["wgu"] += 1
                wt = wt_[wi]
                wk = ("wi", wi)
                P.dma("pool", wt[:, :, :w], w_in[l][:, c0:c0 + w].rearrange("(k p) j -> p k j", p=128), writes=[wk])
                j = 0
                while j < w:
                    cw = min(128, w - j)
                    pi = cnts["ps"] % 2
                    cnts["ps"] += 1
                    for k in range(8):
                        mm(pG[pi][:cw, :n], wt[:, k, j:j + cw], uT[:, k, :n], [wk, ("uT", k)], ["pG%d" % pi],
                           start=(k == 0), stop=(k == 7), inc=(k == 7))
                    si = nst[0] % 4
                    nst[0] += 1
                    act(stg[si][:cw, :n], pG[pi][:cw, :n], AF.Copy, ["pG%d" % pi], [("stg", si)])
                    P.dma("sp", PF[c0 + j:c0 + j + cw, off:off + n], stg[si][:cw, :n], reads=[("stg", si)],
                          writes=[("PF", off)])
                    j += cw
                for ts in range(n // 128):
                    pi = cnts["ps"] % 2
                    cnts["ps"] += 1
                    for k in range(8):
                        mm(pU[pi][:, :w], uT[:, k, ts * 128:(ts + 1) * 128], wt[:, k, :w], [wk, ("uT", k)],
                           ["pU%d" % pi], start=(k == 0), stop=(k == 7), inc=(k == 7))
                    si = nst[0] % 4
                    nst[0] += 1
                    P.op("dve", "tensor_copy", stg[si][:, :w], pU[pi][:, :w], reads=["pU%d" % pi], writes=[("stg", si)])
                    P.dma("sp", PM[off + ts * 128:off + (ts + 1) * 128, c0:c0 + w], stg[si][:, :w],
                          reads=[("stg", si)], writes=[("PM", off)])
        P.barrier()
        es.close()

    def swa(l, do_ctx):
        es = ExitStack()
        P.es = es
        kT_all = P.sb([64, 2, T], BF16, "kT_all")
        V_all = P.sb([128, NTL, 2, 65], BF16, "V_all")
        esink = P.sb([128, 8], F32, "esink")
        sk = P.sb([128, L, 8], F32, "sk")
        P.dma("sp", sk[:], sinkB, writes=["sk"])
        act(esink[:], sk[:, l, :], AF.Exp, ["sk"], ["esink"])
        P.op("dve", "memset", V_all[:], 1.0, writes=["V_all"])
        km = [P.sb([128, 2, 64], F32, "km%d" % i) for i in range(2)]
        vm = [P.sb([128, 2, 64], F32, "vm%d" % i) for i in range(2)]
        rp = [P.sb([128, 2, 32], F32, "rp%d" % i) for i in range(2)]
        kr = P.sb([128, 2, 64], F32, "kr")
        qm = [P.sb([128, 8, 64], F32, "qm%d" % i) for i in range(2)]
        qr = P.sb([128, 8, 64], F32, "qr")
        ta = P.sb([128, 8, 32], F32, "ta")
        tb = P.sb([128, 8, 32], F32, "tb")
        qT = P.sb([64, 8, 128], BF16, "qT")
        pT = [P.sb([128, 512], BF16, "pT%d" % i) for i in range(10)]
        den = P.sb([128, 4], F32, "den")
        mixb = [P.sb([128, 512], F32, "mixb%d" % i) for i in range(2)]

        def rope(dst, src, nh, tab, rk_src, rk_dst, rk_tab):
            sv = src.rearrange("p h (a s i) -> p h a s i", a=2, s=2)
            dv = dst.rearrange("p h (a s i) -> p h a s i", a=2, s=2)
            cos = tab[:, 0, :].rearrange("p (a i) -> p a i", a=2).unsqueeze(1).to_broadcast([128, nh, 2, 16])
            sin = tab[:, 1, :].rearrange("p (a i) -> p a i", a=2).unsqueeze(1).to_broadcast([128, nh, 2, 16])
            tav = ta[:, :nh, :].rearrange("p h (a i) -> p h a i", a=2)
            tbv = tb[:, :nh, :].rearrange("p h (a i) -> p h a i", a=2)
            tt_(tav, sv[:, :, :, 0, :], cos, ALU.mult, [rk_src, rk_tab], ["ta"])
            tt_(tbv, sv[:, :, :, 1, :], sin, ALU.mult, [rk_src, rk_tab], ["tb"])
            tt_(dv[:, :, :, 0, :], tav, tbv, ALU.subtract, ["ta", "tb"], [rk_dst])
            tt_(tav, sv[:, :, :, 0, :], sin, ALU.mult, [rk_src, rk_tab], ["ta"])
            tt_(tbv, sv[:, :, :, 1, :], cos, ALU.mult, [rk_src, rk_tab], ["tb"])
            tt_(dv[:, :, :, 1, :], tav, tbv, ALU.add, ["ta", "tb"], [rk_dst])

        for tt in range(NTL):
            i2 = tt % 2
            r0 = tt * 128
            P.dma("sp", km[i2][:], PM[r0:r0 + 128, 1312:1440].rearrange("t (h d) -> t h d", h=2),
                  reads=[("PM", 0)], writes=[("km", i2)])
            P.dma("sp", vm[i2][:], PM[r0:r0 + 128, 1440:1568].rearrange("t (h d) -> t h d", h=2),
                  reads=[("PM", 0)], writes=[("vm", i2)])
            P.op("dve", "tensor_copy", V_all[:, tt, :, 0:64], vm[i2][:], reads=[("vm", i2)], writes=["V_all"])
            if r0 >= LC:
                P.dma("sp", rp[i2][:], ropeM[r0 - LC:r0 - LC + 128], writes=[("rp", i2)])
                rope(kr[:], km[i2][:], 2, rp[i2], ("km", i2), "kr", ("rp", i2))
                ksrc, kk_ = kr, "kr"
            else:
                ksrc, kk_ = km[i2], ("km", i2)
            for hk in range(2):
                tr(pS[0][:64, hk * 128:(hk + 1) * 128], ksrc[:, hk, :], [kk_], ["pS0"], inc=(hk == 1))
            act(kT_all[:, :, r0:r0 + 128], pS[0][:64, 0:256].rearrange("p (h t) -> p h t", h=2), AF.Copy,
                ["pS0"], ["kT_all"])
        nmix = 0
        for tt in range(NTL):
            r0 = tt * 128
            isx = r0 >= LC
            if not isx and not do_ctx:
                continue
            i2 = tt % 2
            P.dma("sp", qm[i2][:], PM[r0:r0 + 128, 800:1312].rearrange("t (h d) -> t h d", h=8),
                  reads=[("PM", 0)], writes=[("qm", i2)])
            if isx:
                P.dma("sp", rp[i2][:], ropeM[r0 - LC:r0 - LC + 128], writes=[("rp", i2)])
                rope(qr[:], qm[i2][:], 8, rp[i2], ("qm", i2), "qr", ("rp", i2))
                qsrc, qk_ = qr, "qr"
            else:
                qsrc, qk_ = qm[i2], ("qm", i2)
            for half in range(2):
                for hh in range(4):
                    tr(pS[half][:64, hh * 128:(hh + 1) * 128], qsrc[:, half * 4 + hh, :], [qk_], ["pS%d" % half],
                       inc=(hh == 3))
                act(qT[:, half * 4:half * 4 + 4, :], pS[half][:64, :].rearrange("p (h t) -> p h t", h=4), AF.Copy,
                    ["pS%d" % half], ["qT"])
            keys = []
            if isx:
                nct = LC // 128
                if tt - 1 >= nct:
                    keys.append((tt - 1, 0))
                keys.append((tt, None))
                if tt + 1 < NTL:
                    keys.append((tt + 1, 1))
            keys += [(c, None) for c in range(LC // 128)]
            mb = mixb[nmix % 2]
            mbk = ("mixb", nmix % 2)
            nmix += 1
            for hk in range(2):
                for ki, (kt, mid) in enumerate(keys):
                    pi = cnts["ps"] % 2
                    cnts["ps"] += 1
                    for g in range(4):
                        mm(pG[pi][:, g * 128:(g + 1) * 128], kT_all[:, hk, kt * 128:(kt + 1) * 128], qT[:, hk * 4 + g, :],
                           ["kT_all", "qT"], ["pG%d" % pi], inc=(g == 3))
                    pt = pT[hk * 5 + ki]
                    ptk = ("pT", hk * 5 + ki)
                    act(pt[:], pG[pi][:], AF.Exp, ["pG%d" % pi], [ptk], scale=0.125)
                    if mid is not None:
                        mk = cst[:, C_SWM + mid * 128:C_SWM + (mid + 1) * 128].unsqueeze(1).to_broadcast([128, 4, 128])
                        tt_(pt[:].rearrange("p (g q) -> p g q", g=4), pt[:].rearrange("p (g q) -> p g q", g=4), mk,
                            ALU.mult, [ptk, "cst"], [ptk])
                po = pD[hk]
                for g in range(4):
                    for ki, (kt, mid) in enumerate(keys):
                        mm(po[:, g * 65:(g + 1) * 65], pT[hk * 5 + ki][:, g * 128:(g + 1) * 128], V_all[:, kt, hk, :],
                           [("pT", hk * 5 + ki), "V_all"], ["pD%d" % hk], start=(ki == 0), stop=(ki == len(keys) - 1),
                           inc=(ki == len(keys) - 1 and g == 3))
                pov = po[:, 0:260].rearrange("p (g e) -> p g e", g=4)
                tt_(den[:], pov[:, :, 64], esink[:, hk * 4:hk * 4 + 4], ALU.add, ["pD%d" % hk, "esink"], ["den"])
                P.op("dve", "reciprocal", den[:], den[:], reads=["den"], writes=["den"])
                tt_(mb[:, hk * 256:(hk + 1) * 256].rearrange("p (g e) -> p g e", g=4), pov[:, :, 0:64],
                    den[:].unsqueeze(2).to_broadcast([128, 4, 64]), ALU.mult, ["pD%d" % hk, "den"], [mbk])
            P.dma("sp", MIX[r0:r0 + 128, 256:768], mb[:], reads=[mbk], writes=[("MIX", "b", tt)])
        P.barrier()
        es.close()

    def run_interleaved(gens):
        gens = list(gens)
        while gens:
            for g in list(gens):
                try:
                    next(g)
                except StopIteration:
                    gens.remove(g)

    def chunk_order(d):
        nc_c = LC // 64
        if d == 0:
            return list(range(NCH))
        return list(range(nc_c - 1, -1, -1)) + list(range(NCH - 1, nc_c - 1, -1))

    def gla(l, do_ctx, extra=()):
        es = ExitStack()
        P.es = es
        upa = P.sb([17, 2, 128], F32, "upa")
        P.dma("sp", upa[:], gla_up[l].rearrange("d r c -> r d c"), writes=["upa"])
        gng = P.sb([128, 256], F32, "gng")
        P.dma("sp", gng[:], gla_g[l].partition_broadcast(128), writes=["gng"])
        Sx = [P.sb([128, 256], F32, "Sx%d" % d) for d in range(2)]
        for d in range(2):
            P.op("dve", "memset", Sx[d][:], 0.0, writes=[("Sx", d)])
        triI = [cst[0:64, C_TRI + d * 64:C_TRI + (d + 1) * 64] for d in range(2)]
        incl = [cst[0:64, C_M01 + d * 64:C_M01 + (d + 1) * 64] for d in range(2)]
        gbd = cst[:, C_GBD:C_GBD + 256]
        pb = {0: (pG[0], "pG0", pU[0], "pU0", pD[0], "pD0"), 1: (pG[1], "pG1", pU[1], "pU1", pD[1], "pD1")}

        def unit(d):
            pa, pak, pbb, pbk, pc_, pck = pb[d]
            tl = {}
            for nm, shp in (("qT", [128, 64]), ("kT", [128, 64]), ("zT", [17, 64]), ("kM", [64, 128]), ("vM", [64, 256])
                            ):
                tl[nm] = [P.sb(shp, F32, "g%s%d_%d" % (nm, d, i)) for i in range(2)]
            for i in range(2):
                P.op("dve", "memset", tl["zT"][i][:], 1.0, writes=[("zT", d, i)])
            sp_ = P.sb([64, 128], F32, "gsp%d" % d)
            ebT = P.sb([128, 64], F32, "gebT%d" % d)
            enbT = P.sb([128, 64], F32, "genbT%d" % d)
            enbM = P.sb([64, 128], F32, "genbM%d" % d)
            qs = P.sb([128, 64], F32, "gqs%d" % d)
            kmk = P.sb([128, 4, 64], F32, "gkmk%d" % d)
            KtM = P.sb([64, 128], F32, "gKtM%d" % d)
            att = P.sb([64, 4, 64], F32, "gatt%d" % d)
            tmpS = P.sb([128, 256], F32, "gtmpS%d" % d)
            osb = P.sb([64, 4, 64], F32, "gosb%d" % d)
            osq = P.sb([64, 4, 64], F32, "gosq%d" % d)
            ssq = P.sb([64, 4], F32, "gssq%d" % d)
            sga = P.sb([64, 256], F32, "gsga%d" % d)
            K = lambda s: (s, d)
            for ui, c in enumerate(chunk_order(d)):
                i2 = ui % 2
                r0 = c * 64
                isx = r0 >= LC
                KB = lambda s: (s, d, i2)
                P.dma("sp", tl["qT"][i2][:], PF[0:128, r0:r0 + 64], reads=[("PF", 0)], writes=[KB("qT")])
                P.dma("sp", tl["kT"][i2][:], PF[128:256, r0:r0 + 64], reads=[("PF", 0)], writes=[KB("kT")])
                P.dma("sp", tl["zT"][i2][0:16, :], PF[768 + 16 * d:784 + 16 * d, r0:r0 + 64], reads=[("PF", 0)],
                      writes=[KB("zT")])
                P.dma("sp", tl["kM"][i2][:], PM[r0:r0 + 64, 128:256], reads=[("PM", 0)], writes=[KB("kM")])
                P.dma("sp", tl["vM"][i2][:], PM[r0:r0 + 64, 256:512], reads=[("PM", 0)], writes=[KB("vM")])
                qT, kT, zT, kM, vM = (tl[n_][i2] for n_ in ("qT", "kT", "zT", "kM", "vM"))
                yield
                mm(pa[:64, 0:128], zT[:], upa[:, d, :], [KB("zT"), "upa"], [pak])
                act(sp_[:], pa[:64, 0:128], AF.Exp, [pak], [K("sp")], scale=-1.0)
                act(sp_[:], sp_[:], AF.Ln, [K("sp"), "kc"], [K("sp")], bias=kc[:64, 1:2])
                yield
                mm(pa[:, 128:192], sp_[:], triI[d], [K("sp"), "cst"], [pak], inc=False)
                mm(pa[:64, 256:384], triI[d], sp_[:], [K("sp"), "cst"], [pak])
                act(ebT[:], pa[:, 128:192], AF.Exp, [pak], [K("ebT")], scale=1.0 / 16)
                act(enbT[:], pa[:, 128:192], AF.Exp, [pak], [K("enbT")], scale=-1.0 / 16)
                act(enbM[:], pa[:64, 256:384], AF.Exp, [pak], [K("enbM")], scale=-1.0 / 16)
                yield
                stt(qs[:], qT[:], 32.0 ** -0.5, ebT[:], ALU.mult, ALU.mult, [KB("qT"), K("ebT")], [K("qs")])
                for h in range(4):
                    stt(kmk[:, h, :], kT[:], cst[:, C_GHM + h:C_GHM + h + 1], enbT[:], ALU.mult, ALU.mult,
                        [KB("kT"), K("enbT"), "cst"], [K("kmk")])
                tt_(KtM[:], kM[:], enbM[:], ALU.mult, [KB("kM"), K("enbM")], [K("KtM")])
                yield
                for h in range(4):
                    mm(pbb[:64, h * 64:(h + 1) * 64], kmk[:, h, :], qs[:], [K("kmk"), K("qs")], [pbk], inc=(h == 3))
                tt_(att[:], pbb[:64, 0:256].rearrange("p (h t) -> p h t", h=4),
                    incl[d].unsqueeze(1).to_broadcast([64, 4, 64]), ALU.mult, [pbk, "cst"], [K("att")])
                yield
                for h in range(4):
                    mm(pc_[:64, h * 64:(h + 1) * 64], att[:, h, :], vM[:, h * 64:(h + 1) * 64], [K("att"), KB("vM")], [pck],
                       start=True, stop=False, inc=False)
                    mm(pc_[:64, h * 64:(h + 1) * 64], qs[:], Sx[d][:, h * 64:(h + 1) * 64], [K("qs"), ("Sx", d)], [pck],
                       start=False, stop=True, inc=(h == 3))
                mm(pbb[:, 256:512], KtM[:], vM[:], [K("KtM"), KB("vM")], [pbk])
                yield
                tt_(tmpS[:], pbb[:, 256:512], gbd, ALU.mult, [pbk, "cst"], [K("tmpS")])
                tt_(tmpS[:], tmpS[:], Sx[d][:], ALU.add, [K("tmpS"), ("Sx", d)], [K("tmpS")])
                last = 63 if d == 0 else 0
                ts_(Sx[d][:], tmpS[:], ebT[:, last:last + 1], None, ALU.mult, None, [K("tmpS"), K("ebT")], [("Sx", d)])
                P.op("dve", "tensor_copy", osb[:].rearrange("p h e -> p (h e)"), pc_[:64, 0:256], reads=[pck],
                     writes=[K("osb")])
                P.dma("sp", OGD[d, r0:r0 + 64, :], osb[:].rearrange("p h e -> p (h e)"), reads=[K("osb")],
                      writes=[("OGD", d, c)])
                yield

        run_interleaved([unit(0), unit(1)] + list(extra))
        of_ = [P.sb([128, 4, 64], F32, "gof%d" % i) for i in range(2)]
        ob_ = [P.sb([128, 4, 64], F32, "gob%d" % i) for i in range(2)]
        ga_ = [P.sb([128, 256], F32, "gga%d" % i) for i in range(2)]
        fsq = P.sb([128, 4, 64], F32, "gfsq")
        fss = P.sb([128, 4], F32, "gfss")
        ni = 0
        for tt in range(NTL):
            r0 = tt * 128
            if r0 < LC and not do_ctx:
                continue
            i2 = ni % 2
            ni += 1
            P.dma("sp", of_[i2][:].rearrange("p h e -> p (h e)"), OGD[0, r0:r0 + 128, :],
                  reads=[("OGD", 0, 2 * tt), ("OGD", 0, 2 * tt + 1)], writes=[("gof", i2)])
            P.dma("sp", ob_[i2][:].rearrange("p h e -> p (h e)"), OGD[1, r0:r0 + 128, :],
                  reads=[("OGD", 1, 2 * tt), ("OGD", 1, 2 * tt + 1)], writes=[("gob", i2)])
            P.dma("sp", ga_[i2][:], PM[r0:r0 + 128, 512:768], reads=[("PM", 0)], writes=[("gga", i2)])
            o = of_[i2]
            ok = ("gof", i2)
            ov = o[:].rearrange("p h e -> p (h e)")
            tt_(o[:], o[:], ob_[i2][:], ALU.add, [ok, ("gob", i2)], [ok])
            tt_(fsq[:], o[:], o[:], ALU.mult, [ok], ["gfsq"])
            P.op("dve", "tensor_reduce", fss[:], fsq[:], AX.X, ALU.add, reads=["gfsq"], writes=["gfss"])
            act(fss[:], fss[:], AF.Sqrt, ["gfss", "kc"], ["gfss"], scale=1.0 / 64, bias=kc[:, 3:4])
            P.op("dve", "reciprocal", fss[:], fss[:], reads=["gfss"], writes=["gfss"])
            tt_(o[:], o[:], fss[:].unsqueeze(2).to_broadcast([128, 4, 64]), ALU.mult, [ok, "gfss"], [ok])
            tt_(ov, ov, gng[:], ALU.mult, [ok, "gng"], [ok])
            act(ga_[i2][:], ga_[i2][:], AF.Silu, [("gga", i2)], [("gga", i2)])
            tt_(ov, ov, ga_[i2][:], ALU.mult, [ok, ("gga", i2)], [ok])
            P.dma("sp", MIX[r0:r0 + 128, 0:256], ov, reads=[ok], writes=[("MIX", "a", tt)])
        P.barrier()
        es.close()

    def rwkv_prep(l):
        es = ExitStack()
        P.es = es
        run_interleaved([rwkv_prep_gen(l)])
        P.barrier()
        es.close()

    def rwkv_prep_gen(l):
        rv = P.sb([128, 2, 7], F32, "rv")
        P.dma("sp", rv[:], rw_vec[:, l], writes=["rv"])
        nw0 = P.sb([128, 2, 2], F32, "nw0")
        ts_(nw0[:], rv[:, :, 3:5], -1.0, None, ALU.mult, None, ["rv"], ["nw0"])
        mu = P.sb([128, 11], F32, "mu")
        P.dma("sp", mu[:], rw_mu[:, l], writes=["mu"])
        wup = P.sb([64, 2, 256], F32, "wup")
        P.dma("sp", wup[:], rw_wup[l].rearrange("d r c -> r d c"), writes=["wup"])
        aup = P.sb([64, 2, 256], F32, "aup")
        P.dma("sp", aup[:], rw_aup[l].rearrange("d r c -> r d c"), writes=["aup"])
        bo = cst[:, C_BO:C_BO + 128]
        NB = 16
        bufs = [P.sb([128, TG + 2], F32, "rb%d" % i) for i in range(NB)]
        fbufs = [P.sb([128, TG + 2], F32, "rf%d" % i) for i in range(13)]
        nb = [0]

        def newb():
            i = nb[0] % NB
            nb[0] += 1
            return bufs[i], ("rb", i)

        R0 = 1568
        srcs = [(R0 + 128 * j, 128) for j in range(6)] + [(R0 + 768, 64), (R0 + 832, 64), (R0 + 896, 64), (R0 + 960, 64),
                                                          (R0 + 1024, 128)]
        for (st, t0, n) in groups:
            off = toff(st, t0)
            seq_lo = 0 if st == "c" else LC
            seq_hi = LC if st == "c" else T
            lo = max(off - 1, seq_lo)
            hi = min(off + n + 1, seq_hi)
            f = []
            for j, (r0, nr) in enumerate(srcs):
                ld, ldk = newb()
                P.op("dve", "memset", ld[:, 0:n + 2], 0.0, writes=[ldk])
                P.dma("sp", ld[:nr, lo - off + 1:hi - off + 1], PF[r0:r0 + nr, lo:hi], reads=[("PF", 0)], writes=[ldk])
                sh, shk = newb()
                tt_(sh[:nr, :n], ld[:nr, 0:n], ld[:nr, 2:n + 2], ALU.add, [ldk], [shk])
                stt(sh[:nr, :n], sh[:nr, :n], 0.5, ld[:nr, 1:n + 1], ALU.mult, ALU.subtract, [shk, ldk], [shk])
                fo, fok = fbufs[j], ("rf", j)
                stt(fo[:nr, :n], sh[:nr, :n], mu[:nr, j:j + 1], ld[:nr, 1:n + 1], ALU.mult, ALU.add, [shk, ldk, "mu"], [fok])
                f.append((fo, fok))
                yield
            rr, kk_, vv_ = f[0:2], f[2:4], f[4:6]
            zw, za, zg = f[6:8], f[8:10], f[10]
            for pc in range(2):
                P.dma("sp", RF[0 + pc, :, off:off + n], rr[pc][0][:, :n], reads=[rr[pc][1]], writes=[("RF", off)])
                P.dma("sp", RF[2 + pc, :, off:off + n], vv_[pc][0][:, :n], reads=[vv_[pc][1]], writes=[("RF", off)])
            kkn = []
            for pc in range(2):
                k_, kk2 = kk_[pc]
                sq, sqk = newb()
                act(sq[:, :n], k_[:, :n], AF.Square, [kk2, "rv"], [sqk], scale=rv[:, pc, 0:1])
                mm(pS[0][:, :n], bo, sq[:, :n], ["cst", sqk], ["pS0"])
                act(sq[:, :n], pS[0][:, :n], AF.Sqrt, ["pS0"], [sqk])
                ts_(sq[:, :n], sq[:, :n], 1e-12, None, ALU.max, None, [sqk], [sqk])
                P.op("dve", "reciprocal", sq[:, :n], sq[:, :n], reads=[sqk], writes=[sqk])
                kn, knk = fbufs[11 + pc], ("rf", 11 + pc)
                stt(kn[:, :n], k_[:, :n], rv[:, pc, 0:1], sq[:, :n], ALU.mult, ALU.mult, [kk2, "rv", sqk], [knk])
                P.dma("sp", RF[4 + pc, :, off:off + n], kn[:, :n], reads=[knk], writes=[("RF", off)])
                kkn.append((kn, knk))
                yield
            ksum = [None, None]
            for d in range(2):
                th, thk = newb()
                act(th[:64, :n], zw[d][0][:64, :n], AF.Tanh, [zw[d][1]], [thk])
                for pc in range(2):
                    mm(pS[1][:, :n], wup[:, d, pc * 128:(pc + 1) * 128], th[:64, :n], ["wup", thk], ["pS1"])
                    ew, ewk = newb()
                    act(ew[:, :n], pS[1][:, :n], AF.Exp, ["pS1", "nw0"], [ewk], scale=-1.0, bias=nw0[:, pc, d:d + 1])
                    act(ew[:, :n], ew[:, :n], AF.Ln, [ewk, "kc"], [ewk], bias=kc[:, 1:2])
                    act(ew[:, :n], ew[:, :n], AF.Exp, [ewk, "kc"], [ewk], scale=-1.0, bias=kc[:, 2:3])
                    P.dma("sp", RF[10 + 6 * d + pc, :, off:off + n], ew[:, :n], reads=[ewk], writes=[("RF", off)])
                    mm(pS[0][:, :n], aup[:, d, pc * 128:(pc + 1) * 128], za[d][0][:64, :n], ["aup", za[d][1]], ["pS0"])
                    a_, ak = newb()
                    act(a_[:, :n], pS[0][:, :n], AF.Sigmoid, ["pS0", "rv"], [ak], bias=rv[:, pc, 5 + d:6 + d])
                    bd, bdk = newb()
                    tt_(bd[:, :n], a_[:, :n], kkn[pc][0][:, :n], ALU.mult, [ak, kkn[pc][1]], [bdk])
                    P.dma("sp", RF[8 + 6 * d + pc, :, off:off + n], bd[:, :n], reads=[bdk], writes=[("RF", off)])
                    ts_(a_[:, :n], a_[:, :n], 1.0, rv[:, pc, 1:2], ALU.subtract, ALU.mult, [ak, "rv"], [ak])
                    kd, kdk = newb()
                    stt(kd[:, :n], a_[:, :n], 1.0, kk_[pc][0][:, :n], ALU.add, ALU.mult, [ak, kk_[pc][1]], [kdk])
                    P.dma("sp", RF[6 + 6 * d + pc, :, off:off + n], kd[:, :n], reads=[kdk], writes=[("RF", off)])
                    if d == 0:
                        ksum[pc] = (kd, kdk)
                    else:
                        kf, kfk = ksum[pc]
                        tt_(kd[:, :n], kd[:, :n], kf[:, :n], ALU.add, [kdk, kfk], [kdk])
                        ts_(kd[:, :n], kd[:, :n], 0.5, rv[:, pc, 2:3], ALU.mult, ALU.mult, [kdk, "rv"], [kdk])
                        tt_(kd[:, :n], kd[:, :n], rr[pc][0][:, :n], ALU.mult, [kdk, rr[pc][1]], [kdk])
                        P.dma("sp", RF[18 + pc, :, off:off + n], kd[:, :n], reads=[kdk], writes=[("RF", off)])
                    yield
            sz, szk = newb()
            act(sz[:, :n], zg[0][:, :n], AF.Sigmoid, [zg[1]], [szk])
            P.dma("sp", RF[20, :, off:off + n], sz[:, :n], reads=[szk], writes=[("RF", off)])
            yield

    def rwkv_scan(l):
        import os as _os
        _rwcut = int(_os.environ.get('RW_CUT', '99'))
        es = ExitStack()
        P.es = es
        tri = [cst[0:64, C_TRI + i * 64:C_TRI + (i + 1) * 64] for i in range(4)]
        m01 = [cst[0:64, C_M01 + i * 64:C_M01 + (i + 1) * 64] for i in range(4)]
        id64 = cst[0:64, C_ID:C_ID + 64]
        Hx = [[P.sb([128, 256], F32, "Hx%d%d" % (d, pc)) for pc in range(2)] for d in range(2)]
        for d in range(2):
            for pc in range(2):
                P.op("dve", "memset", Hx[d][pc][:], 0.0, writes=[("Hx", d, pc)])
        bc4 = lambda m: m.unsqueeze(1).to_broadcast([64, 4, 64])
        banks = {"A": (pG[0], "pG0"), "B": (pG[1], "pG1"), "C": (pU[0], "pU0"), "D": (pU[1], "pU1"),
                 "E": (pD[0], "pD0"), "F": (pD[1], "pD1"), "G": (pS[0], "pS0"), "H": (pS[1], "pS1")}

        def unit(d):
            K = lambda s: (s, d)
            cm = [P.sb([128, 6, 64], F32, "cm%d_%d" % (d, i)) for i in range(2)]
            dm = [P.sb([128, 6, 64], F32, "dm%d_%d" % (d, i)) for i in range(2)]
            ewM = P.sb([64, 256], F32, "ewM%d" % d)
            vM = P.sb([64, 256], F32, "rvM%d" % d)
            Pd = P.sb([128, 2, 3, 64], F32, "Pd%d" % d)
            fT = P.sb([128, 2, 4, 64], F32, "fT%d" % d)
            mT_ = P.sb([128, 2, 2, 2, 64], F32, "mK%d" % d)
            KBM = P.sb([64, 2, 256], F32, "KBM%d" % d)
            sc = P.sb([64, 5, 4, 64], F32, "sc%d" % d)
            Tt = P.sb([64, 4, 64], F32, "Tt%d" % d)
            TtT = P.sb([64, 4, 64], F32, "TtT%d" % d)
            Nn = [P.sb([64, 2, 4, 64], F32, "Nn%d_%d" % (d, i)) for i in range(2)]
            Xs = P.sb([64, 4, 64], F32, "Xs%d" % d)
            Us = P.sb([64, 4, 64], F32, "Us%d" % d)
            Ys = P.sb([64, 256], F32, "Ys%d" % d)
            tmpH = P.sb([128, 256], F32, "tmpH%d" % d)
            phys = [(pG[d], "pG%d" % d), (pU[d], "pU%d" % d), (pD[d], "pD%d" % d), (pS[d], "pS%d" % d)]
            bA, kA = phys[0]; bB, kB = phys[1]; bC, kC = phys[2]; bD, kD = phys[3]
            bE, kE = phys[0]; bF, kF = phys[1]; bG, kG = phys[2]; bH, kH = phys[3]
            v4 = lambda ap: ap.rearrange("p (h t) -> p h t", h=4)
            for ui, c in enumerate(chunk_order(d)):
                i2 = ui % 2
                r0 = c * 64
                KB = lambda s: (s, d, i2)
                P.dma("sp", cm[i2][:], RF[0:6, :, r0:r0 + 64].rearrange("j p t -> p j t"), reads=[("RF", 0)],
                      writes=[KB("cm")])
                P.dma("sp", dm[i2][:], RF[6 + 6 * d:12 + 6 * d, :, r0:r0 + 64].rearrange("j p t -> p j t"),
                      reads=[("RF", 0)], writes=[KB("dm")])
                cmt, dmt = cm[i2], dm[i2]
                yield
                if _rwcut <= 1:
                    continue
                for pc in range(2):
                    tr(bG[:64, pc * 128:(pc + 1) * 128], dmt[:, 4 + pc, :], [KB("dm")], [kG], inc=False)
                    tr(bG[:64, 256 + pc * 128:256 + (pc + 1) * 128], cmt[:, 2 + pc, :], [KB("cm")], [kG], inc=(pc == 1))
                P.op("dve", "tensor_copy", ewM[:], bG[:64, 0:256], reads=[kG], writes=[K("ewM")])
                P.op("dve", "tensor_copy", vM[:], bG[:64, 256:512], reads=[kG], writes=[K("vM")])
                yield
                if _rwcut <= 2:
                    continue
                for pc in range(2):
                    for ie in range(2):
                        mm(bH[:, (pc * 2 + ie) * 64:(pc * 2 + ie + 1) * 64], ewM[:, pc * 128:(pc + 1) * 128], tri[2 * ie + d],
                           [K("ewM"), "cst"], [kH], inc=(pc == 1 and ie == 1))
                lv = bH[:, 0:256].rearrange("p (c i t) -> p c i t", c=2, i=2)
                act(Pd[:, :, 0:2, :], lv, AF.Exp, [kH], [K("Pd")])
                act(Pd[:, :, 2, :], lv[:, :, 0, :], AF.Exp, [kH], [K("Pd")], scale=-1.0)
                yield
                if _rwcut <= 3:
                    continue
                tt_(fT[:, :, 0, :], cmt[:, 0:2, :], Pd[:, :, 0, :], ALU.mult, [KB("cm"), K("Pd")], [K("fT")])
                tt_(fT[:, :, 1, :], cmt[:, 4:6, :], Pd[:, :, 1, :], ALU.mult, [KB("cm"), K("Pd")], [K("fT")])
                tt_(fT[:, :, 2, :], dmt[:, 0:2, :], Pd[:, :, 2, :], ALU.mult, [KB("dm"), K("Pd")], [K("fT")])
                tt_(fT[:, :, 3, :], dmt[:, 2:4, :], Pd[:, :, 2, :], ALU.mult, [KB("dm"), K("Pd")], [K("fT")])
                for hh in range(2):
                    ts_(mT_[:, :, :, hh, :], fT[:, :, 2:4, :], cst[:, C_RHM + hh:C_RHM + hh + 1], None, ALU.mult, None,
                        [K("fT"), "cst"], [K("mK")])
                yield
                if _rwcut <= 4:
                    continue
                for pc in range(2):
                    tr(bG[:64, pc * 128:(pc + 1) * 128], fT[:, pc, 2, :], [K("fT")], [kG], inc=False)
                    tr(bG[:64, 256 + pc * 128:256 + (pc + 1) * 128], fT[:, pc, 3, :], [K("fT")], [kG], inc=(pc == 1))
                act(KBM[:, 0, :], bG[:64, 0:256], AF.Copy, [kG], [K("KBM")])
                act(KBM[:, 1, :], bG[:64, 256:512], AF.Copy, [kG], [K("KBM")], scale=-1.0)
                for h in range(4):
                    pc, hh = h // 2, h % 2
                    KdTm, BdTm = mT_[:, pc, 0, hh, :], mT_[:, pc, 1, hh, :]
                    KKeT, RpT = fT[:, pc, 1, :], fT[:, pc, 0, :]
                    rd_ = [K("mK"), K("fT")]
                    mm(bA[:64, h * 64:(h + 1) * 64], KdTm, KKeT, rd_, [kA], inc=False)
                    mm(bA[:64, 256 + h * 64:256 + (h + 1) * 64], BdTm, KKeT, rd_, [kA], inc=(h == 3))
                    mm(bB[:64, h * 64:(h + 1) * 64], KKeT, BdTm, rd_, [kB], inc=False)
                    mm(bB[:64, 256 + h * 64:256 + (h + 1) * 64], KdTm, RpT, rd_, [kB], inc=(h == 3))
                    mm(bC[:64, h * 64:(h + 1) * 64], BdTm, RpT, rd_, [kC], inc=(h == 3))
                yield
                if _rwcut <= 5:
                    continue
                tt_(sc[:, 0], v4(bA[:64, 0:256]), bc4(m01[2 + d]), ALU.mult, [kA, "cst"], [K("sc0")])
                tt_(sc[:, 1], v4(bA[:64, 256:512]), bc4(m01[2 + d]), ALU.mult, [kA, "cst"], [K("sc1")])
                tt_(sc[:, 2], v4(bB[:64, 0:256]), bc4(m01[3 - d]), ALU.mult, [kB, "cst"], [K("sc2")])
                tt_(sc[:, 3], v4(bB[:64, 256:512]), bc4(m01[d]), ALU.mult, [kB, "cst"], [K("sc3")])
                stt(sc[:, 4], v4(bC[:64, 0:256]), -1.0, bc4(m01[d]), ALU.mult, ALU.mult, [kC, "cst"], [K("sc4")])
                stt(Tt[:], sc[:, 1], -1.0, bc4(id64), ALU.mult, ALU.add, [K("sc1"), "cst"], [K("Tt")])
                stt(TtT[:], sc[:, 2], -1.0, bc4(id64), ALU.mult, ALU.add, [K("sc2"), "cst"], [K("TtT")])
                yield
                if _rwcut <= 6:
                    continue
                Ncur, NTcur, nk = sc[:, 1], sc[:, 2], [K("sc1"), K("sc2")]
                for lev in range(5):
                    lastl = lev == 4
                    for h in range(4):
                        mm(bD[:64, h * 64:(h + 1) * 64], NTcur[:, h, :], Ncur[:, h, :], nk, [kD], inc=(lastl and h == 3))
                        if not lastl:
                            mm(bD[:64, 256 + h * 64:256 + (h + 1) * 64], Ncur[:, h, :], NTcur[:, h, :], nk, [kD],
                               inc=(h == 3))
                    nn = Nn[lev % 2]
                    nnk = (K("Nn"), lev % 2)
                    if lastl:
                        act(nn[:, 0], v4(bD[:64, 0:256]), AF.Copy, [kD], [nnk])
                    else:
                        act(nn[:].rearrange("p a h t -> p (a h) t"), bD[:64, :].rearrange("p (a t) -> p a t", a=8),
                            AF.Copy, [kD], [nnk])
                    yield
                    for h in range(4):
                        mm(bE[:64, h * 64:(h + 1) * 64], TtT[:, h, :], nn[:, 0, h, :], [K("TtT"), nnk], [kE],
                           inc=(lastl and h == 3))
                        if not lastl:
                            mm(bE[:64, 256 + h * 64:256 + (h + 1) * 64], nn[:, 0, h, :], TtT[:, h, :], [K("TtT"), nnk], [kE],
                               inc=(h == 3))
                    tt_(Tt[:], Tt[:], v4(bE[:64, 0:256]), ALU.add, [K("Tt"), kE], [K("Tt")])
                    if not lastl:
                        tt_(TtT[:], TtT[:], v4(bE[:64, 256:512]), ALU.add, [K("TtT"), kE], [K("TtT")])
                    Ncur, NTcur, nk = nn[:, 0], nn[:, 1], [nnk]
                    yield
                for h in range(4):
                    pc = h // 2
                    mm(bC[:64, 256 + h * 64:256 + (h + 1) * 64], fT[:, pc, 1, :], Hx[d][pc][:, h * 64:(h + 1) * 64],
                       [K("fT"), ("Hx", d, pc)], [kC], start=True, stop=False, inc=False)
                    mm(bC[:64, 256 + h * 64:256 + (h + 1) * 64], sc[:, 0, h, :], vM[:, h * 64:(h + 1) * 64],
                       [K("sc0"), K("vM")], [kC], start=False, stop=True, inc=(h == 3))
                P.op("dve", "tensor_copy", Xs[:], v4(bC[:64, 256:512]), reads=[kC], writes=[K("Xs")])
                yield
                if _rwcut <= 7:
                    continue
                for h in range(4):
                    mm(bF[:64, h * 64:(h + 1) * 64], Tt[:, h, :], Xs[:, h, :], [K("Tt"), K("Xs")], [kF], inc=(h == 3))
                act(Us[:], v4(bF[:64, 0:256]), AF.Copy, [kF], [K("Us")])
                yield
                if _rwcut <= 8:
                    continue
                for h in range(4):
                    pc = h // 2
                    o_ = bF[:64, 256 + h * 64:256 + (h + 1) * 64]
                    mm(o_, fT[:, pc, 0, :], Hx[d][pc][:, h * 64:(h + 1) * 64], [K("fT"), ("Hx", d, pc)], [kF],
                       start=True, stop=False, inc=False)
                    mm(o_, sc[:, 3, h, :], vM[:, h * 64:(h + 1) * 64], [K("sc3"), K("vM")], [kF], start=False, stop=False,
                       inc=False)
                    mm(o_, sc[:, 4, h, :], Us[:, h, :], [K("sc4"), K("Us")], [kF], start=False, stop=True, inc=(h == 3))
                P.op("dve", "tensor_copy", Ys[:], bF[:64, 256:512], reads=[kF], writes=[K("Ys")])
                P.dma("sp", YD[d, r0:r0 + 64, :], Ys[:], reads=[K("Ys")], writes=[("YD", d, c)])
                lastc = 63 if d == 0 else 0
                for pc in range(2):
                    o_ = bH[:, 256:512] if pc == 0 else bG[:, 0:256]
                    ok_ = kH if pc == 0 else kG
                    mm(o_, KBM[:, 0, pc * 128:(pc + 1) * 128], vM[:], [K("KBM"), K("vM")], [ok_], start=True, stop=False,
                       inc=False)
                    mm(o_, KBM[:, 1, pc * 128:(pc + 1) * 128], Us[:].rearrange("p h e -> p (h e)"), [K("KBM"), K("Us")],
                       [ok_], start=False, stop=True, inc=True)
                    tt_(tmpH[:], o_, cst[:, C_RBD + pc * 256:C_RBD + (pc + 1) * 256], ALU.mult, [ok_, "cst"], [K("tmpH")])
                    tt_(tmpH[:], tmpH[:], Hx[d][pc][:], ALU.add, [K("tmpH"), ("Hx", d, pc)], [K("tmpH")])
                    ts_(Hx[d][pc][:], tmpH[:], Pd[:, pc, 0, lastc:lastc + 1], None, ALU.mult, None, [K("tmpH"), K("Pd")],
                        [("Hx", d, pc)])
                yield
                if _rwcut <= 9:
                    continue

        run_interleaved([unit(0), unit(1)])
        P.barrier()
        es.close()

    def rwkv_final(l, do_ctx):
        es = ExitStack()
        P.es = es
        gup = P.sb([128, 256], F32, "gup")
        P.dma("sp", gup[:], rw_gup[l], writes=["gup"])
        gnb = P.sb([128, 2, 256], F32, "gnb")
        for i in range(2):
            P.dma("sp", gnb[:, i, :], rw_gn[l, i].partition_broadcast(128), writes=["gnb"])
        yf = [P.sb([128, 4, 64], F32, "yf%d" % i) for i in range(2)]
        yb = [P.sb([128, 4, 64], F32, "yb%d" % i) for i in range(2)]
        fp = [P.sb([128, 5, 128], F32, "fp%d" % i) for i in range(2)]
        tm = P.sb([128, 2, 256], F32, "ftm")
        ysq = P.sb([128, 4, 64], F32, "ysq")
        st4 = P.sb([128, 4, 4], F32, "st4")
        gsb = P.sb([128, 256], F32, "gsb")
        ni = 0
        for tt in range(NTL):
            r0 = tt * 128
            if r0 < LC and not do_ctx:
                continue
            i2 = ni % 2
            ni += 1
            P.dma("sp", yf[i2][:].rearrange("p h e -> p (h e)"), YD[0, r0:r0 + 128, :],
                  reads=[("YD", 0, 2 * tt), ("YD", 0, 2 * tt + 1)], writes=[("yf", i2)])
            P.dma("sp", yb[i2][:].rearrange("p h e -> p (h e)"), YD[1, r0:r0 + 128, :],
                  reads=[("YD", 1, 2 * tt), ("YD", 1, 2 * tt + 1)], writes=[("yb", i2)])
            for j, pn in enumerate([18, 19, 2, 3, 20]):
                P.dma("sp", fp[i2][:, j, :], RF[pn, :, r0:r0 + 128], reads=[("RF", 0)], writes=[("fp", i2)])
            y = yf[i2]
            tt_(y[:], y[:], yb[i2][:], ALU.add, [("yf", i2), ("yb", i2)], [("yf", i2)])
            for j in range(4):
                tr(pG[0][:, j * 128:(j + 1) * 128], fp[i2][:, j, :], [("fp", i2)], ["pG0"], inc=(j == 3))
            act(tm[:].rearrange("p a c -> p (a c)"), pG[0][:, :], AF.Copy, ["pG0"], ["ftm"])
            mm(pG[1][:, 0:256], fp[i2][:, 4, :], gup[:], [("fp", i2), "gup"], ["pG1"])
            act(gsb[:], pG[1][:, 0:256], AF.Copy, ["pG1"], ["gsb"])
            P.op("dve", "tensor_reduce", st4[:, 0, :], y[:], AX.X, ALU.add, reads=[("yf", i2)], writes=["st4"])
            tt_(ysq[:], y[:], y[:], ALU.mult, [("yf", i2)], ["ysq"])
            P.op("dve", "tensor_reduce", st4[:, 1, :], ysq[:], AX.X, ALU.add, reads=["ysq"], writes=["st4"])
            P.op("dve", "tensor_reduce", st4[:, 2, :], tm[:, 0, :].rearrange("p (h e) -> p h e", h=4), AX.X, ALU.add,
                 reads=["ftm"], writes=["st4"])
            ts_(st4[:, 0, :], st4[:, 0, :], 1.0 / 64, None, ALU.mult, None, ["st4"], ["st4"])
            tt_(st4[:, 3, :], st4[:, 0, :], st4[:, 0, :], ALU.mult, ["st4"], ["st4"])
            stt(st4[:, 1, :], st4[:, 1, :], 1.0 / 64, st4[:, 3, :], ALU.mult, ALU.subtract, ["st4"], ["st4"])
            act(st4[:, 1, :], st4[:, 1, :], AF.Sqrt, ["st4", "kc"], ["st4"], bias=kc[:, 4:5])
            P.op("dve", "reciprocal", st4[:, 1, :], st4[:, 1, :], reads=["st4"], writes=["st4"])
            tt_(y[:], y[:], st4[:, 0, :].unsqueeze(2).to_broadcast([128, 4, 64]), ALU.subtract, [("yf", i2), "st4"],
                [("yf", i2)])
            tt_(y[:], y[:], st4[:, 1, :].unsqueeze(2).to_broadcast([128, 4, 64]), ALU.mult, [("yf", i2), "st4"],
                [("yf", i2)])
            yv = y[:].rearrange("p h e -> p (h e)")
            tt_(yv, yv, gnb[:, 0, :], ALU.mult, [("yf", i2), "gnb"], [("yf", i2)])
            tt_(yv, yv, gnb[:, 1, :], ALU.add, [("yf", i2), "gnb"], [("yf", i2)])
            tt_(ysq[:], tm[:, 1, :].rearrange("p (h e) -> p h e", h=4),
                st4[:, 2, :].unsqueeze(2).to_broadcast([128, 4, 64]), ALU.mult, ["ftm", "st4"], ["ysq"])
            tt_(y[:], y[:], ysq[:], ALU.add, [("yf", i2), "ysq"], [("yf", i2)])
            tt_(yv, yv, gsb[:], ALU.mult, [("yf", i2), "gsb"], [("yf", i2)])
            P.dma("sp", MIX[r0:r0 + 128, 768:1024], yv, reads=[("yf", i2)], writes=[("MIX", "c", tt)])
        P.barrier()
        es.close()

    def outproj(l, do_ctx):
        es = ExitStack()
        P.es = es
        hT = [P.sb([128, 8, TG], F32, "hT%d" % i) for i in range(2)]
        lt = alloc_ln_tiles()
        vv = lt["vv"]
        mixT = P.sb([128, 8, TG], BF16, "mixT")
        mtile = [P.sb([128, 1024], F32, "mtile%d" % i) for i in range(2)]
        wo = P.sb([128, 8, 1024], BF16, "wo")
        for hf in range(2):
            P.dma("pool", wo[:, :, hf * 512:(hf + 1) * 512],
                  w_out[l][:, hf * 512:(hf + 1) * 512].rearrange("(k p) j -> p k j", p=128), writes=["wo"])
        cur_s = None
        nm = 0
        for (st, t0, n) in groups:
            if st == "c" and not do_ctx:
                continue
            s = 1 if st == "c" else 0
            off = toff(st, t0)
            if s != cur_s:
                mod_scalars(l, 1, s)
                cur_s = s
            gi = cnts["g"]
            cnts["g"] += 1
            h = hT[gi % 2]
            hk = "hT%d" % (gi % 2)
            P.dma("sp", h[:, :, :n], hsrc(st, t0, n), reads=[("H", st, t0)], writes=[hk])
            for ts in range(n // 128):
                mt = mtile[nm % 2]
                mk = ("mtile", nm % 2)
                nm += 1
                P.dma("sp", mt[:], MIX[off + ts * 128:off + (ts + 1) * 128, :],
                      reads=[kk for kk in list(P.lastw.keys()) if isinstance(kk, tuple) and kk[0] == "MIX"], writes=[mk])
                for half in range(2):
                    pi = half
                    for j in range(4):
                        tr(pG[pi][:, j * 128:(j + 1) * 128], mt[:, (half * 4 + j) * 128:(half * 4 + j + 1) * 128], [mk],
                           ["pG%d" % pi], inc=(j == 3))
                    act(mixT[:, half * 4:half * 4 + 4, ts * 128:(ts + 1) * 128],
                        pG[pi][:, :].rearrange("p (j t) -> p j t", j=4), AF.Copy, ["pG%d" % pi], ["mixT"])
            for c in range(8):
                pi = c % 2
                for k in range(8):
                    mm(pD[pi][:, :n], wo[:, k, c * 128:(c + 1) * 128], mixT[:, k, :n], ["wo", "mixT"], ["pD%d" % pi],
                       start=(k == 0), stop=(k == 7), inc=(k == 7))
                stt(vv[:, c, :n], pD[pi][:, :n], scl[:, 2, c:c + 1], h[:, c, :n], ALU.mult, ALU.add,
                    ["pD%d" % pi, "scl2", hk], [("vv", c)])
            layer_norm_store(lt, st, t0, n, l, 1)
        P.barrier()
        es.close()

    for l in range(L):
        last = l == L - 1
        if USE_WSCRATCH:
            convert_weights(l)
        ffn_sublayer(l, 0, ffn_w["ffn1_wg"][l], ffn_w["ffn1_wu"][l], ffn_w["ffn1_wd"][l])
        if dbg == "ffn1":
            break
        inproj(l)
        if dbg == "inproj":
            break
        if dbg in (None, "swa", "mix"):
            swa(l, not last)
        if dbg is None:
            gla(l, not last, extra=[rwkv_prep_gen(l)])
        elif dbg in ("gla", "mix"):
            gla(l, not last)
        if dbg in (None, "rwkv", "mix"):
            import os as _os
            _rs = int(_os.environ.get("RW_STOP", "9"))
            if dbg is not None:
                rwkv_prep(l)
            if _rs >= 2:
                rwkv_scan(l)
            if _rs >= 3:
                rwkv_final(l, not last)
        if dbg in ("swa", "gla", "rwkv", "mix"):
            break
        outproj(l, not last)
        if dbg == "outproj":
            break
        ffn_sublayer(l, 2, ffn_w["ffn2_wg"][l], ffn_w["ffn2_wu"][l], ffn_w["ffn2_wd"][l], do_ctx=not last)

    es = ExitStack()
    P.es = es
    evs = []
    if dbg in ("swa", "gla", "rwkv", "mix"):
        mixo = dram("mixo", [SEQ, D], kind="ExternalOutput")
        ob = [P.sb([128, 1024], F32, "ob%d" % i) for i in range(2)]
        for tt in range(SEQ // 128):
            r0 = LC + tt * 128
            P.dma("sp", ob[tt % 2][:], MIX[r0:r0 + 128, :], writes=[("ob", tt % 2)])
            evs.append(P.dma("sp", mixo[tt * 128:(tt + 1) * 128, :], ob[tt % 2][:], reads=[("ob", tt % 2)],
                             writes=[("mixo", tt)]))
    ob2 = [P.sb([128, 8, TG], F32, "ob2%d" % i) for i in range(2)]
    for gi, (st, t0, n) in enumerate(groups):
        if st == "c":
            continue
        b = ob2[gi % 2]
        bk = "ob2%d" % (gi % 2)
        P.dma("sp", b[:, :, :n], hsrc(st, t0, n), reads=[("H", st, t0)], writes=[bk])
        evs.append(P.dma("sp", outT[:, :, t0:t0 + n].rearrange("c p t -> p c t"), b[:, :, :n], reads=[bk],
                         writes=[("out", t0)]))
    P.finish("sp", evs)
    es.close()
    es0.close()
    print("program instructions:", P.ninst)
    return nc


def _consts():
    c = np.zeros((128, C_END), np.float32)
    c[:, C_ID:C_ID + 128] = np.eye(128)
    s = np.arange(64)[:, None]
    t = np.arange(64)[None, :]
    c[0:64, C_TRI + 0:C_TRI + 64] = -1.0 * (s <= t)
    c[0:64, C_TRI + 64:C_TRI + 128] = -1.0 * (s >= t)
    c[0:64, C_TRI + 128:C_TRI + 192] = -1.0 * (s < t)
    c[0:64, C_TRI + 192:C_TRI + 256] = -1.0 * (s > t)
    c[0:64, C_M01 + 0:C_M01 + 64] = (s <= t)
    c[0:64, C_M01 + 64:C_M01 + 128] = (s >= t)
    c[0:64, C_M01 + 128:C_M01 + 192] = (s < t)
    c[0:64, C_M01 + 192:C_M01 + 256] = (s > t)
    p = np.arange(128)[:, None]
    col = np.arange(256)[None, :]
    c[:, C_GBD:C_GBD + 256] = (p // 32 == col // 64)
    for h in range(4):
        c[:, C_GHM + h] = (np.arange(128) // 32 == h)
    for pc in range(2):
        c[:, C_RBD + pc * 256:C_RBD + (pc + 1) * 256] = ((2 * pc + p // 64) == col // 64)
    for hh in range(2):
        c[:, C_RHM + hh] = (np.arange(128) // 64 == hh)
    q = np.arange(128)[None, :]
    c[:, C_BO:C_BO + 128] = (p // 64 == q // 64)
    c[:, C_SWM:C_SWM + 128] = (p >= q)
    c[:, C_SWM + 128:C_SWM + 256] = (p <= q)
    c[0:64, C_NI:C_NI + 64] = -np.eye(64)
    return c


def _rope_table(SEQ):
    pos = np.arange(SEQ)
    row = (pos // 64).astype(np.float32)
    col = (pos % 64).astype(np.float32)
    inv = (np.float32(10000.0) ** (-np.arange(16, dtype=np.float32) / np.float32(16))).astype(np.float32)
    tab = np.zeros((SEQ, 2, 32), np.float32)
    for a, pp in enumerate((row, col)):
        ang = (pp[:, None] * inv[None, :]).astype(np.float32)
        tab[:, 0, a * 16:(a + 1) * 16] = np.cos(ang)
        tab[:, 1, a * 16:(a + 1) * 16] = np.sin(ang)
    return tab


def _prep_shared(inp, L, SEQ):
    f = lambda a: np.ascontiguousarray(a, dtype=np.float32)
    m = {}
    m["w_ada"] = f(inp["w_ada"][:L])
    m["b_adaT"] = f(inp["b_ada"][:L].reshape(L, 72, 128).transpose(2, 0, 1))
    ln = np.stack([inp["ln_g"][:L], inp["ln_b"][:L]], axis=2)
    m["lnT"] = f(ln.reshape(L, 3, 2, 8, 128).transpose(4, 0, 1, 2, 3))
    for nm in ("ffn1_wg", "ffn1_wu", "ffn1_wd", "ffn2_wg", "ffn2_wu", "ffn2_wd", "w_in", "w_out"):
        m[nm] = f(inp[nm][:L])
    m["consts"] = _consts()
    m["ropeM"] = _rope_table(SEQ)
    m["gla_up"] = f(np.concatenate([inp["gla_gate_up"][:L], inp["gla_gate_bias"][:L][:, :, None, :]], axis=2))
    m["gla_g"] = f(inp["gla_norm_g"][:L])
    m["sinkB"] = f(np.broadcast_to(inp["swa_sink"][:L][None], (128, L, 8)))
    vecs = np.stack([inp["rwkv_k_k"][:L], inp["rwkv_k_a"][:L], inp["rwkv_r_k"][:L].reshape(L, 256),
                     inp["rwkv_w0"][:L, 0], inp["rwkv_w0"][:L, 1], inp["rwkv_a0"][:L, 0], inp["rwkv_a0"][:L, 1]],
                    axis=-1)
    m["rw_vec"] = f(vecs.reshape(L, 2, 128, 7).transpose(2, 0, 1, 3))
    mu = inp["rwkv_mu"][:L]
    mut = np.zeros((128, L, 11), np.float32)
    for j in range(6):
        mut[:, :, j] = mu[:, 128 * j:128 * (j + 1)].T
    for j, r0 in enumerate((768, 832, 896, 960)):
        mut[:64, :, 6 + j] = mu[:, r0:r0 + 64].T
    mut[:, :, 10] = mu[:, 1024:1152].T
    m["rw_mu"] = mut
    m["rw_wup"] = f(inp["rwkv_w_up"][:L])
    m["rw_aup"] = f(inp["rwkv_a_up"][:L])
    m["rw_gup"] = f(inp["rwkv_g_up"][:L])
    m["rw_gn"] = f(np.stack([inp["rwkv_gn_g"][:L], inp["rwkv_gn_b"][:L]], axis=1))
    return m


def _prep_core(inp, b):
    f = lambda a: np.ascontiguousarray(a, dtype=np.float32)
    m = {}
    m["xT"] = f(inp["x"][b].T.reshape(8, 128, -1))
    m["cxT"] = f(inp["ctx"][b].T.reshape(8, 128, -1))
    cc = np.stack([inp["c"][b], inp["c_ctx"]], axis=-1)
    m["ccT"] = f(cc.reshape(8, 128, 2).transpose(1, 0, 2))
    return m


def run(inp, SEQ, LC, DEPTH, ncores, dbg=None, trace=False):
    nc = build(SEQ, LC, DEPTH, dbg=dbg)
    shared = _prep_shared(inp, DEPTH, SEQ)
    in_maps = []
    for b in range(ncores):
        m = dict(shared)
        m.update(_prep_core(inp, b))
        in_maps.append(m)
    res = run_bass_kernel_spmd(nc, in_maps, core_ids=list(range(ncores)), trace=trace)
    outs = [r["outT"].reshape(1024, SEQ).T for r in res.results]
    return np.stack(outs, axis=0), res


def kernel(**inputs):
    inp = {k: np.asarray(v) for k, v in inputs.items()}
    out, _ = run(inp, 4096, 256, 4, 8)
    return np.ascontiguousarray(out.astype(np.float32))
```

```python
import numpy as np
from contextlib import ExitStack
import concourse.bass as bass
import concourse.mybir as mybir
from concourse.bass_utils import run_bass_kernel_spmd

F32 = mybir.dt.float32
BF16 = mybir.dt.bfloat16
AF = mybir.ActivationFunctionType
ALU = mybir.AluOpType
AX = mybir.AxisListType

D = 1024
DFF = 2816
NFF = DFF // 128
DIN = 2720
ALPHA = 8.0 ** 0.25
LN_EPS = 1e-6
USE_WSCRATCH = False
GN_EPS = 64e-5
C_ID = 0
C_TRI = 128
C_M01 = 384
C_GBD = 640
C_GHM = 896
C_RBD = 900
C_RHM = 1412
C_BO = 1414
C_SWM = 1542
C_NI = 1798
C_END = 1862


class Prog:
    def __init__(self, nc, es):
        self.nc = nc
        self.es = es
        self.eng = {"pe": nc.tensor, "act": nc.scalar, "dve": nc.vector, "pool": nc.gpsimd, "sp": nc.sync}
        self.sem = {e: es.enter_context(nc.semaphore("s_" + e)) for e in ("pe", "act", "dve", "pool")}
        self.cnt = {e: 0 for e in self.sem}
        self.known = {e: {} for e in self.eng}
        self.lastw = {}
        self.rd = {}
        self.NDS = 12
        self.dsem = {q: [es.enter_context(nc.semaphore("d_%s%d" % (q, i))) for i in range(self.NDS)]
                     for q in ("sp", "pool", "act")}
        self.dcnt = {q: 0 for q in self.dsem}
        self.semobj = {}
        for e in self.sem:
            self.semobj[e] = self.sem[e]
        for q in self.dsem:
            for i, s in enumerate(self.dsem[q]):
                self.semobj[(q, i)] = s
        self.ntiles = 0
        self.ninst = 0

    def sb(self, shape, dt=F32, name=None):
        self.ntiles += 1
        return self.es.enter_context(self.nc.sbuf_tensor("%s_%d" % (name or "t", self.ntiles), list(shape), dt))

    def ps(self, shape, dt=F32, name=None):
        self.ntiles += 1
        return self.es.enter_context(self.nc.psum_tensor("%s_%d" % (name or "p", self.ntiles), list(shape), dt))

    def _wait(self, e, ev):
        s, v = ev
        if self.known[e].get(s, 0) >= v:
            return
        self.eng[e].wait_ge(self.semobj[s], v)
        self.known[e][s] = v
        self.ninst += 1

    def _deps(self, e, reads, writes):
        for k in reads:
            ev = self.lastw.get(k)
            if ev is not None and not (e == "pe" and ev[0] == "pe"):
                self._wait(e, ev)
        for k in writes:
            ev = self.lastw.get(k)
            if ev is not None and not (e == "pe" and ev[0] == "pe"):
                self._wait(e, ev)
            for ev in self.rd.get(k, {}).values():
                if ev[0] == e:
                    continue
                self._wait(e, ev)

    def _record(self, ev, reads, writes):
        for k in reads:
            self.rd.setdefault(k, {})[ev[0]] = ev
        for k in writes:
            self.lastw[k] = ev
            self.rd[k] = {}

    def op(self, e, fn, *args, reads=(), writes=(), inc=True, **kw):
        self._deps(e, reads, writes)
        inst = getattr(self.eng[e], fn)(*args, **kw)
        self.ninst += 1
        ev = (e, self.cnt[e] + 1)
        if inc:
            inst.then_inc(self.sem[e], 1)
            self.cnt[e] += 1
        self._record(ev, reads, writes)
        return inst

    def dma(self, q, out, in_, reads=(), writes=(), **kw):
        self._deps(q, reads, writes)
        j = self.dcnt[q]
        self.dcnt[q] += 1
        slot = j % self.NDS
        s = (q, slot)
        need = 16 * (j // self.NDS)
        if need > 0:
            self._wait(q, (s, need))
        inst = self.eng[q].dma_start(out=out, in_=in_, **kw)
        inst.then_inc(self.dsem[q][slot], 16)
        self.ninst += 1
        ev = (s, need + 16)
        self._record(ev, reads, writes)
        return ev

    def barrier(self):
        evs = [(e, self.cnt[e]) for e in self.sem if self.cnt[e] > 0]
        for q in self.dsem:
            j = self.dcnt[q]
            for slot in range(self.NDS):
                n = (j - slot + self.NDS - 1) // self.NDS if j > slot else 0
                if n > 0:
                    evs.append(((q, slot), 16 * n))
        for e in self.eng:
            for ev in evs:
                self._wait(e, ev)
        self.lastw = {}
        self.rd = {}

    def finish(self, e, evs):
        for ev in evs:
            self._wait(e, ev)


def _ffn_pieces():
    out = []
    f = 0
    while f < NFF:
        w = min(4, NFF - f)
        out.append((f, w))
        f += w
    return out


def build(SEQ, LC, DEPTH, dbg=None):
    nc = bass.Bass("TRN2", target_bir_lowering=False)
    es0 = ExitStack()
    P = Prog(nc, es0)
    dram = lambda name, shape, dt=F32, kind="ExternalInput": nc.dram_tensor(name, list(shape), dt, kind=kind).ap()
    L = DEPTH
    T = LC + SEQ
    NTL = T // 128
    NCH = T // 64
    xT = dram("xT", [8, 128, SEQ])
    cxT = dram("cxT", [8, 128, LC])
    ccT = dram("ccT", [128, 8, 2])
    w_ada = dram("w_ada", [L, D, 9 * D])
    b_adaT = dram("b_adaT", [128, L, 72])
    lnT = dram("lnT", [128, L, 3, 2, 8])
    ffn_w = {}
    for nm in ("ffn1_wg", "ffn1_wu", "ffn2_wg", "ffn2_wu"):
        ffn_w[nm] = dram(nm, [L, D, DFF])
    for nm in ("ffn1_wd", "ffn2_wd"):
        ffn_w[nm] = dram(nm, [L, DFF, D])
    w_in = dram("w_in", [L, D, DIN])
    w_out = dram("w_out", [L, D, D])
    consts = dram("consts", [128, C_END])
    ropeM = dram("ropeM", [SEQ, 2, 32])
    gla_up = dram("gla_up", [L, 2, 17, 128])
    gla_g = dram("gla_g", [L, 256])
    sinkB = dram("sinkB", [128, L, 8])
    rw_vec = dram("rw_vec", [128, L, 2, 7])
    rw_mu = dram("rw_mu", [128, L, 11])
    rw_wup = dram("rw_wup", [L, 2, 64, 256])
    rw_aup = dram("rw_aup", [L, 2, 64, 256])
    rw_gup = dram("rw_gup", [L, 128, 256])
    rw_gn = dram("rw_gn", [L, 2, 256])
    outT = dram("outT", [8, 128, SEQ], kind="ExternalOutput")
    HX = dram("HX", [8, 128, SEQ], kind="Internal")
    HC = dram("HC", [8, 128, LC], kind="Internal")
    PF = dram("PF", [DIN, T], kind="Internal")
    PM = dram("PM", [T, DIN], kind="Internal")
    MIX = dram("MIX", [T, D], kind="Internal")
    OGD = dram("OGD", [2, T, 256], kind="Internal")
    RF = dram("RF", [21, 128, T], kind="Internal")
    YD = dram("YD", [2, T, 256], kind="Internal")
    NPC = len(_ffn_pieces())
    WGU = dram("WGU", [1, 2, 2, NPC, 128, 8, 512], BF16, kind="Internal") if USE_WSCRATCH else None
    WDS = dram("WDS", [1, 2, 8, 128, NFF, 128], BF16, kind="Internal") if USE_WSCRATCH else None

    TG = 512
    groups = [("c", 0, LC)] + [("x", t0, min(TG, SEQ - t0)) for t0 in range(0, SEQ, TG)]

    def toff(st, t0):
        return t0 if st == "c" else LC + t0

    cst = P.sb([128, C_END], F32, "cst")
    P.dma("sp", cst[:], consts, writes=["cst"])
    ident = cst[:, C_ID:C_ID + 128]
    onesm = P.sb([128, 128], F32, "onesm")
    P.op("dve", "memset", onesm[:], 1.0 / D, writes=["onesm"])
    cc = P.sb([128, 8, 2], F32, "cc")
    P.dma("sp", cc[:], ccT, writes=["cc"])
    sil = P.sb([128, 8, 2], F32, "sil")
    P.op("act", "activation", sil[:], cc[:], AF.Silu, reads=["cc"], writes=["sil"])
    bada = P.sb([128, L, 72], F32, "bada")
    P.dma("sp", bada[:], b_adaT, writes=["bada"])
    lnp = P.sb([128, L, 3, 2, 8], F32, "lnp")
    P.dma("sp", lnp[:], lnT, writes=["lnp"])
    mT = P.sb([128, L, 72, 2], F32, "mT")
    kc = P.sb([128, 8], F32, "kc")
    for i, val in enumerate([LN_EPS / (ALPHA * ALPHA), 1.0, -0.5, LN_EPS, GN_EPS, 0.0, 1e-24]):
        P.op("dve", "memset", kc[:, i:i + 1], val, writes=["kc"])
    epsb = kc[:, 0:1]
    scl = P.sb([128, 4, 8], F32, "scl")

    pG = [P.ps([128, 512], F32, "pG%d" % i) for i in range(2)]
    pU = [P.ps([128, 512], F32, "pU%d" % i) for i in range(2)]
    pD = [P.ps([128, 512], F32, "pD%d" % i) for i in range(2)]
    pS = [P.ps([128, 512], F32, "pS%d" % i) for i in range(2)]

    es = ExitStack()
    P.es = es
    wa = [P.sb([128, 8, 512], F32, "wa%d" % i) for i in range(2)]
    nblk = 0
    for l in range(L):
        pm = pS[l % 2]
        for cb in range(18):
            wt = wa[nblk % 2]
            wk = "wa%d" % (nblk % 2)
            nblk += 1
            src = w_ada[l, :, cb * 512:(cb + 1) * 512].rearrange("(k p) j -> p k j", p=128)
            P.dma("sp", wt[:], src, writes=[wk])
            for jj in range(4):
                j = cb * 4 + jj
                for k in range(8):
                    P.op("pe", "matmul", pm[:, 2 * j:2 * j + 2], wt[:, k, jj * 128:(jj + 1) * 128], sil[:, k, :],
                         start=(k == 0), stop=(k == 7), reads=[wk, "sil"], writes=["pS%d" % (l % 2)],
                         inc=(k == 7 and jj == 3))
        P.op("dve", "tensor_tensor", mT[:, l, :, :], pm[:, 0:144].rearrange("p (j s) -> p j s", s=2),
             bada[:, l, :].unsqueeze(2).to_broadcast([128, 72, 2]), ALU.add,
             reads=["pS%d" % (l % 2), "bada"], writes=["mT"])
    for gi, (st, t0, n) in enumerate(groups):
        b = wa[gi % 2]
        bk = "wa%d" % (gi % 2)
        src = (cxT if st == "c" else xT)[:, :, t0:t0 + n].rearrange("c p t -> p c t")
        P.dma("sp", b[:, :, :n], src, writes=[bk])
        P.dma("sp", (HC if st == "c" else HX)[:, :, t0:t0 + n].rearrange("c p t -> p c t"), b[:, :, :n],
              reads=[bk], writes=[("H", st, t0)])
    P.barrier()
    es.close()

    def convert_weights(l):
        es = ExitStack()
        P.es = es
        cvt = [P.sb([128, 8, 512], BF16, "cvt%d" % i) for i in range(4)]
        cvd = [P.sb([128, NFF, 128], BF16, "cvd%d" % i) for i in range(4)]
        ncv = [0, 0]
        for fi, pre in enumerate(("ffn1", "ffn2")):
            for gu, nm in enumerate(("_wg", "_wu")):
                wap = ffn_w[pre + nm][l]
                for pi_, (f0, fw) in enumerate(_ffn_pieces()):
                    bi = ncv[0] % 4
                    ncv[0] += 1
                    P.dma("pool", cvt[bi][:, :, :fw * 128],
                          wap[:, f0 * 128:(f0 + fw) * 128].rearrange("(k p) j -> p k j", p=128), writes=[("cvt", bi)])
                    P.dma("sp", WGU[0, fi, gu, pi_][:, :, :fw * 128], cvt[bi][:, :, :fw * 128], reads=[("cvt", bi)],
                          writes=[("WGU", fi)])
            wap = ffn_w[pre + "_wd"][l]
            for c in range(8):
                bi = ncv[1] % 4
                ncv[1] += 1
                P.dma("pool", cvd[bi][:], wap[:, c * 128:(c + 1) * 128].rearrange("(f p) j -> p f j", p=128),
                      writes=[("cvd", bi)])
                P.dma("sp", WDS[0, fi, c], cvd[bi][:], reads=[("cvd", bi)], writes=[("WDS", fi)])
        P.barrier()
        es.close()

    def hsrc(st, t0, n):
        return (HC if st == "c" else HX)[:, :, t0:t0 + n].rearrange("c p t -> p c t")

    cnts = {"g": 0, "wgu": 0, "wd": 0, "ps": 0}

    def mod_scalars(l, sub, s):
        j0 = 3 * sub * 8
        coef = (0.5 if sub != 1 else 1.0) / ALPHA
        P.op("dve", "tensor_scalar", scl[:, 0, :], mT[:, l, j0 + 8:j0 + 16, s], 1.0, None, ALU.add,
             reads=["mT"], writes=["scl0"])
        P.op("dve", "tensor_copy", scl[:, 1, :], mT[:, l, j0:j0 + 8, s], reads=["mT"], writes=["scl1"])
        P.op("dve", "tensor_scalar", scl[:, 2, :], mT[:, l, j0 + 16:j0 + 24, s], coef, None, ALU.mult,
             reads=["mT"], writes=["scl2"])

    def alloc_ln_tiles():
        t = {}
        t["vv"] = P.sb([128, 8, TG], F32, "vv")
        t["yo"] = P.sb([128, 8, TG], F32, "yo")
        t["vsq"] = [P.sb([128, TG], F32, "vsq%d" % i) for i in range(2)]
        t["mean_sb"] = P.sb([128, TG], F32, "mean_sb")
        t["msq"] = P.sb([128, TG], F32, "msq")
        t["var"] = P.sb([128, TG], F32, "var")
        t["rstd"] = P.sb([128, TG], F32, "rstd")
        t["tt"] = [P.sb([128, TG], F32, "tt%d" % i) for i in range(2)]
        return t

    def layer_norm_store(t, st, t0, n, l, sub):
        vv, yo, vsq, mean_sb, msq, var, rstd, tt = (t[k] for k in ("vv", "yo", "vsq", "mean_sb", "msq", "var", "rstd", "tt"))
        pmean, pev2 = pS[0], pS[1]
        for c in range(8):
            q = vsq[c % 2]
            qk = "vsq%d" % (c % 2)
            P.op("act", "activation", q[:, :n], vv[:, c, :n], AF.Square, reads=[("vv", c)], writes=[qk])
            P.op("pe", "matmul", pmean[:, :n], onesm[:], vv[:, c, :n], start=(c == 0), stop=(c == 7),
                 reads=["onesm", ("vv", c)], writes=["pS0"], inc=(c == 7))
            P.op("pe", "matmul", pev2[:, :n], onesm[:], q[:, :n], start=(c == 0), stop=(c == 7),
                 reads=["onesm", qk], writes=["pS1"], inc=True)
        P.op("act", "activation", mean_sb[:, :n], pmean[:, :n], AF.Copy, reads=["pS0"], writes=["mean_sb"])
        P.op("dve", "tensor_tensor", msq[:, :n], mean_sb[:, :n], mean_sb[:, :n], ALU.mult,
             reads=["mean_sb"], writes=["msq"])
        P.op("dve", "tensor_tensor", var[:, :n], pev2[:, :n], msq[:, :n], ALU.subtract,
             reads=["pS1", "msq"], writes=["var"])
        P.op("act", "activation", var[:, :n], var[:, :n], AF.Sqrt, bias=epsb, reads=["var", "kc"], writes=["var"])
        P.op("dve", "reciprocal", rstd[:, :n], var[:, :n], reads=["var"], writes=["rstd"])
        for c in range(8):
            tq = tt[c % 2]
            tk = "tt%d" % (c % 2)
            P.op("dve", "tensor_tensor", tq[:, :n], vv[:, c, :n], mean_sb[:, :n], ALU.subtract,
                 reads=[("vv", c), "mean_sb"], writes=[tk])
            P.op("dve", "tensor_tensor", tq[:, :n], tq[:, :n], rstd[:, :n], ALU.mult,
                 reads=[tk, "rstd"], writes=[tk])
            P.op("act", "activation", yo[:, c, :n], tq[:, :n], AF.Identity,
                 scale=lnp[:, l, sub, 0, c:c + 1], bias=lnp[:, l, sub, 1, c:c + 1],
                 reads=[tk, "lnp"], writes=[("yo", c)])
        return P.dma("pool" if USE_WSCRATCH else "sp", hsrc(st, t0, n), yo[:, :, :n], reads=[("yo", c) for c in range(8)],
                     writes=[("H", st, t0)])

    def ffn_sublayer(l, sub, wg_ap, wu_ap, wd_ap, do_ctx=True):
        fi = 0 if sub == 0 else 1
        es = ExitStack()
        P.es = es
        hT = [P.sb([128, 8, TG], F32, "hT%d" % i) for i in range(2)]
        uTb = [P.sb([128, 8, TG], BF16, "uT%d" % i) for i in range(2)]
        hff = P.sb([128, NFF, TG], BF16, "hff")
        lt = alloc_ln_tiles()
        vv = lt["vv"]
        sg = [P.sb([128, TG], F32, "sg%d" % i) for i in range(2)]
        wgu = [P.sb([128, 2, 8, 512], BF16, "wgu%d" % i) for i in range(2)]
        wd = [P.sb([128, NFF, 128], BF16, "wd%d" % i) for i in range(3)]
        glist = [g for g in groups if not (g[0] == "c" and not do_ctx)]
        cur = {"s": None}
        dq = "pool" if USE_WSCRATCH else "sp"

        def a_load(j):
            st_, t0_, n_ = glist[j]
            P.dma(dq, hT[j % 2][:, :, :n_], hsrc(st_, t0_, n_), reads=[("H", st_, t0_)], writes=["hT%d" % (j % 2)])

        def a_mod(j):
            st_, t0_, n_ = glist[j]
            s_ = 1 if st_ == "c" else 0
            if s_ != cur["s"]:
                mod_scalars(l, sub, s_)
                cur["s"] = s_
            for c in range(8):
                P.op("act", "activation", uTb[j % 2][:, c, :n_], hT[j % 2][:, c, :n_], AF.Identity,
                     scale=scl[:, 0, c:c + 1], bias=scl[:, 1, c:c + 1],
                     reads=["hT%d" % (j % 2), "scl0", "scl1"], writes=[("uT", j % 2, c)])

        a_load(0)
        a_mod(0)
        for j, (st, t0, n) in enumerate(glist):
            if j + 1 < len(glist):
                a_load(j + 1)
            h = hT[j % 2]
            hk = "hT%d" % (j % 2)
            uT = uTb[j % 2]
            ub = j % 2
            for pi_, (f0, fw) in enumerate(_ffn_pieces()):
                wi = cnts["wgu"] % 2
                cnts["wgu"] += 1
                wt = wgu[wi]
                if USE_WSCRATCH:
                    P.dma("sp", wt[:, 0, :, :fw * 128], WGU[0, fi, 0, pi_][:, :, :fw * 128], writes=[("wgu", wi, 0)])
                    P.dma("sp", wt[:, 1, :, :fw * 128], WGU[0, fi, 1, pi_][:, :, :fw * 128], writes=[("wgu", wi, 1)])
                else:
                    P.dma("pool", wt[:, 0, :, :fw * 128],
                          wg_ap[:, f0 * 128:(f0 + fw) * 128].rearrange("(k p) j -> p k j", p=128), writes=[("wgu", wi, 0)])
                    P.dma("pool", wt[:, 1, :, :fw * 128],
                          wu_ap[:, f0 * 128:(f0 + fw) * 128].rearrange("(k p) j -> p k j", p=128), writes=[("wgu", wi, 1)])
                for ff in range(fw):
                    f = f0 + ff
                    pi = cnts["ps"] % 2
                    cnts["ps"] += 1
                    for k in range(8):
                        P.op("pe", "matmul", pG[pi][:, :n], wt[:, 0, k, ff * 128:(ff + 1) * 128], uT[:, k, :n],
                             start=(k == 0), stop=(k == 7), reads=[("wgu", wi, 0), ("uT", ub, k)], writes=["pG%d" % pi],
                             inc=(k == 7))
                    for k in range(8):
                        P.op("pe", "matmul", pU[pi][:, :n], wt[:, 1, k, ff * 128:(ff + 1) * 128], uT[:, k, :n],
                             start=(k == 0), stop=(k == 7), reads=[("wgu", wi, 1), ("uT", ub, k)], writes=["pU%d" % pi],
                             inc=(k == 7))
                    P.op("act", "activation", sg[pi][:, :n], pG[pi][:, :n], AF.Silu,
                         reads=["pG%d" % pi], writes=["sg%d" % pi])
                    P.op("dve", "tensor_tensor", hff[:, f, :n], sg[pi][:, :n], pU[pi][:, :n], ALU.mult,
                         reads=["sg%d" % pi, "pU%d" % pi], writes=[("hff", f)])
            for c in range(8):
                wi = cnts["wd"] % 3
                cnts["wd"] += 1
                if USE_WSCRATCH:
                    P.dma("sp", wd[wi][:], WDS[0, fi, c], writes=[("wd", wi)])
                else:
                    P.dma("pool", wd[wi][:], wd_ap[:, c * 128:(c + 1) * 128].rearrange("(f p) j -> p f j", p=128),
                          writes=[("wd", wi)])
                pi = c % 2
                for f in range(NFF):
                    P.op("pe", "matmul", pD[pi][:, :n], wd[wi][:, f, :], hff[:, f, :n],
                         start=(f == 0), stop=(f == NFF - 1), reads=[("wd", wi), ("hff", f)], writes=["pD%d" % pi],
                         inc=(f == NFF - 1))
                P.op("dve", "scalar_tensor_tensor", vv[:, c, :n], pD[pi][:, :n], scl[:, 2, c:c + 1], h[:, c, :n],
                     ALU.mult, ALU.add, reads=["pD%d" % pi, "scl2", hk], writes=[("vv", c)])
            if j + 1 < len(glist):
                a_mod(j + 1)
            layer_norm_store(lt, st, t0, n, l, sub)
        P.barrier()
        es.close()

    def mm(out, lhsT, rhs, reads, writes, start=True, stop=True, inc=True):
        P.op("pe", "matmul", out, lhsT, rhs, start=start, stop=stop, reads=reads, writes=writes, inc=inc)

    def act(out, in_, func, reads, writes, **kw):
        P.op("act", "activation", out, in_, func, reads=reads, writes=writes, **kw)

    def tt_(out, a, b, op, reads, writes, e="dve"):
        P.op(e, "tensor_tensor", out, a, b, op, reads=reads, writes=writes)

    def ts_(out, a, s1, s2, op0, op1, reads, writes, e="dve"):
        if op1 is None:
            P.op(e, "tensor_scalar", out, a, s1, None, op0, reads=reads, writes=writes)
        else:
            P.op(e, "tensor_scalar", out, a, s1, s2, op0, op1, reads=reads, writes=writes)

    def stt(out, a, s, b, op0, op1, reads, writes):
        P.op("dve", "scalar_tensor_tensor", out, a, s, b, op0, op1, reads=reads, writes=writes)

    def tr(out, in_, reads, writes, np_=128, inc=True):
        P.op("pe", "transpose", out, in_, ident[:np_, :np_], reads=list(reads) + ["cst"], writes=writes, inc=inc)

    def in_pieces():
        out = []
        c = 0
        while c < DIN:
            w = min(512, DIN - c)
            out.append((c, w))
            c += w
        return out

    def inproj(l, do_ctx=True):
        es = ExitStack()
        P.es = es
        hT = [P.sb([128, 8, TG], F32, "hT%d" % i) for i in range(2)]
        uT = P.sb([128, 8, TG], BF16, "uT")
        wt_ = [P.sb([128, 8, 512], BF16, "wi%d" % i) for i in range(2)]
        stg = [P.sb([128, 512], F32, "stg%d" % i) for i in range(4)]
        nst = [0]
        cur_s = None
        for (st, t0, n) in groups:
            s = 1 if st == "c" else 0
            off = toff(st, t0)
            if s != cur_s:
                mod_scalars(l, 1, s)
                cur_s = s
            gi = cnts["g"]
            cnts["g"] += 1
            h = hT[gi % 2]
            hk = "hT%d" % (gi % 2)
            P.dma("sp", h[:, :, :n], hsrc(st, t0, n), reads=[("H", st, t0)], writes=[hk])
            for c in range(8):
                act(uT[:, c, :n], h[:, c, :n], AF.Identity, [hk, "scl0", "scl1"], [("uT", c)],
                    scale=scl[:, 0, c:c + 1], bias=scl[:, 1, c:c + 1])
            for (c0, w) in in_pieces():
                wi = cnts["wgu"] % 2
                cnts["wgu"] += 1
                wt = wt_[wi]
                wk = ("wi", wi)
                P.dma("pool", wt[:, :, :w], w_in[l][:, c0:c0 + w].rearrange("(k p) j -> p k j", p=128), writes=[wk])
                j = 0
                while j < w:
                    cw = min(128, w - j)
                    a0, a1 = c0 + j, c0 + j + cw
                    if not any(a0 < hi_ and a1 > lo_ for (lo_, hi_) in ((0, 256), (768, 800), (1568, DIN))):
                        j += cw
                        continue
                    pi = cnts["ps"] % 2
                    cnts["ps"] += 1
                    for k in range(8):
                        mm(pG[pi][:cw, :n], wt[:, k, j:j + cw], uT[:, k, :n], [wk, ("uT", k)], ["pG%d" % pi],
                           start=(k == 0), stop=(k == 7), inc=(k == 7))
                    si = nst[0] % 4
                    nst[0] += 1
                    act(stg[si][:cw, :n], pG[pi][:cw, :n], AF.Copy, ["pG%d" % pi], [("stg", si)])
                    P.dma("sp", PF[c0 + j:c0 + j + cw, off:off + n], stg[si][:cw, :n], reads=[("stg", si)],
                          writes=[("PF", off)])
                    j += cw
                wm = min(w, max(0, 1568 - c0))
                for ts in range(n // 128 if wm > 0 else 0):
                    pi = cnts["ps"] % 2
                    cnts["ps"] += 1
                    for k in range(8):
                        mm(pU[pi][:, :wm], uT[:, k, ts * 128:(ts + 1) * 128], wt[:, k, :wm], [wk, ("uT", k)],
                           ["pU%d" % pi], start=(k == 0), stop=(k == 7), inc=(k == 7))
                    si = nst[0] % 4
                    nst[0] += 1
                    P.op("dve", "tensor_copy", stg[si][:, :wm], pU[pi][:, :wm], reads=["pU%d" % pi], writes=[("stg", si)])
                    P.dma("sp", PM[off + ts * 128:off + (ts + 1) * 128, c0:c0 + wm], stg[si][:, :wm],
                          reads=[("stg", si)], writes=[("PM", off)])
        P.barrier()
        es.close()

    def swa(l, do_ctx):
        es = ExitStack()
        P.es = es
        kT_all = P.sb([64, 2, T], BF16, "kT_all")
        V_all = P.sb([128, NTL, 2, 65], BF16, "V_all")
        esink = P.sb([128, 8], F32, "esink")
        sk = P.sb([128, L, 8], F32, "sk")
        P.dma("sp", sk[:], sinkB, writes=["sk"])
        act(esink[:], sk[:, l, :], AF.Exp, ["sk"], ["esink"])
        P.op("dve", "memset", V_all[:], 1.0, writes=["V_all"])
        km = [P.sb([128, 2, 64], F32, "km%d" % i) for i in range(2)]
        vm = [P.sb([128, 2, 64], F32, "vm%d" % i) for i in range(2)]
        rp = [P.sb([128, 2, 32], F32, "rp%d" % i) for i in range(2)]
        kr = P.sb([128, 2, 64], F32, "kr")
        qm = [P.sb([128, 8, 64], F32, "qm%d" % i) for i in range(2)]
        qr = P.sb([128, 8, 64], F32, "qr")
        ta = P.sb([128, 8, 32], F32, "ta")
        tb = P.sb([128, 8, 32], F32, "tb")
        qT = P.sb([64, 8, 128], BF16, "qT")
        pT = [P.sb([128, 512], BF16, "pT%d" % i) for i in range(10)]
        den = P.sb([128, 4], F32, "den")
        mixb = [P.sb([128, 512], F32, "mixb%d" % i) for i in range(2)]

        def rope(dst, src, nh, tab, rk_src, rk_dst, rk_tab):
            sv = src.rearrange("p h (a s i) -> p h a s i", a=2, s=2)
            dv = dst.rearrange("p h (a s i) -> p h a s i", a=2, s=2)
            cos = tab[:, 0, :].rearrange("p (a i) -> p a i", a=2).unsqueeze(1).to_broadcast([128, nh, 2, 16])
            sin = tab[:, 1, :].rearrange("p (a i) -> p a i", a=2).unsqueeze(1).to_broadcast([128, nh, 2, 16])
            tav = ta[:, :nh, :].rearrange("p h (a i) -> p h a i", a=2)
            tbv = tb[:, :nh, :].rearrange("p h (a i) -> p h a i", a=2)
            tt_(tav, sv[:, :, :, 0, :], cos, ALU.mult, [rk_src, rk_tab], ["ta"])
            tt_(tbv, sv[:, :, :, 1, :], sin, ALU.mult, [rk_src, rk_tab], ["tb"])
            tt_(dv[:, :, :, 0, :], tav, tbv, ALU.subtract, ["ta", "tb"], [rk_dst])
            tt_(tav, sv[:, :, :, 0, :], sin, ALU.mult, [rk_src, rk_tab], ["ta"])
            tt_(tbv, sv[:, :, :, 1, :], cos, ALU.mult, [rk_src, rk_tab], ["tb"])
            tt_(dv[:, :, :, 1, :], tav, tbv, ALU.add, ["ta", "tb"], [rk_dst])

        for tt in range(NTL):
            i2 = tt % 2
            r0 = tt * 128
            P.dma("sp", km[i2][:], PM[r0:r0 + 128, 1312:1440].rearrange("t (h d) -> t h d", h=2),
                  reads=[("PM", 0)], writes=[("km", i2)])
            P.dma("sp", vm[i2][:], PM[r0:r0 + 128, 1440:1568].rearrange("t (h d) -> t h d", h=2),
                  reads=[("PM", 0)], writes=[("vm", i2)])
            P.op("dve", "tensor_copy", V_all[:, tt, :, 0:64], vm[i2][:], reads=[("vm", i2)], writes=["V_all"])
            if r0 >= LC:
                P.dma("sp", rp[i2][:], ropeM[r0 - LC:r0 - LC + 128], writes=[("rp", i2)])
                rope(kr[:], km[i2][:], 2, rp[i2], ("km", i2), "kr", ("rp", i2))
                ksrc, kk_ = kr, "kr"
            else:
                ksrc, kk_ = km[i2], ("km", i2)
            for hk in range(2):
                tr(pS[0][:64, hk * 128:(hk + 1) * 128], ksrc[:, hk, :], [kk_], ["pS0"], inc=(hk == 1))
            act(kT_all[:, :, r0:r0 + 128], pS[0][:64, 0:256].rearrange("p (h t) -> p h t", h=2), AF.Copy,
                ["pS0"], ["kT_all"])
        nmix = 0
        for tt in range(NTL):
            r0 = tt * 128
            isx = r0 >= LC
            if not isx and not do_ctx:
                continue
            i2 = tt % 2
            P.dma("sp", qm[i2][:], PM[r0:r0 + 128, 800:1312].rearrange("t (h d) -> t h d", h=8),
                  reads=[("PM", 0)], writes=[("qm", i2)])
            if isx:
                P.dma("sp", rp[i2][:], ropeM[r0 - LC:r0 - LC + 128], writes=[("rp", i2)])
                rope(qr[:], qm[i2][:], 8, rp[i2], ("qm", i2), "qr", ("rp", i2))
                qsrc, qk_ = qr, "qr"
            else:
                qsrc, qk_ = qm[i2], ("qm", i2)
            for half in range(2):
                for hh in range(4):
                    tr(pS[half][:64, hh * 128:(hh + 1) * 128], qsrc[:, half * 4 + hh, :], [qk_], ["pS%d" % half],
                       inc=(hh == 3))
                act(qT[:, half * 4:half * 4 + 4, :], pS[half][:64, :].rearrange("p (h t) -> p h t", h=4), AF.Copy,
                    ["pS%d" % half], ["qT"])
            keys = []
            if isx:
                nct = LC // 128
                if tt - 1 >= nct:
                    keys.append((tt - 1, 0))
                keys.append((tt, None))
                if tt + 1 < NTL:
                    keys.append((tt + 1, 1))
            keys += [(c, None) for c in range(LC // 128)]
            mb = mixb[nmix % 2]
            mbk = ("mixb", nmix % 2)
            nmix += 1
            for hk in range(2):
                for ki, (kt, mid) in enumerate(keys):
                    pi = cnts["ps"] % 2
                    cnts["ps"] += 1
                    for g in range(4):
                        mm(pG[pi][:, g * 128:(g + 1) * 128], kT_all[:, hk, kt * 128:(kt + 1) * 128], qT[:, hk * 4 + g, :],
                           ["kT_all", "qT"], ["pG%d" % pi], inc=(g == 3))
                    pt = pT[hk * 5 + ki]
                    ptk = ("pT", hk * 5 + ki)
                    act(pt[:], pG[pi][:], AF.Exp, ["pG%d" % pi], [ptk], scale=0.125)
                    if mid is not None:
                        mk = cst[:, C_SWM + mid * 128:C_SWM + (mid + 1) * 128].unsqueeze(1).to_broadcast([128, 4, 128])
                        tt_(pt[:].rearrange("p (g q) -> p g q", g=4), pt[:].rearrange("p (g q) -> p g q", g=4), mk,
                            ALU.mult, [ptk, "cst"], [ptk])
                po = pD[hk]
                for g in range(4):
                    for ki, (kt, mid) in enumerate(keys):
                        mm(po[:, g * 65:(g + 1) * 65], pT[hk * 5 + ki][:, g * 128:(g + 1) * 128], V_all[:, kt, hk, :],
                           [("pT", hk * 5 + ki), "V_all"], ["pD%d" % hk], start=(ki == 0), stop=(ki == len(keys) - 1),
                           inc=(ki == len(keys) - 1 and g == 3))
                pov = po[:, 0:260].rearrange("p (g e) -> p g e", g=4)
                tt_(den[:], pov[:, :, 64], esink[:, hk * 4:hk * 4 + 4], ALU.add, ["pD%d" % hk, "esink"], ["den"])
                P.op("dve", "reciprocal", den[:], den[:], reads=["den"], writes=["den"])
                tt_(mb[:, hk * 256:(hk + 1) * 256].rearrange("p (g e) -> p g e", g=4), pov[:, :, 0:64],
                    den[:].unsqueeze(2).to_broadcast([128, 4, 64]), ALU.mult, ["pD%d" % hk, "den"], [mbk])
            P.dma("sp", MIX[r0:r0 + 128, 256:768], mb[:], reads=[mbk], writes=[("MIX", "b", tt)])
        P.barrier()
        es.close()

    def run_interleaved(gens):
        gens = list(gens)
        while gens:
            for g in list(gens):
                try:
                    next(g)
                except StopIteration:
                    gens.remove(g)

    def chunk_order(d):
        nc_c = LC // 64
        if d == 0:
            return list(range(NCH))
        return list(range(nc_c - 1, -1, -1)) + list(range(NCH - 1, nc_c - 1, -1))

    def gla(l, do_ctx, extra=()):
        es = ExitStack()
        P.es = es
        upa = P.sb([17, 2, 128], F32, "upa")
        P.dma("sp", upa[:], gla_up[l].rearrange("d r c -> r d c"), writes=["upa"])
        gng = P.sb([128, 256], F32, "gng")
        P.dma("sp", gng[:], gla_g[l].partition_broadcast(128), writes=["gng"])
        Sx = [P.sb([128, 256], F32, "Sx%d" % d) for d in range(2)]
        for d in range(2):
            P.op("dve", "memset", Sx[d][:], 0.0, writes=[("Sx", d)])
        triI = [cst[0:64, C_TRI + d * 64:C_TRI + (d + 1) * 64] for d in range(2)]
        incl = [cst[0:64, C_M01 + d * 64:C_M01 + (d + 1) * 64] for d in range(2)]
        gbd = cst[:, C_GBD:C_GBD + 256]
        pb = {0: (pG[0], "pG0", pU[0], "pU0", pD[0], "pD0"), 1: (pG[1], "pG1", pU[1], "pU1", pD[1], "pD1")}

        def unit(d):
            pa, pak, pbb, pbk, pc_, pck = pb[d]
            tl = {}
            for nm, shp in (("qT", [128, 64]), ("kT", [128, 64]), ("zT", [17, 64]), ("kM", [64, 128]), ("vM", [64, 256])
                            ):
                tl[nm] = [P.sb(shp, F32, "g%s%d_%d" % (nm, d, i)) for i in range(2)]
            for i in range(2):
                P.op("dve", "memset", tl["zT"][i][:], 1.0, writes=[("zT", d, i)])
            sp_ = P.sb([64, 128], F32, "gsp%d" % d)
            ebT = P.sb([128, 64], F32, "gebT%d" % d)
            enbT = P.sb([128, 64], F32, "genbT%d" % d)
            enbM = P.sb([64, 128], F32, "genbM%d" % d)
            qs = P.sb([128, 64], F32, "gqs%d" % d)
            kmk = P.sb([128, 4, 64], F32, "gkmk%d" % d)
            KtM = P.sb([64, 128], F32, "gKtM%d" % d)
            att = P.sb([64, 4, 64], F32, "gatt%d" % d)
            tmpS = P.sb([128, 256], F32, "gtmpS%d" % d)
            osb = P.sb([64, 4, 64], F32, "gosb%d" % d)
            osq = P.sb([64, 4, 64], F32, "gosq%d" % d)
            ssq = P.sb([64, 4], F32, "gssq%d" % d)
            sga = P.sb([64, 256], F32, "gsga%d" % d)
            K = lambda s: (s, d)
            for ui, c in enumerate(chunk_order(d)):
                i2 = ui % 2
                r0 = c * 64
                isx = r0 >= LC
                KB = lambda s: (s, d, i2)
                P.dma("sp", tl["qT"][i2][:], PF[0:128, r0:r0 + 64], reads=[("PF", 0)], writes=[KB("qT")])
                P.dma("sp", tl["kT"][i2][:], PF[128:256, r0:r0 + 64], reads=[("PF", 0)], writes=[KB("kT")])
                P.dma("sp", tl["zT"][i2][0:16, :], PF[768 + 16 * d:784 + 16 * d, r0:r0 + 64], reads=[("PF", 0)],
                      writes=[KB("zT")])
                P.dma("sp", tl["kM"][i2][:], PM[r0:r0 + 64, 128:256], reads=[("PM", 0)], writes=[KB("kM")])
                P.dma("sp", tl["vM"][i2][:], PM[r0:r0 + 64, 256:512], reads=[("PM", 0)], writes=[KB("vM")])
                qT, kT, zT, kM, vM = (tl[n_][i2] for n_ in ("qT", "kT", "zT", "kM", "vM"))
                yield
                mm(pa[:64, 0:128], zT[:], upa[:, d, :], [KB("zT"), "upa"], [pak])
                act(sp_[:], pa[:64, 0:128], AF.Exp, [pak], [K("sp")], scale=-1.0)
                act(sp_[:], sp_[:], AF.Ln, [K("sp"), "kc"], [K("sp")], bias=kc[:64, 1:2])
                yield
                mm(pa[:, 128:192], sp_[:], triI[d], [K("sp"), "cst"], [pak], inc=False)
                mm(pa[:64, 256:384], triI[d], sp_[:], [K("sp"), "cst"], [pak])
                act(ebT[:], pa[:, 128:192], AF.Exp, [pak], [K("ebT")], scale=1.0 / 16)
                act(enbT[:], pa[:, 128:192], AF.Exp, [pak], [K("enbT")], scale=-1.0 / 16)
                act(enbM[:], pa[:64, 256:384], AF.Exp, [pak], [K("enbM")], scale=-1.0 / 16)
                yield
                stt(qs[:], qT[:], 32.0 ** -0.5, ebT[:], ALU.mult, ALU.mult, [KB("qT"), K("ebT")], [K("qs")])
                for h in range(4):
                    stt(kmk[:, h, :], kT[:], cst[:, C_GHM + h:C_GHM + h + 1], enbT[:], ALU.mult, ALU.mult,
                        [KB("kT"), K("enbT"), "cst"], [K("kmk")])
                tt_(KtM[:], kM[:], enbM[:], ALU.mult, [KB("kM"), K("enbM")], [K("KtM")])
                yield
                for h in range(4):
                    mm(pbb[:64, h * 64:(h + 1) * 64], kmk[:, h, :], qs[:], [K("kmk"), K("qs")], [pbk], inc=(h == 3))
                tt_(att[:], pbb[:64, 0:256].rearrange("p (h t) -> p h t", h=4),
                    incl[d].unsqueeze(1).to_broadcast([64, 4, 64]), ALU.mult, [pbk, "cst"], [K("att")])
                yield
                for h in range(4):
                    mm(pc_[:64, h * 64:(h + 1) * 64], att[:, h, :], vM[:, h * 64:(h + 1) * 64], [K("att"), KB("vM")], [pck],
                       start=True, stop=False, inc=False)
                    mm(pc_[:64, h * 64:(h + 1) * 64], qs[:], Sx[d][:, h * 64:(h + 1) * 64], [K("qs"), ("Sx", d)], [pck],
                       start=False, stop=True, inc=(h == 3))
                mm(pbb[:, 256:512], KtM[:], vM[:], [K("KtM"), KB("vM")], [pbk])
                yield
                tt_(tmpS[:], pbb[:, 256:512], gbd, ALU.mult, [pbk, "cst"], [K("tmpS")])
                tt_(tmpS[:], tmpS[:], Sx[d][:], ALU.add, [K("tmpS"), ("Sx", d)], [K("tmpS")])
                last = 63 if d == 0 else 0
                ts_(Sx[d][:], tmpS[:], ebT[:, last:last + 1], None, ALU.mult, None, [K("tmpS"), K("ebT")], [("Sx", d)])
                P.op("dve", "tensor_copy", osb[:].rearrange("p h e -> p (h e)"), pc_[:64, 0:256], reads=[pck],
                     writes=[K("osb")])
                P.dma("sp", OGD[d, r0:r0 + 64, :], osb[:].rearrange("p h e -> p (h e)"), reads=[K("osb")],
                      writes=[("OGD", d, c)])
                yield

        run_interleaved([unit(0), unit(1)] + list(extra))
        of_ = [P.sb([128, 4, 64], F32, "gof%d" % i) for i in range(2)]
        ob_ = [P.sb([128, 4, 64], F32, "gob%d" % i) for i in range(2)]
        ga_ = [P.sb([128, 256], F32, "gga%d" % i) for i in range(2)]
        fsq = P.sb([128, 4, 64], F32, "gfsq")
        fss = P.sb([128, 4], F32, "gfss")
        ni = 0
        for tt in range(NTL):
            r0 = tt * 128
            if r0 < LC and not do_ctx:
                continue
            i2 = ni % 2
            ni += 1
            P.dma("sp", of_[i2][:].rearrange("p h e -> p (h e)"), OGD[0, r0:r0 + 128, :],
                  reads=[("OGD", 0, 2 * tt), ("OGD", 0, 2 * tt + 1)], writes=[("gof", i2)])
            P.dma("sp", ob_[i2][:].rearrange("p h e -> p (h e)"), OGD[1, r0:r0 + 128, :],
                  reads=[("OGD", 1, 2 * tt), ("OGD", 1, 2 * tt + 1)], writes=[("gob", i2)])
            P.dma("sp", ga_[i2][:], PM[r0:r0 + 128, 512:768], reads=[("PM", 0)], writes=[("gga", i2)])
            o = of_[i2]
            ok = ("gof", i2)
            ov = o[:].rearrange("p h e -> p (h e)")
            tt_(o[:], o[:], ob_[i2][:], ALU.add, [ok, ("gob", i2)], [ok])
            tt_(fsq[:], o[:], o[:], ALU.mult, [ok], ["gfsq"])
            P.op("dve", "tensor_reduce", fss[:], fsq[:], AX.X, ALU.add, reads=["gfsq"], writes=["gfss"])
            act(fss[:], fss[:], AF.Sqrt, ["gfss", "kc"], ["gfss"], scale=1.0 / 64, bias=kc[:, 3:4])
            P.op("dve", "reciprocal", fss[:], fss[:], reads=["gfss"], writes=["gfss"])
            tt_(o[:], o[:], fss[:].unsqueeze(2).to_broadcast([128, 4, 64]), ALU.mult, [ok, "gfss"], [ok])
            tt_(ov, ov, gng[:], ALU.mult, [ok, "gng"], [ok])
            act(ga_[i2][:], ga_[i2][:], AF.Silu, [("gga", i2)], [("gga", i2)])
            tt_(ov, ov, ga_[i2][:], ALU.mult, [ok, ("gga", i2)], [ok])
            P.dma("sp", MIX[r0:r0 + 128, 0:256], ov, reads=[ok], writes=[("MIX", "a", tt)])
        P.barrier()
        es.close()

    def rwkv_prep(l):
        es = ExitStack()
        P.es = es
        run_interleaved([rwkv_prep_gen(l)])
        P.barrier()
        es.close()

    def rwkv_prep_gen(l):
        rv = P.sb([128, 2, 7], F32, "rv")
        P.dma("sp", rv[:], rw_vec[:, l], writes=["rv"])
        nw0 = P.sb([128, 2, 2], F32, "nw0")
        ts_(nw0[:], rv[:, :, 3:5], -1.0, None, ALU.mult, None, ["rv"], ["nw0"])
        mu = P.sb([128, 11], F32, "mu")
        P.dma("sp", mu[:], rw_mu[:, l], writes=["mu"])
        wup = P.sb([64, 2, 256], F32, "wup")
        P.dma("sp", wup[:], rw_wup[l].rearrange("d r c -> r d c"), writes=["wup"])
        aup = P.sb([64, 2, 256], F32, "aup")
        P.dma("sp", aup[:], rw_aup[l].rearrange("d r c -> r d c"), writes=["aup"])
        bo = cst[:, C_BO:C_BO + 128]
        NB = 16
        bufs = [P.sb([128, TG + 2], F32, "rb%d" % i) for i in range(NB)]
        fbufs = [P.sb([128, TG + 2], F32, "rf%d" % i) for i in range(13)]
        nb = [0]

        def newb():
            i = nb[0] % NB
            nb[0] += 1
            return bufs[i], ("rb", i)

        R0 = 1568
        srcs = [(R0 + 128 * j, 128) for j in range(6)] + [(R0 + 768, 64), (R0 + 832, 64), (R0 + 896, 64), (R0 + 960, 64),
                                                          (R0 + 1024, 128)]
        for (st, t0, n) in groups:
            off = toff(st, t0)
            seq_lo = 0 if st == "c" else LC
            seq_hi = LC if st == "c" else T
            lo = max(off - 1, seq_lo)
            hi = min(off + n + 1, seq_hi)
            f = []
            for j, (r0, nr) in enumerate(srcs):
                ld, ldk = newb()
                P.op("dve", "memset", ld[:, 0:n + 2], 0.0, writes=[ldk])
                P.dma("sp", ld[:nr, lo - off + 1:hi - off + 1], PF[r0:r0 + nr, lo:hi], reads=[("PF", 0)], writes=[ldk])
                sh, shk = newb()
                tt_(sh[:nr, :n], ld[:nr, 0:n], ld[:nr, 2:n + 2], ALU.add, [ldk], [shk])
                stt(sh[:nr, :n], sh[:nr, :n], 0.5, ld[:nr, 1:n + 1], ALU.mult, ALU.subtract, [shk, ldk], [shk])
                fo, fok = fbufs[j], ("rf", j)
                stt(fo[:nr, :n], sh[:nr, :n], mu[:nr, j:j + 1], ld[:nr, 1:n + 1], ALU.mult, ALU.add, [shk, ldk, "mu"], [fok])
                f.append((fo, fok))
                yield
            rr, kk_, vv_ = f[0:2], f[2:4], f[4:6]
            zw, za, zg = f[6:8], f[8:10], f[10]
            for pc in range(2):
                P.dma("sp", RF[0 + pc, :, off:off + n], rr[pc][0][:, :n], reads=[rr[pc][1]], writes=[("RF", off)])
                P.dma("sp", RF[2 + pc, :, off:off + n], vv_[pc][0][:, :n], reads=[vv_[pc][1]], writes=[("RF", off)])
            kkn = []
            for pc in range(2):
                k_, kk2 = kk_[pc]
                sq, sqk = newb()
                act(sq[:, :n], k_[:, :n], AF.Square, [kk2, "rv"], [sqk], scale=rv[:, pc, 0:1])
                mm(pS[0][:, :n], bo, sq[:, :n], ["cst", sqk], ["pS0"])
                act(sq[:, :n], pS[0][:, :n], AF.Sqrt, ["pS0"], [sqk])
                ts_(sq[:, :n], sq[:, :n], 1e-12, None, ALU.max, None, [sqk], [sqk])
                P.op("dve", "reciprocal", sq[:, :n], sq[:, :n], reads=[sqk], writes=[sqk])
                kn, knk = fbufs[11 + pc], ("rf", 11 + pc)
                stt(kn[:, :n], k_[:, :n], rv[:, pc, 0:1], sq[:, :n], ALU.mult, ALU.mult, [kk2, "rv", sqk], [knk])
                P.dma("sp", RF[4 + pc, :, off:off + n], kn[:, :n], reads=[knk], writes=[("RF", off)])
                kkn.append((kn, knk))
                yield
            ksum = [None, None]
            for d in range(2):
                th, thk = newb()
                act(th[:64, :n], zw[d][0][:64, :n], AF.Tanh, [zw[d][1]], [thk])
                for pc in range(2):
                    mm(pS[1][:, :n], wup[:, d, pc * 128:(pc + 1) * 128], th[:64, :n], ["wup", thk], ["pS1"])
                    ew, ewk = newb()
                    act(ew[:, :n], pS[1][:, :n], AF.Exp, ["pS1", "nw0"], [ewk], scale=-1.0, bias=nw0[:, pc, d:d + 1])
                    act(ew[:, :n], ew[:, :n], AF.Ln, [ewk, "kc"], [ewk], bias=kc[:, 1:2])
                    act(ew[:, :n], ew[:, :n], AF.Exp, [ewk, "kc"], [ewk], scale=-1.0, bias=kc[:, 2:3])
                    P.dma("sp", RF[10 + 6 * d + pc, :, off:off + n], ew[:, :n], reads=[ewk], writes=[("RF", off)])
                    mm(pS[0][:, :n], aup[:, d, pc * 128:(pc + 1) * 128], za[d][0][:64, :n], ["aup", za[d][1]], ["pS0"])
                    a_, ak = newb()
                    act(a_[:, :n], pS[0][:, :n], AF.Sigmoid, ["pS0", "rv"], [ak], bias=rv[:, pc, 5 + d:6 + d])
                    bd, bdk = newb()
                    tt_(bd[:, :n], a_[:, :n], kkn[pc][0][:, :n], ALU.mult, [ak, kkn[pc][1]], [bdk])
                    P.dma("sp", RF[8 + 6 * d + pc, :, off:off + n], bd[:, :n], reads=[bdk], writes=[("RF", off)])
                    ts_(a_[:, :n], a_[:, :n], 1.0, rv[:, pc, 1:2], ALU.subtract, ALU.mult, [ak, "rv"], [ak])
                    kd, kdk = newb()
                    stt(kd[:, :n], a_[:, :n], 1.0, kk_[pc][0][:, :n], ALU.add, ALU.mult, [ak, kk_[pc][1]], [kdk])
                    P.dma("sp", RF[6 + 6 * d + pc, :, off:off + n], kd[:, :n], reads=[kdk], writes=[("RF", off)])
                    if d == 0:
                        ksum[pc] = (kd, kdk)
                    else:
                        kf, kfk = ksum[pc]
                        tt_(kd[:, :n], kd[:, :n], kf[:, :n], ALU.add, [kdk, kfk], [kdk])
                        ts_(kd[:, :n], kd[:, :n], 0.5, rv[:, pc, 2:3], ALU.mult, ALU.mult, [kdk, "rv"], [kdk])
                        tt_(kd[:, :n], kd[:, :n], rr[pc][0][:, :n], ALU.mult, [kdk, rr[pc][1]], [kdk])
                        P.dma("sp", RF[18 + pc, :, off:off + n], kd[:, :n], reads=[kdk], writes=[("RF", off)])
                    yield
            sz, szk = newb()
            act(sz[:, :n], zg[0][:, :n], AF.Sigmoid, [zg[1]], [szk])
            P.dma("sp", RF[20, :, off:off + n], sz[:, :n], reads=[szk], writes=[("RF", off)])
            yield

    def rwkv_scan(l):
        import os as _os
        _rwcut = int(_os.environ.get('RW_CUT', '99'))
        es = ExitStack()
        P.es = es
        tri = [cst[0:64, C_TRI + i * 64:C_TRI + (i + 1) * 64] for i in range(4)]
        m01 = [cst[0:64, C_M01 + i * 64:C_M01 + (i + 1) * 64] for i in range(4)]
        id64 = cst[0:64, C_ID:C_ID + 64]
        Hx = [[P.sb([128, 256], F32, "Hx%d%d" % (d, pc)) for pc in range(2)] for d in range(2)]
        for d in range(2):
            for pc in range(2):
                P.op("dve", "memset", Hx[d][pc][:], 0.0, writes=[("Hx", d, pc)])
        bc4 = lambda m: m.unsqueeze(1).to_broadcast([64, 4, 64])
        banks = {"A": (pG[0], "pG0"), "B": (pG[1], "pG1"), "C": (pU[0], "pU0"), "D": (pU[1], "pU1"),
                 "E": (pD[0], "pD0"), "F": (pD[1], "pD1"), "G": (pS[0], "pS0"), "H": (pS[1], "pS1")}

        def unit(d):
            K = lambda s: (s, d)
            cm = [P.sb([128, 6, 64], F32, "cm%d_%d" % (d, i)) for i in range(2)]
            dm = [P.sb([128, 6, 64], F32, "dm%d_%d" % (d, i)) for i in range(2)]
            ewM = P.sb([64, 256], F32, "ewM%d" % d)
            vM = P.sb([64, 256], F32, "rvM%d" % d)
            Pd = P.sb([128, 2, 3, 64], F32, "Pd%d" % d)
            fT = P.sb([128, 2, 4, 64], F32, "fT%d" % d)
            mT_ = P.sb([128, 2, 2, 2, 64], F32, "mK%d" % d)
            KBM = P.sb([64, 2, 256], F32, "KBM%d" % d)
            sc = P.sb([64, 5, 4, 64], F32, "sc%d" % d)
            Tt = P.sb([64, 4, 64], F32, "Tt%d" % d)
            TtT = P.sb([64, 4, 64], F32, "TtT%d" % d)
            Nn = [P.sb([64, 2, 4, 64], F32, "Nn%d_%d" % (d, i)) for i in range(2)]
            Xs = P.sb([64, 4, 64], F32, "Xs%d" % d)
            Us = P.sb([64, 4, 64], F32, "Us%d" % d)
            Ys = P.sb([64, 256], F32, "Ys%d" % d)
            tmpH = P.sb([128, 256], F32, "tmpH%d" % d)
            phys = [(pG[d], "pG%d" % d), (pU[d], "pU%d" % d), (pD[d], "pD%d" % d), (pS[d], "pS%d" % d)]
            bA, kA = phys[0]; bB, kB = phys[1]; bC, kC = phys[2]; bD, kD = phys[3]
            bE, kE = phys[0]; bF, kF = phys[1]; bG, kG = phys[2]; bH, kH = phys[3]
            v4 = lambda ap: ap.rearrange("p (h t) -> p h t", h=4)
            for ui, c in enumerate(chunk_order(d)):
                i2 = ui % 2
                r0 = c * 64
                KB = lambda s: (s, d, i2)
                P.dma("sp", cm[i2][:], RF[0:6, :, r0:r0 + 64].rearrange("j p t -> p j t"), reads=[("RF", 0)],
                      writes=[KB("cm")])
                P.dma("sp", dm[i2][:], RF[6 + 6 * d:12 + 6 * d, :, r0:r0 + 64].rearrange("j p t -> p j t"),
                      reads=[("RF", 0)], writes=[KB("dm")])
                cmt, dmt = cm[i2], dm[i2]
                yield
                if _rwcut <= 1:
                    continue
                for pc in range(2):
                    tr(bG[:64, pc * 128:(pc + 1) * 128], dmt[:, 4 + pc, :], [KB("dm")], [kG], inc=False)
                    tr(bG[:64, 256 + pc * 128:256 + (pc + 1) * 128], cmt[:, 2 + pc, :], [KB("cm")], [kG], inc=(pc == 1))
                P.op("dve", "tensor_copy", ewM[:], bG[:64, 0:256], reads=[kG], writes=[K("ewM")])
                P.op("dve", "tensor_copy", vM[:], bG[:64, 256:512], reads=[kG], writes=[K("vM")])
                yield
                if _rwcut <= 2:
                    continue
                for pc in range(2):
                    for ie in range(2):
                        mm(bH[:, (pc * 2 + ie) * 64:(pc * 2 + ie + 1) * 64], ewM[:, pc * 128:(pc + 1) * 128], tri[2 * ie + d],
                           [K("ewM"), "cst"], [kH], inc=(pc == 1 and ie == 1))
                lv = bH[:, 0:256].rearrange("p (c i t) -> p c i t", c=2, i=2)
                act(Pd[:, :, 0:2, :], lv, AF.Exp, [kH], [K("Pd")])
                act(Pd[:, :, 2, :], lv[:, :, 0, :], AF.Exp, [kH], [K("Pd")], scale=-1.0)
                yield
                if _rwcut <= 3:
                    continue
                tt_(fT[:, :, 0, :], cmt[:, 0:2, :], Pd[:, :, 0, :], ALU.mult, [KB("cm"), K("Pd")], [K("fT")])
                tt_(fT[:, :, 1, :], cmt[:, 4:6, :], Pd[:, :, 1, :], ALU.mult, [KB("cm"), K("Pd")], [K("fT")])
                tt_(fT[:, :, 2, :], dmt[:, 0:2, :], Pd[:, :, 2, :], ALU.mult, [KB("dm"), K("Pd")], [K("fT")])
                tt_(fT[:, :, 3, :], dmt[:, 2:4, :], Pd[:, :, 2, :], ALU.mult, [KB("dm"), K("Pd")], [K("fT")])
                for hh in range(2):
                    ts_(mT_[:, :, :, hh, :], fT[:, :, 2:4, :], cst[:, C_RHM + hh:C_RHM + hh + 1], None, ALU.mult, None,
                        [K("fT"), "cst"], [K("mK")])
                yield
                if _rwcut <= 4:
                    continue
                for pc in range(2):
                    tr(bG[:64, pc * 128:(pc + 1) * 128], fT[:, pc, 2, :], [K("fT")], [kG], inc=False)
                    tr(bG[:64, 256 + pc * 128:256 + (pc + 1) * 128], fT[:, pc, 3, :], [K("fT")], [kG], inc=(pc == 1))
                act(KBM[:, 0, :], bG[:64, 0:256], AF.Copy, [kG], [K("KBM")])
                act(KBM[:, 1, :], bG[:64, 256:512], AF.Copy, [kG], [K("KBM")], scale=-1.0)
                for h in range(4):
                    pc, hh = h // 2, h % 2
                    KdTm, BdTm = mT_[:, pc, 0, hh, :], mT_[:, pc, 1, hh, :]
                    KKeT, RpT = fT[:, pc, 1, :], fT[:, pc, 0, :]
                    rd_ = [K("mK"), K("fT")]
                    mm(bA[:64, h * 64:(h + 1) * 64], KdTm, KKeT, rd_, [kA], inc=False)
                    mm(bA[:64, 256 + h * 64:256 + (h + 1) * 64], BdTm, KKeT, rd_, [kA], inc=(h == 3))
                    mm(bB[:64, h * 64:(h + 1) * 64], KKeT, BdTm, rd_, [kB], inc=False)
                    mm(bB[:64, 256 + h * 64:256 + (h + 1) * 64], KdTm, RpT, rd_, [kB], inc=(h == 3))
                    mm(bC[:64, h * 64:(h + 1) * 64], BdTm, RpT, rd_, [kC], inc=(h == 3))
                yield
                if _rwcut <= 5:
                    continue
                tt_(sc[:, 0], v4(bA[:64, 0:256]), bc4(m01[2 + d]), ALU.mult, [kA, "cst"], [K("sc0")])
                tt_(sc[:, 1], v4(bA[:64, 256:512]), bc4(m01[2 + d]), ALU.mult, [kA, "cst"], [K("sc1")])
                tt_(sc[:, 2], v4(bB[:64, 0:256]), bc4(m01[3 - d]), ALU.mult, [kB, "cst"], [K("sc2")])
                tt_(sc[:, 3], v4(bB[:64, 256:512]), bc4(m01[d]), ALU.mult, [kB, "cst"], [K("sc3")])
                stt(sc[:, 4], v4(bC[:64, 0:256]), -1.0, bc4(m01[d]), ALU.mult, ALU.mult, [kC, "cst"], [K("sc4")])
                stt(Tt[:], sc[:, 1], -1.0, bc4(id64), ALU.mult, ALU.add, [K("sc1"), "cst"], [K("Tt")])
                stt(TtT[:], sc[:, 2], -1.0, bc4(id64), ALU.mult, ALU.add, [K("sc2"), "cst"], [K("TtT")])
                yield
                if _rwcut <= 6:
                    continue
                Ncur, NTcur, nk = sc[:, 1], sc[:, 2], [K("sc1"), K("sc2")]
                for lev in range(5):
                    lastl = lev == 4
                    for h in range(4):
                        mm(bD[:64, h * 64:(h + 1) * 64], NTcur[:, h, :], Ncur[:, h, :], nk, [kD], inc=(lastl and h == 3))
                        if not lastl:
                            mm(bD[:64, 256 + h * 64:256 + (h + 1) * 64], Ncur[:, h, :], NTcur[:, h, :], nk, [kD],
                               inc=(h == 3))
                    nn = Nn[lev % 2]
                    nnk = (K("Nn"), lev % 2)
                    if lastl:
                        act(nn[:, 0], v4(bD[:64, 0:256]), AF.Copy, [kD], [nnk])
                    else:
                        act(nn[:].rearrange("p a h t -> p (a h) t"), bD[:64, :].rearrange("p (a t) -> p a t", a=8),
                            AF.Copy, [kD], [nnk])
                    yield
                    for h in range(4):
                        mm(bE[:64, h * 64:(h + 1) * 64], TtT[:, h, :], nn[:, 0, h, :], [K("TtT"), nnk], [kE],
                           inc=(lastl and h == 3))
                        if not lastl:
                            mm(bE[:64, 256 + h * 64:256 + (h + 1) * 64], nn[:, 0, h, :], TtT[:, h, :], [K("TtT"), nnk], [kE],
                               inc=(h == 3))
                    tt_(Tt[:], Tt[:], v4(bE[:64, 0:256]), ALU.add, [K("Tt"), kE], [K("Tt")])
                    if not lastl:
                        tt_(TtT[:], TtT[:], v4(bE[:64, 256:512]), ALU.add, [K("TtT"), kE], [K("TtT")])
                    Ncur, NTcur, nk = nn[:, 0], nn[:, 1], [nnk]
                    yield
                for h in range(4):
                    pc = h // 2
                    mm(bC[:64, 256 + h * 64:256 + (h + 1) * 64], fT[:, pc, 1, :], Hx[d][pc][:, h * 64:(h + 1) * 64],
                       [K("fT"), ("Hx", d, pc)], [kC], start=True, stop=False, inc=False)
                    mm(bC[:64, 256 + h * 64:256 + (h + 1) * 64], sc[:, 0, h, :], vM[:, h * 64:(h + 1) * 64],
                       [K("sc0"), K("vM")], [kC], start=False, stop=True, inc=(h == 3))
                P.op("dve", "tensor_copy", Xs[:], v4(bC[:64, 256:512]), reads=[kC], writes=[K("Xs")])
                yield
                if _rwcut <= 7:
                    continue
                for h in range(4):
                    mm(bF[:64, h * 64:(h + 1) * 64], Tt[:, h, :], Xs[:, h, :], [K("Tt"), K("Xs")], [kF], inc=(h == 3))
                act(Us[:], v4(bF[:64, 0:256]), AF.Copy, [kF], [K("Us")])
                yield
                if _rwcut <= 8:
                    continue
                for h in range(4):
                    pc = h // 2
                    o_ = bF[:64, 256 + h * 64:256 + (h + 1) * 64]
                    mm(o_, fT[:, pc, 0, :], Hx[d][pc][:, h * 64:(h + 1) * 64], [K("fT"), ("Hx", d, pc)], [kF],
                       start=True, stop=False, inc=False)
                    mm(o_, sc[:, 3, h, :], vM[:, h * 64:(h + 1) * 64], [K("sc3"), K("vM")], [kF], start=False, stop=False,
                       inc=False)
                    mm(o_, sc[:, 4, h, :], Us[:, h, :], [K("sc4"), K("Us")], [kF], start=False, stop=True, inc=(h == 3))
                P.op("dve", "tensor_copy", Ys[:], bF[:64, 256:512], reads=[kF], writes=[K("Ys")])
                P.dma("sp", YD[d, r0:r0 + 64, :], Ys[:], reads=[K("Ys")], writes=[("YD", d, c)])
                lastc = 63 if d == 0 else 0
                for pc in range(2):
                    o_ = bH[:, 256:512] if pc == 0 else bG[:, 0:256]
                    ok_ = kH if pc == 0 else kG
                    mm(o_, KBM[:, 0, pc * 128:(pc + 1) * 128], vM[:], [K("KBM"), K("vM")], [ok_], start=True, stop=False,
                       inc=False)
                    mm(o_, KBM[:, 1, pc * 128:(pc + 1) * 128], Us[:].rearrange("p h e -> p (h e)"), [K("KBM"), K("Us")],
                       [ok_], start=False, stop=True, inc=True)
                    tt_(tmpH[:], o_, cst[:, C_RBD + pc * 256:C_RBD + (pc + 1) * 256], ALU.mult, [ok_, "cst"], [K("tmpH")])
                    tt_(tmpH[:], tmpH[:], Hx[d][pc][:], ALU.add, [K("tmpH"), ("Hx", d, pc)], [K("tmpH")])
                    ts_(Hx[d][pc][:], tmpH[:], Pd[:, pc, 0, lastc:lastc + 1], None, ALU.mult, None, [K("tmpH"), K("Pd")],
                        [("Hx", d, pc)])
                yield
                if _rwcut <= 9:
                    continue

        run_interleaved([unit(0), unit(1)])
        P.barrier()
        es.close()

    def rwkv_final(l, do_ctx):
        es = ExitStack()
        P.es = es
        gup = P.sb([128, 256], F32, "gup")
        P.dma("sp", gup[:], rw_gup[l], writes=["gup"])
        gnb = P.sb([128, 2, 256], F32, "gnb")
        for i in range(2):
            P.dma("sp", gnb[:, i, :], rw_gn[l, i].partition_broadcast(128), writes=["gnb"])
        yf = [P.sb([128, 4, 64], F32, "yf%d" % i) for i in range(2)]
        yb = [P.sb([128, 4, 64], F32, "yb%d" % i) for i in range(2)]
        fp = [P.sb([128, 5, 128], F32, "fp%d" % i) for i in range(2)]
        tm = P.sb([128, 2, 256], F32, "ftm")
        ysq = P.sb([128, 4, 64], F32, "ysq")
        st4 = P.sb([128, 4, 4], F32, "st4")
        gsb = P.sb([128, 256], F32, "gsb")
        ni = 0
        for tt in range(NTL):
            r0 = tt * 128
            if r0 < LC and not do_ctx:
                continue
            i2 = ni % 2
            ni += 1
            P.dma("sp", yf[i2][:].rearrange("p h e -> p (h e)"), YD[0, r0:r0 + 128, :],
                  reads=[("YD", 0, 2 * tt), ("YD", 0, 2 * tt + 1)], writes=[("yf", i2)])
            P.dma("sp", yb[i2][:].rearrange("p h e -> p (h e)"), YD[1, r0:r0 + 128, :],
                  reads=[("YD", 1, 2 * tt), ("YD", 1, 2 * tt + 1)], writes=[("yb", i2)])
            for j, pn in enumerate([18, 19, 2, 3, 20]):
                P.dma("sp", fp[i2][:, j, :], RF[pn, :, r0:r0 + 128], reads=[("RF", 0)], writes=[("fp", i2)])
            y = yf[i2]
            tt_(y[:], y[:], yb[i2][:], ALU.add, [("yf", i2), ("yb", i2)], [("yf", i2)])
            for j in range(4):
                tr(pG[0][:, j * 128:(j + 1) * 128], fp[i2][:, j, :], [("fp", i2)], ["pG0"], inc=(j == 3))
            act(tm[:].rearrange("p a c -> p (a c)"), pG[0][:, :], AF.Copy, ["pG0"], ["ftm"])
            mm(pG[1][:, 0:256], fp[i2][:, 4, :], gup[:], [("fp", i2), "gup"], ["pG1"])
            act(gsb[:], pG[1][:, 0:256], AF.Copy, ["pG1"], ["gsb"])
            P.op("dve", "tensor_reduce", st4[:, 0, :], y[:], AX.X, ALU.add, reads=[("yf", i2)], writes=["st4"])
            tt_(ysq[:], y[:], y[:], ALU.mult, [("yf", i2)], ["ysq"])
            P.op("dve", "tensor_reduce", st4[:, 1, :], ysq[:], AX.X, ALU.add, reads=["ysq"], writes=["st4"])
            P.op("dve", "tensor_reduce", st4[:, 2, :], tm[:, 0, :].rearrange("p (h e) -> p h e", h=4), AX.X, ALU.add,
                 reads=["ftm"], writes=["st4"])
            ts_(st4[:, 0, :], st4[:, 0, :], 1.0 / 64, None, ALU.mult, None, ["st4"], ["st4"])
            tt_(st4[:, 3, :], st4[:, 0, :], st4[:, 0, :], ALU.mult, ["st4"], ["st4"])
            stt(st4[:, 1, :], st4[:, 1, :], 1.0 / 64, st4[:, 3, :], ALU.mult, ALU.subtract, ["st4"], ["st4"])
            act(st4[:, 1, :], st4[:, 1, :], AF.Sqrt, ["st4", "kc"], ["st4"], bias=kc[:, 4:5])
            P.op("dve", "reciprocal", st4[:, 1, :], st4[:, 1, :], reads=["st4"], writes=["st4"])
            tt_(y[:], y[:], st4[:, 0, :].unsqueeze(2).to_broadcast([128, 4, 64]), ALU.subtract, [("yf", i2), "st4"],
                [("yf", i2)])
            tt_(y[:], y[:], st4[:, 1, :].unsqueeze(2).to_broadcast([128, 4, 64]), ALU.mult, [("yf", i2), "st4"],
                [("yf", i2)])
            yv = y[:].rearrange("p h e -> p (h e)")
            tt_(yv, yv, gnb[:, 0, :], ALU.mult, [("yf", i2), "gnb"], [("yf", i2)])
            tt_(yv, yv, gnb[:, 1, :], ALU.add, [("yf", i2), "gnb"], [("yf", i2)])
            tt_(ysq[:], tm[:, 1, :].rearrange("p (h e) -> p h e", h=4),
                st4[:, 2, :].unsqueeze(2).to_broadcast([128, 4, 64]), ALU.mult, ["ftm", "st4"], ["ysq"])
            tt_(y[:], y[:], ysq[:], ALU.add, [("yf", i2), "ysq"], [("yf", i2)])
            tt_(yv, yv, gsb[:], ALU.mult, [("yf", i2), "gsb"], [("yf", i2)])
            P.dma("sp", MIX[r0:r0 + 128, 768:1024], yv, reads=[("yf", i2)], writes=[("MIX", "c", tt)])
        P.barrier()
        es.close()

    def outproj(l, do_ctx):
        es = ExitStack()
        P.es = es
        hT = [P.sb([128, 8, TG], F32, "hT%d" % i) for i in range(2)]
        lt = alloc_ln_tiles()
        vv = lt["vv"]
        mixT = P.sb([128, 8, TG], BF16, "mixT")
        mtile = [P.sb([128, 1024], F32, "mtile%d" % i) for i in range(2)]
        wo = P.sb([128, 8, 1024], BF16, "wo")
        for hf in range(2):
            P.dma("pool", wo[:, :, hf * 512:(hf + 1) * 512],
                  w_out[l][:, hf * 512:(hf + 1) * 512].rearrange("(k p) j -> p k j", p=128), writes=["wo"])
        cur_s = None
        nm = 0
        for (st, t0, n) in groups:
            if st == "c" and not do_ctx:
                continue
            s = 1 if st == "c" else 0
            off = toff(st, t0)
            if s != cur_s:
                mod_scalars(l, 1, s)
                cur_s = s
            gi = cnts["g"]
            cnts["g"] += 1
            h = hT[gi % 2]
            hk = "hT%d" % (gi % 2)
            P.dma("sp", h[:, :, :n], hsrc(st, t0, n), reads=[("H", st, t0)], writes=[hk])
            for ts in range(n // 128):
                mt = mtile[nm % 2]
                mk = ("mtile", nm % 2)
                nm += 1
                P.dma("sp", mt[:], MIX[off + ts * 128:off + (ts + 1) * 128, :],
                      reads=[kk for kk in list(P.lastw.keys()) if isinstance(kk, tuple) and kk[0] == "MIX"], writes=[mk])
                for half in range(2):
                    pi = half
                    for j in range(4):
                        tr(pG[pi][:, j * 128:(j + 1) * 128], mt[:, (half * 4 + j) * 128:(half * 4 + j + 1) * 128], [mk],
                           ["pG%d" % pi], inc=(j == 3))
                    act(mixT[:, half * 4:half * 4 + 4, ts * 128:(ts + 1) * 128],
                        pG[pi][:, :].rearrange("p (j t) -> p j t", j=4), AF.Copy, ["pG%d" % pi], ["mixT"])
            for c in range(8):
                pi = c % 2
                for k in range(8):
                    mm(pD[pi][:, :n], wo[:, k, c * 128:(c + 1) * 128], mixT[:, k, :n], ["wo", "mixT"], ["pD%d" % pi],
                       start=(k == 0), stop=(k == 7), inc=(k == 7))
                stt(vv[:, c, :n], pD[pi][:, :n], scl[:, 2, c:c + 1], h[:, c, :n], ALU.mult, ALU.add,
                    ["pD%d" % pi, "scl2", hk], [("vv", c)])
            layer_norm_store(lt, st, t0, n, l, 1)
        P.barrier()
        es.close()

    for l in range(L):
        last = l == L - 1
        if USE_WSCRATCH:
            convert_weights(l)
        ffn_sublayer(l, 0, ffn_w["ffn1_wg"][l], ffn_w["ffn1_wu"][l], ffn_w["ffn1_wd"][l])
        if dbg == "ffn1":
            break
        inproj(l)
        if dbg == "inproj":
            break
        if dbg in (None, "swa", "mix"):
            swa(l, not last)
        if dbg is None:
            gla(l, not last, extra=[rwkv_prep_gen(l)])
        elif dbg in ("gla", "mix"):
            gla(l, not last)
        if dbg in (None, "rwkv", "mix"):
            import os as _os
            _rs = int(_os.environ.get("RW_STOP", "9"))
            if dbg is not None:
                rwkv_prep(l)
            if _rs >= 2:
                rwkv_scan(l)
            if _rs >= 3:
                rwkv_final(l, not last)
        if dbg in ("swa", "gla", "rwkv", "mix"):
            break
        outproj(l, not last)
        if dbg == "outproj":
            break
        ffn_sublayer(l, 2, ffn_w["ffn2_wg"][l], ffn_w["ffn2_wu"][l], ffn_w["ffn2_wd"][l], do_ctx=not last)

    es = ExitStack()
    P.es = es
    evs = []
    if dbg in ("swa", "gla", "rwkv", "mix"):
        mixo = dram("mixo", [SEQ, D], kind="ExternalOutput")
        ob = [P.sb([128, 1024], F32, "ob%d" % i) for i in range(2)]
        for tt in range(SEQ // 128):
            r0 = LC + tt * 128
            P.dma("sp", ob[tt % 2][:], MIX[r0:r0 + 128, :], writes=[("ob", tt % 2)])
            evs.append(P.dma("sp", mixo[tt * 128:(tt + 1) * 128, :], ob[tt % 2][:], reads=[("ob", tt % 2)],
                             writes=[("mixo", tt)]))
    ob2 = [P.sb([128, 8, TG], F32, "ob2%d" % i) for i in range(2)]
    for gi, (st, t0, n) in enumerate(groups):
        if st == "c":
            continue
        b = ob2[gi % 2]
        bk = "ob2%d" % (gi % 2)
        P.dma("sp", b[:, :, :n], hsrc(st, t0, n), reads=[("H", st, t0)], writes=[bk])
        evs.append(P.dma("sp", outT[:, :, t0:t0 + n].rearrange("c p t -> p c t"), b[:, :, :n], reads=[bk],
                         writes=[("out", t0)]))
    P.finish("sp", evs)
    es.close()
    es0.close()
    print("program instructions:", P.ninst)
    return nc


def _consts():
    c = np.zeros((128, C_END), np.float32)
    c[:, C_ID:C_ID + 128] = np.eye(128)
    s = np.arange(64)[:, None]
    t = np.arange(64)[None, :]
    c[0:64, C_TRI + 0:C_TRI + 64] = -1.0 * (s <= t)
    c[0:64, C_TRI + 64:C_TRI + 128] = -1.0 * (s >= t)
    c[0:64, C_TRI + 128:C_TRI + 192] = -1.0 * (s < t)
    c[0:64, C_TRI + 192:C_TRI + 256] = -1.0 * (s > t)
    c[0:64, C_M01 + 0:C_M01 + 64] = (s <= t)
    c[0:64, C_M01 + 64:C_M01 + 128] = (s >= t)
    c[0:64, C_M01 + 128:C_M01 + 192] = (s < t)
    c[0:64, C_M01 + 192:C_M01 + 256] = (s > t)
    p = np.arange(128)[:, None]
    col = np.arange(256)[None, :]
    c[:, C_GBD:C_GBD + 256] = (p // 32 == col // 64)
    for h in range(4):
        c[:, C_GHM + h] = (np.arange(128) // 32 == h)
    for pc in range(2):
        c[:, C_RBD + pc * 256:C_RBD + (pc + 1) * 256] = ((2 * pc + p // 64) == col // 64)
    for hh in range(2):
        c[:, C_RHM + hh] = (np.arange(128) // 64 == hh)
    q = np.arange(128)[None, :]
    c[:, C_BO:C_BO + 128] = (p // 64 == q // 64)
    c[:, C_SWM:C_SWM + 128] = (p >= q)
    c[:, C_SWM + 128:C_SWM + 256] = (p <= q)
    c[0:64, C_NI:C_NI + 64] = -np.eye(64)
    return c


def _rope_table(SEQ):
    pos = np.arange(SEQ)
    row = (pos // 64).astype(np.float32)
    col = (pos % 64).astype(np.float32)
    inv = (np.float32(10000.0) ** (-np.arange(16, dtype=np.float32) / np.float32(16))).astype(np.float32)
    tab = np.zeros((SEQ, 2, 32), np.float32)
    for a, pp in enumerate((row, col)):
        ang = (pp[:, None] * inv[None, :]).astype(np.float32)
        tab[:, 0, a * 16:(a + 1) * 16] = np.cos(ang)
        tab[:, 1, a * 16:(a + 1) * 16] = np.sin(ang)
    return tab


def _prep_shared(inp, L, SEQ):
    f = lambda a: np.ascontiguousarray(a, dtype=np.float32)
    m = {}
    m["w_ada"] = f(inp["w_ada"][:L])
    m["b_adaT"] = f(inp["b_ada"][:L].reshape(L, 72, 128).transpose(2, 0, 1))
    ln = np.stack([inp["ln_g"][:L], inp["ln_b"][:L]], axis=2)
    m["lnT"] = f(ln.reshape(L, 3, 2, 8, 128).transpose(4, 0, 1, 2, 3))
    for nm in ("ffn1_wg", "ffn1_wu", "ffn1_wd", "ffn2_wg", "ffn2_wu", "ffn2_wd", "w_in", "w_out"):
        m[nm] = f(inp[nm][:L])
    m["consts"] = _consts()
    m["ropeM"] = _rope_table(SEQ)
    m["gla_up"] = f(np.concatenate([inp["gla_gate_up"][:L], inp["gla_gate_bias"][:L][:, :, None, :]], axis=2))
    m["gla_g"] = f(inp["gla_norm_g"][:L])
    m["sinkB"] = f(np.broadcast_to(inp["swa_sink"][:L][None], (128, L, 8)))
    vecs = np.stack([inp["rwkv_k_k"][:L], inp["rwkv_k_a"][:L], inp["rwkv_r_k"][:L].reshape(L, 256),
                     inp["rwkv_w0"][:L, 0], inp["rwkv_w0"][:L, 1], inp["rwkv_a0"][:L, 0], inp["rwkv_a0"][:L, 1]],
                    axis=-1)
    m["rw_vec"] = f(vecs.reshape(L, 2, 128, 7).transpose(2, 0, 1, 3))
    mu = inp["rwkv_mu"][:L]
    mut = np.zeros((128, L, 11), np.float32)
    for j in range(6):
        mut[:, :, j] = mu[:, 128 * j:128 * (j + 1)].T
    for j, r0 in enumerate((768, 832, 896, 960)):
        mut[:64, :, 6 + j] = mu[:, r0:r0 + 64].T
    mut[:, :, 10] = mu[:, 1024:1152].T
    m["rw_mu"] = mut
    m["rw_wup"] = f(inp["rwkv_w_up"][:L])
    m["rw_aup"] = f(inp["rwkv_a_up"][:L])
    m["rw_gup"] = f(inp["rwkv_g_up"][:L])
    m["rw_gn"] = f(np.stack([inp["rwkv_gn_g"][:L], inp["rwkv_gn_b"][:L]], axis=1))
    return m


def _prep_core(inp, b):
    f = lambda a: np.ascontiguousarray(a, dtype=np.float32)
    m = {}
    m["xT"] = f(inp["x"][b].T.reshape(8, 128, -1))
    m["cxT"] = f(inp["ctx"][b].T.reshape(8, 128, -1))
    cc = np.stack([inp["c"][b], inp["c_ctx"]], axis=-1)
    m["ccT"] = f(cc.reshape(8, 128, 2).transpose(1, 0, 2))
    return m


def run(inp, SEQ, LC, DEPTH, ncores, dbg=None, trace=False):
    nc = build(SEQ, LC, DEPTH, dbg=dbg)
    shared = _prep_shared(inp, DEPTH, SEQ)
    in_maps = []
    for b in range(ncores):
        m = dict(shared)
        m.update(_prep_core(inp, b))
        in_maps.append(m)
    res = run_bass_kernel_spmd(nc, in_maps, core_ids=list(range(ncores)), trace=trace)
    outs = [r["outT"].reshape(1024, SEQ).T for r in res.results]
    return np.stack(outs, axis=0), res


def kernel(**inputs):
    inp = {k: np.asarray(v) for k, v in inputs.items()}
    out, _ = run(inp, 4096, 256, 4, 8)
    return np.ascontiguousarray(out.astype(np.float32))
```

```python
import numpy as np
from contextlib import ExitStack
import concourse.bass as bass
import concourse.mybir as mybir
from concourse.bass_utils import run_bass_kernel_spmd

F32 = mybir.dt.float32
BF16 = mybir.dt.bfloat16
AF = mybir.ActivationFunctionType
ALU = mybir.AluOpType
AX = mybir.AxisListType

D = 1024
DFF = 2816
NFF = DFF // 128
DIN = 2720
ALPHA = 8.0 ** 0.25
LN_EPS = 1e-6
USE_WSCRATCH = False
GN_EPS = 64e-5
C_ID = 0
C_TRI = 128
C_M01 = 384
C_GBD = 640
C_GHM = 896
C_RBD = 900
C_RHM = 1412
C_BO = 1414
C_SWM = 1542
C_NI = 1798
C_END = 1862


class Prog:
    def __init__(self, nc, es):
        self.nc = nc
        self.es = es
        self.eng = {"pe": nc.tensor, "act": nc.scalar, "dve": nc.vector, "pool": nc.gpsimd, "sp": nc.sync}
        self.sem = {e: es.enter_context(nc.semaphore("s_" + e)) for e in ("pe", "act", "dve", "pool")}
        self.cnt = {e: 0 for e in self.sem}
        self.known = {e: {} for e in self.eng}
        self.lastw = {}
        self.rd = {}
        self.NDS = 12
        self.dsem = {q: [es.enter_context(nc.semaphore("d_%s%d" % (q, i))) for i in range(self.NDS)]
                     for q in ("sp", "pool", "act")}
        self.dcnt = {q: 0 for q in self.dsem}
        self.semobj = {}
        for e in self.sem:
            self.semobj[e] = self.sem[e]
        for q in self.dsem:
            for i, s in enumerate(self.dsem[q]):
                self.semobj[(q, i)] = s
        self.ntiles = 0
        self.ninst = 0

    def sb(self, shape, dt=F32, name=None):
        self.ntiles += 1
        return self.es.enter_context(self.nc.sbuf_tensor("%s_%d" % (name or "t", self.ntiles), list(shape), dt))

    def ps(self, shape, dt=F32, name=None):
        self.ntiles += 1
        return self.es.enter_context(self.nc.psum_tensor("%s_%d" % (name or "p", self.ntiles), list(shape), dt))

    def _wait(self, e, ev):
        s, v = ev
        if self.known[e].get(s, 0) >= v:
            return
        self.eng[e].wait_ge(self.semobj[s], v)
        self.known[e][s] = v
        self.ninst += 1

    def _deps(self, e, reads, writes):
        for k in reads:
            ev = self.lastw.get(k)
            if ev is not None and not (e == "pe" and ev[0] == "pe"):
                self._wait(e, ev)
        for k in writes:
            ev = self.lastw.get(k)
            if ev is not None and not (e == "pe" and ev[0] == "pe"):
                self._wait(e, ev)
            for ev in self.rd.get(k, {}).values():
                if ev[0] == e and e == "pe":
                    continue
                self._wait(e, ev)

    def _record(self, ev, reads, writes):
        for k in reads:
            self.rd.setdefault(k, {})[ev[0]] = ev
        for k in writes:
            self.lastw[k] = ev
            self.rd[k] = {}

    def op(self, e, fn, *args, reads=(), writes=(), inc=True, **kw):
        self._deps(e, reads, writes)
        inst = getattr(self.eng[e], fn)(*args, **kw)
        self.ninst += 1
        ev = (e, self.cnt[e] + 1)
        if inc:
            inst.then_inc(self.sem[e], 1)
            self.cnt[e] += 1
        self._record(ev, reads, writes)
        return inst

    def dma(self, q, out, in_, reads=(), writes=(), **kw):
        self._deps(q, reads, writes)
        j = self.dcnt[q]
        self.dcnt[q] += 1
        slot = j % self.NDS
        s = (q, slot)
        need = 16 * (j // self.NDS)
        if need > 0:
            self._wait(q, (s, need))
        inst = self.eng[q].dma_start(out=out, in_=in_, **kw)
        inst.then_inc(self.dsem[q][slot], 16)
        self.ninst += 1
        ev = (s, need + 16)
        self._record(ev, reads, writes)
        return ev

    def barrier(self):
        evs = [(e, self.cnt[e]) for e in self.sem if self.cnt[e] > 0]
        for q in self.dsem:
            j = self.dcnt[q]
            for slot in range(self.NDS):
                n = (j - slot + self.NDS - 1) // self.NDS if j > slot else 0
                if n > 0:
                    evs.append(((q, slot), 16 * n))
        for e in self.eng:
            for ev in evs:
                self._wait(e, ev)
        self.lastw = {}
        self.rd = {}

    def finish(self, e, evs):
        for ev in evs:
            self._wait(e, ev)


def _ffn_pieces():
    out = []
    f = 0
    while f < NFF:
        w = min(4, NFF - f)
        out.append((f, w))
        f += w
    return out


def build(SEQ, LC, DEPTH, dbg=None):
    nc = bass.Bass("TRN2", target_bir_lowering=False)
    es0 = ExitStack()
    P = Prog(nc, es0)
    dram = lambda name, shape, dt=F32, kind="ExternalInput": nc.dram_tensor(name, list(shape), dt, kind=kind).ap()
    L = DEPTH
    T = LC + SEQ
    NTL = T // 128
    NCH = T // 64
    xT = dram("xT", [8, 128, SEQ])
    cxT = dram("cxT", [8, 128, LC])
    ccT = dram("ccT", [128, 8, 2])
    w_ada = dram("w_ada", [L, D, 9 * D])
    b_adaT = dram("b_adaT", [128, L, 72])
    lnT = dram("lnT", [128, L, 3, 2, 8])
    ffn_w = {}
    for nm in ("ffn1_wg", "ffn1_wu", "ffn2_wg", "ffn2_wu"):
        ffn_w[nm] = dram(nm, [L, D, DFF])
    for nm in ("ffn1_wd", "ffn2_wd"):
        ffn_w[nm] = dram(nm, [L, DFF, D])
    w_in = dram("w_in", [L, D, DIN])
    w_out = dram("w_out", [L, D, D])
    consts = dram("consts", [128, C_END])
    ropeM = dram("ropeM", [SEQ, 2, 32])
    gla_up = dram("gla_up", [L, 2, 17, 128])
    gla_g = dram("gla_g", [L, 256])
    sinkB = dram("sinkB", [128, L, 8])
    rw_vec = dram("rw_vec", [128, L, 2, 7])
    rw_mu = dram("rw_mu", [128, L, 11])
    rw_wup = dram("rw_wup", [L, 2, 64, 256])
    rw_aup = dram("rw_aup", [L, 2, 64, 256])
    rw_gup = dram("rw_gup", [L, 128, 256])
    rw_gn = dram("rw_gn", [L, 2, 256])
    outT = dram("outT", [8, 128, SEQ], kind="ExternalOutput")
    HX = dram("HX", [8, 128, SEQ], kind="Internal")
    HC = dram("HC", [8, 128, LC], kind="Internal")
    PF = dram("PF", [DIN, T], kind="Internal")
    PM = dram("PM", [T, DIN], kind="Internal")
    MIX = dram("MIX", [T, D], kind="Internal")
    OGD = dram("OGD", [2, T, 256], kind="Internal")
    RF = dram("RF", [21, 128, T], kind="Internal")
    YD = dram("YD", [2, T, 256], kind="Internal")
    NPC = len(_ffn_pieces())
    WGU = dram("WGU", [1, 2, 2, NPC, 128, 8, 512], BF16, kind="Internal") if USE_WSCRATCH else None
    WDS = dram("WDS", [1, 2, 8, 128, NFF, 128], BF16, kind="Internal") if USE_WSCRATCH else None

    TG = 512
    groups = [("c", 0, LC)] + [("x", t0, min(TG, SEQ - t0)) for t0 in range(0, SEQ, TG)]

    def toff(st, t0):
        return t0 if st == "c" else LC + t0

    cst = P.sb([128, C_END], F32, "cst")
    P.dma("sp", cst[:], consts, writes=["cst"])
    ident = cst[:, C_ID:C_ID + 128]
    onesm = P.sb([128, 128], F32, "onesm")
    P.op("dve", "memset", onesm[:], 1.0 / D, writes=["onesm"])
    cc = P.sb([128, 8, 2], F32, "cc")
    P.dma("sp", cc[:], ccT, writes=["cc"])
    sil = P.sb([128, 8, 2], F32, "sil")
    P.op("act", "activation", sil[:], cc[:], AF.Silu, reads=["cc"], writes=["sil"])
    bada = P.sb([128, L, 72], F32, "bada")
    P.dma("sp", bada[:], b_adaT, writes=["bada"])
    lnp = P.sb([128, L, 3, 2, 8], F32, "lnp")
    P.dma("sp", lnp[:], lnT, writes=["lnp"])
    mT = P.sb([128, L, 72, 2], F32, "mT")
    kc = P.sb([128, 8], F32, "kc")
    for i, val in enumerate([LN_EPS / (ALPHA * ALPHA), 1.0, -0.5, LN_EPS, GN_EPS, 0.0, 1e-24]):
        P.op("dve", "memset", kc[:, i:i + 1], val, writes=["kc"])
    epsb = kc[:, 0:1]
    scl = P.sb([128, 4, 8], F32, "scl")

    pG = [P.ps([128, 512], F32, "pG%d" % i) for i in range(2)]
    pU = [P.ps([128, 512], F32, "pU%d" % i) for i in range(2)]
    pD = [P.ps([128, 512], F32, "pD%d" % i) for i in range(2)]
    pS = [P.ps([128, 512], F32, "pS%d" % i) for i in range(2)]

    es = ExitStack()
    P.es = es
    wa = [P.sb([128, 8, 512], F32, "wa%d" % i) for i in range(2)]
    nblk = 0
    for l in range(L):
        pm = pS[l % 2]
        for cb in range(18):
            wt = wa[nblk % 2]
            wk = "wa%d" % (nblk % 2)
            nblk += 1
            src = w_ada[l, :, cb * 512:(cb + 1) * 512].rearrange("(k p) j -> p k j", p=128)
            P.dma("sp", wt[:], src, writes=[wk])
            for jj in range(4):
                j = cb * 4 + jj
                for k in range(8):
                    P.op("pe", "matmul", pm[:, 2 * j:2 * j + 2], wt[:, k, jj * 128:(jj + 1) * 128], sil[:, k, :],
                         start=(k == 0), stop=(k == 7), reads=[wk, "sil"], writes=["pS%d" % (l % 2)],
                         inc=(k == 7 and jj == 3))
        P.op("dve", "tensor_tensor", mT[:, l, :, :], pm[:, 0:144].rearrange("p (j s) -> p j s", s=2),
             bada[:, l, :].unsqueeze(2).to_broadcast([128, 72, 2]), ALU.add,
             reads=["pS%d" % (l % 2), "bada"], writes=["mT"])
    for gi, (st, t0, n) in enumerate(groups):
        b = wa[gi % 2]
        bk = "wa%d" % (gi % 2)
        src = (cxT if st == "c" else xT)[:, :, t0:t0 + n].rearrange("c p t -> p c t")
        P.dma("sp", b[:, :, :n], src, writes=[bk])
        P.dma("sp", (HC if st == "c" else HX)[:, :, t0:t0 + n].rearrange("c p t -> p c t"), b[:, :, :n],
              reads=[bk], writes=[("H", st, t0)])
    P.barrier()
    es.close()

    def convert_weights(l):
        es = ExitStack()
        P.es = es
        cvt = [P.sb([128, 8, 512], BF16, "cvt%d" % i) for i in range(4)]
        cvd = [P.sb([128, NFF, 128], BF16, "cvd%d" % i) for i in range(4)]
        ncv = [0, 0]
        for fi, pre in enumerate(("ffn1", "ffn2")):
            for gu, nm in enumerate(("_wg", "_wu")):
                wap = ffn_w[pre + nm][l]
                for pi_, (f0, fw) in enumerate(_ffn_pieces()):
                    bi = ncv[0] % 4
                    ncv[0] += 1
                    P.dma("pool", cvt[bi][:, :, :fw * 128],
                          wap[:, f0 * 128:(f0 + fw) * 128].rearrange("(k p) j -> p k j", p=128), writes=[("cvt", bi)])
                    P.dma("sp", WGU[0, fi, gu, pi_][:, :, :fw * 128], cvt[bi][:, :, :fw * 128], reads=[("cvt", bi)],
                          writes=[("WGU", fi)])
            wap = ffn_w[pre + "_wd"][l]
            for c in range(8):
                bi = ncv[1] % 4
                ncv[1] += 1
                P.dma("pool", cvd[bi][:], wap[:, c * 128:(c + 1) * 128].rearrange("(f p) j -> p f j", p=128),
                      writes=[("cvd", bi)])
                P.dma("sp", WDS[0, fi, c], cvd[bi][:], reads=[("cvd", bi)], writes=[("WDS", fi)])
        P.barrier()
        es.close()

    def hsrc(st, t0, n):
        return (HC if st == "c" else HX)[:, :, t0:t0 + n].rearrange("c p t -> p c t")

    cnts = {"g": 0, "wgu": 0, "wd": 0, "ps": 0}

    def mod_scalars(l, sub, s):
        j0 = 3 * sub * 8
        coef = (0.5 if sub != 1 else 1.0) / ALPHA
        P.op("dve", "tensor_scalar", scl[:, 0, :], mT[:, l, j0 + 8:j0 + 16, s], 1.0, None, ALU.add,
             reads=["mT"], writes=["scl0"])
        P.op("dve", "tensor_copy", scl[:, 1, :], mT[:, l, j0:j0 + 8, s], reads=["mT"], writes=["scl1"])
        P.op("dve", "tensor_scalar", scl[:, 2, :], mT[:, l, j0 + 16:j0 + 24, s], coef, None, ALU.mult,
             reads=["mT"], writes=["scl2"])

    def alloc_ln_tiles():
        t = {}
        t["vv"] = P.sb([128, 8, TG], F32, "vv")
        t["yo"] = P.sb([128, 8, TG], F32, "yo")
        t["vsq"] = [P.sb([128, TG], F32, "vsq%d" % i) for i in range(2)]
        t["mean_sb"] = P.sb([128, TG], F32, "mean_sb")
        t["msq"] = P.sb([128, TG], F32, "msq")
        t["var"] = P.sb([128, TG], F32, "var")
        t["rstd"] = P.sb([128, TG], F32, "rstd")
        t["tt"] = [P.sb([128, TG], F32, "tt%d" % i) for i in range(2)]
        return t

    def layer_norm_store(t, st, t0, n, l, sub):
        vv, yo, vsq, mean_sb, msq, var, rstd, tt = (t[k] for k in ("vv", "yo", "vsq", "mean_sb", "msq", "var", "rstd", "tt"))
        pmean, pev2 = pS[0], pS[1]
        for c in range(8):
            q = vsq[c % 2]
            qk = "vsq%d" % (c % 2)
            P.op("act", "activation", q[:, :n], vv[:, c, :n], AF.Square, reads=[("vv", c)], writes=[qk])
            P.op("pe", "matmul", pmean[:, :n], onesm[:], vv[:, c, :n], start=(c == 0), stop=(c == 7),
                 reads=["onesm", ("vv", c)], writes=["pS0"], inc=(c == 7))
            P.op("pe", "matmul", pev2[:, :n], onesm[:], q[:, :n], start=(c == 0), stop=(c == 7),
                 reads=["onesm", qk], writes=["pS1"], inc=True)
        P.op("act", "activation", mean_sb[:, :n], pmean[:, :n], AF.Copy, reads=["pS0"], writes=["mean_sb"])
        P.op("dve", "tensor_tensor", msq[:, :n], mean_sb[:, :n], mean_sb[:, :n], ALU.mult,
             reads=["mean_sb"], writes=["msq"])
        P.op("dve", "tensor_tensor", var[:, :n], pev2[:, :n], msq[:, :n], ALU.subtract,
             reads=["pS1", "msq"], writes=["var"])
        P.op("act", "activation", var[:, :n], var[:, :n], AF.Sqrt, bias=epsb, reads=["var", "kc"], writes=["var"])
        P.op("dve", "reciprocal", rstd[:, :n], var[:, :n], reads=["var"], writes=["rstd"])
        for c in range(8):
            tq = tt[c % 2]
            tk = "tt%d" % (c % 2)
            P.op("dve", "tensor_tensor", tq[:, :n], vv[:, c, :n], mean_sb[:, :n], ALU.subtract,
                 reads=[("vv", c), "mean_sb"], writes=[tk])
            P.op("dve", "tensor_tensor", tq[:, :n], tq[:, :n], rstd[:, :n], ALU.mult,
                 reads=[tk, "rstd"], writes=[tk])
            P.op("act", "activation", yo[:, c, :n], tq[:, :n], AF.Identity,
                 scale=lnp[:, l, sub, 0, c:c + 1], bias=lnp[:, l, sub, 1, c:c + 1],
                 reads=[tk, "lnp"], writes=[("yo", c)])
        return P.dma("pool" if USE_WSCRATCH else "sp", hsrc(st, t0, n), yo[:, :, :n], reads=[("yo", c) for c in range(8)],
                     writes=[("H", st, t0)])

    def ffn_sublayer(l, sub, wg_ap, wu_ap, wd_ap, do_ctx=True):
        fi = 0 if sub == 0 else 1
        es = ExitStack()
        P.es = es
        hT = [P.sb([128, 8, TG], F32, "hT%d" % i) for i in range(2)]
        uTb = [P.sb([128, 8, TG], BF16, "uT%d" % i) for i in range(2)]
        hff = P.sb([128, NFF, TG], BF16, "hff")
        lt = alloc_ln_tiles()
        vv = lt["vv"]
        sg = [P.sb([128, TG], F32, "sg%d" % i) for i in range(2)]
        wgu = [P.sb([128, 2, 8, 512], BF16, "wgu%d" % i) for i in range(2)]
        wd = [P.sb([128, NFF, 128], BF16, "wd%d" % i) for i in range(3)]
        glist = [g for g in groups if not (g[0] == "c" and not do_ctx)]
        cur = {"s": None}
        dq = "pool" if USE_WSCRATCH else "sp"

        def a_load(j):
            st_, t0_, n_ = glist[j]
            P.dma(dq, hT[j % 2][:, :, :n_], hsrc(st_, t0_, n_), reads=[("H", st_, t0_)], writes=["hT%d" % (j % 2)])

        def a_mod(j):
            st_, t0_, n_ = glist[j]
            s_ = 1 if st_ == "c" else 0
            if s_ != cur["s"]:
                mod_scalars(l, sub, s_)
                cur["s"] = s_
            for c in range(8):
                P.op("act", "activation", uTb[j % 2][:, c, :n_], hT[j % 2][:, c, :n_], AF.Identity,
                     scale=scl[:, 0, c:c + 1], bias=scl[:, 1, c:c + 1],
                     reads=["hT%d" % (j % 2), "scl0", "scl1"], writes=[("uT", j % 2, c)])

        a_load(0)
        a_mod(0)
        for j, (st, t0, n) in enumerate(glist):
            if j + 1 < len(glist):
                a_load(j + 1)
            h = hT[j % 2]
            hk = "hT%d" % (j % 2)
            uT = uTb[j % 2]
            ub = j % 2
            for pi_, (f0, fw) in enumerate(_ffn_pieces()):
                wi = cnts["wgu"] % 2
                cnts["wgu"] += 1
                wt = wgu[wi]
                if USE_WSCRATCH:
                    P.dma("sp", wt[:, 0, :, :fw * 128], WGU[0, fi, 0, pi_][:, :, :fw * 128], writes=[("wgu", wi, 0)])
                    P.dma("sp", wt[:, 1, :, :fw * 128], WGU[0, fi, 1, pi_][:, :, :fw * 128], writes=[("wgu", wi, 1)])
                else:
                    P.dma("pool", wt[:, 0, :, :fw * 128],
                          wg_ap[:, f0 * 128:(f0 + fw) * 128].rearrange("(k p) j -> p k j", p=128), writes=[("wgu", wi, 0)])
                    P.dma("pool", wt[:, 1, :, :fw * 128],
                          wu_ap[:, f0 * 128:(f0 + fw) * 128].rearrange("(k p) j -> p k j", p=128), writes=[("wgu", wi, 1)])
                for ff in range(fw):
                    f = f0 + ff
                    pi = cnts["ps"] % 2
                    cnts["ps"] += 1
                    for k in range(8):
                        P.op("pe", "matmul", pG[pi][:, :n], wt[:, 0, k, ff * 128:(ff + 1) * 128], uT[:, k, :n],
                             start=(k == 0), stop=(k == 7), reads=[("wgu", wi, 0), ("uT", ub, k)], writes=["pG%d" % pi],
                             inc=(k == 7))
                    for k in range(8):
                        P.op("pe", "matmul", pU[pi][:, :n], wt[:, 1, k, ff * 128:(ff + 1) * 128], uT[:, k, :n],
                             start=(k == 0), stop=(k == 7), reads=[("wgu", wi, 1), ("uT", ub, k)], writes=["pU%d" % pi],
                             inc=(k == 7))
                    P.op("act", "activation", sg[pi][:, :n], pG[pi][:, :n], AF.Silu,
                         reads=["pG%d" % pi], writes=["sg%d" % pi])
                    P.op("dve", "tensor_tensor", hff[:, f, :n], sg[pi][:, :n], pU[pi][:, :n], ALU.mult,
                         reads=["sg%d" % pi, "pU%d" % pi], writes=[("hff", f)])
            for c in range(8):
                wi = cnts["wd"] % 3
                cnts["wd"] += 1
                if USE_WSCRATCH:
                    P.dma("sp", wd[wi][:], WDS[0, fi, c], writes=[("wd", wi)])
                else:
                    P.dma("pool", wd[wi][:], wd_ap[:, c * 128:(c + 1) * 128].rearrange("(f p) j -> p f j", p=128),
                          writes=[("wd", wi)])
                pi = c % 2
                for f in range(NFF):
                    P.op("pe", "matmul", pD[pi][:, :n], wd[wi][:, f, :], hff[:, f, :n],
                         start=(f == 0), stop=(f == NFF - 1), reads=[("wd", wi), ("hff", f)], writes=["pD%d" % pi],
                         inc=(f == NFF - 1))
                P.op("dve", "scalar_tensor_tensor", vv[:, c, :n], pD[pi][:, :n], scl[:, 2, c:c + 1], h[:, c, :n],
                     ALU.mult, ALU.add, reads=["pD%d" % pi, "scl2", hk], writes=[("vv", c)])
            if j + 1 < len(glist):
                a_mod(j + 1)
            layer_norm_store(lt, st, t0, n, l, sub)
        P.barrier()
        es.close()

    def mm(out, lhsT, rhs, reads, writes, start=True, stop=True, inc=True):
        P.op("pe", "matmul", out, lhsT, rhs, start=start, stop=stop, reads=reads, writes=writes, inc=inc)

    def act(out, in_, func, reads, writes, **kw):
        P.op("act", "activation", out, in_, func, reads=reads, writes=writes, **kw)

    def tt_(out, a, b, op, reads, writes, e="dve"):
        P.op(e, "tensor_tensor", out, a, b, op, reads=reads, writes=writes)

    def ts_(out, a, s1, s2, op0, op1, reads, writes, e="dve"):
        if op1 is None:
            P.op(e, "tensor_scalar", out, a, s1, None, op0, reads=reads, writes=writes)
        else:
            P.op(e, "tensor_scalar", out, a, s1, s2, op0, op1, reads=reads, writes=writes)

    def stt(out, a, s, b, op0, op1, reads, writes):
        P.op("dve", "scalar_tensor_tensor", out, a, s, b, op0, op1, reads=reads, writes=writes)

    def tr(out, in_, reads, writes, np_=128, inc=True):
        P.op("pe", "transpose", out, in_, ident[:np_, :np_], reads=list(reads) + ["cst"], writes=writes, inc=inc)

    def in_pieces():
        out = []
        c = 0
        while c < DIN:
            w = min(512, DIN - c)
            out.append((c, w))
            c += w
        return out

    def inproj(l, do_ctx=True):
        es = ExitStack()
        P.es = es
        hT = [P.sb([128, 8, TG], F32, "hT%d" % i) for i in range(2)]
        uT = P.sb([128, 8, TG], BF16, "uT")
        wt_ = [P.sb([128, 8, 512], BF16, "wi%d" % i) for i in range(2)]
        stg = [P.sb([128, 512], F32, "stg%d" % i) for i in range(4)]
        nst = [0]
        cur_s = None
        for (st, t0, n) in groups:
            s = 1 if st == "c" else 0
            off = toff(st, t0)
            if s != cur_s:
                mod_scalars(l, 1, s)
                cur_s = s
            gi = cnts["g"]
            cnts["g"] += 1
            h = hT[gi % 2]
            hk = "hT%d" % (gi % 2)
            P.dma("sp", h[:, :, :n], hsrc(st, t0, n), reads=[("H", st, t0)], writes=[hk])
            for c in range(8):
                act(uT[:, c, :n], h[:, c, :n], AF.Identity, [hk, "scl0", "scl1"], [("uT", c)],
                    scale=scl[:, 0, c:c + 1], bias=scl[:, 1, c:c + 1])
            for (c0, w) in in_pieces():
                wi = cnts["wgu"] % 2
                cnts["wgu"] += 1
                wt = wt_[wi]
                wk = ("wi", wi)
                P.dma("pool", wt[:, :, :w], w_in[l][:, c0:c0 + w].rearrange("(k p) j -> p k j", p=128), writes=[wk])
                j = 0
                while j < w:
                    cw = min(128, w - j)
                    a0, a1 = c0 + j, c0 + j + cw
                    if not any(a0 < hi_ and a1 > lo_ for (lo_, hi_) in ((0, 256), (768, 800), (1568, DIN))):
                        j += cw
                        continue
                    pi = cnts["ps"] % 2
                    cnts["ps"] += 1
                    for k in range(8):
                        mm(pG[pi][:cw, :n], wt[:, k, j:j + cw], uT[:, k, :n], [wk, ("uT", k)], ["pG%d" % pi],
                           start=(k == 0), stop=(k == 7), inc=(k == 7))
                    si = nst[0] % 4
                    nst[0] += 1
                    act(stg[si][:cw, :n], pG[pi][:cw, :n], AF.Copy, ["pG%d" % pi], [("stg", si)])
                    P.dma("sp", PF[c0 + j:c0 + j + cw, off:off + n], stg[si][:cw, :n], reads=[("stg", si)],
                          writes=[("PF", off)])
                    j += cw
                wm = min(w, max(0, 1568 - c0))
                for ts in range(n // 128 if wm > 0 else 0):
                    pi = cnts["ps"] % 2
                    cnts["ps"] += 1
                    for k in range(8):
                        mm(pU[pi][:, :wm], uT[:, k, ts * 128:(ts + 1) * 128], wt[:, k, :wm], [wk, ("uT", k)],
                           ["pU%d" % pi], start=(k == 0), stop=(k == 7), inc=(k == 7))
                    si = nst[0] % 4
                    nst[0] += 1
                    P.op("dve", "tensor_copy", stg[si][:, :wm], pU[pi][:, :wm], reads=["pU%d" % pi], writes=[("stg", si)])
                    P.dma("sp", PM[off + ts * 128:off + (ts + 1) * 128, c0:c0 + wm], stg[si][:, :wm],
                          reads=[("stg", si)], writes=[("PM", off)])
        P.barrier()
        es.close()

    def swa(l, do_ctx):
        es = ExitStack()
        P.es = es
        kT_all = P.sb([64, 2, T], BF16, "kT_all")
        V_all = P.sb([128, NTL, 2, 65], BF16, "V_all")
        esink = P.sb([128, 8], F32, "esink")
        sk = P.sb([128, L, 8], F32, "sk")
        P.dma("sp", sk[:], sinkB, writes=["sk"])
        act(esink[:], sk[:, l, :], AF.Exp, ["sk"], ["esink"])
        P.op("dve", "memset", V_all[:], 1.0, writes=["V_all"])
        km = [P.sb([128, 2, 64], F32, "km%d" % i) for i in range(2)]
        vm = [P.sb([128, 2, 64], F32, "vm%d" % i) for i in range(2)]
        rp = [P.sb([128, 2, 32], F32, "rp%d" % i) for i in range(2)]
        kr = P.sb([128, 2, 64], F32, "kr")
        qm = [P.sb([128, 8, 64], F32, "qm%d" % i) for i in range(2)]
        qr = P.sb([128, 8, 64], F32, "qr")
        ta = P.sb([128, 8, 32], F32, "ta")
        tb = P.sb([128, 8, 32], F32, "tb")
        qT = P.sb([64, 8, 128], BF16, "qT")
        pT = [P.sb([128, 512], BF16, "pT%d" % i) for i in range(10)]
        den = P.sb([128, 4], F32, "den")
        mixb = [P.sb([128, 512], F32, "mixb%d" % i) for i in range(2)]

        def rope(dst, src, nh, tab, rk_src, rk_dst, rk_tab):
            sv = src.rearrange("p h (a s i) -> p h a s i", a=2, s=2)
            dv = dst.rearrange("p h (a s i) -> p h a s i", a=2, s=2)
            cos = tab[:, 0, :].rearrange("p (a i) -> p a i", a=2).unsqueeze(1).to_broadcast([128, nh, 2, 16])
            sin = tab[:, 1, :].rearrange("p (a i) -> p a i", a=2).unsqueeze(1).to_broadcast([128, nh, 2, 16])
            tav = ta[:, :nh, :].rearrange("p h (a i) -> p h a i", a=2)
            tbv = tb[:, :nh, :].rearrange("p h (a i) -> p h a i", a=2)
            tt_(tav, sv[:, :, :, 0, :], cos, ALU.mult, [rk_src, rk_tab], ["ta"])
            tt_(tbv, sv[:, :, :, 1, :], sin, ALU.mult, [rk_src, rk_tab], ["tb"])
            tt_(dv[:, :, :, 0, :], tav, tbv, ALU.subtract, ["ta", "tb"], [rk_dst])
            tt_(tav, sv[:, :, :, 0, :], sin, ALU.mult, [rk_src, rk_tab], ["ta"])
            tt_(tbv, sv[:, :, :, 1, :], cos, ALU.mult, [rk_src, rk_tab], ["tb"])
            tt_(dv[:, :, :, 1, :], tav, tbv, ALU.add, ["ta", "tb"], [rk_dst])

        for tt in range(NTL):
            i2 = tt % 2
            r0 = tt * 128
            P.dma("sp", km[i2][:], PM[r0:r0 + 128, 1312:1440].rearrange("t (h d) -> t h d", h=2),
                  reads=[("PM", 0)], writes=[("km", i2)])
            P.dma("sp", vm[i2][:], PM[r0:r0 + 128, 1440:1568].rearrange("t (h d) -> t h d", h=2),
                  reads=[("PM", 0)], writes=[("vm", i2)])
            P.op("dve", "tensor_copy", V_all[:, tt, :, 0:64], vm[i2][:], reads=[("vm", i2)], writes=["V_all"])
            if r0 >= LC:
                P.dma("sp", rp[i2][:], ropeM[r0 - LC:r0 - LC + 128], writes=[("rp", i2)])
                rope(kr[:], km[i2][:], 2, rp[i2], ("km", i2), "kr", ("rp", i2))
                ksrc, kk_ = kr, "kr"
            else:
                ksrc, kk_ = km[i2], ("km", i2)
            for hk in range(2):
                tr(pS[0][:64, hk * 128:(hk + 1) * 128], ksrc[:, hk, :], [kk_], ["pS0"], inc=(hk == 1))
            act(kT_all[:, :, r0:r0 + 128], pS[0][:64, 0:256].rearrange("p (h t) -> p h t", h=2), AF.Copy,
                ["pS0"], ["kT_all"])
        nmix = 0
        for tt in range(NTL):
            r0 = tt * 128
            isx = r0 >= LC
            if not isx and not do_ctx:
                continue
            i2 = tt % 2
            P.dma("sp", qm[i2][:], PM[r0:r0 + 128, 800:1312].rearrange("t (h d) -> t h d", h=8),
                  reads=[("PM", 0)], writes=[("qm", i2)])
            if isx:
                P.dma("sp", rp[i2][:], ropeM[r0 - LC:r0 - LC + 128], writes=[("rp", i2)])
                rope(qr[:], qm[i2][:], 8, rp[i2], ("qm", i2), "qr", ("rp", i2))
                qsrc, qk_ = qr, "qr"
            else:
                qsrc, qk_ = qm[i2], ("qm", i2)
            for half in range(2):
                for hh in range(4):
                    tr(pS[half][:64, hh * 128:(hh + 1) * 128], qsrc[:, half * 4 + hh, :], [qk_], ["pS%d" % half],
                       inc=(hh == 3))
                act(qT[:, half * 4:half * 4 + 4, :], pS[half][:64, :].rearrange("p (h t) -> p h t", h=4), AF.Copy,
                    ["pS%d" % half], ["qT"])
            keys = []
            if isx:
                nct = LC // 128
                if tt - 1 >= nct:
                    keys.append((tt - 1, 0))
                keys.append((tt, None))
                if tt + 1 < NTL:
                    keys.append((tt + 1, 1))
            keys += [(c, None) for c in range(LC // 128)]
            mb = mixb[nmix % 2]
            mbk = ("mixb", nmix % 2)
            nmix += 1
            for hk in range(2):
                for ki, (kt, mid) in enumerate(keys):
                    pi = cnts["ps"] % 2
                    cnts["ps"] += 1
                    for g in range(4):
                        mm(pG[pi][:, g * 128:(g + 1) * 128], kT_all[:, hk, kt * 128:(kt + 1) * 128], qT[:, hk * 4 + g, :],
                           ["kT_all", "qT"], ["pG%d" % pi], inc=(g == 3))
                    pt = pT[hk * 5 + ki]
                    ptk = ("pT", hk * 5 + ki)
                    act(pt[:], pG[pi][:], AF.Exp, ["pG%d" % pi], [ptk], scale=0.125)
                    if mid is not None:
                        mk = cst[:, C_SWM + mid * 128:C_SWM + (mid + 1) * 128].unsqueeze(1).to_broadcast([128, 4, 128])
                        tt_(pt[:].rearrange("p (g q) -> p g q", g=4), pt[:].rearrange("p (g q) -> p g q", g=4), mk,
                            ALU.mult, [ptk, "cst"], [ptk])
                po = pD[hk]
                for g in range(4):
                    for ki, (kt, mid) in enumerate(keys):
                        mm(po[:, g * 65:(g + 1) * 65], pT[hk * 5 + ki][:, g * 128:(g + 1) * 128], V_all[:, kt, hk, :],
                           [("pT", hk * 5 + ki), "V_all"], ["pD%d" % hk], start=(ki == 0), stop=(ki == len(keys) - 1),
                           inc=(ki == len(keys) - 1 and g == 3))
                pov = po[:, 0:260].rearrange("p (g e) -> p g e", g=4)
                tt_(den[:], pov[:, :, 64], esink[:, hk * 4:hk * 4 + 4], ALU.add, ["pD%d" % hk, "esink"], ["den"])
                P.op("dve", "reciprocal", den[:], den[:], reads=["den"], writes=["den"])
                tt_(mb[:, hk * 256:(hk + 1) * 256].rearrange("p (g e) -> p g e", g=4), pov[:, :, 0:64],
                    den[:].unsqueeze(2).to_broadcast([128, 4, 64]), ALU.mult, ["pD%d" % hk, "den"], [mbk])
            P.dma("sp", MIX[r0:r0 + 128, 256:768], mb[:], reads=[mbk], writes=[("MIX", "b", tt)])
        P.barrier()
        es.close()

    def run_interleaved(gens):
        gens = list(gens)
        while gens:
            for g in list(gens):
                try:
                    next(g)
                except StopIteration:
                    gens.remove(g)

    def chunk_order(d):
        nc_c = LC // 64
        if d == 0:
            return list(range(NCH))
        return list(range(nc_c - 1, -1, -1)) + list(range(NCH - 1, nc_c - 1, -1))

    def gla(l, do_ctx, extra=()):
        es = ExitStack()
        P.es = es
        upa = P.sb([17, 2, 128], F32, "upa")
        P.dma("sp", upa[:], gla_up[l].rearrange("d r c -> r d c"), writes=["upa"])
        gng = P.sb([128, 256], F32, "gng")
        P.dma("sp", gng[:], gla_g[l].partition_broadcast(128), writes=["gng"])
        Sx = [P.sb([128, 256], F32, "Sx%d" % d) for d in range(2)]
        for d in range(2):
            P.op("dve", "memset", Sx[d][:], 0.0, writes=[("Sx", d)])
        triI = [cst[0:64, C_TRI + d * 64:C_TRI + (d + 1) * 64] for d in range(2)]
        incl = [cst[0:64, C_M01 + d * 64:C_M01 + (d + 1) * 64] for d in range(2)]
        gbd = cst[:, C_GBD:C_GBD + 256]
        pb = {0: (pG[0], "pG0", pU[0], "pU0", pD[0], "pD0"), 1: (pG[1], "pG1", pU[1], "pU1", pD[1], "pD1")}

        def unit(d):
            pa, pak, pbb, pbk, pc_, pck = pb[d]
            tl = {}
            for nm, shp in (("qT", [128, 64]), ("kT", [128, 64]), ("zT", [17, 64]), ("kM", [64, 128]), ("vM", [64, 256])
                            ):
                tl[nm] = [P.sb(shp, F32, "g%s%d_%d" % (nm, d, i)) for i in range(2)]
            for i in range(2):
                P.op("dve", "memset", tl["zT"][i][:], 1.0, writes=[("zT", d, i)])
            sp_ = P.sb([64, 128], F32, "gsp%d" % d)
            ebT = P.sb([128, 64], F32, "gebT%d" % d)
            enbT = P.sb([128, 64], F32, "genbT%d" % d)
            enbM = P.sb([64, 128], F32, "genbM%d" % d)
            qs = P.sb([128, 64], F32, "gqs%d" % d)
            kmk = P.sb([128, 4, 64], F32, "gkmk%d" % d)
            KtM = P.sb([64, 128], F32, "gKtM%d" % d)
            att = P.sb([64, 4, 64], F32, "gatt%d" % d)
            tmpS = P.sb([128, 256], F32, "gtmpS%d" % d)
            osb = P.sb([64, 4, 64], F32, "gosb%d" % d)
            osq = P.sb([64, 4, 64], F32, "gosq%d" % d)
            ssq = P.sb([64, 4], F32, "gssq%d" % d)
            sga = P.sb([64, 256], F32, "gsga%d" % d)
            K = lambda s: (s, d)
            for ui, c in enumerate(chunk_order(d)):
                i2 = ui % 2
                r0 = c * 64
                isx = r0 >= LC
                KB = lambda s: (s, d, i2)
                P.dma("sp", tl["qT"][i2][:], PF[0:128, r0:r0 + 64], reads=[("PF", 0)], writes=[KB("qT")])
                P.dma("sp", tl["kT"][i2][:], PF[128:256, r0:r0 + 64], reads=[("PF", 0)], writes=[KB("kT")])
                P.dma("sp", tl["zT"][i2][0:16, :], PF[768 + 16 * d:784 + 16 * d, r0:r0 + 64], reads=[("PF", 0)],
                      writes=[KB("zT")])
                P.dma("sp", tl["kM"][i2][:], PM[r0:r0 + 64, 128:256], reads=[("PM", 0)], writes=[KB("kM")])
                P.dma("sp", tl["vM"][i2][:], PM[r0:r0 + 64, 256:512], reads=[("PM", 0)], writes=[KB("vM")])
                qT, kT, zT, kM, vM = (tl[n_][i2] for n_ in ("qT", "kT", "zT", "kM", "vM"))
                yield
                mm(pa[:64, 0:128], zT[:], upa[:, d, :], [KB("zT"), "upa"], [pak])
                act(sp_[:], pa[:64, 0:128], AF.Exp, [pak], [K("sp")], scale=-1.0)
                act(sp_[:], sp_[:], AF.Ln, [K("sp"), "kc"], [K("sp")], bias=kc[:64, 1:2])
                yield
                mm(pa[:, 128:192], sp_[:], triI[d], [K("sp"), "cst"], [pak], inc=False)
                mm(pa[:64, 256:384], triI[d], sp_[:], [K("sp"), "cst"], [pak])
                act(ebT[:], pa[:, 128:192], AF.Exp, [pak], [K("ebT")], scale=1.0 / 16)
                act(enbT[:], pa[:, 128:192], AF.Exp, [pak], [K("enbT")], scale=-1.0 / 16)
                act(enbM[:], pa[:64, 256:384], AF.Exp, [pak], [K("enbM")], scale=-1.0 / 16)
                yield
                stt(qs[:], qT[:], 32.0 ** -0.5, ebT[:], ALU.mult, ALU.mult, [KB("qT"), K("ebT")], [K("qs")])
                for h in range(4):
                    stt(kmk[:, h, :], kT[:], cst[:, C_GHM + h:C_GHM + h + 1], enbT[:], ALU.mult, ALU.mult,
                        [KB("kT"), K("enbT"), "cst"], [K("kmk")])
                tt_(KtM[:], kM[:], enbM[:], ALU.mult, [KB("kM"), K("enbM")], [K("KtM")])
                yield
                for h in range(4):
                    mm(pbb[:64, h * 64:(h + 1) * 64], kmk[:, h, :], qs[:], [K("kmk"), K("qs")], [pbk], inc=(h == 3))
                tt_(att[:], pbb[:64, 0:256].rearrange("p (h t) -> p h t", h=4),
                    incl[d].unsqueeze(1).to_broadcast([64, 4, 64]), ALU.mult, [pbk, "cst"], [K("att")])
                yield
                for h in range(4):
                    mm(pc_[:64, h * 64:(h + 1) * 64], att[:, h, :], vM[:, h * 64:(h + 1) * 64], [K("att"), KB("vM")], [pck],
                       start=True, stop=False, inc=False)
                    mm(pc_[:64, h * 64:(h + 1) * 64], qs[:], Sx[d][:, h * 64:(h + 1) * 64], [K("qs"), ("Sx", d)], [pck],
                       start=False, stop=True, inc=(h == 3))
                mm(pbb[:, 256:512], KtM[:], vM[:], [K("KtM"), KB("vM")], [pbk])
                yield
                tt_(tmpS[:], pbb[:, 256:512], gbd, ALU.mult, [pbk, "cst"], [K("tmpS")])
                tt_(tmpS[:], tmpS[:], Sx[d][:], ALU.add, [K("tmpS"), ("Sx", d)], [K("tmpS")])
                last = 63 if d == 0 else 0
                ts_(Sx[d][:], tmpS[:], ebT[:, last:last + 1], None, ALU.mult, None, [K("tmpS"), K("ebT")], [("Sx", d)])
                P.op("dve", "tensor_copy", osb[:].rearrange("p h e -> p (h e)"), pc_[:64, 0:256], reads=[pck],
                     writes=[K("osb")])
                P.dma("sp", OGD[d, r0:r0 + 64, :], osb[:].rearrange("p h e -> p (h e)"), reads=[K("osb")],
                      writes=[("OGD", d, c)])
                yield

        run_interleaved([unit(0), unit(1)] + list(extra))
        of_ = [P.sb([128, 4, 64], F32, "gof%d" % i) for i in range(2)]
        ob_ = [P.sb([128, 4, 64], F32, "gob%d" % i) for i in range(2)]
        ga_ = [P.sb([128, 256], F32, "gga%d" % i) for i in range(2)]
        fsq = P.sb([128, 4, 64], F32, "gfsq")
        fss = P.sb([128, 4], F32, "gfss")
        ni = 0
        for tt in range(NTL):
            r0 = tt * 128
            if r0 < LC and not do_ctx:
                continue
            i2 = ni % 2
            ni += 1
            P.dma("sp", of_[i2][:].rearrange("p h e -> p (h e)"), OGD[0, r0:r0 + 128, :],
                  reads=[("OGD", 0, 2 * tt), ("OGD", 0, 2 * tt + 1)], writes=[("gof", i2)])
            P.dma("sp", ob_[i2][:].rearrange("p h e -> p (h e)"), OGD[1, r0:r0 + 128, :],
                  reads=[("OGD", 1, 2 * tt), ("OGD", 1, 2 * tt + 1)], writes=[("gob", i2)])
            P.dma("sp", ga_[i2][:], PM[r0:r0 + 128, 512:768], reads=[("PM", 0)], writes=[("gga", i2)])
            o = of_[i2]
            ok = ("gof", i2)
            ov = o[:].rearrange("p h e -> p (h e)")
            tt_(o[:], o[:], ob_[i2][:], ALU.add, [ok, ("gob", i2)], [ok])
            tt_(fsq[:], o[:], o[:], ALU.mult, [ok], ["gfsq"])
            P.op("dve", "tensor_reduce", fss[:], fsq[:], AX.X, ALU.add, reads=["gfsq"], writes=["gfss"])
            act(fss[:], fss[:], AF.Sqrt, ["gfss", "kc"], ["gfss"], scale=1.0 / 64, bias=kc[:, 3:4])
            P.op("dve", "reciprocal", fss[:], fss[:], reads=["gfss"], writes=["gfss"])
            tt_(o[:], o[:], fss[:].unsqueeze(2).to_broadcast([128, 4, 64]), ALU.mult, [ok, "gfss"], [ok])
            tt_(ov, ov, gng[:], ALU.mult, [ok, "gng"], [ok])
            act(ga_[i2][:], ga_[i2][:], AF.Silu, [("gga", i2)], [("gga", i2)])
            tt_(ov, ov, ga_[i2][:], ALU.mult, [ok, ("gga", i2)], [ok])
            P.dma("sp", MIX[r0:r0 + 128, 0:256], ov, reads=[ok], writes=[("MIX", "a", tt)])
        P.barrier()
        es.close()

    def rwkv_prep(l):
        es = ExitStack()
        P.es = es
        run_interleaved([rwkv_prep_gen(l)])
        P.barrier()
        es.close()

    def rwkv_prep_gen(l):
        rv = P.sb([128, 2, 7], F32, "rv")
        P.dma("sp", rv[:], rw_vec[:, l], writes=["rv"])
        nw0 = P.sb([128, 2, 2], F32, "nw0")
        ts_(nw0[:], rv[:, :, 3:5], -1.0, None, ALU.mult, None, ["rv"], ["nw0"])
        mu = P.sb([128, 11], F32, "mu")
        P.dma("sp", mu[:], rw_mu[:, l], writes=["mu"])
        wup = P.sb([64, 2, 256], F32, "wup")
        P.dma("sp", wup[:], rw_wup[l].rearrange("d r c -> r d c"), writes=["wup"])
        aup = P.sb([64, 2, 256], F32, "aup")
        P.dma("sp", aup[:], rw_aup[l].rearrange("d r c -> r d c"), writes=["aup"])
        bo = cst[:, C_BO:C_BO + 128]
        NB = 16
        bufs = [P.sb([128, TG + 2], F32, "rb%d" % i) for i in range(NB)]
        fbufs = [P.sb([128, TG + 2], F32, "rf%d" % i) for i in range(13)]
        nb = [0]

        def newb():
            i = nb[0] % NB
            nb[0] += 1
            return bufs[i], ("rb", i)

        R0 = 1568
        srcs = [(R0 + 128 * j, 128) for j in range(6)] + [(R0 + 768, 64), (R0 + 832, 64), (R0 + 896, 64), (R0 + 960, 64),
                                                          (R0 + 1024, 128)]
        for (st, t0, n) in groups:
            off = toff(st, t0)
            seq_lo = 0 if st == "c" else LC
            seq_hi = LC if st == "c" else T
            lo = max(off - 1, seq_lo)
            hi = min(off + n + 1, seq_hi)
            f = []
            for j, (r0, nr) in enumerate(srcs):
                ld, ldk = newb()
                P.op("dve", "memset", ld[:, 0:n + 2], 0.0, writes=[ldk])
                P.dma("sp", ld[:nr, lo - off + 1:hi - off + 1], PF[r0:r0 + nr, lo:hi], reads=[("PF", 0)], writes=[ldk])
                sh, shk = newb()
                tt_(sh[:nr, :n], ld[:nr, 0:n], ld[:nr, 2:n + 2], ALU.add, [ldk], [shk])
                stt(sh[:nr, :n], sh[:nr, :n], 0.5, ld[:nr, 1:n + 1], ALU.mult, ALU.subtract, [shk, ldk], [shk])
                fo, fok = fbufs[j], ("rf", j)
                stt(fo[:nr, :n], sh[:nr, :n], mu[:nr, j:j + 1], ld[:nr, 1:n + 1], ALU.mult, ALU.add, [shk, ldk, "mu"], [fok])
                f.append((fo, fok))
                yield
            rr, kk_, vv_ = f[0:2], f[2:4], f[4:6]
            zw, za, zg = f[6:8], f[8:10], f[10]
            for pc in range(2):
                P.dma("sp", RF[0 + pc, :, off:off + n], rr[pc][0][:, :n], reads=[rr[pc][1]], writes=[("RF", off)])
                P.dma("sp", RF[2 + pc, :, off:off + n], vv_[pc][0][:, :n], reads=[vv_[pc][1]], writes=[("RF", off)])
            kkn = []
            for pc in range(2):
                k_, kk2 = kk_[pc]
                sq, sqk = newb()
                act(sq[:, :n], k_[:, :n], AF.Square, [kk2, "rv"], [sqk], scale=rv[:, pc, 0:1])
                mm(pS[0][:, :n], bo, sq[:, :n], ["cst", sqk], ["pS0"])
                act(sq[:, :n], pS[0][:, :n], AF.Sqrt, ["pS0"], [sqk])
                ts_(sq[:, :n], sq[:, :n], 1e-12, None, ALU.max, None, [sqk], [sqk])
                P.op("dve", "reciprocal", sq[:, :n], sq[:, :n], reads=[sqk], writes=[sqk])
                kn, knk = fbufs[11 + pc], ("rf", 11 + pc)
                stt(kn[:, :n], k_[:, :n], rv[:, pc, 0:1], sq[:, :n], ALU.mult, ALU.mult, [kk2, "rv", sqk], [knk])
                P.dma("sp", RF[4 + pc, :, off:off + n], kn[:, :n], reads=[knk], writes=[("RF", off)])
                kkn.append((kn, knk))
                yield
            ksum = [None, None]
            for d in range(2):
                th, thk = newb()
                act(th[:64, :n], zw[d][0][:64, :n], AF.Tanh, [zw[d][1]], [thk])
                for pc in range(2):
                    mm(pS[1][:, :n], wup[:, d, pc * 128:(pc + 1) * 128], th[:64, :n], ["wup", thk], ["pS1"])
                    ew, ewk = newb()
                    act(ew[:, :n], pS[1][:, :n], AF.Exp, ["pS1", "nw0"], [ewk], scale=-1.0, bias=nw0[:, pc, d:d + 1])
                    act(ew[:, :n], ew[:, :n], AF.Ln, [ewk, "kc"], [ewk], bias=kc[:, 1:2])
                    act(ew[:, :n], ew[:, :n], AF.Exp, [ewk, "kc"], [ewk], scale=-1.0, bias=kc[:, 2:3])
                    P.dma("sp", RF[10 + 6 * d + pc, :, off:off + n], ew[:, :n], reads=[ewk], writes=[("RF", off)])
                    mm(pS[0][:, :n], aup[:, d, pc * 128:(pc + 1) * 128], za[d][0][:64, :n], ["aup", za[d][1]], ["pS0"])
                    a_, ak = newb()
                    act(a_[:, :n], pS[0][:, :n], AF.Sigmoid, ["pS0", "rv"], [ak], bias=rv[:, pc, 5 + d:6 + d])
                    bd, bdk = newb()
                    tt_(bd[:, :n], a_[:, :n], kkn[pc][0][:, :n], ALU.mult, [ak, kkn[pc][1]], [bdk])
                    P.dma("sp", RF[8 + 6 * d + pc, :, off:off + n], bd[:, :n], reads=[bdk], writes=[("RF", off)])
                    ts_(a_[:, :n], a_[:, :n], 1.0, rv[:, pc, 1:2], ALU.subtract, ALU.mult, [ak, "rv"], [ak])
                    kd, kdk = newb()
                    stt(kd[:, :n], a_[:, :n], 1.0, kk_[pc][0][:, :n], ALU.add, ALU.mult, [ak, kk_[pc][1]], [kdk])
                    P.dma("sp", RF[6 + 6 * d + pc, :, off:off + n], kd[:, :n], reads=[kdk], writes=[("RF", off)])
                    if d == 0:
                        ksum[pc] = (kd, kdk)
                    else:
                        kf, kfk = ksum[pc]
                        tt_(kd[:, :n], kd[:, :n], kf[:, :n], ALU.add, [kdk, kfk], [kdk])
                        ts_(kd[:, :n], kd[:, :n], 0.5, rv[:, pc, 2:3], ALU.mult, ALU.mult, [kdk, "rv"], [kdk])
                        tt_(kd[:, :n], kd[:, :n], rr[pc][0][:, :n], ALU.mult, [kdk, rr[pc][1]], [kdk])
                        P.dma("sp", RF[18 + pc, :, off:off + n], kd[:, :n], reads=[kdk], writes=[("RF", off)])
                    yield
            sz, szk = newb()
            act(sz[:, :n], zg[0][:, :n], AF.Sigmoid, [zg[1]], [szk])
            P.dma("sp", RF[20, :, off:off + n], sz[:, :n], reads=[szk], writes=[("RF", off)])
            yield

    def rwkv_scan(l):
        import os as _os
        _rwcut = int(_os.environ.get('RW_CUT', '99'))
        es = ExitStack()
        P.es = es
        tri = [cst[0:64, C_TRI + i * 64:C_TRI + (i + 1) * 64] for i in range(4)]
        m01 = [cst[0:64, C_M01 + i * 64:C_M01 + (i + 1) * 64] for i in range(4)]
        id64 = cst[0:64, C_ID:C_ID + 64]
        Hx = [[P.sb([128, 256], F32, "Hx%d%d" % (d, pc)) for pc in range(2)] for d in range(2)]
        for d in range(2):
            for pc in range(2):
                P.op("dve", "memset", Hx[d][pc][:], 0.0, writes=[("Hx", d, pc)])
        bc4 = lambda m: m.unsqueeze(1).to_broadcast([64, 4, 64])
        banks = {"A": (pG[0], "pG0"), "B": (pG[1], "pG1"), "C": (pU[0], "pU0"), "D": (pU[1], "pU1"),
                 "E": (pD[0], "pD0"), "F": (pD[1], "pD1"), "G": (pS[0], "pS0"), "H": (pS[1], "pS1")}

        def unit(d):
            K = lambda s: (s, d)
            cm = [P.sb([128, 6, 64], F32, "cm%d_%d" % (d, i)) for i in range(2)]
            dm = [P.sb([128, 6, 64], F32, "dm%d_%d" % (d, i)) for i in range(2)]
            ewM = P.sb([64, 256], F32, "ewM%d" % d)
            vM = P.sb([64, 256], F32, "rvM%d" % d)
            Pd = P.sb([128, 2, 3, 64], F32, "Pd%d" % d)
            fT = P.sb([128, 2, 4, 64], F32, "fT%d" % d)
            mT_ = P.sb([128, 2, 2, 2, 64], F32, "mK%d" % d)
            KBM = P.sb([64, 2, 256], F32, "KBM%d" % d)
            sc = P.sb([64, 5, 4, 64], F32, "sc%d" % d)
            Tt = P.sb([64, 4, 64], F32, "Tt%d" % d)
            TtT = P.sb([64, 4, 64], F32, "TtT%d" % d)
            Nn = [P.sb([64, 2, 4, 64], F32, "Nn%d_%d" % (d, i)) for i in range(2)]
            Xs = P.sb([64, 4, 64], F32, "Xs%d" % d)
            Us = P.sb([64, 4, 64], F32, "Us%d" % d)
            Ys = P.sb([64, 256], F32, "Ys%d" % d)
            tmpH = P.sb([128, 256], F32, "tmpH%d" % d)
            phys = [(pG[d], "pG%d" % d), (pU[d], "pU%d" % d), (pD[d], "pD%d" % d), (pS[d], "pS%d" % d)]
            bA, kA = phys[0]; bB, kB = phys[1]; bC, kC = phys[2]; bD, kD = phys[3]
            bE, kE = phys[0]; bF, kF = phys[1]; bG, kG = phys[2]; bH, kH = phys[3]
            v4 = lambda ap: ap.rearrange("p (h t) -> p h t", h=4)
            for ui, c in enumerate(chunk_order(d)):
                i2 = ui % 2
                r0 = c * 64
                KB = lambda s: (s, d, i2)
                P.dma("sp", cm[i2][:], RF[0:6, :, r0:r0 + 64].rearrange("j p t -> p j t"), reads=[("RF", 0)],
                      writes=[KB("cm")])
                P.dma("sp", dm[i2][:], RF[6 + 6 * d:12 + 6 * d, :, r0:r0 + 64].rearrange("j p t -> p j t"),
                      reads=[("RF", 0)], writes=[KB("dm")])
                cmt, dmt = cm[i2], dm[i2]
                yield
                if _rwcut <= 1:
                    continue
                for pc in range(2):
                    tr(bG[:64, pc * 128:(pc + 1) * 128], dmt[:, 4 + pc, :], [KB("dm")], [kG], inc=False)
                    tr(bG[:64, 256 + pc * 128:256 + (pc + 1) * 128], cmt[:, 2 + pc, :], [KB("cm")], [kG], inc=(pc == 1))
                P.op("dve", "tensor_copy", ewM[:], bG[:64, 0:256], reads=[kG], writes=[K("ewM")])
                P.op("dve", "tensor_copy", vM[:], bG[:64, 256:512], reads=[kG], writes=[K("vM")])
                yield
                if _rwcut <= 2:
                    continue
                for pc in range(2):
                    for ie in range(2):
                        mm(bH[:, (pc * 2 + ie) * 64:(pc * 2 + ie + 1) * 64], ewM[:, pc * 128:(pc + 1) * 128], tri[2 * ie + d],
                           [K("ewM"), "cst"], [kH], inc=(pc == 1 and ie == 1))
                lv = bH[:, 0:256].rearrange("p (c i t) -> p c i t", c=2, i=2)
                act(Pd[:, :, 0:2, :], lv, AF.Exp, [kH], [K("Pd")])
                act(Pd[:, :, 2, :], lv[:, :, 0, :], AF.Exp, [kH], [K("Pd")], scale=-1.0)
                yield
                if _rwcut <= 3:
                    continue
                tt_(fT[:, :, 0, :], cmt[:, 0:2, :], Pd[:, :, 0, :], ALU.mult, [KB("cm"), K("Pd")], [K("fT")])
                tt_(fT[:, :, 1, :], cmt[:, 4:6, :], Pd[:, :, 1, :], ALU.mult, [KB("cm"), K("Pd")], [K("fT")])
                tt_(fT[:, :, 2, :], dmt[:, 0:2, :], Pd[:, :, 2, :], ALU.mult, [KB("dm"), K("Pd")], [K("fT")])
                tt_(fT[:, :, 3, :], dmt[:, 2:4, :], Pd[:, :, 2, :], ALU.mult, [KB("dm"), K("Pd")], [K("fT")])
                for hh in range(2):
                    ts_(mT_[:, :, :, hh, :], fT[:, :, 2:4, :], cst[:, C_RHM + hh:C_RHM + hh + 1], None, ALU.mult, None,
                        [K("fT"), "cst"], [K("mK")])
                yield
                if _rwcut <= 4:
                    continue
                for pc in range(2):
                    tr(bG[:64, pc * 128:(pc + 1) * 128], fT[:, pc, 2, :], [K("fT")], [kG], inc=False)
                    tr(bG[:64, 256 + pc * 128:256 + (pc + 1) * 128], fT[:, pc, 3, :], [K("fT")], [kG], inc=(pc == 1))
                act(KBM[:, 0, :], bG[:64, 0:256], AF.Copy, [kG], [K("KBM")])
                act(KBM[:, 1, :], bG[:64, 256:512], AF.Copy, [kG], [K("KBM")], scale=-1.0)
                for h in range(4):
                    pc, hh = h // 2, h % 2
                    KdTm, BdTm = mT_[:, pc, 0, hh, :], mT_[:, pc, 1, hh, :]
                    KKeT, RpT = fT[:, pc, 1, :], fT[:, pc, 0, :]
                    rd_ = [K("mK"), K("fT")]
                    mm(bA[:64, h * 64:(h + 1) * 64], KdTm, KKeT, rd_, [kA], inc=False)
                    mm(bA[:64, 256 + h * 64:256 + (h + 1) * 64], BdTm, KKeT, rd_, [kA], inc=(h == 3))
                    mm(bB[:64, h * 64:(h + 1) * 64], KKeT, BdTm, rd_, [kB], inc=False)
                    mm(bB[:64, 256 + h * 64:256 + (h + 1) * 64], KdTm, RpT, rd_, [kB], inc=(h == 3))
                    mm(bC[:64, h * 64:(h + 1) * 64], BdTm, RpT, rd_, [kC], inc=(h == 3))
                yield
                if _rwcut <= 5:
                    continue
                tt_(sc[:, 0], v4(bA[:64, 0:256]), bc4(m01[2 + d]), ALU.mult, [kA, "cst"], [K("sc0")])
                tt_(sc[:, 1], v4(bA[:64, 256:512]), bc4(m01[2 + d]), ALU.mult, [kA, "cst"], [K("sc1")])
                tt_(sc[:, 2], v4(bB[:64, 0:256]), bc4(m01[3 - d]), ALU.mult, [kB, "cst"], [K("sc2")])
                tt_(sc[:, 3], v4(bB[:64, 256:512]), bc4(m01[d]), ALU.mult, [kB, "cst"], [K("sc3")])
                stt(sc[:, 4], v4(bC[:64, 0:256]), -1.0, bc4(m01[d]), ALU.mult, ALU.mult, [kC, "cst"], [K("sc4")])
                stt(Tt[:], sc[:, 1], -1.0, bc4(id64), ALU.mult, ALU.add, [K("sc1"), "cst"], [K("Tt")])
                stt(TtT[:], sc[:, 2], -1.0, bc4(id64), ALU.mult, ALU.add, [K("sc2"), "cst"], [K("TtT")])
                yield
                if _rwcut <= 6:
                    continue
                Ncur, NTcur, nk = sc[:, 1], sc[:, 2], [K("sc1"), K("sc2")]
                for lev in range(5):
                    lastl = lev == 4
                    for h in range(4):
                        mm(bD[:64, h * 64:(h + 1) * 64], NTcur[:, h, :], Ncur[:, h, :], nk, [kD], inc=(lastl and h == 3))
                        if not lastl:
                            mm(bD[:64, 256 + h * 64:256 + (h + 1) * 64], Ncur[:, h, :], NTcur[:, h, :], nk, [kD],
                               inc=(h == 3))
                    nn = Nn[lev % 2]
                    nnk = (K("Nn"), lev % 2)
                    if lastl:
                        act(nn[:, 0], v4(bD[:64, 0:256]), AF.Copy, [kD], [nnk])
                    else:
                        act(nn[:].rearrange("p a h t -> p (a h) t"), bD[:64, :].rearrange("p (a t) -> p a t", a=8),
                            AF.Copy, [kD], [nnk])
                    yield
                    for h in range(4):
                        mm(bE[:64, h * 64:(h + 1) * 64], TtT[:, h, :], nn[:, 0, h, :], [K("TtT"), nnk], [kE],
                           inc=(lastl and h == 3))
                        if not lastl:
                            mm(bE[:64, 256 + h * 64:256 + (h + 1) * 64], nn[:, 0, h, :], TtT[:, h, :], [K("TtT"), nnk], [kE],
                               inc=(h == 3))
                    tt_(Tt[:], Tt[:], v4(bE[:64, 0:256]), ALU.add, [K("Tt"), kE], [K("Tt")])
                    if not lastl:
                        tt_(TtT[:], TtT[:], v4(bE[:64, 256:512]), ALU.add, [K("TtT"), kE], [K("TtT")])
                    Ncur, NTcur, nk = nn[:, 0], nn[:, 1], [nnk]
                    yield
                for h in range(4):
                    pc = h // 2
                    mm(bC[:64, 256 + h * 64:256 + (h + 1) * 64], fT[:, pc, 1, :], Hx[d][pc][:, h * 64:(h + 1) * 64],
                       [K("fT"), ("Hx", d, pc)], [kC], start=True, stop=False, inc=False)
                    mm(bC[:64, 256 + h * 64:256 + (h + 1) * 64], sc[:, 0, h, :], vM[:, h * 64:(h + 1) * 64],
                       [K("sc0"), K("vM")], [kC], start=False, stop=True, inc=(h == 3))
                P.op("dve", "tensor_copy", Xs[:], v4(bC[:64, 256:512]), reads=[kC], writes=[K("Xs")])
                yield
                if _rwcut <= 7:
                    continue
                for h in range(4):
                    mm(bF[:64, h * 64:(h + 1) * 64], Tt[:, h, :], Xs[:, h, :], [K("Tt"), K("Xs")], [kF], inc=(h == 3))
                act(Us[:], v4(bF[:64, 0:256]), AF.Copy, [kF], [K("Us")])
                yield
                if _rwcut <= 8:
                    continue
                for h in range(4):
                    pc = h // 2
                    o_ = bF[:64, 256 + h * 64:256 + (h + 1) * 64]
                    mm(o_, fT[:, pc, 0, :], Hx[d][pc][:, h * 64:(h + 1) * 64], [K("fT"), ("Hx", d, pc)], [kF],
                       start=True, stop=False, inc=False)
                    mm(o_, sc[:, 3, h, :], vM[:, h * 64:(h + 1) * 64], [K("sc3"), K("vM")], [kF], start=False, stop=False,
                       inc=False)
                    mm(o_, sc[:, 4, h, :], Us[:, h, :], [K("sc4"), K("Us")], [kF], start=False, stop=True, inc=(h == 3))
                P.op("dve", "tensor_copy", Ys[:], bF[:64, 256:512], reads=[kF], writes=[K("Ys")])
                P.dma("sp", YD[d, r0:r0 + 64, :], Ys[:], reads=[K("Ys")], writes=[("YD", d, c)])
                lastc = 63 if d == 0 else 0
                for pc in range(2):
                    o_ = bH[:, 256:512] if pc == 0 else bG[:, 0:256]
                    ok_ = kH if pc == 0 else kG
                    mm(o_, KBM[:, 0, pc * 128:(pc + 1) * 128], vM[:], [K("KBM"), K("vM")], [ok_], start=True, stop=False,
                       inc=False)
                    mm(o_, KBM[:, 1, pc * 128:(pc + 1) * 128], Us[:].rearrange("p h e -> p (h e)"), [K("KBM"), K("Us")],
                       [ok_], start=False, stop=True, inc=True)
                    tt_(tmpH[:], o_, cst[:, C_RBD + pc * 256:C_RBD + (pc + 1) * 256], ALU.mult, [ok_, "cst"], [K("tmpH")])
                    tt_(tmpH[:], tmpH[:], Hx[d][pc][:], ALU.add, [K("tmpH"), ("Hx", d, pc)], [K("tmpH")])
                    ts_(Hx[d][pc][:], tmpH[:], Pd[:, pc, 0, lastc:lastc + 1], None, ALU.mult, None, [K("tmpH"), K("Pd")],
                        [("Hx", d, pc)])
                yield
                if _rwcut <= 9:
                    continue

        run_interleaved([unit(0), unit(1)])
        P.barrier()
        es.close()

    def rwkv_final(l, do_ctx):
        es = ExitStack()
        P.es = es
        gup = P.sb([128, 256], F32, "gup")
        P.dma("sp", gup[:], rw_gup[l], writes=["gup"])
        gnb = P.sb([128, 2, 256], F32, "gnb")
        for i in range(2):
            P.dma("sp", gnb[:, i, :], rw_gn[l, i].partition_broadcast(128), writes=["gnb"])
        yf = [P.sb([128, 4, 64], F32, "yf%d" % i) for i in range(2)]
        yb = [P.sb([128, 4, 64], F32, "yb%d" % i) for i in range(2)]
        fp = [P.sb([128, 5, 128], F32, "fp%d" % i) for i in range(2)]
        tm = P.sb([128, 2, 256], F32, "ftm")
        ysq = P.sb([128, 4, 64], F32, "ysq")
        st4 = P.sb([128, 4, 4], F32, "st4")
        gsb = P.sb([128, 256], F32, "gsb")
        ni = 0
        for tt in range(NTL):
            r0 = tt * 128
            if r0 < LC and not do_ctx:
                continue
            i2 = ni % 2
            ni += 1
            P.dma("sp", yf[i2][:].rearrange("p h e -> p (h e)"), YD[0, r0:r0 + 128, :],
                  reads=[("YD", 0, 2 * tt), ("YD", 0, 2 * tt + 1)], writes=[("yf", i2)])
            P.dma("sp", yb[i2][:].rearrange("p h e -> p (h e)"), YD[1, r0:r0 + 128, :],
                  reads=[("YD", 1, 2 * tt), ("YD", 1, 2 * tt + 1)], writes=[("yb", i2)])
            for j, pn in enumerate([18, 19, 2, 3, 20]):
                P.dma("sp", fp[i2][:, j, :], RF[pn, :, r0:r0 + 128], reads=[("RF", 0)], writes=[("fp", i2)])
            y = yf[i2]
            tt_(y[:], y[:], yb[i2][:], ALU.add, [("yf", i2), ("yb", i2)], [("yf", i2)])
            for j in range(4):
                tr(pG[0][:, j * 128:(j + 1) * 128], fp[i2][:, j, :], [("fp", i2)], ["pG0"], inc=(j == 3))
            act(tm[:].rearrange("p a c -> p (a c)"), pG[0][:, :], AF.Copy, ["pG0"], ["ftm"])
            mm(pG[1][:, 0:256], fp[i2][:, 4, :], gup[:], [("fp", i2), "gup"], ["pG1"])
            act(gsb[:], pG[1][:, 0:256], AF.Copy, ["pG1"], ["gsb"])
            P.op("dve", "tensor_reduce", st4[:, 0, :], y[:], AX.X, ALU.add, reads=[("yf", i2)], writes=["st4"])
            tt_(ysq[:], y[:], y[:], ALU.mult, [("yf", i2)], ["ysq"])
            P.op("dve", "tensor_reduce", st4[:, 1, :], ysq[:], AX.X, ALU.add, reads=["ysq"], writes=["st4"])
            P.op("dve", "tensor_reduce", st4[:, 2, :], tm[:, 0, :].rearrange("p (h e) -> p h e", h=4), AX.X, ALU.add,
                 reads=["ftm"], writes=["st4"])
            ts_(st4[:, 0, :], st4[:, 0, :], 1.0 / 64, None, ALU.mult, None, ["st4"], ["st4"])
            tt_(st4[:, 3, :], st4[:, 0, :], st4[:, 0, :], ALU.mult, ["st4"], ["st4"])
            stt(st4[:, 1, :], st4[:, 1, :], 1.0 / 64, st4[:, 3, :], ALU.mult, ALU.subtract, ["st4"], ["st4"])
            act(st4[:, 1, :], st4[:, 1, :], AF.Sqrt, ["st4", "kc"], ["st4"], bias=kc[:, 4:5])
            P.op("dve", "reciprocal", st4[:, 1, :], st4[:, 1, :], reads=["st4"], writes=["st4"])
            tt_(y[:], y[:], st4[:, 0, :].unsqueeze(2).to_broadcast([128, 4, 64]), ALU.subtract, [("yf", i2), "st4"],
                [("yf", i2)])
            tt_(y[:], y[:], st4[:, 1, :].unsqueeze(2).to_broadcast([128, 4, 64]), ALU.mult, [("yf", i2), "st4"],
                [("yf", i2)])
            yv = y[:].rearrange("p h e -> p (h e)")
            tt_(yv, yv, gnb[:, 0, :], ALU.mult, [("yf", i2), "gnb"], [("yf", i2)])
            tt_(yv, yv, gnb[:, 1, :], ALU.add, [("yf", i2), "gnb"], [("yf", i2)])
            tt_(ysq[:], tm[:, 1, :].rearrange("p (h e) -> p h e", h=4),
                st4[:, 2, :].unsqueeze(2).to_broadcast([128, 4, 64]), ALU.mult, ["ftm", "st4"], ["ysq"])
            tt_(y[:], y[:], ysq[:], ALU.add, [("yf", i2), "ysq"], [("yf", i2)])
            tt_(yv, yv, gsb[:], ALU.mult, [("yf", i2), "gsb"], [("yf", i2)])
            P.dma("sp", MIX[r0:r0 + 128, 768:1024], yv, reads=[("yf", i2)], writes=[("MIX", "c", tt)])
        P.barrier()
        es.close()

    def outproj(l, do_ctx):
        es = ExitStack()
        P.es = es
        hT = [P.sb([128, 8, TG], F32, "hT%d" % i) for i in range(2)]
        lt = alloc_ln_tiles()
        vv = lt["vv"]
        mixT = P.sb([128, 8, TG], BF16, "mixT")
        mtile = [P.sb([128, 1024], F32, "mtile%d" % i) for i in range(2)]
        wo = P.sb([128, 8, 1024], BF16, "wo")
        for hf in range(2):
            P.dma("pool", wo[:, :, hf * 512:(hf + 1) * 512],
                  w_out[l][:, hf * 512:(hf + 1) * 512].rearrange("(k p) j -> p k j", p=128), writes=["wo"])
        cur_s = None
        nm = 0
        for (st, t0, n) in groups:
            if st == "c" and not do_ctx:
                continue
            s = 1 if st == "c" else 0
            off = toff(st, t0)
            if s != cur_s:
                mod_scalars(l, 1, s)
                cur_s = s
            gi = cnts["g"]
            cnts["g"] += 1
            h = hT[gi % 2]
            hk = "hT%d" % (gi % 2)
            P.dma("sp", h[:, :, :n], hsrc(st, t0, n), reads=[("H", st, t0)], writes=[hk])
            for ts in range(n // 128):
                mt = mtile[nm % 2]
                mk = ("mtile", nm % 2)
                nm += 1
                P.dma("sp", mt[:], MIX[off + ts * 128:off + (ts + 1) * 128, :],
                      reads=[kk for kk in list(P.lastw.keys()) if isinstance(kk, tuple) and kk[0] == "MIX"], writes=[mk])
                for half in range(2):
                    pi = half
                    for j in range(4):
                        tr(pG[pi][:, j * 128:(j + 1) * 128], mt[:, (half * 4 + j) * 128:(half * 4 + j + 1) * 128], [mk],
                           ["pG%d" % pi], inc=(j == 3))
                    act(mixT[:, half * 4:half * 4 + 4, ts * 128:(ts + 1) * 128],
                        pG[pi][:, :].rearrange("p (j t) -> p j t", j=4), AF.Copy, ["pG%d" % pi], ["mixT"])
            for c in range(8):
                pi = c % 2
                for k in range(8):
                    mm(pD[pi][:, :n], wo[:, k, c * 128:(c + 1) * 128], mixT[:, k, :n], ["wo", "mixT"], ["pD%d" % pi],
                       start=(k == 0), stop=(k == 7), inc=(k == 7))
                stt(vv[:, c, :n], pD[pi][:, :n], scl[:, 2, c:c + 1], h[:, c, :n], ALU.mult, ALU.add,
                    ["pD%d" % pi, "scl2", hk], [("vv", c)])
            layer_norm_store(lt, st, t0, n, l, 1)
        P.barrier()
        es.close()

    for l in range(L):
        last = l == L - 1
        if USE_WSCRATCH:
            convert_weights(l)
        ffn_sublayer(l, 0, ffn_w["ffn1_wg"][l], ffn_w["ffn1_wu"][l], ffn_w["ffn1_wd"][l])
        if dbg == "ffn1":
            break
        inproj(l)
        if dbg == "inproj":
            break
        if dbg in (None, "swa", "mix"):
            swa(l, not last)
        if dbg is None:
            gla(l, not last, extra=[rwkv_prep_gen(l)])
        elif dbg in ("gla", "mix"):
            gla(l, not last)
        if dbg in (None, "rwkv", "mix"):
            import os as _os
            _rs = int(_os.environ.get("RW_STOP", "9"))
            if dbg is not None:
                rwkv_prep(l)
            if _rs >= 2:
                rwkv_scan(l)
            if _rs >= 3:
                rwkv_final(l, not last)
        if dbg in ("swa", "gla", "rwkv", "mix"):
            break
        outproj(l, not last)
        if dbg == "outproj":
            break
        ffn_sublayer(l, 2, ffn_w["ffn2_wg"][l], ffn_w["ffn2_wu"][l], ffn_w["ffn2_wd"][l], do_ctx=not last)

    es = ExitStack()
    P.es = es
    evs = []
    if dbg in ("swa", "gla", "rwkv", "mix"):
        mixo = dram("mixo", [SEQ, D], kind="ExternalOutput")
        ob = [P.sb([128, 1024], F32, "ob%d" % i) for i in range(2)]
        for tt in range(SEQ // 128):
            r0 = LC + tt * 128
            P.dma("sp", ob[tt % 2][:], MIX[r0:r0 + 128, :], writes=[("ob", tt % 2)])
            evs.append(P.dma("sp", mixo[tt * 128:(tt + 1) * 128, :], ob[tt % 2][:], reads=[("ob", tt % 2)],
                             writes=[("mixo", tt)]))
    ob2 = [P.sb([128, 8, TG], F32, "ob2%d" % i) for i in range(2)]
    for gi, (st, t0, n) in enumerate(groups):
        if st == "c":
            continue
        b = ob2[gi % 2]
        bk = "ob2%d" % (gi % 2)
        P.dma("sp", b[:, :, :n], hsrc(st, t0, n), reads=[("H", st, t0)], writes=[bk])
        evs.append(P.dma("sp", outT[:, :, t0:t0 + n].rearrange("c p t -> p c t"), b[:, :, :n], reads=[bk],
                         writes=[("out", t0)]))
    P.finish("sp", evs)
    es.close()
    es0.close()
    print("program instructions:", P.ninst)
    return nc


def _consts():
    c = np.zeros((128, C_END), np.float32)
    c[:, C_ID:C_ID + 128] = np.eye(128)
    s = np.arange(64)[:, None]
    t = np.arange(64)[None, :]
    c[0:64, C_TRI + 0:C_TRI + 64] = -1.0 * (s <= t)
    c[0:64, C_TRI + 64:C_TRI + 128] = -1.0 * (s >= t)
    c[0:64, C_TRI + 128:C_TRI + 192] = -1.0 * (s < t)
    c[0:64, C_TRI + 192:C_TRI + 256] = -1.0 * (s > t)
    c[0:64, C_M01 + 0:C_M01 + 64] = (s <= t)
    c[0:64, C_M01 + 64:C_M01 + 128] = (s >= t)
    c[0:64, C_M01 + 128:C_M01 + 192] = (s < t)
    c[0:64, C_M01 + 192:C_M01 + 256] = (s > t)
    p = np.arange(128)[:, None]
    col = np.arange(256)[None, :]
    c[:, C_GBD:C_GBD + 256] = (p // 32 == col // 64)
    for h in range(4):
        c[:, C_GHM + h] = (np.arange(128) // 32 == h)
    for pc in range(2):
        c[:, C_RBD + pc * 256:C_RBD + (pc + 1) * 256] = ((2 * pc + p // 64) == col // 64)
    for hh in range(2):
        c[:, C_RHM + hh] = (np.arange(128) // 64 == hh)
    q = np.arange(128)[None, :]
    c[:, C_BO:C_BO + 128] = (p // 64 == q // 64)
    c[:, C_SWM:C_SWM + 128] = (p >= q)
    c[:, C_SWM + 128:C_SWM + 256] = (p <= q)
    c[0:64, C_NI:C_NI + 64] = -np.eye(64)
    return c


def _rope_table(SEQ):
    pos = np.arange(SEQ)
    row = (pos // 64).astype(np.float32)
    col = (pos % 64).astype(np.float32)
    inv = (np.float32(10000.0) ** (-np.arange(16, dtype=np.float32) / np.float32(16))).astype(np.float32)
    tab = np.zeros((SEQ, 2, 32), np.float32)
    for a, pp in enumerate((row, col)):
        ang = (pp[:, None] * inv[None, :]).astype(np.float32)
        tab[:, 0, a * 16:(a + 1) * 16] = np.cos(ang)
        tab[:, 1, a * 16:(a + 1) * 16] = np.sin(ang)
    return tab


def _prep_shared(inp, L, SEQ):
    f = lambda a: np.ascontiguousarray(a, dtype=np.float32)
    m = {}
    m["w_ada"] = f(inp["w_ada"][:L])
    m["b_adaT"] = f(inp["b_ada"][:L].reshape(L, 72, 128).transpose(2, 0, 1))
    ln = np.stack([inp["ln_g"][:L], inp["ln_b"][:L]], axis=2)
    m["lnT"] = f(ln.reshape(L, 3, 2, 8, 128).transpose(4, 0, 1, 2, 3))
    for nm in ("ffn1_wg", "ffn1_wu", "ffn1_wd", "ffn2_wg", "ffn2_wu", "ffn2_wd", "w_in", "w_out"):
        m[nm] = f(inp[nm][:L])
    m["consts"] = _consts()
    m["ropeM"] = _rope_table(SEQ)
    m["gla_up"] = f(np.concatenate([inp["gla_gate_up"][:L], inp["gla_gate_bias"][:L][:, :, None, :]], axis=2))
    m["gla_g"] = f(inp["gla_norm_g"][:L])
    m["sinkB"] = f(np.broadcast_to(inp["swa_sink"][:L][None], (128, L, 8)))
    vecs = np.stack([inp["rwkv_k_k"][:L], inp["rwkv_k_a"][:L], inp["rwkv_r_k"][:L].reshape(L, 256),
                     inp["rwkv_w0"][:L, 0], inp["rwkv_w0"][:L, 1], inp["rwkv_a0"][:L, 0], inp["rwkv_a0"][:L, 1]],
                    axis=-1)
    m["rw_vec"] = f(vecs.reshape(L, 2, 128, 7).transpose(2, 0, 1, 3))
    mu = inp["rwkv_mu"][:L]
    mut = np.zeros((128, L, 11), np.float32)
    for j in range(6):
        mut[:, :, j] = mu[:, 128 * j:128 * (j + 1)].T
    for j, r0 in enumerate((768, 832, 896, 960)):
        mut[:64, :, 6 + j] = mu[:, r0:r0 + 64].T
    mut[:, :, 10] = mu[:, 1024:1152].T
    m["rw_mu"] = mut
    m["rw_wup"] = f(inp["rwkv_w_up"][:L])
    m["rw_aup"] = f(inp["rwkv_a_up"][:L])
    m["rw_gup"] = f(inp["rwkv_g_up"][:L])
    m["rw_gn"] = f(np.stack([inp["rwkv_gn_g"][:L], inp["rwkv_gn_b"][:L]], axis=1))
    return m


def _prep_core(inp, b):
    f = lambda a: np.ascontiguousarray(a, dtype=np.float32)
    m = {}
    m["xT"] = f(inp["x"][b].T.reshape(8, 128, -1))
    m["cxT"] = f(inp["ctx"][b].T.reshape(8, 128, -1))
    cc = np.stack([inp["c"][b], inp["c_ctx"]], axis=-1)
    m["ccT"] = f(cc.reshape(8, 128, 2).transpose(1, 0, 2))
    return m


def run(inp, SEQ, LC, DEPTH, ncores, dbg=None, trace=False):
    nc = build(SEQ, LC, DEPTH, dbg=dbg)
    shared = _prep_shared(inp, DEPTH, SEQ)
    in_maps = []
    for b in range(ncores):
        m = dict(shared)
        m.update(_prep_core(inp, b))
        in_maps.append(m)
    res = run_bass_kernel_spmd(nc, in_maps, core_ids=list(range(ncores)), trace=trace)
    outs = [r["outT"].reshape(1024, SEQ).T for r in res.results]
    return np.stack(outs, axis=0), res


def kernel(**inputs):
    inp = {k: np.asarray(v) for k, v in inputs.items()}
    out, _ = run(inp, 4096, 256, 4, 8)
    return np.ascontiguousarray(out.astype(np.float32))
```
